# Optimizing a Trainium2 kernel written in Bass

```python
import math
import jax
import jax.numpy as jnp
from jax import lax
import numpy as np

D_MODEL = 1024
BATCH = 32
SEQ = 2048
DEPTH = 1

NSA_HEADS = 16
NSA_GROUPS = 4
NSA_HPG = NSA_HEADS // NSA_GROUPS
NSA_DK = 64
NSA_DV = 64
CMP_LEN = 32
CMP_STRIDE = 16
CMP_HIDDEN = 256
SEL_BLOCK = 64
SEL_TOPK = 16
WINDOW = 512
MLA_HEADS = 8
MLA_Q_RANK = 256
MLA_KV_RANK = 128
MLA_NOPE = 64
MLA_ROPE = 32
MLA_V = 128
ROPE_THETA = 10000.0
REL_BUCKETS = 32
REL_MAX_DIST = 128
D_FF = 2816
CONV_WIDTH = 3
Q_BLOCK = 128
RMS_EPS = 1e-6
NEG_INF = -1e30

IN_SPLITS = (
    ("nsa_q", NSA_HEADS * NSA_DK),
    ("k_cmp", NSA_GROUPS * NSA_DK),
    ("v_cmp", NSA_GROUPS * NSA_DV),
    ("k_slc", NSA_GROUPS * NSA_DK),
    ("v_slc", NSA_GROUPS * NSA_DV),
    ("k_win", NSA_GROUPS * NSA_DK),
    ("v_win", NSA_GROUPS * NSA_DV),
    ("nsa_gate", NSA_HEADS * 3),
    ("mla_cq", MLA_Q_RANK),
    ("mla_ckv", MLA_KV_RANK),
    ("mla_krope", MLA_ROPE),
    ("merge_a", D_MODEL),
    ("merge_b", D_MODEL),
)
IN_COLS = sum(w for _, w in IN_SPLITS)

kernel_name = "hybrid_nsa_mla_convglu_block"


def rmsnorm(x, g):
    xf = x.astype(jnp.float32)
    y = xf * lax.rsqrt(jnp.mean(xf * xf, axis=-1, keepdims=True) + RMS_EPS)
    return (y * g.astype(jnp.float32)).astype(x.dtype)


def split_columns(z):
    parts, off = {}, 0
    for name, width in IN_SPLITS:
        parts[name] = z[..., off:off + width]
        off += width
    return parts


def t5_bucket(dist):
    n = jnp.maximum(dist, 0)
    exact = REL_BUCKETS // 2
    log_ratio = jnp.log(jnp.maximum(n, exact).astype(jnp.float32) / exact) / math.log(REL_MAX_DIST / exact)
    large = jnp.minimum(exact + (log_ratio * (REL_BUCKETS - exact)).astype(jnp.int32), REL_BUCKETS - 1)
    return jnp.where(n < exact, n, large)


def compress_blocks(kv, pos_emb, w1, w2):
    B, S, G, d = kv.shape
    nc = (S - CMP_LEN) // CMP_STRIDE + 1
    idx = np.arange(nc)[:, None] * CMP_STRIDE + np.arange(CMP_LEN)[None, :]
    blocks = kv[:, idx] + pos_emb[None, None, :, None, :]
    blocks = blocks.transpose(0, 1, 3, 2, 4).reshape(B, nc, G, CMP_LEN * d)
    return jax.nn.gelu(blocks @ w1) @ w2


def nsa_one(q, kc, vc, ks, vs, kw, vw, gates, rel_table):
    S = q.shape[0]
    nc = kc.shape[0]
    nb = S // SEL_BLOCK
    n_sel = min(SEL_TOPK, nb)
    G, HPG = NSA_GROUPS, NSA_HPG
    scale = NSA_DK ** -0.5
    t = jnp.arange(S)
    qg = q.reshape(S, G, HPG, NSA_DK)
    rel_g = rel_table.reshape(REL_BUCKETS, G, HPG)

    dist_c = t[:, None] - (jnp.arange(nc) * CMP_STRIDE + CMP_LEN - 1)[None, :]
    valid_c = dist_c >= 0
    bias_c = rel_g[t5_bucket(dist_c)].transpose(2, 3, 0, 1)
    logit_c = jnp.einsum("sghd,cgd->ghsc", qg, kc).astype(jnp.float32) * scale + bias_c
    p_c = jax.nn.softmax(jnp.where(valid_c, logit_c, NEG_INF), axis=-1) * valid_c
    o_cmp = jnp.einsum("ghsc,cgd->sghd", p_c.astype(vc.dtype), vc).reshape(S, NSA_HEADS, NSA_DV)

    cs = np.arange(nc) * CMP_STRIDE
    bs = np.arange(nb) * SEL_BLOCK
    overlap = np.clip(np.minimum(cs[:, None] + CMP_LEN, bs[None, :] + SEL_BLOCK)
                      - np.maximum(cs[:, None], bs[None, :]), 0, None) / CMP_LEN
    score = jnp.einsum("ghsc,cj->gsj", p_c, jnp.asarray(overlap, dtype=jnp.float32))
    cur = (t // SEL_BLOCK)[:, None]
    j = jnp.arange(nb)[None, :]
    forced = (j == 0) | (j == cur) | (j == cur - 1)
    score = jnp.where(forced, jnp.inf, jnp.where(j > cur, -jnp.inf, score))
    sel_idx = lax.top_k(score, n_sel)[1]
    ks_blk = ks.reshape(nb, SEL_BLOCK, G, NSA_DK).transpose(2, 0, 1, 3)
    vs_blk = vs.reshape(nb, SEL_BLOCK, G, NSA_DV).transpose(2, 0, 1, 3)
    kw_pad = jnp.pad(kw, ((WINDOW, 0), (0, 0), (0, 0)))
    vw_pad = jnp.pad(vw, ((WINDOW, 0), (0, 0), (0, 0)))
    g_ix = jnp.arange(G)[:, None, None]
    n_win = WINDOW + Q_BLOCK

    def query_block(i):
        s0 = i * Q_BLOCK
        tq = s0 + jnp.arange(Q_BLOCK)
        qb = lax.dynamic_slice_in_dim(qg, s0, Q_BLOCK, 0)
        idx = lax.dynamic_slice_in_dim(sel_idx, s0, Q_BLOCK, 1)
        k_sel = ks_blk[g_ix, idx]
        v_sel = vs_blk[g_ix, idx]
        dist_s = tq[None, :, None, None] - (idx[..., None] * SEL_BLOCK + jnp.arange(SEL_BLOCK))
        bias_s = rel_g[t5_bucket(dist_s), g_ix[..., None]].transpose(0, 4, 1, 2, 3)
        logit_s = jnp.einsum("qghd,gqnkd->ghqnk", qb, k_sel).astype(jnp.float32) * scale + bias_s
        logit_s = jnp.where((dist_s >= 0)[:, None], logit_s, NEG_INF)
        p_s = jax.nn.softmax(logit_s.reshape(G, HPG, Q_BLOCK, n_sel * SEL_BLOCK), axis=-1)
        p_s = p_s.reshape(G, HPG, Q_BLOCK, n_sel, SEL_BLOCK)
        o_s = jnp.einsum("ghqnk,gqnkd->qghd", p_s.astype(v_sel.dtype), v_sel)
        k_w = lax.dynamic_slice_in_dim(kw_pad, s0, n_win, 0)
        v_w = lax.dynamic_slice_in_dim(vw_pad, s0, n_win, 0)
        kpos_w = s0 - WINDOW + jnp.arange(n_win)
        dist_w = tq[:, None] - kpos_w[None, :]
        valid_w = (dist_w >= 0) & (dist_w < WINDOW) & (kpos_w >= 0)[None, :]
        bias_w = rel_g[t5_bucket(dist_w)].transpose(2, 3, 0, 1)
        logit_w = jnp.einsum("qghd,kgd->ghqk", qb, k_w).astype(jnp.float32) * scale + bias_w
        p_w = jax.nn.softmax(jnp.where(valid_w, logit_w, NEG_INF), axis=-1)
        o_w = jnp.einsum("ghqk,kgd->qghd", p_w.astype(v_w.dtype), v_w)
        return o_s, o_w

    o_slc, o_win = lax.map(query_block, jnp.arange(S // Q_BLOCK))
    o_slc = o_slc.reshape(S, NSA_HEADS, NSA_DV)
    o_win = o_win.reshape(S, NSA_HEADS, NSA_DV)
    g = jax.nn.sigmoid(gates)
    out = g[..., 0:1] * o_cmp + g[..., 1:2] * o_slc + g[..., 2:3] * o_win
    return out.reshape(S, NSA_HEADS * NSA_DV)


def rope(x, cos, sin):
    half = x.shape[-1] // 2
    x1, x2 = x[..., :half], x[..., half:]
    return jnp.concatenate([x1 * cos - x2 * sin, x2 * cos + x1 * sin], axis=-1).astype(x.dtype)


def mla_attention(c_q, c_kv, k_rope, positions, q_norm_g, w_uq, kv_norm_g, w_ukv):
    B, S, _ = c_q.shape
    q = (rmsnorm(c_q, q_norm_g) @ w_uq).reshape(B, S, MLA_HEADS, MLA_NOPE + MLA_ROPE)
    kv = (rmsnorm(c_kv, kv_norm_g) @ w_ukv).reshape(B, S, MLA_HEADS, MLA_NOPE + MLA_V)
    q_nope, q_rope = q[..., :MLA_NOPE], q[..., MLA_NOPE:]
    k_nope, v = kv[..., :MLA_NOPE], kv[..., MLA_NOPE:]
    inv_freq = ROPE_THETA ** (-jnp.arange(0, MLA_ROPE, 2, dtype=jnp.float32) / MLA_ROPE)
    ang = positions.astype(jnp.float32)[..., None] * inv_freq
    cos, sin = jnp.cos(ang), jnp.sin(ang)
    q_rope = rope(q_rope, cos[:, :, None], sin[:, :, None])
    k_rope = rope(k_rope, cos, sin)
    scale = (MLA_NOPE + MLA_ROPE) ** -0.5
    k_pos = jnp.arange(S)

    def query_block(i):
        s0 = i * Q_BLOCK
        qn = lax.dynamic_slice_in_dim(q_nope, s0, Q_BLOCK, 1)
        qr = lax.dynamic_slice_in_dim(q_rope, s0, Q_BLOCK, 1)
        logit = (jnp.einsum("bqhd,bkhd->bhqk", qn, k_nope)
                 + jnp.einsum("bqhd,bkd->bhqk", qr, k_rope)).astype(jnp.float32) * scale
        causal = (s0 + jnp.arange(Q_BLOCK))[:, None] >= k_pos[None, :]
        p = jax.nn.softmax(jnp.where(causal, logit, NEG_INF), axis=-1)
        return jnp.einsum("bhqk,bkhd->bqhd", p.astype(v.dtype), v)

    o = lax.map(query_block, jnp.arange(S // Q_BLOCK))
    return o.transpose(1, 0, 2, 3, 4).reshape(B, S, MLA_HEADS * MLA_V)


def hybrid_mixer(h, positions, rel_table, w_in, cmp_pos_k, cmp_w1_k, cmp_w2_k,
                 cmp_pos_v, cmp_w1_v, cmp_w2_v, mla_q_norm_g, mla_w_uq,
                 mla_kv_norm_g, mla_w_ukv, w_o):
    B, S, _ = h.shape
    G = NSA_GROUPS
    p = split_columns(h @ w_in)
    q = p["nsa_q"].reshape(B, S, NSA_HEADS, NSA_DK)
    kc = compress_blocks(p["k_cmp"].reshape(B, S, G, NSA_DK), cmp_pos_k, cmp_w1_k, cmp_w2_k)
    vc = compress_blocks(p["v_cmp"].reshape(B, S, G, NSA_DV), cmp_pos_v, cmp_w1_v, cmp_w2_v)
    ks = p["k_slc"].reshape(B, S, G, NSA_DK)
    vs = p["v_slc"].reshape(B, S, G, NSA_DV)
    kw = p["k_win"].reshape(B, S, G, NSA_DK)
    vw = p["v_win"].reshape(B, S, G, NSA_DV)
    gates = p["nsa_gate"].reshape(B, S, NSA_HEADS, 3)
    o_nsa = lax.map(lambda a: nsa_one(*a, rel_table), (q, kc, vc, ks, vs, kw, vw, gates))
    o_mla = mla_attention(p["mla_cq"], p["mla_ckv"], p["mla_krope"], positions,
                          mla_q_norm_g, mla_w_uq, mla_kv_norm_g, mla_w_ukv)
    y = jax.nn.sigmoid(p["merge_a"]) * o_nsa + jax.nn.sigmoid(p["merge_b"]) * o_mla
    return y @ w_o


def conv_glu_ffn(h, w_gate, w_up, conv_w, conv_b, w_down):
    S = h.shape[1]
    g = h @ w_gate
    g_pad = jnp.pad(g, ((0, 0), (CONV_WIDTH - 1, 0), (0, 0)))
    g_conv = conv_b
    for tap in range(CONV_WIDTH):
        g_conv = g_conv + conv_w[tap] * g_pad[:, tap:tap + S]
    return (jax.nn.silu(g_conv) * (h @ w_up)) @ w_down


def setup_inputs(seed: int = 0) -> dict:
    key = jax.random.key(seed)
    keys = iter(jax.random.split(key, 40))
    L = DEPTH

    def nrm(shape, scale):
        return jax.random.normal(next(keys), shape, jnp.float32) * scale

    def gain(shape):
        return 1.0 + 0.01 * jax.random.normal(next(keys), shape, jnp.float32)

    x = nrm((BATCH, SEQ, D_MODEL), 1.0)
    c = nrm((BATCH, D_MODEL), 1.0)
    offsets = jax.random.randint(next(keys), (BATCH, 1), 0, 4096, dtype=jnp.int32)
    positions = offsets + jnp.arange(SEQ, dtype=jnp.int32)[None, :]
    return {
        "x": x,
        "c": c,
        "positions": positions,
        "rel_bias_table": nrm((REL_BUCKETS, NSA_HEADS), 0.2),
        "ada_w": nrm((L, D_MODEL, 6 * D_MODEL), D_MODEL ** -0.5),
        "ada_b": nrm((L, 6 * D_MODEL), 0.01),
        "norm_mix_g": gain((L, D_MODEL)),
        "w_in": nrm((L, D_MODEL, IN_COLS), D_MODEL ** -0.5),
        "cmp_pos_k": nrm((L, CMP_LEN, NSA_DK), 0.1),
        "cmp_w1_k": nrm((L, CMP_LEN * NSA_DK, CMP_HIDDEN), (CMP_LEN * NSA_DK) ** -0.5),
        "cmp_w2_k": nrm((L, CMP_HIDDEN, NSA_DK), CMP_HIDDEN ** -0.5),
        "cmp_pos_v": nrm((L, CMP_LEN, NSA_DV), 0.1),
        "cmp_w1_v": nrm((L, CMP_LEN * NSA_DV, CMP_HIDDEN), (CMP_LEN * NSA_DV) ** -0.5),
        "cmp_w2_v": nrm((L, CMP_HIDDEN, NSA_DV), CMP_HIDDEN ** -0.5),
        "mla_q_norm_g": gain((L, MLA_Q_RANK)),
        "mla_w_uq": nrm((L, MLA_Q_RANK, MLA_HEADS * (MLA_NOPE + MLA_ROPE)), MLA_Q_RANK ** -0.5),
        "mla_kv_norm_g": gain((L, MLA_KV_RANK)),
        "mla_w_ukv": nrm((L, MLA_KV_RANK, MLA_HEADS * (MLA_NOPE + MLA_V)), MLA_KV_RANK ** -0.5),
        "w_o": nrm((L, D_MODEL, D_MODEL), D_MODEL ** -0.5),
        "norm_ffn_g": gain((L, D_MODEL)),
        "ffn_w_gate": nrm((L, D_MODEL, D_FF), D_MODEL ** -0.5),
        "ffn_w_up": nrm((L, D_MODEL, D_FF), D_MODEL ** -0.5),
        "ffn_conv_w": nrm((L, CONV_WIDTH, D_FF), CONV_WIDTH ** -0.5),
        "ffn_conv_b": nrm((L, D_FF), 0.01),
        "ffn_w_down": nrm((L, D_FF, D_MODEL), D_FF ** -0.5),
        "final_norm_g": gain((D_MODEL,)),
    }


def reference(x, c, positions, rel_bias_table, ada_w, ada_b, norm_mix_g, w_in,
              cmp_pos_k, cmp_w1_k, cmp_w2_k, cmp_pos_v, cmp_w1_v, cmp_w2_v,
              mla_q_norm_g, mla_w_uq, mla_kv_norm_g, mla_w_ukv, w_o,
              norm_ffn_g, ffn_w_gate, ffn_w_up, ffn_conv_w, ffn_conv_b, ffn_w_down,
              final_norm_g):
    cond = jax.nn.silu(c)
    for layer in range(DEPTH):
        mod = (cond @ ada_w[layer] + ada_b[layer])[:, None, :]
        shift_m, scale_m, gate_m, shift_f, scale_f, gate_f = jnp.split(mod, 6, axis=-1)
        h = rmsnorm(x, norm_mix_g[layer]) * (1.0 + scale_m) + shift_m
        x = x + gate_m * hybrid_mixer(h, positions, rel_bias_table, w_in[layer],
                                      cmp_pos_k[layer], cmp_w1_k[layer], cmp_w2_k[layer],
                                      cmp_pos_v[layer], cmp_w1_v[layer], cmp_w2_v[layer],
                                      mla_q_norm_g[layer], mla_w_uq[layer],
                                      mla_kv_norm_g[layer], mla_w_ukv[layer], w_o[layer])
        h = rmsnorm(x, norm_ffn_g[layer]) * (1.0 + scale_f) + shift_f
        x = x + gate_f * conv_glu_ffn(h, ffn_w_gate[layer], ffn_w_up[layer],
                                      ffn_conv_w[layer], ffn_conv_b[layer], ffn_w_down[layer])
    return rmsnorm(x, final_norm_g)
```

```python
from contextlib import ExitStack
import math
import numpy as np
import ml_dtypes
import concourse.bass as bass
import concourse.mybir as mybir
from concourse.bass_utils import run_bass_kernel_spmd

F32 = mybir.dt.float32
BF16 = mybir.dt.bfloat16
I32 = mybir.dt.int32
AF = mybir.ActivationFunctionType
ALU = mybir.AluOpType
AX = mybir.AxisListType

EPOCH = 30000
NDSEM = 12

D = 1024
S = 2048
NQT = 16
NCH = 4
DFF = 2816
NFC = 22
NEG = -30000.0
INCOLS = 5072
C_Q, C_KC, C_VC, C_KS, C_VS, C_KW, C_VW, C_G, C_CQ, C_CKV, C_KR, C_MA, C_MB = (
    0, 1024, 1280, 1536, 1792, 2048, 2304, 2560, 2608, 2864, 2992, 3024, 4048)


class KB:
    def __init__(self, nc):
        self.nc = nc
        self.es = ExitStack()
        self.eng = {"pe": nc.tensor, "act": nc.scalar, "dve": nc.vector,
                    "pool": nc.gpsimd, "sp": nc.sync}
        self.cnt = {e: 0 for e in ("pe", "act", "dve", "pool")}
        self.csem = {e: [] for e in self.cnt}
        self.dcnt = {}
        self.dsem = {}
        self.waited = {}
        self.recs = {}
        self.nwaits = 0
        self.ninst = 0

    def sb(self, name, shape, dt, es=None):
        self.uid = getattr(self, "uid", 0) + 1
        return (es or self.es).enter_context(self.nc.sbuf_tensor(f"s{self.uid}_{name}", list(shape), dt))

    def ps(self, name, shape, dt):
        return self.es.enter_context(self.nc.psum_tensor(name, list(shape), dt))

    def _sem(self, name):
        return self.es.enter_context(self.nc.semaphore(name))

    @staticmethod
    def box(ap):
        t = ap.tensor
        space = str(ap.space)
        if space == "DRAM":
            lo = int(ap.offset)
            hi = lo
            for st, n in ap.ap:
                if n > 1:
                    if st >= 0:
                        hi += st * (n - 1)
                    else:
                        lo += st * (n - 1)
            return (t.name, 0, 1, lo, hi + 1, False)
        if space == "PSUM":
            return (t.name, 0, 128, 0, 1 << 30, True)
        row = 1
        for s_ in list(t.shape)[1:]:
            row *= s_
        off = int(ap.offset)
        p0 = off // row
        f0 = off % row
        apl = list(ap.ap)
        lo = f0
        hi = f0
        for st, n in apl[1:]:
            if n > 1:
                if st >= 0:
                    hi += st * (n - 1)
                else:
                    lo += st * (n - 1)
        return (t.name, p0, p0 + apl[0][1], lo, hi + 1, False)

    def _wait_compute(self, waiter, e, idx):
        key = (waiter, e)
        if self.waited.get(key, -1) >= idx:
            return
        self.waited[key] = idx
        self.eng[waiter].wait_ge(self.csem[e][idx // EPOCH], (idx % EPOCH) + 1)
        self.nwaits += 1

    def _wait_dma(self, waiter, q, k):
        slot = k % NDSEM
        val = 16 * (k // NDSEM + 1)
        key = (waiter, "dma", q, slot)
        if self.waited.get(key, 0) >= val:
            return
        self.waited[key] = val
        self.eng[waiter].wait_ge(self.dsem[q][slot], val)
        self.nwaits += 1

    def _wait(self, waiter, who):
        if who[0] == "dma":
            self._wait_dma(waiter, who[1], who[2])
        else:
            self._wait_compute(waiter, who[0], who[1])

    @staticmethod
    def _ovl(a, b):
        return a[1] < b[2] and b[1] < a[2] and a[3] < b[4] and b[3] < a[4]

    @staticmethod
    def _covers(a, b):
        return a[1] <= b[1] and a[2] >= b[2] and a[3] <= b[3] and a[4] >= b[4]

    def _deps(self, eng, reads, writes, is_dma):
        deps = []
        for ap in reads:
            bx = self.box(ap)
            for (rb, kind, who) in self.recs.get(bx[0], ()):
                if not self._ovl(bx, rb):
                    continue
                same = (not is_dma) and who[0] == eng
                if kind == "w":
                    if same and eng == "pe":
                        continue
                    deps.append(who)
                elif bx[5] and not same:
                    deps.append(who)
        for ap in writes:
            bx = self.box(ap)
            for (rb, kind, who) in self.recs.get(bx[0], ()):
                if not self._ovl(bx, rb):
                    continue
                same = (not is_dma) and who[0] == eng
                if same and eng == "pe":
                    continue
                deps.append(who)
        return deps

    def _record(self, who, reads, writes):
        for ap in writes:
            bx = self.box(ap)
            lst = self.recs.setdefault(bx[0], [])
            if bx[5]:
                lst[:] = []
            else:
                lst[:] = [r for r in lst if not self._covers(bx, r[0])]
            lst.append((bx, "w", who))
        for ap in reads:
            bx = self.box(ap)
            lst = self.recs.setdefault(bx[0], [])
            if who[0] != "dma":
                lst[:] = [r for r in lst
                          if not (r[1] == "r" and r[2][0] == who[0] and r[0] == bx)]
            lst.append((bx, "r", who))

    def op(self, eng, fn, reads, writes):
        for who in self._deps(eng, reads, writes, False):
            self._wait(eng, who)
        idx = self.cnt[eng]
        ep = idx // EPOCH
        while len(self.csem[eng]) <= ep:
            self.csem[eng].append(self._sem(f"c_{eng}_{len(self.csem[eng])}"))
        ins = fn()
        ins.then_inc(self.csem[eng][ep], 1)
        self.cnt[eng] = idx + 1
        self.ninst += 1
        self._record((eng, idx), reads, writes)
        return (eng, idx)

    def dma(self, q, out, in_, **kw):
        if q not in self.dsem:
            self.dsem[q] = [self._sem(f"d_{q}_{i}") for i in range(NDSEM)]
            self.dcnt[q] = 0
        k = self.dcnt[q]
        if k >= NDSEM:
            self._wait_dma(q, q, k - NDSEM)
        for who in self._deps(q, [in_], [out], True):
            self._wait(q, who)
        ins = self.eng[q].dma_start(out=out, in_=in_, **kw)
        ins.then_inc(self.dsem[q][k % NDSEM], 16)
        self.dcnt[q] = k + 1
        self.ninst += 1
        who = ("dma", q, k)
        self._record(who, [in_], [out])
        return who

    def finish(self, whos, eng="sp"):
        for who in whos:
            self._wait(eng, who)

    def barrier(self):
        for w in ("pe", "act", "dve", "pool", "sp"):
            for e in ("pe", "act", "dve", "pool"):
                if e != w and self.cnt[e] > 0:
                    self._wait_compute(w, e, self.cnt[e] - 1)
            for q in self.dsem:
                n = self.dcnt[q]
                for k in range(max(0, n - NDSEM), n):
                    self._wait_dma(w, q, k)
        self.recs = {}

    def mm(self, out, lhsT, rhs, start=True, stop=True):
        return self.op("pe", lambda: self.nc.tensor.matmul(
            out, lhsT, rhs, start=start, stop=stop, skip_group_check=True), [lhsT, rhs], [out])

    def tr(self, out, in_, ident):
        return self.op("pe", lambda: self.nc.tensor.transpose(out, in_, ident), [in_, ident], [out])

    def act(self, out, in_, func, bias=None, scale=None, accum_out=None):
        kw = {}
        rd = [in_]
        wr = [out]
        if bias is not None:
            kw["bias"] = bias
            if not isinstance(bias, (int, float)):
                rd.append(bias)
        if scale is not None:
            kw["scale"] = scale
            if not isinstance(scale, (int, float)):
                rd.append(scale)
        if accum_out is not None:
            kw["accum_out"] = accum_out
            wr.append(accum_out)
        return self.op("act", lambda: self.nc.scalar.activation(out, in_, func, **kw), rd, wr)

    def _ve(self, eng):
        return self.nc.vector if eng == "dve" else self.nc.gpsimd

    def tt(self, out, in0, in1, op, eng="dve"):
        return self.op(eng, lambda: self._ve(eng).tensor_tensor(out, in0, in1, op), [in0, in1], [out])

    def ts(self, out, in0, s1, s2, op0, op1=None, eng="dve"):
        rd = [in0]
        for s_ in (s1, s2):
            if s_ is not None and not isinstance(s_, (int, float)):
                rd.append(s_)
        kw = {}
        if op1 is not None:
            kw["op1"] = op1
        return self.op(eng, lambda: self._ve(eng).tensor_scalar(out, in0, s1, s2, op0, **kw), rd, [out])

    def stt(self, out, in0, scalar, in1, op0, op1, eng="dve"):
        rd = [in0, in1]
        if not isinstance(scalar, (int, float)):
            rd.append(scalar)
        return self.op(eng, lambda: self._ve(eng).scalar_tensor_tensor(
            out, in0, scalar, in1, op0, op1), rd, [out])

    def cp(self, out, in_, eng="dve"):
        if eng == "act":
            return self.op("act", lambda: self.nc.scalar.copy(out, in_), [in_], [out])
        return self.op(eng, lambda: self._ve(eng).tensor_copy(out, in_), [in_], [out])

    def red(self, out, in_, op, axis=AX.X, eng="dve"):
        return self.op(eng, lambda: self._ve(eng).tensor_reduce(out, in_, axis, op), [in_], [out])

    def memset(self, ap, v, eng="dve"):
        return self.op(eng, lambda: self._ve(eng).memset(ap, v), [], [ap])

    def recip(self, out, in_):
        return self.op("dve", lambda: self.nc.vector.reciprocal(out, in_), [in_], [out])


def _t5_bucket(n):
    n = np.maximum(n, 0)
    exact = 16
    lr = np.log(np.maximum(n, exact).astype(np.float32) / np.float32(exact)) / np.float32(math.log(128 / exact))
    large = np.minimum(exact + (lr.astype(np.float32) * np.float32(16)).astype(np.int32), 31)
    return np.where(n < exact, n, large)


def make_consts():
    bf = ml_dtypes.bfloat16
    c = {}
    c["ident_bf"] = np.eye(128, dtype=np.float32).astype(bf)
    c["ident_f"] = np.eye(128, dtype=np.float32)
    k = np.arange(128)[:, None]
    q = np.arange(128)[None, :]
    c["tri"] = np.where(q >= k, 0.0, NEG).astype(bf)
    c["anti"] = np.where(q < k, 0.0, NEG).astype(bf)
    t = np.arange(S)[None, :]
    c["blockind"] = (t // 64 == np.arange(32)[:, None]).astype(np.float32).astype(bf)
    n = np.arange(1152)
    dist = n - 511
    oh = np.zeros((33, 1152), np.float32)
    bk = _t5_bucket(dist)
    for i in range(1152):
        if dist[i] >= 0:
            oh[bk[i], i] = 1.0
        else:
            oh[32, i] = 1.0
    c["ohe"] = oh
    a = np.zeros((4, 41, 127), np.float32)
    for ch in range(4):
        for cc in range(127):
            r = cc - 32 * ch
            if r >= 31:
                a[ch, 0, cc] = 1.0
            elif r >= -9:
                a[ch, 31 - r, cc] = 1.0
    ap_ = np.zeros((128, 4, 127), np.float32)
    ap_[:41] = a.transpose(1, 0, 2)
    c["ach"] = ap_.astype(bf)
    m1 = np.ones((128, 16, 32), np.float32)
    m2 = np.zeros((128, 16, 32), np.float32)
    for qt in range(16):
        for p in range(128):
            cur = (qt * 128 + p) // 64
            for j in range(32):
                if j == 0 or j == cur or j == cur - 1:
                    m1[p, qt, j] = 0.0
                    m2[p, qt, j] = 1e30
                elif j > cur:
                    m1[p, qt, j] = 0.0
                    m2[p, qt, j] = -1e30
    c["m1"] = m1
    c["m2"] = m2
    cs = np.arange(127) * 16
    bs = np.arange(32) * 64
    ov = np.clip(np.minimum(cs[:, None] + 32, bs[None, :] + 64) - np.maximum(cs[:, None], bs[None, :]), 0, None) / 32.0
    vcc = np.zeros((128, 33), np.float32)
    vcc[:127, 0] = 1.0
    vcc[:127, 1:] = ov
    c["vcconst"] = vcc.astype(bf)
    invf = np.zeros((128, 1), np.float32)
    sgn = np.ones((128, 1), np.float32)
    fr = (np.float32(10000.0) ** (-np.arange(0, 32, 2, dtype=np.float32) / np.float32(32))).astype(np.float32)
    for i in range(16):
        invf[64 + i, 0] = fr[i]
        invf[80 + i, 0] = fr[i]
        sgn[64 + i, 0] = -1.0
    c["invf"] = invf
    c["sgn"] = sgn
    return c


CONST_DT = {"ident_bf": BF16, "ident_f": F32, "tri": BF16, "anti": BF16, "blockind": BF16, "ohe": F32,
            "ach": BF16, "m1": F32, "m2": F32, "vcconst": BF16, "invf": F32, "sgn": F32}

IN_SPECS = [
    ("x", None, F32), ("cT", None, F32), ("pos", None, I32), ("rel", [32, 16], F32),
    ("ada_w", [D, 6 * D], F32), ("ada_bT", [128, 48], F32), ("nmg", [128, 8], F32),
    ("w_in", [D, INCOLS], F32), ("posk", [128, 16], F32), ("posv", [128, 16], F32),
    ("w1k", [2048, 256], F32), ("w2k", [256, 64], F32), ("w1v", [2048, 256], F32), ("w2v", [256, 64], F32),
    ("qg", [128, 2], F32), ("kvg", [128, 1], F32), ("w_uq", [256, 768], F32), ("w_uq_sw", [256, 768], F32),
    ("w_ukv", [128, 1536], F32), ("w_o", [D, D], F32), ("nfg", [128, 8], F32),
    ("w_gate", [D, DFF], F32), ("w_up", [D, DFF], F32), ("convw", [128, NFC, 3], F32), ("convb", [128, NFC], F32),
    ("w_down", [DFF, D], F32), ("fng", [D], F32),
]


def build(nseq=4, stop_after=None, dbg=()):
    nc = bass.Bass("TRN2", target_bir_lowering=False)
    I = {}
    for name, shape, dt in IN_SPECS:
        if name == "x":
            shape = [nseq, S, D]
        elif name == "cT":
            shape = [128, 8, nseq]
        elif name == "pos":
            shape = [nseq, S]
        I[name] = nc.dram_tensor(name, list(shape), dt, kind="ExternalInput").ap()
    consts = make_consts()
    for name, arr in consts.items():
        I[name] = nc.dram_tensor(name, list(arr.shape), CONST_DT[name], kind="ExternalInput").ap()
    out = nc.dram_tensor("out", [nseq, S, D], F32, kind="ExternalOutput").ap()
    DBG = {}
    for name, shape, dt in dbg:
        DBG[name] = nc.dram_tensor("dbg_" + name, list(shape), dt, kind="ExternalOutput").ap()
    veD = nc.dram_tensor("veD", [16, 1152], F32, kind="Internal")
    vP = nc.dram_tensor("vP", [16, 128, 384], F32, kind="Internal")
    gD = nc.dram_tensor("gD", [nseq, 2, D], F32, kind="Internal")
    x1D = nc.dram_tensor("x1D", [nseq, S, D], F32, kind="Internal")

    k = KB(nc)
    finals = []
    with k.es:
        PS = [k.ps(f"ps{i}", [128, 512], F32) for i in range(8)]
        S_B = PS[0:2]
        O_B = PS[2:6]
        M_B = PS[6:8]

        def bf_view(bank):
            return bank[:].bitcast(BF16)

        def gconst(name, shape, dt, q="sp"):
            t = k.sb("c_" + name, shape, dt)
            k.dma(q, t[:], I[name])
            return t
        ident_bf = gconst("ident_bf", [128, 128], BF16)
        ident_f = gconst("ident_f", [128, 128], F32)
        tri = gconst("tri", [128, 128], BF16)
        anti = gconst("anti", [128, 128], BF16)
        ach = gconst("ach", [128, 4, 127], BF16)
        m1 = gconst("m1", [128, 16, 32], F32)
        m2 = gconst("m2", [128, 16, 32], F32)
        vcconst = gconst("vcconst", [128, 33], BF16)
        invf = gconst("invf", [128, 1], F32)
        sgn = gconst("sgn", [128, 1], F32)
        ada_bT = gconst("ada_bT", [128, 48], F32)
        nmg = gconst("nmg", [128, 8], F32)
        nfg = gconst("nfg", [128, 8], F32)
        qg = gconst("qg", [128, 2], F32)
        kvg = gconst("kvg", [128, 1], F32)
        convw = gconst("convw", [128, NFC, 3], F32)
        convb = gconst("convb", [128, NFC], F32)
        posk = gconst("posk", [128, 16], F32)
        posv = gconst("posv", [128, 16], F32)
        fng_b = k.sb("fng_b", [128, D], F32)
        k.dma("sp", fng_b[:], I["fng"].partition_broadcast(128))
        ones_bf = k.sb("ones_bf", [128, 128], BF16)
        k.memset(ones_bf[:], 1.0)
        w2k = k.sb("w2k", [128, 2, 64], BF16)
        w2v = k.sb("w2v", [128, 2, 64], BF16)
        k.dma("pool", w2k[:], I["w2k"].rearrange("(c p) n -> p c n", p=128))
        k.dma("pool", w2v[:], I["w2v"].rearrange("(c p) n -> p c n", p=128))
        modFM = k.sb("modFM", [128, 48, nseq], F32)
        sc1 = k.sb("sc1", [128, 8, nseq], F32)
        sc2 = k.sb("sc2", [128, 8, nseq], F32)
        posb = k.sb("posb", [128, 2, 2], F32)
        WB = [k.sb(f"wb{i}", [128, 8192], BF16) for i in range(2)]
        wb_i = [0]
        negpi = k.sb("negpi", [128, 1], F32)
        k.memset(negpi[:], -math.pi)

        def wbuf():
            t = WB[wb_i[0] % 2]
            wb_i[0] += 1
            return t

        def wview(t, kc, n, off=0):
            return t[:, off:off + kc * n].rearrange("p (k n) -> p k n", k=kc)

        with ExitStack() as pes:
            cT_sb = k.sb("cT_sb", [128, 8, nseq], F32, pes)
            cond = k.sb("cond", [128, 8, nseq], BF16, pes)
            k.dma("sp", cT_sb[:], I["cT"])
            k.act(cond[:], cT_sb[:], AF.Silu)
            adav = I["ada_w"].rearrange("(kc p) n -> p kc n", p=128)
            mbank = M_B[0]
            first = True
            for jt in range(12):
                wt = wbuf()
                wv = wview(wt, 8, 512)
                k.dma("pool", wv, adav[:, :, jt * 512:(jt + 1) * 512])
                for jj in range(4):
                    j = jt * 4 + jj
                    for kc in range(8):
                        k.mm(mbank[:, j * nseq:(j + 1) * nseq], wv[:, kc, jj * 128:(jj + 1) * 128],
                             cond[:, kc, :], start=first, stop=False)
                        first = False
            k.tt(modFM[:], mbank[:, 0:48 * nseq].rearrange("p (j b) -> p j b", b=nseq),
                 ada_bT[:].unsqueeze(2).to_broadcast([128, 48, nseq]), ALU.add)
            k.stt(sc1[:], modFM[:, 8:16, :], 1.0, nmg[:].unsqueeze(2).to_broadcast([128, 8, nseq]),
                  ALU.add, ALU.mult)
            k.stt(sc2[:], modFM[:, 32:40, :], 1.0, nfg[:].unsqueeze(2).to_broadcast([128, 8, nseq]),
                  ALU.add, ALU.mult)
            gts = k.sb("gts", [8, 2 * nseq, 128], F32, pes)
            for b in range(nseq):
                gbank = (M_B[1], S_B[0])[b % 2]
                for gi, j0 in enumerate((16, 40)):
                    k.tr(gbank[0:8, gi * 128:(gi + 1) * 128], modFM[:, j0:j0 + 8, b], ident_f[:])
                k.cp(gts[:, 2 * b:2 * b + 2, :], gbank[0:8, 0:256].rearrange("p (a n) -> p a n", n=128))
            for b in range(nseq):
                for gi in range(2):
                    k.dma("sp", gD.ap()[b, gi, :].rearrange("(j p) -> j p", p=128), gts[:, b * 2 + gi, :])
            tab = k.sb("tab", [33, 16], F32, pes)
            t31 = k.sb("t31", [32, 16], F32, pes)
            ohe = k.sb("ohe", [33, 1152], F32, pes)
            ve_sb = k.sb("ve_sb", [16, 1152], F32, pes)
            k.dma("sp", tab[0:32, :], I["rel"])
            k.dma("sp", t31[:], I["rel"][31, :].partition_broadcast(32))
            k.dma("sp", ohe[:], I["ohe"])
            k.tt(tab[0:32, :], tab[0:32, :], t31[:], ALU.subtract)
            k.ts(tab[0:32, :], tab[0:32, :], 8.0, None, ALU.mult)
            k.memset(tab[32:33, :], NEG)
            for i in range(3):
                k.mm(mbank[0:16, 0:384], tab[:], ohe[:, i * 384:(i + 1) * 384])
                k.cp(ve_sb[:, i * 384:(i + 1) * 384], mbank[0:16, 0:384])
            k.dma("sp", veD.ap(), ve_sb[:])
            k.dma("sp", vP.ap(), ve_sb[:, 384:768].unsqueeze(1).to_broadcast([16, 128, 384]))
            w1f = k.sb("w1f", [128, 16, 256], F32, pes)
            for kv, (wn, pt) in enumerate((("w1k", posk), ("w1v", posv))):
                k.dma("sp", w1f[:], I[wn].rearrange("(c p) n -> p c n", p=128))
                for hc in range(2):
                    for lp in range(16):
                        k.mm(mbank[:, 256 + kv * 2 + hc:256 + kv * 2 + hc + 1], w1f[:, lp, hc * 128:(hc + 1) * 128],
                             pt[:, lp:lp + 1], start=(lp == 0), stop=(lp == 15))
            k.cp(posb[:], mbank[:, 256:260].rearrange("p (a b) -> p a b", b=2))
            if "modFM" in DBG:
                finals.append(k.dma("sp", DBG["modFM"], modFM[:]))
            if "ve" in DBG:
                finals.append(k.dma("sp", DBG["ve"], ve_sb[:]))
            if "posb" in DBG:
                finals.append(k.dma("sp", DBG["posb"], posb[:]))
            k.barrier()

        if stop_after == "prologue":
            k.finish(finals)
            return nc, k

        xv = I["x"]
        for b in range(nseq):
            with ExitStack() as ses, ExitStack() as ses2:
                gm_b = k.sb("gm_b", [128, D], F32, ses)
                gf_b = k.sb("gf_b", [128, D], F32, ses)
                ss4 = k.sb("ss4", [128, 8], F32, ses)
                k.dma("sp", gm_b[:], gD.ap()[b, 0, :].partition_broadcast(128))
                k.dma("sp", gf_b[:], gD.ap()[b, 1, :].partition_broadcast(128))
                hT = k.sb("hT", [128, 8, S], BF16, ses)
                ons = k.sb("ons", [128, NQT, D], BF16, ses2)

                def norm_to_T(src_tiles_fn, dstT, scv, shv, es_):
                    xts = [k.sb(f"xt{i}", [128, D], F32, es_) for i in range(8)]
                    xns_all = [k.sb(f"xn{i}", [128, D], BF16, es_) for i in range(8)]
                    junk = k.sb("junk", [128, D], BF16, es_)
                    ss_all = k.sb("ss", [128, 16], F32, es_)
                    for ch in range(NCH):
                        xns = xns_all[(ch % 2) * 4:(ch % 2) * 4 + 4]
                        ss = ss_all[:, ch * 4:ch * 4 + 4]
                        xtl = []
                        for qi in range(4):
                            qt = ch * 4 + qi
                            xt = src_tiles_fn(qt, xts[qt % 8])
                            xtl.append(xt)
                            k.act(junk[:], xt, AF.Square, accum_out=ss[:, qi:qi + 1])
                        k.act(ss, ss, AF.Sqrt, bias=1e-6, scale=1.0 / D)
                        k.recip(ss, ss)
                        for qi in range(4):
                            k.ts(xns[qi][:], xtl[qi], ss[:, qi:qi + 1], None, ALU.mult)
                        for kp in range(4):
                            bank = M_B[kp % 2]
                            bv = bf_view(bank)
                            for kk in range(2):
                                kc = kp * 2 + kk
                                for qi in range(4):
                                    k.tr(bv[:, (kk * 4 + qi) * 128:(kk * 4 + qi + 1) * 128],
                                         xns[qi][:, kc * 128:(kc + 1) * 128], ident_bf[:])
                            for kk in range(2):
                                kc = kp * 2 + kk
                                k.act(dstT[:, kc, ch * 512:(ch + 1) * 512], bv[:, kk * 512:(kk + 1) * 512],
                                      AF.Identity, bias=shv(kc), scale=scv(kc))

                with ExitStack() as e1:
                    def load_x(qt, buf):
                        k.dma("sp", buf[:], xv[b, qt * 128:(qt + 1) * 128, :])
                        return buf[:]
                    norm_to_T(load_x, hT, lambda kc: sc1[:, kc, b:b + 1], lambda kc: modFM[:, kc, b:b + 1], e1)
                    k.barrier()
                if "hT" in DBG and b == 0:
                    finals.append(k.dma("sp", DBG["hT"], hT[:]))
                if stop_after == "s1":
                    continue
                STAGES(k, nc, I, DBG, finals, b, locals())
        k.finish(finals)
    return nc, k


class AttnStream:
    def __init__(self, k, S_B, PTs, look=1):
        self.k, self.S_B, self.PTs, self.look = k, S_B, PTs, look
        self.si = 0
        self.pi = 0
        self.pend = []

    def add(self, s_fn, npart, n, scale, pv_fn, after=None):
        k = self.k
        Sb = self.S_B[self.si % len(self.S_B)]
        self.si += 1
        s_fn(Sb)
        pt = self.PTs[self.pi % len(self.PTs)]
        self.pi += 1
        k.act(pt[0:npart, 0:n], Sb[0:npart, 0:n], AF.Exp, scale=scale)
        self.pend.append((pv_fn, pt, after))
        while len(self.pend) > self.look:
            self._pop()

    def _pop(self):
        pv_fn, pt, after = self.pend.pop(0)
        pv_fn(pt)
        if after is not None:
            after()

    def flush(self):
        while self.pend:
            self._pop()


def STAGES(k, nc, I, DBG, finals, b, L):
    hT, ons = L["hT"], L["ons"]
    stop_after = L["stop_after"]
    nsa_stage(k, nc, I, DBG, finals, b, L)
    if stop_after == "nsa":
        return
    mla_stage(k, nc, I, DBG, finals, b, L)
    if stop_after == "mla":
        return
    out_stage(k, nc, I, DBG, finals, b, L)


def nsa_stage(k, nc, I, DBG, finals, b, L):
    hT, ons, wbuf, wview = L["hT"], L["ons"], L["wbuf"], L["wview"]
    S_B, O_B, M_B = L["S_B"], L["O_B"], L["M_B"]
    ident_bf, tri, anti, ach, m1, m2, vcconst = (L[n] for n in ("ident_bf", "tri", "anti", "ach", "m1", "m2", "vcconst"))
    posb, w2k, w2v = L["posb"], L["w2k"], L["w2v"]
    veD, vP = L["veD"], L["vP"]
    w_inv = I["w_in"].rearrange("(kc p) n -> p kc n", p=128)
    rr = {"s": 0, "o": 0, "m": 0, "pt": 0, "ev": 0}

    def nxt(key, lst):
        v = lst[rr[key] % len(lst)]
        rr[key] += 1
        return v

    def evac(out, in_):
        rr["ev"] += 1
        if rr["ev"] % 2:
            k.cp(out, in_, eng="act")
        else:
            k.cp(out, in_, eng="dve")

    with ExitStack() as e2:
        w1s = [k.sb(f"w1s{i}", [128, 16, 256], BF16, e2) for i in range(2)]
        k.dma("pool", w1s[0][:], I["w1k"].rearrange("(c p) n -> p c n", p=128))
        k.dma("pool", w1s[1][:], I["w1v"].rearrange("(c p) n -> p c n", p=128))
        Qaug = k.sb("Qaug", [128, 4, S], BF16, e2)
        KSaug = k.sb("KSaug", [128, S], BF16, e2)
        KWt = k.sb("KWt", [128, S], BF16, e2)
        KKs = [k.sb(f"KK{i}", [128, S], BF16, e2) for i in range(2)]
        VS = k.sb("VS", [128, NQT, 65], BF16, e2)
        VW = k.sb("VW", [128, NQT, 65], BF16, e2)
        VCaug = k.sb("VCaug", [128, 97], BF16, e2)
        sig = k.sb("sig", [128, NQT, 12], F32, e2)
        hid = [k.sb(f"hid{i}", [128, 2, 127], BF16, e2) for i in range(2)]
        kcT = k.sb("kcT", [128, 127], BF16, e2)
        Tn = k.sb("Tn", [128, 4, 256], BF16, e2)
        Wc = k.sb("Wc", [128, 4, 512], BF16, e2)
        PTs = [k.sb(f"pt{i}", [128, 512], BF16, e2) for i in range(4)]
        accn = k.sb("accn", [128, 4, 4, 64], F32, e2)
        sc = k.sb("sc", [128, 4, 32], F32, e2)
        s2 = k.sb("s2", [128, 4, 32], F32, e2)
        cmpm = k.sb("cmpm", [128, 32, 32], BF16, e2)
        rank = k.sb("rank", [128, 4, 32], F32, e2)
        negpad = k.sb("negpad", [128, 4, 96], BF16, e2)
        rd = k.sb("rd", [128, 8, 4], F32, e2)
        ff = k.sb("ff", [128, 8, 4], F32, e2)
        tmp64 = k.sb("tmp64", [128, 4, 64], F32, e2)
        tmp32 = k.sb("tmp32", [128, 4, 32], F32, e2)

        k.memset(Qaug[64:128, :, :], 0.0, eng="pool")
        k.memset(KSaug[64:128, :], 0.0, eng="pool")
        k.memset(KWt[64:128, :], 0.0, eng="pool")
        k.memset(kcT[64:128, :], 0.0)
        k.memset(Wc[:], 0.0, eng="pool")
        k.dma("sp", KSaug[64:96, :], I["blockind"])
        k.memset(VS[:, :, 64:65], 1.0)
        k.memset(VW[:, :, 64:65], 1.0)
        k.memset(VCaug[:], 0.0)
        k.cp(VCaug[:, 64:97], vcconst[:])
        k.memset(Wc[0:1, :, :], NEG)
        k.memset(negpad[:], 0.0)

        rdi = [0]

        def norm_f(Ob, width, gate_ap):
            i = rdi[0] % 8
            rdi[0] += 1
            den = Ob[:, 0:4 * width].rearrange("p (q c) -> p q c", c=width)[:, :, 64]
            k.ts(rd[:, i, :], den, 1e-30, None, ALU.max)
            k.recip(rd[:, i, :], rd[:, i, :])
            if gate_ap is None:
                return rd[:, i, :]
            k.tt(ff[:, i, :], rd[:, i, :], gate_ap, ALU.mult)
            return ff[:, i, :]

        for g in range(4):
            wt = wbuf()
            wv = wview(wt, 8, 780)
            segs = [(0, 256, C_Q + 256 * g), (256, 64, C_KS + 64 * g), (320, 64, C_KW + 64 * g),
                    (384, 64, C_KC + 64 * g), (448, 64, C_KC + 64 * g), (512, 64, C_VC + 64 * g),
                    (576, 64, C_VC + 64 * g), (640, 64, C_VS + 64 * g), (704, 64, C_VW + 64 * g),
                    (768, 12, C_G + 12 * g)]
            for (o, n, c0) in segs:
                k.dma("pool", wv[:, :, o:o + n], w_inv[:, :, c0:c0 + n])
            for hl in range(4):
                h = 4 * g + hl
                k.dma("pool", Tn[:, hl, :], bass.AP(vP, h * 128 * 384 + 127, [[383, 128], [1, 256]]))
                k.dma("pool", Wc[1:41, hl, :], bass.AP(veD, h * 1152, [[16, 40], [1, 512]]))
            for ch in range(NCH):
                cs = slice(ch * 512, (ch + 1) * 512)
                for hl in range(4):
                    bank = nxt("m", M_B)
                    for kc in range(8):
                        k.mm(bank[0:64, :], wv[:, kc, hl * 64:(hl + 1) * 64], hT[:, kc, cs],
                             start=(kc == 0), stop=(kc == 7))
                    evac(Qaug[0:64, hl, cs], bank[0:64, :])
                for (o, dst) in ((256, KSaug), (320, KWt)):
                    bank = nxt("m", M_B)
                    for kc in range(8):
                        k.mm(bank[0:64, :], wv[:, kc, o:o + 64], hT[:, kc, cs], start=(kc == 0), stop=(kc == 7))
                    evac(dst[0:64, cs], bank[0:64, :])
                for (o, dst) in ((384, KKs[0]), (512, KKs[1])):
                    bank = nxt("m", M_B)
                    for kc in range(8):
                        k.mm(bank[:, :], wv[:, kc, o:o + 128], hT[:, kc, cs], start=(kc == 0), stop=(kc == 7))
                    evac(dst[0:64, cs], bank[0:64, :])
                    if ch == 0:
                        evac(dst[64:128, 0:511], bank[64:128, 1:512])
                    else:
                        evac(dst[64:128, ch * 512 - 1:(ch + 1) * 512 - 1], bank[64:128, :])
            for qt in range(NQT):
                bank = nxt("m", M_B)
                for kc in range(8):
                    k.mm(bank[:, 0:140], hT[:, kc, qt * 128:(qt + 1) * 128], wv[:, kc, 640:780],
                         start=(kc == 0), stop=(kc == 7))
                k.cp(VS[:, qt, 0:64], bank[:, 0:64], eng="dve")
                k.cp(VW[:, qt, 0:64], bank[:, 64:128], eng="dve")
                k.act(sig[:, qt, :], bank[:, 128:140], AF.Sigmoid)
            for kv in range(2):
                src = KKs[kv]
                for hc in range(2):
                    bank = nxt("m", M_B)
                    for lp in range(16):
                        k.mm(bank[:, 0:127], w1s[kv][:, lp, hc * 128:(hc + 1) * 128],
                             src[:, 2 * lp:2 * lp + 16 * 126 + 1:16], start=(lp == 0), stop=(lp == 15))
                    k.act(hid[kv][:, hc, :], bank[:, 0:127], AF.Gelu_apprx_tanh, bias=posb[:, kv, hc:hc + 1])
            bank = nxt("m", M_B)
            for hc in range(2):
                k.mm(bank[0:64, 0:127], w2k[:, hc, :], hid[0][:, hc, :], start=(hc == 0), stop=(hc == 1))
            k.cp(kcT[0:64, :], bank[0:64, 0:127], eng="dve")
            bank = nxt("m", M_B)
            for hc in range(2):
                k.mm(bank[0:127, 0:64], hid[1][:, hc, :], w2v[:, hc, :], start=(hc == 0), stop=(hc == 1))
            k.cp(VCaug[0:127, 0:64], bank[0:127, 0:64], eng="dve")
            if b == 0 and g == 0 and "kcT" in DBG:
                finals.append(k.dma("sp", DBG["kcT"], kcT[0:64, :]))
                finals.append(k.dma("sp", DBG["vc"], VCaug[:]))
            st = AttnStream(k, S_B, PTs, look=1)
            for ch in range(NCH):
                cs = slice(ch * 512, (ch + 1) * 512)
                q0 = 4 * ch

                def cmp_item(hl):
                    Ob = nxt("o", O_B)

                    def s_fn(Sb):
                        k.mm(Sb[0:127, :], kcT[:, :], Qaug[:, hl, cs], start=True, stop=False)
                        k.mm(Sb[0:127, :], ach[:, ch, :], Wc[:, hl, :], start=False, stop=True)

                    def pv_fn(pt):
                        for qi in range(4):
                            k.mm(Ob[:, qi * 97:(qi + 1) * 97], pt[0:127, qi * 128:(qi + 1) * 128], VCaug[0:127, :],
                                 start=(qi == 0), stop=False)

                    def after():
                        Ov = Ob[:, 0:388].rearrange("p (q c) -> p q c", c=97)
                        r1 = norm_f(Ob, 97, None)
                        if hl == 0:
                            k.tt(sc[:], Ov[:, :, 65:97], r1.unsqueeze(2).to_broadcast([128, 4, 32]), ALU.mult)
                        else:
                            k.tt(tmp32[:], Ov[:, :, 65:97], r1.unsqueeze(2).to_broadcast([128, 4, 32]), ALU.mult)
                            k.tt(sc[:], sc[:], tmp32[:], ALU.add)
                        i_ = rdi[0] % 8
                        rdi[0] += 1
                        k.tt(ff[:, i_, :], r1, sig[:, q0:q0 + 4, 3 * hl], ALU.mult)
                        k.tt(accn[:, :, hl, :], Ov[:, :, 0:64], ff[:, i_, :].unsqueeze(2).to_broadcast([128, 4, 64]), ALU.mult)
                    st.add(s_fn, 127, 512, 0.125, pv_fn, after)
                for hl in range(4):
                    cmp_item(hl)
                st.flush()
                k.tt(s2[:], sc[:], m1[:, q0:q0 + 4, :], ALU.mult)
                k.tt(s2[:], s2[:], m2[:, q0:q0 + 4, :], ALU.add)
                for qi in range(4):
                    k.tt(cmpm[:], s2[:, qi, :].unsqueeze(1).to_broadcast([128, 32, 32]),
                         s2[:, qi, :].unsqueeze(2).to_broadcast([128, 32, 32]), ALU.is_gt)
                    k.red(rank[:, qi, :], cmpm[:], ALU.add)
                k.ts(negpad[:, :, 64:96], rank[:], 15.5, NEG, ALU.is_gt, ALU.mult)
                if b == 0 and g == 0 and ch == 3 and "rank" in DBG:
                    finals.append(k.dma("sp", DBG["rank"], rank[:]))
                    finals.append(k.dma("sp", DBG["sc"], sc[:]))

                def br_items(hl, br):
                    Ob = nxt("o", O_B)
                    Ov = Ob[:, 0:260].rearrange("p (q c) -> p q c", c=65)
                    kts = list(range(max(0, q0 - 4), q0 + 4)) if br == 0 else list(range(0, q0 + 4))
                    state = {"first": True}
                    for kt in kts:
                        qlo = max(kt, q0)
                        qhi = min(kt + 4, q0 + 3) if br == 0 else q0 + 3
                        n = 128 * (qhi - qlo + 1)
                        ks = slice(kt * 128, (kt + 1) * 128)
                        qs = slice(qlo * 128, (qhi + 1) * 128)

                        def s_fn(Sb, kt=kt, qlo=qlo, qhi=qhi, n=n, ks=ks, qs=qs):
                            if br == 0:
                                k.mm(Sb[:, 0:n], KWt[:, ks], Qaug[:, hl, qs], start=True, stop=False)
                            else:
                                k.mm(Sb[:, 0:n], KSaug[:, ks], Qaug[:, hl, qs], start=True, stop=False)
                            nlo = max(kt, qlo)
                            nhi = min(kt + 1, qhi)
                            if nlo <= nhi:
                                o = (nlo - qlo) * 128
                                w_ = (nhi - nlo + 1) * 128
                                k.mm(Sb[:, o:o + w_], ident_bf[:], Tn[:, hl, (nlo - kt) * 128:(nhi - kt + 1) * 128],
                                     start=False, stop=False)
                            if br == 0 and qlo <= kt + 4 <= qhi:
                                o = (kt + 4 - qlo) * 128
                                k.mm(Sb[:, o:o + 128], ident_bf[:], anti[:], start=False, stop=False)

                        def pv_fn(pt, kt=kt, qlo=qlo, qhi=qhi):
                            V = VW if br == 0 else VS
                            for qt in range(qlo, qhi + 1):
                                o = (qt - qlo) * 128
                                k.mm(Ob[:, (qt - q0) * 65:(qt - q0 + 1) * 65], pt[:, o:o + 128], V[:, kt, :],
                                     start=state["first"], stop=False)
                                state["first"] = False

                        after = None
                        if kt == kts[-1]:
                            def after():
                                f = norm_f(Ob, 65, sig[:, q0:q0 + 4, 3 * hl + (2 if br == 0 else 1)])
                                k.tt(tmp64[:], Ov[:, :, 0:64], f.unsqueeze(2).to_broadcast([128, 4, 64]), ALU.mult)
                                k.tt(accn[:, :, hl, :], accn[:, :, hl, :], tmp64[:], ALU.add)
                        st.add(s_fn, 128, n, 0.125, pv_fn, after)
                for hl in range(4):
                    br_items(hl, 0)
                st.flush()
                bank = nxt("m", M_B)
                for qi in range(4):
                    k.mm(bank[0:96, qi * 128:(qi + 1) * 128], negpad[:, qi, :], ident_bf[:],
                         start=(qi == 0), stop=False)
                for hl in range(4):
                    evac(Qaug[64:96, hl, cs], bank[64:96, :])
                for hl in range(4):
                    br_items(hl, 1)
                st.flush()
                k.cp(ons[:, q0:q0 + 4, 256 * g:256 * (g + 1)], accn[:].rearrange("p q h d -> p q (h d)"), eng="pool")
        if b == 0 and "ons" in DBG:
            finals.append(k.dma("sp", DBG["ons"], ons[:]))
        k.barrier()
    print("sbuf remaining after nsa scope", nc.sbuf_bytes_remaining)


def mla_stage(k, nc, I, DBG, finals, b, L):
    hT, ons, wbuf, wview = L["hT"], L["ons"], L["wbuf"], L["wview"]
    S_B, O_B, M_B = L["S_B"], L["O_B"], L["M_B"]
    ident_bf, tri, ones_bf, invf, sgn, qg, kvg = (L[n] for n in ("ident_bf", "tri", "ones_bf", "invf", "sgn", "qg", "kvg"))
    w_inv = I["w_in"].rearrange("(kc p) n -> p kc n", p=128)
    rr = {"s": 0, "o": 0, "m": 0, "pt": 0, "ev": 0}
    PI = math.pi

    def nxt(key, lst):
        v = lst[rr[key] % len(lst)]
        rr[key] += 1
        return v

    def evac(out, in_):
        rr["ev"] += 1
        if rr["ev"] % 2:
            k.cp(out, in_, eng="act")
        else:
            k.cp(out, in_, eng="dve")

    with ExitStack() as e3:
        cqn = k.sb("cqn", [128, 2, S], BF16, e3)
        ckvn = k.sb("ckvn", [128, S], BF16, e3)
        COS2 = k.sb("COS2", [96, S], F32, e3)
        SIN2 = k.sb("SIN2", [96, S], F32, e3)
        krr = k.sb("krr", [96, S], BF16, e3)
        QmT = [k.sb(f"QmT{i}", [128, S], BF16, e3) for i in range(2)]
        KmT = [k.sb(f"KmT{i}", [128, S], BF16, e3) for i in range(2)]
        for i in range(2):
            k.memset(QmT[i][64:128, :], 0.0, eng="pool")
            k.memset(KmT[i][64:128, :], 0.0, eng="pool")
        Vh = [k.sb(f"Vh{i}", [128, NQT, 129], BF16, e3) for i in range(2)]
        sbh = [k.sb(f"sbh{i}", [128, NQT, 128], BF16, e3) for i in range(2)]
        wmb = [k.sb(f"wmb{i}", [128, 8, 128], BF16, e3) for i in range(2)]
        PTs = [k.sb(f"mpt{i}", [128, 512], BF16, e3) for i in range(3)]
        raw = [k.sb(f"raw{i}", [128, 512], F32, e3) for i in range(3)]
        sq = [k.sb(f"sq{i}", [128, 512], BF16, e3) for i in range(3)]
        rstd = k.sb("rstd", [128, 512], F32, e3)
        posi = k.sb("posi", [96, 512], I32, e3)
        ang = raw[0]
        targ = raw[1]
        t1 = k.sb("t1", [96, 512], F32, e3)
        t2 = k.sb("t2", [96, 512], F32, e3)
        satmp = sq[0]
        otmp = k.sb("otmp", [128, 2, 128], F32, e3)
        rd = k.sb("mrd", [128, 8, 2], F32, e3)
        for i in range(2):
            k.memset(Vh[i][:, :, 128:129], 1.0)

        R = slice(64, 96)
        for ch in range(NCH):
            cs = slice(ch * 512, (ch + 1) * 512)
            k.dma("sp", posi[:], I["pos"][b, cs].partition_broadcast(96))
            k.cp(ang[R, :], posi[R, :])
            k.ts(ang[R, :], ang[R, :], invf[R, 0:1], None, ALU.mult)
            k.ts(targ[R, :], ang[R, :], 1.0 / (2 * PI), None, ALU.mult)
            k.cp(posi[R, :], targ[R, :])
            k.cp(targ[R, :], posi[R, :])
            k.stt(ang[R, :], targ[R, :], -2 * PI, ang[R, :], ALU.mult, ALU.add)
            k.ts(targ[R, :], ang[R, :], PI, -2 * PI, ALU.is_gt, ALU.mult)
            k.tt(ang[R, :], ang[R, :], targ[R, :], ALU.add)
            k.act(SIN2[R, cs], ang[R, :], AF.Sin)
            k.ts(SIN2[R, cs], SIN2[R, cs], sgn[R, 0:1], None, ALU.mult)
            k.ts(ang[R, :], ang[R, :], 0.5 * PI, None, ALU.add)
            k.ts(targ[R, :], ang[R, :], PI, -2 * PI, ALU.is_gt, ALU.mult)
            k.tt(ang[R, :], ang[R, :], targ[R, :], ALU.add)
            k.act(COS2[R, cs], ang[R, :], AF.Sin)

        for j in range(2):
            wt = wbuf()
            wv = wview(wt, 8, 512)
            k.dma("pool", wv, w_inv[:, :, C_MA + 512 * j:C_MA + 512 * (j + 1)])
            for qt in range(NQT):
                bank = nxt("m", M_B)
                for kc in range(8):
                    k.mm(bank[:, :], hT[:, kc, qt * 128:(qt + 1) * 128], wv[:, kc, :], start=(kc == 0), stop=(kc == 7))
                k.act(satmp[:], bank[:, :], AF.Sigmoid)
                k.tt(ons[:, qt, 512 * j:512 * (j + 1)], ons[:, qt, 512 * j:512 * (j + 1)], satmp[:], ALU.mult, eng="pool")

        wt = wbuf()
        wv = wview(wt, 8, 576)
        k.memset(wv[:, :, 384:576], 0.0)
        k.dma("pool", wv[:, :, 0:256], w_inv[:, :, C_CQ:C_CQ + 256])
        k.dma("pool", wv[:, :, 256:384], w_inv[:, :, C_CKV:C_CKV + 128])
        k.dma("pool", wv[:, :, 448:480], w_inv[:, :, C_KR:C_KR + 32])
        k.dma("pool", wv[:, :, 544:560], w_inv[:, :, C_KR + 16:C_KR + 32])
        k.dma("pool", wv[:, :, 560:576], w_inv[:, :, C_KR:C_KR + 16])
        for ch in range(NCH):
            cs = slice(ch * 512, (ch + 1) * 512)
            for m in range(3):
                bank = nxt("m", M_B)
                for kc in range(8):
                    k.mm(bank[:, :], wv[:, kc, m * 128:(m + 1) * 128], hT[:, kc, cs], start=(kc == 0), stop=(kc == 7))
                k.act(sq[m][:], bank[:, :], AF.Square)
                k.cp(raw[m][:], bank[:, :], eng="dve")
            bank = nxt("m", M_B)
            k.mm(bank[:, :], ones_bf[:], sq[0][:], start=True, stop=False)
            k.mm(bank[:, :], ones_bf[:], sq[1][:], start=False, stop=True)
            k.act(rstd[:], bank[:, :], AF.Sqrt, bias=1e-6, scale=1.0 / 256)
            k.recip(rstd[:], rstd[:])
            for m in range(2):
                k.stt(cqn[:, m, cs], raw[m][:], qg[:, m:m + 1], rstd[:], ALU.mult, ALU.mult)
            bank = nxt("m", M_B)
            k.mm(bank[:, :], ones_bf[:], sq[2][:], start=True, stop=True)
            k.act(rstd[:], bank[:, :], AF.Sqrt, bias=1e-6, scale=1.0 / 128)
            k.recip(rstd[:], rstd[:])
            k.stt(ckvn[:, cs], raw[2][:], kvg[:, 0:1], rstd[:], ALU.mult, ALU.mult)
            bank = nxt("m", M_B)
            for kc in range(8):
                k.mm(bank[0:96, :], wv[:, kc, 384:480], hT[:, kc, cs], start=(kc == 0), stop=(kc == 7))
            k.tt(t1[R, :], bank[R, :], COS2[R, cs], ALU.mult)
            bank = nxt("m", M_B)
            for kc in range(8):
                k.mm(bank[0:96, :], wv[:, kc, 480:576], hT[:, kc, cs], start=(kc == 0), stop=(kc == 7))
            k.tt(t2[R, :], bank[R, :], SIN2[R, cs], ALU.mult)
            k.tt(krr[R, cs], t1[R, :], t2[R, :], ALU.add)

        wt = wbuf()
        wuq = wview(wt, 2, 768, 0)
        wus = wview(wt, 2, 768, 1536)
        wkv = wt[:, 3072:3072 + 1536]
        k.dma("pool", wuq, I["w_uq"].rearrange("(c p) n -> p c n", p=128))
        k.dma("pool", wus, I["w_uq_sw"].rearrange("(c p) n -> p c n", p=128))
        k.dma("pool", wkv, I["w_ukv"])
        scale = 96 ** -0.5
        def proj(h):
            Q, Kt, V, sb_, wm = QmT[h % 2], KmT[h % 2], Vh[h % 2], sbh[h % 2], wmb[h % 2]
            k.dma("pool", wm[:], w_inv[:, :, C_MB + 128 * h:C_MB + 128 * (h + 1)])
            k.cp(Kt[R, :], krr[R, :], eng="pool")
            for ch in range(NCH):
                cs = slice(ch * 512, (ch + 1) * 512)
                bankA = nxt("m", M_B)
                for kc in range(2):
                    k.mm(bankA[0:96, :], wuq[:, kc, h * 96:(h + 1) * 96], cqn[:, kc, cs], start=(kc == 0), stop=(kc == 1))
                k.cp(Q[0:64, cs], bankA[0:64, :], eng="dve")
                k.tt(t1[R, :], bankA[R, :], COS2[R, cs], ALU.mult)
                bankB = nxt("m", M_B)
                for kc in range(2):
                    k.mm(bankB[0:96, :], wus[:, kc, h * 96:(h + 1) * 96], cqn[:, kc, cs], start=(kc == 0), stop=(kc == 1))
                k.tt(t2[R, :], bankB[R, :], SIN2[R, cs], ALU.mult)
                k.tt(Q[R, cs], t1[R, :], t2[R, :], ALU.add)
                yield
                bank = nxt("m", M_B)
                k.mm(bank[0:64, :], wkv[:, h * 192:h * 192 + 64], ckvn[:, cs], start=True, stop=True)
                k.cp(Kt[0:64, cs], bank[0:64, :], eng="dve")
                yield
                bank = nxt("m", M_B)
                for qi in range(4):
                    qt = ch * 4 + qi
                    k.mm(bank[:, qi * 128:(qi + 1) * 128], ckvn[:, qt * 128:(qt + 1) * 128],
                         wkv[:, h * 192 + 64:h * 192 + 192], start=(qi == 0), stop=False)
                k.cp(V[:, ch * 4:ch * 4 + 4, 0:128], bank[:, :].rearrange("p (q c) -> p q c", c=128), eng="dve")
                yield
                bank = nxt("m", M_B)
                first = True
                for qi in range(4):
                    qt = ch * 4 + qi
                    for kc in range(8):
                        k.mm(bank[:, qi * 128:(qi + 1) * 128], hT[:, kc, qt * 128:(qt + 1) * 128], wm[:, kc, :],
                             start=first, stop=False)
                        first = False
                k.act(sb_[:, ch * 4:ch * 4 + 4, :], bank[:, :].rearrange("p (q c) -> p q c", c=128), AF.Sigmoid)
                yield

        st = AttnStream(k, S_B, PTs, look=1)

        def attn(h, gen):
            Q, Kt, V, sb_ = QmT[h % 2], KmT[h % 2], Vh[h % 2], sbh[h % 2]
            cnt = [0]
            for ch in range(NCH):
                q0 = 4 * ch
                OA = O_B[(rr["o"] % 2) * 2]
                OBk = O_B[(rr["o"] % 2) * 2 + 1]
                rr["o"] += 1
                firsts = {0: True, 1: True}
                for kt in range(0, q0 + 4):
                    qlo = max(kt, q0)
                    qhi = q0 + 3
                    n = 128 * (qhi - qlo + 1)

                    def s_fn(Sb, kt=kt, qlo=qlo, qhi=qhi, n=n, q0=q0):
                        k.mm(Sb[:, 0:n], Kt[:, kt * 128:(kt + 1) * 128], Q[:, qlo * 128:(qhi + 1) * 128],
                             start=True, stop=False)
                        if kt >= q0:
                            k.mm(Sb[:, 0:128], ident_bf[:], tri[:], start=False, stop=False)

                    def pv_fn(pt, kt=kt, qlo=qlo, qhi=qhi, firsts=firsts, OA=OA, OBk=OBk, q0=q0):
                        for qt in range(qlo, qhi + 1):
                            o = (qt - qlo) * 128
                            bi = (qt - q0) // 2
                            sl = (qt - q0) % 2
                            bank = OA if bi == 0 else OBk
                            k.mm(bank[:, sl * 129:(sl + 1) * 129], pt[:, o:o + 128], V[:, kt, :],
                                 start=firsts[bi], stop=False)
                            firsts[bi] = False

                    after = None
                    if kt == q0 + 3:
                        def after(OA=OA, OBk=OBk, q0=q0):
                            for bi, bank in enumerate((OA, OBk)):
                                qa = q0 + 2 * bi
                                Ov = bank[:, 0:258].rearrange("p (q c) -> p q c", c=129)
                                i_ = rr["ev"] % 8
                                rr["ev"] += 1
                                k.ts(rd[:, i_, :], Ov[:, :, 128], 1e-30, None, ALU.max)
                                k.recip(rd[:, i_, :], rd[:, i_, :])
                                k.tt(otmp[:], Ov[:, :, 0:128], rd[:, i_, :].unsqueeze(2).to_broadcast([128, 2, 128]), ALU.mult)
                                k.tt(otmp[:], otmp[:], sb_[:, qa:qa + 2, :], ALU.mult)
                                k.tt(ons[:, qa:qa + 2, 128 * h:128 * (h + 1)], ons[:, qa:qa + 2, 128 * h:128 * (h + 1)],
                                     otmp[:], ALU.add, eng="pool")
                    st.add(s_fn, 128, n, scale, pv_fn, after)
                    cnt[0] += 1
                    if gen is not None and cnt[0] % 2 == 0:
                        next(gen, None)

        for _ in proj(0):
            pass
        for h in range(8):
            gen = proj(h + 1) if h + 1 < 8 else None
            attn(h, gen)
            st.flush()
            if gen is not None:
                for _ in gen:
                    pass
        if b == 0 and "y" in DBG:
            finals.append(k.dma("sp", DBG["y"], ons[:]))
        k.barrier()


def sgn_negpi(L, k):
    if "negpi" not in L["cache"]:
        t = k.sb("negpi", [128, 1], F32)
        k.memset(t[:], -math.pi)
        L["cache"]["negpi"] = t
    return L["cache"]["negpi"]


def out_stage(k, nc, I, DBG, finals, b, L):
    ons, wview, WB = L["ons"], L["wview"], L["WB"]
    PS = L["PS"]
    M_B = L["M_B"]
    ident_bf, sc2, modFM, fng_b, convw, convb = (L[n] for n in ("ident_bf", "sc2", "modFM", "fng_b", "convw", "convb"))
    gD, x1D, out = L["gD"], L["x1D"], L["out"]
    xv = I["x"]
    rr = {"b": 0, "ev": 0}

    def nb():
        v = PS[rr["b"] % 8]
        rr["b"] += 1
        return v

    def bfv(bank):
        return bank[:].bitcast(BF16)

    with ExitStack() as e4:
        h2T = L["hT"]
        gm_b, gf_b, ss = L["gm_b"], L["gf_b"], L["ss4"]
        with ExitStack() as e4a:
            yT = k.sb("yT", [128, 8, 512], BF16, e4a)
            xts = [k.sb(f"x4t{i}", [128, D], F32, e4a) for i in range(2)]
            x1ts = [k.sb(f"x1t{i}", [128, D], F32, e4a) for i in range(2)]
            xns = [k.sb(f"x4n{i}", [128, D], BF16, e4a) for i in range(4)]
            junk = k.sb("junk4", [128, D], BF16, e4a)
            tmp = k.sb("tmp4", [128, 512], F32, e4a)
            wo = wview(WB[0], 8, 1024)
            k.dma("pool", wo, I["w_o"].rearrange("(kc p) n -> p kc n", p=128))
            for ch in range(NCH):
                for kp in range(4):
                    bank = nb()
                    bv = bfv(bank)
                    for kk in range(2):
                        kc = kp * 2 + kk
                        for qi in range(4):
                            k.tr(bv[:, (kk * 4 + qi) * 128:(kk * 4 + qi + 1) * 128],
                                 ons[:, ch * 4 + qi, kc * 128:(kc + 1) * 128], ident_bf[:])
                    k.cp(yT[:, kp * 2:kp * 2 + 2, :], bv[:, :].rearrange("p (a n) -> p a n", a=2),
                         eng=("act" if kp % 2 else "dve"))
                for qi in range(4):
                    qt = ch * 4 + qi
                    xt = xts[qt % 2]
                    x1t = x1ts[qt % 2]
                    k.dma("sp", xt[:], xv[b, qt * 128:(qt + 1) * 128, :])
                    for n in range(2):
                        bank = nb()
                        for kc in range(8):
                            k.mm(bank[:, :], yT[:, kc, qi * 128:(qi + 1) * 128], wo[:, kc, n * 512:(n + 1) * 512],
                                 start=(kc == 0), stop=(kc == 7))
                        k.tt(tmp[:], bank[:, :], gm_b[:, n * 512:(n + 1) * 512], ALU.mult)
                        k.tt(x1t[:, n * 512:(n + 1) * 512], tmp[:], xt[:, n * 512:(n + 1) * 512], ALU.add)
                    k.dma("sp", x1D.ap()[b, qt * 128:(qt + 1) * 128, :], x1t[:])
                    k.act(junk[:], x1t[:], AF.Square, accum_out=ss[:, qi:qi + 1])
                    k.act(ss[:, qi:qi + 1], ss[:, qi:qi + 1], AF.Sqrt, bias=1e-6, scale=1.0 / D)
                    k.recip(ss[:, qi:qi + 1], ss[:, qi:qi + 1])
                    k.ts(xns[qi][:], x1t[:], ss[:, qi:qi + 1], None, ALU.mult)
                for kp in range(4):
                    bank = nb()
                    bv = bfv(bank)
                    for kk in range(2):
                        kc = kp * 2 + kk
                        for qi in range(4):
                            k.tr(bv[:, (kk * 4 + qi) * 128:(kk * 4 + qi + 1) * 128],
                                 xns[qi][:, kc * 128:(kc + 1) * 128], ident_bf[:])
                    for kk in range(2):
                        kc = kp * 2 + kk
                        k.act(h2T[:, kc, ch * 512:(ch + 1) * 512], bv[:, kk * 512:(kk + 1) * 512],
                              AF.Identity, bias=modFM[:, 24 + kc, b:b + 1], scale=sc2[:, kc, b:b + 1])
            k.barrier()
        L["ses2"].close()
        if b == 0 and "h2T" in DBG:
            finals.append(k.dma("sp", DBG["h2T"], h2T[:]))
        with ExitStack() as e4b:
            aT = k.sb("aT", [128, NFC, S], BF16, e4b)
            gbuf = k.sb("gbuf", [128, S + 2], F32, e4b)
            cv = k.sb("cv", [128, S], F32, e4b)
            sg = k.sb("sg", [128, S], BF16, e4b)
            k.memset(gbuf[:, 0:2], 0.0)
            wgv = I["w_gate"].rearrange("(kc p) n -> p kc n", p=128)
            wuv = I["w_up"].rearrange("(kc p) n -> p kc n", p=128)
            gi = 0
            for f0 in range(0, NFC, 4):
                nf = min(4, NFC - f0)
                wt = WB[gi % 2]
                gi += 1
                wg = wview(wt, 8, 512, 0)
                wu = wview(wt, 8, 512, 4096)
                k.dma("pool", wg[:, :, 0:nf * 128], wgv[:, :, f0 * 128:(f0 + nf) * 128])
                k.dma("pool", wu[:, :, 0:nf * 128], wuv[:, :, f0 * 128:(f0 + nf) * 128])
                for fl in range(nf):
                    fc = f0 + fl
                    for tch in range(NCH):
                        cs = slice(tch * 512, (tch + 1) * 512)
                        bank = nb()
                        for kc in range(8):
                            k.mm(bank[:, :], wg[:, kc, fl * 128:(fl + 1) * 128], h2T[:, kc, cs], start=(kc == 0), stop=(kc == 7))
                        k.cp(gbuf[:, 2 + tch * 512:2 + (tch + 1) * 512], bank[:, :], eng="act")
                    k.ts(cv[:], gbuf[:, 2:S + 2], convw[:, fc, 2:3], convb[:, fc:fc + 1], ALU.mult, ALU.add)
                    k.stt(cv[:], gbuf[:, 1:S + 1], convw[:, fc, 1:2], cv[:], ALU.mult, ALU.add)
                    k.stt(cv[:], gbuf[:, 0:S], convw[:, fc, 0:1], cv[:], ALU.mult, ALU.add)
                    k.act(sg[:], cv[:], AF.Silu)
                    for tch in range(NCH):
                        cs = slice(tch * 512, (tch + 1) * 512)
                        bank = nb()
                        for kc in range(8):
                            k.mm(bank[:, :], wu[:, kc, fl * 128:(fl + 1) * 128], h2T[:, kc, cs], start=(kc == 0), stop=(kc == 7))
                        k.tt(aT[:, fc, cs], bank[:, :], sg[:, cs], ALU.mult)
            x1r = [cv[:, i * D:(i + 1) * D] for i in range(2)]
            x2 = [gbuf[:, i * D:(i + 1) * D] for i in range(2)]
            junk = sg[:, 0:D]
            tmp = k.sb("tmp5", [128, 512], F32, e4b)
            wdv = I["w_down"].rearrange("(fc p) n -> p fc n", p=128)
            for qg_ in range(4):
                banks = [[nb() for _ in range(2)] for _ in range(4)]
                for f0 in range(0, NFC, 8):
                    nf = min(8, NFC - f0)
                    wt = WB[gi % 2]
                    gi += 1
                    wd = wview(wt, 8, 1024)
                    k.dma("pool", wd[:, 0:nf, :], wdv[:, f0:f0 + nf, :])
                    for fl in range(nf):
                        fc = f0 + fl
                        for qi in range(4):
                            qt = qg_ * 4 + qi
                            for n in range(2):
                                k.mm(banks[qi][n][:, :], aT[:, fc, qt * 128:(qt + 1) * 128], wd[:, fl, n * 512:(n + 1) * 512],
                                     start=(fc == 0), stop=(fc == NFC - 1))
                for qi in range(4):
                    qt = qg_ * 4 + qi
                    xr = x1r[qt % 2]
                    xo = x2[qt % 2]
                    k.dma("sp", xr, x1D.ap()[b, qt * 128:(qt + 1) * 128, :])
                    for n in range(2):
                        k.tt(tmp[:], banks[qi][n][:, :], gf_b[:, n * 512:(n + 1) * 512], ALU.mult)
                        k.tt(xo[:, n * 512:(n + 1) * 512], tmp[:], xr[:, n * 512:(n + 1) * 512], ALU.add)
                    k.act(junk, xo, AF.Square, accum_out=ss[:, 4 + qi:5 + qi])
                    k.act(ss[:, 4 + qi:5 + qi], ss[:, 4 + qi:5 + qi], AF.Sqrt, bias=1e-6, scale=1.0 / D)
                    k.recip(ss[:, 4 + qi:5 + qi], ss[:, 4 + qi:5 + qi])
                    k.stt(xo, xo, ss[:, 4 + qi:5 + qi], fng_b[:], ALU.mult, ALU.mult)
                    finals.append(k.dma("sp", out[b, qt * 128:(qt + 1) * 128, :], xo))
            k.barrier()
        k.barrier()


def shared_inputs(inp):
    f = np.float32
    m = {}
    m["rel"] = inp["rel_bias_table"].astype(f)
    m["ada_w"] = inp["ada_w"][0]
    m["ada_bT"] = np.ascontiguousarray(inp["ada_b"][0].reshape(48, 128).T)
    m["nmg"] = np.ascontiguousarray(inp["norm_mix_g"][0].reshape(8, 128).T)
    m["w_in"] = inp["w_in"][0]
    m["posk"] = np.ascontiguousarray(inp["cmp_pos_k"][0].reshape(16, 128).T)
    m["posv"] = np.ascontiguousarray(inp["cmp_pos_v"][0].reshape(16, 128).T)
    m["w1k"] = inp["cmp_w1_k"][0]
    m["w2k"] = inp["cmp_w2_k"][0]
    m["w1v"] = inp["cmp_w1_v"][0]
    m["w2v"] = inp["cmp_w2_v"][0]
    m["qg"] = np.ascontiguousarray(inp["mla_q_norm_g"][0].reshape(2, 128).T)
    m["kvg"] = np.ascontiguousarray(inp["mla_kv_norm_g"][0].reshape(1, 128).T)
    wuq = inp["mla_w_uq"][0]
    m["w_uq"] = wuq
    sw = wuq.reshape(256, 8, 96).copy()
    sw[:, :, 64:80] = wuq.reshape(256, 8, 96)[:, :, 80:96]
    sw[:, :, 80:96] = wuq.reshape(256, 8, 96)[:, :, 64:80]
    m["w_uq_sw"] = np.ascontiguousarray(sw.reshape(256, 768))
    m["w_ukv"] = inp["mla_w_ukv"][0]
    m["w_o"] = inp["w_o"][0]
    m["nfg"] = np.ascontiguousarray(inp["norm_ffn_g"][0].reshape(8, 128).T)
    m["w_gate"] = inp["ffn_w_gate"][0]
    m["w_up"] = inp["ffn_w_up"][0]
    m["convw"] = np.ascontiguousarray(inp["ffn_conv_w"][0].reshape(3, NFC, 128).transpose(2, 1, 0))
    m["convb"] = np.ascontiguousarray(inp["ffn_conv_b"][0].reshape(NFC, 128).T)
    m["w_down"] = inp["ffn_w_down"][0]
    m["fng"] = inp["final_norm_g"]
    m.update(make_consts())
    return {k_: np.ascontiguousarray(v) for k_, v in m.items()}


_CACHE = {}


def kernel(**inputs):
    inp = {k_: np.asarray(v) for k_, v in inputs.items()}
    shared = shared_inputs(inp)
    if "nc" not in _CACHE:
        _CACHE["nc"] = build(nseq=4)[0]
    nc = _CACHE["nc"]
    in_maps = []
    for core in range(8):
        bsel = list(range(core * 4, core * 4 + 4))
        m = dict(shared)
        m["x"] = np.ascontiguousarray(inp["x"][bsel], dtype=np.float32)
        m["cT"] = np.ascontiguousarray(inp["c"][bsel].reshape(4, 8, 128).transpose(2, 1, 0), dtype=np.float32)
        m["pos"] = np.ascontiguousarray(inp["positions"][bsel], dtype=np.int32)
        in_maps.append(m)
    res = run_bass_kernel_spmd(nc, in_maps, core_ids=list(range(8)))
    out = np.concatenate([np.asarray(r["out"]) for r in res.results], axis=0)
    return out.astype(np.float32)
```

```python
from contextlib import ExitStack
import math
import numpy as np
import ml_dtypes
import concourse.bass as bass
import concourse.mybir as mybir
from concourse.bass_utils import run_bass_kernel_spmd

F32 = mybir.dt.float32
BF16 = mybir.dt.bfloat16
I32 = mybir.dt.int32
AF = mybir.ActivationFunctionType
ALU = mybir.AluOpType
AX = mybir.AxisListType

EPOCH = 30000
NDSEM = 12

D = 1024
S = 2048
NQT = 16
NCH = 4
DFF = 2816
NFC = 22
NEG = -30000.0
INCOLS = 5072
C_Q, C_KC, C_VC, C_KS, C_VS, C_KW, C_VW, C_G, C_CQ, C_CKV, C_KR, C_MA, C_MB = (
    0, 1024, 1280, 1536, 1792, 2048, 2304, 2560, 2608, 2864, 2992, 3024, 4048)


class KB:
    def __init__(self, nc):
        self.nc = nc
        self.es = ExitStack()
        self.eng = {"pe": nc.tensor, "act": nc.scalar, "dve": nc.vector,
                    "pool": nc.gpsimd, "sp": nc.sync}
        self.cnt = {e: 0 for e in ("pe", "act", "dve", "pool")}
        self.csem = {e: [] for e in self.cnt}
        self.dcnt = {}
        self.dsem = {}
        self.waited = {}
        self.recs = {}
        self.nwaits = 0
        self.ninst = 0

    def sb(self, name, shape, dt, es=None):
        self.uid = getattr(self, "uid", 0) + 1
        return (es or self.es).enter_context(self.nc.sbuf_tensor(f"s{self.uid}_{name}", list(shape), dt))

    def ps(self, name, shape, dt):
        return self.es.enter_context(self.nc.psum_tensor(name, list(shape), dt))

    def _sem(self, name):
        return self.es.enter_context(self.nc.semaphore(name))

    @staticmethod
    def box(ap):
        t = ap.tensor
        space = str(ap.space)
        if space == "DRAM":
            lo = int(ap.offset)
            hi = lo
            for st, n in ap.ap:
                if n > 1:
                    if st >= 0:
                        hi += st * (n - 1)
                    else:
                        lo += st * (n - 1)
            return (t.name, 0, 1, lo, hi + 1, False)
        if space == "PSUM":
            return (t.name, 0, 128, 0, 1 << 30, True)
        row = 1
        for s_ in list(t.shape)[1:]:
            row *= s_
        off = int(ap.offset)
        p0 = off // row
        f0 = off % row
        apl = list(ap.ap)
        lo = f0
        hi = f0
        for st, n in apl[1:]:
            if n > 1:
                if st >= 0:
                    hi += st * (n - 1)
                else:
                    lo += st * (n - 1)
        return (t.name, p0, p0 + apl[0][1], lo, hi + 1, False)

    def _wait_compute(self, waiter, e, idx):
        key = (waiter, e)
        if self.waited.get(key, -1) >= idx:
            return
        self.waited[key] = idx
        self.eng[waiter].wait_ge(self.csem[e][idx // EPOCH], (idx % EPOCH) + 1)
        self.nwaits += 1

    def _wait_dma(self, waiter, q, k):
        slot = k % NDSEM
        val = 16 * (k // NDSEM + 1)
        key = (waiter, "dma", q, slot)
        if self.waited.get(key, 0) >= val:
            return
        self.waited[key] = val
        self.eng[waiter].wait_ge(self.dsem[q][slot], val)
        self.nwaits += 1

    def _wait(self, waiter, who):
        if who[0] == "dma":
            self._wait_dma(waiter, who[1], who[2])
        else:
            self._wait_compute(waiter, who[0], who[1])

    @staticmethod
    def _ovl(a, b):
        return a[1] < b[2] and b[1] < a[2] and a[3] < b[4] and b[3] < a[4]

    @staticmethod
    def _covers(a, b):
        return a[1] <= b[1] and a[2] >= b[2] and a[3] <= b[3] and a[4] >= b[4]

    def _deps(self, eng, reads, writes, is_dma):
        deps = []
        for ap in reads:
            bx = self.box(ap)
            for (rb, kind, who) in self.recs.get(bx[0], ()):
                if not self._ovl(bx, rb):
                    continue
                same = (not is_dma) and who[0] == eng
                if kind == "w":
                    if same and eng == "pe":
                        continue
                    deps.append(who)
                elif bx[5] and not same:
                    deps.append(who)
        for ap in writes:
            bx = self.box(ap)
            for (rb, kind, who) in self.recs.get(bx[0], ()):
                if not self._ovl(bx, rb):
                    continue
                same = (not is_dma) and who[0] == eng
                if same and eng == "pe":
                    continue
                deps.append(who)
        return deps

    def _record(self, who, reads, writes):
        for ap in writes:
            bx = self.box(ap)
            lst = self.recs.setdefault(bx[0], [])
            if bx[5]:
                lst[:] = []
            else:
                lst[:] = [r for r in lst if not self._covers(bx, r[0])]
            lst.append((bx, "w", who))
        for ap in reads:
            bx = self.box(ap)
            lst = self.recs.setdefault(bx[0], [])
            if who[0] != "dma":
                lst[:] = [r for r in lst
                          if not (r[1] == "r" and r[2][0] == who[0] and r[0] == bx)]
            lst.append((bx, "r", who))

    def op(self, eng, fn, reads, writes):
        for who in self._deps(eng, reads, writes, False):
            self._wait(eng, who)
        idx = self.cnt[eng]
        ep = idx // EPOCH
        while len(self.csem[eng]) <= ep:
            self.csem[eng].append(self._sem(f"c_{eng}_{len(self.csem[eng])}"))
        ins = fn()
        ins.then_inc(self.csem[eng][ep], 1)
        self.cnt[eng] = idx + 1
        self.ninst += 1
        self._record((eng, idx), reads, writes)
        return (eng, idx)

    def dma(self, q, out, in_, **kw):
        if q not in self.dsem:
            self.dsem[q] = [self._sem(f"d_{q}_{i}") for i in range(NDSEM)]
            self.dcnt[q] = 0
        k = self.dcnt[q]
        if k >= NDSEM:
            self._wait_dma(q, q, k - NDSEM)
        for who in self._deps(q, [in_], [out], True):
            self._wait(q, who)
        ins = self.eng[q].dma_start(out=out, in_=in_, **kw)
        ins.then_inc(self.dsem[q][k % NDSEM], 16)
        self.dcnt[q] = k + 1
        self.ninst += 1
        who = ("dma", q, k)
        self._record(who, [in_], [out])
        return who

    def finish(self, whos, eng="sp"):
        for who in whos:
            self._wait(eng, who)

    def barrier(self):
        for w in ("pe", "act", "dve", "pool", "sp"):
            for e in ("pe", "act", "dve", "pool"):
                if e != w and self.cnt[e] > 0:
                    self._wait_compute(w, e, self.cnt[e] - 1)
            for q in self.dsem:
                n = self.dcnt[q]
                for k in range(max(0, n - NDSEM), n):
                    self._wait_dma(w, q, k)
        self.recs = {}

    def mm(self, out, lhsT, rhs, start=True, stop=True):
        return self.op("pe", lambda: self.nc.tensor.matmul(
            out, lhsT, rhs, start=start, stop=stop, skip_group_check=True), [lhsT, rhs], [out])

    def tr(self, out, in_, ident):
        return self.op("pe", lambda: self.nc.tensor.transpose(out, in_, ident), [in_, ident], [out])

    def act(self, out, in_, func, bias=None, scale=None, accum_out=None):
        kw = {}
        rd = [in_]
        wr = [out]
        if bias is not None:
            kw["bias"] = bias
            if not isinstance(bias, (int, float)):
                rd.append(bias)
        if scale is not None:
            kw["scale"] = scale
            if not isinstance(scale, (int, float)):
                rd.append(scale)
        if accum_out is not None:
            kw["accum_out"] = accum_out
            wr.append(accum_out)
        return self.op("act", lambda: self.nc.scalar.activation(out, in_, func, **kw), rd, wr)

    def _ve(self, eng):
        return self.nc.vector if eng == "dve" else self.nc.gpsimd

    def tt(self, out, in0, in1, op, eng="dve"):
        return self.op(eng, lambda: self._ve(eng).tensor_tensor(out, in0, in1, op), [in0, in1], [out])

    def ts(self, out, in0, s1, s2, op0, op1=None, eng="dve"):
        rd = [in0]
        for s_ in (s1, s2):
            if s_ is not None and not isinstance(s_, (int, float)):
                rd.append(s_)
        kw = {}
        if op1 is not None:
            kw["op1"] = op1
        return self.op(eng, lambda: self._ve(eng).tensor_scalar(out, in0, s1, s2, op0, **kw), rd, [out])

    def stt(self, out, in0, scalar, in1, op0, op1, eng="dve"):
        rd = [in0, in1]
        if not isinstance(scalar, (int, float)):
            rd.append(scalar)
        return self.op(eng, lambda: self._ve(eng).scalar_tensor_tensor(
            out, in0, scalar, in1, op0, op1), rd, [out])

    def cp(self, out, in_, eng="dve"):
        if eng == "act":
            return self.op("act", lambda: self.nc.scalar.copy(out, in_), [in_], [out])
        return self.op(eng, lambda: self._ve(eng).tensor_copy(out, in_), [in_], [out])

    def red(self, out, in_, op, axis=AX.X, eng="dve"):
        return self.op(eng, lambda: self._ve(eng).tensor_reduce(out, in_, axis, op), [in_], [out])

    def memset(self, ap, v, eng="dve"):
        return self.op(eng, lambda: self._ve(eng).memset(ap, v), [], [ap])

    def recip(self, out, in_):
        return self.op("dve", lambda: self.nc.vector.reciprocal(out, in_), [in_], [out])


def _t5_bucket(n):
    n = np.maximum(n, 0)
    exact = 16
    lr = np.log(np.maximum(n, exact).astype(np.float32) / np.float32(exact)) / np.float32(math.log(128 / exact))
    large = np.minimum(exact + (lr.astype(np.float32) * np.float32(16)).astype(np.int32), 31)
    return np.where(n < exact, n, large)


def make_consts():
    bf = ml_dtypes.bfloat16
    c = {}
    c["ident_bf"] = np.eye(128, dtype=np.float32).astype(bf)
    c["ident_f"] = np.eye(128, dtype=np.float32)
    k = np.arange(128)[:, None]
    q = np.arange(128)[None, :]
    c["tri"] = np.where(q >= k, 0.0, NEG).astype(bf)
    c["anti"] = np.where(q < k, 0.0, NEG).astype(bf)
    t = np.arange(S)[None, :]
    c["blockind"] = (t // 64 == np.arange(32)[:, None]).astype(np.float32).astype(bf)
    n = np.arange(1152)
    dist = n - 511
    oh = np.zeros((33, 1152), np.float32)
    bk = _t5_bucket(dist)
    for i in range(1152):
        if dist[i] >= 0:
            oh[bk[i], i] = 1.0
        else:
            oh[32, i] = 1.0
    c["ohe"] = oh
    a = np.zeros((4, 41, 127), np.float32)
    for ch in range(4):
        for cc in range(127):
            r = cc - 32 * ch
            if r >= 31:
                a[ch, 0, cc] = 1.0
            elif r >= -9:
                a[ch, 31 - r, cc] = 1.0
    ap_ = np.zeros((128, 4, 127), np.float32)
    ap_[:41] = a.transpose(1, 0, 2)
    c["ach"] = ap_.astype(bf)
    m1 = np.ones((128, 16, 32), np.float32)
    m2 = np.zeros((128, 16, 32), np.float32)
    for qt in range(16):
        for p in range(128):
            cur = (qt * 128 + p) // 64
            for j in range(32):
                if j == 0 or j == cur or j == cur - 1:
                    m1[p, qt, j] = 0.0
                    m2[p, qt, j] = 1e30
                elif j > cur:
                    m1[p, qt, j] = 0.0
                    m2[p, qt, j] = -1e30
    c["m1"] = m1
    c["m2"] = m2
    cs = np.arange(127) * 16
    bs = np.arange(32) * 64
    ov = np.clip(np.minimum(cs[:, None] + 32, bs[None, :] + 64) - np.maximum(cs[:, None], bs[None, :]), 0, None) / 32.0
    vcc = np.zeros((128, 33), np.float32)
    vcc[:127, 0] = 1.0
    vcc[:127, 1:] = ov
    c["vcconst"] = vcc.astype(bf)
    invf = np.zeros((128, 1), np.float32)
    sgn = np.ones((128, 1), np.float32)
    fr = (np.float32(10000.0) ** (-np.arange(0, 32, 2, dtype=np.float32) / np.float32(32))).astype(np.float32)
    for i in range(16):
        invf[64 + i, 0] = fr[i]
        invf[80 + i, 0] = fr[i]
        sgn[64 + i, 0] = -1.0
    c["invf"] = invf
    c["sgn"] = sgn
    return c


CONST_DT = {"ident_bf": BF16, "ident_f": F32, "tri": BF16, "anti": BF16, "blockind": BF16, "ohe": F32,
            "ach": BF16, "m1": F32, "m2": F32, "vcconst": BF16, "invf": F32, "sgn": F32}

IN_SPECS = [
    ("x", None, F32), ("cT", None, F32), ("pos", None, I32), ("rel", [32, 16], F32),
    ("ada_w", [D, 6 * D], F32), ("ada_bT", [128, 48], F32), ("nmg", [128, 8], F32),
    ("w_in", [D, INCOLS], F32), ("posk", [128, 16], F32), ("posv", [128, 16], F32),
    ("w1k", [2048, 256], F32), ("w2k", [256, 64], F32), ("w1v", [2048, 256], F32), ("w2v", [256, 64], F32),
    ("qg", [128, 2], F32), ("kvg", [128, 1], F32), ("w_uq", [256, 768], F32), ("w_uq_sw", [256, 768], F32),
    ("w_ukv", [128, 1536], F32), ("w_o", [D, D], F32), ("nfg", [128, 8], F32),
    ("w_gate", [D, DFF], F32), ("w_up", [D, DFF], F32), ("convw", [128, NFC, 3], F32), ("convb", [128, NFC], F32),
    ("w_down", [DFF, D], F32), ("fng", [D], F32),
]


def build(nseq=4, stop_after=None, dbg=()):
    nc = bass.Bass("TRN2", target_bir_lowering=False)
    I = {}
    for name, shape, dt in IN_SPECS:
        if name == "x":
            shape = [nseq, S, D]
        elif name == "cT":
            shape = [128, 8, nseq]
        elif name == "pos":
            shape = [nseq, S]
        I[name] = nc.dram_tensor(name, list(shape), dt, kind="ExternalInput").ap()
    consts = make_consts()
    for name, arr in consts.items():
        I[name] = nc.dram_tensor(name, list(arr.shape), CONST_DT[name], kind="ExternalInput").ap()
    out = nc.dram_tensor("out", [nseq, S, D], F32, kind="ExternalOutput").ap()
    DBG = {}
    for name, shape, dt in dbg:
        DBG[name] = nc.dram_tensor("dbg_" + name, list(shape), dt, kind="ExternalOutput").ap()
    veD = nc.dram_tensor("veD", [16, 1152], F32, kind="Internal")
    vP = nc.dram_tensor("vP", [16, 128, 384], F32, kind="Internal")
    gD = nc.dram_tensor("gD", [nseq, 2, D], F32, kind="Internal")
    x1D = nc.dram_tensor("x1D", [nseq, S, D], F32, kind="Internal")

    k = KB(nc)
    finals = []
    with k.es:
        PS = [k.ps(f"ps{i}", [128, 512], F32) for i in range(8)]
        S_B = PS[0:2]
        O_B = PS[2:6]
        M_B = PS[6:8]

        def bf_view(bank):
            return bank[:].bitcast(BF16)

        def gconst(name, shape, dt, q="sp"):
            t = k.sb("c_" + name, shape, dt)
            k.dma(q, t[:], I[name])
            return t
        ident_bf = gconst("ident_bf", [128, 128], BF16)
        ident_f = gconst("ident_f", [128, 128], F32)
        tri = gconst("tri", [128, 128], BF16)
        anti = gconst("anti", [128, 128], BF16)
        ach = gconst("ach", [128, 4, 127], BF16)
        m1 = gconst("m1", [128, 16, 32], F32)
        m2 = gconst("m2", [128, 16, 32], F32)
        vcconst = gconst("vcconst", [128, 33], BF16)
        invf = gconst("invf", [128, 1], F32)
        sgn = gconst("sgn", [128, 1], F32)
        ada_bT = gconst("ada_bT", [128, 48], F32)
        nmg = gconst("nmg", [128, 8], F32)
        nfg = gconst("nfg", [128, 8], F32)
        qg = gconst("qg", [128, 2], F32)
        kvg = gconst("kvg", [128, 1], F32)
        convw = gconst("convw", [128, NFC, 3], F32)
        convb = gconst("convb", [128, NFC], F32)
        posk = gconst("posk", [128, 16], F32)
        posv = gconst("posv", [128, 16], F32)
        fng_b = k.sb("fng_b", [128, D], F32)
        k.dma("sp", fng_b[:], I["fng"].partition_broadcast(128))
        ones_bf = k.sb("ones_bf", [128, 128], BF16)
        k.memset(ones_bf[:], 1.0)
        w2k = k.sb("w2k", [128, 2, 64], BF16)
        w2v = k.sb("w2v", [128, 2, 64], BF16)
        k.dma("pool", w2k[:], I["w2k"].rearrange("(c p) n -> p c n", p=128))
        k.dma("pool", w2v[:], I["w2v"].rearrange("(c p) n -> p c n", p=128))
        modFM = k.sb("modFM", [128, 48, nseq], F32)
        sc1 = k.sb("sc1", [128, 8, nseq], F32)
        sc2 = k.sb("sc2", [128, 8, nseq], F32)
        posb = k.sb("posb", [128, 2, 2], F32)
        WB = [k.sb(f"wb{i}", [128, 8192], BF16) for i in range(2)]
        wb_i = [0]
        negpi = k.sb("negpi", [128, 1], F32)
        k.memset(negpi[:], -math.pi)

        def wbuf():
            t = WB[wb_i[0] % 2]
            wb_i[0] += 1
            return t

        def wview(t, kc, n, off=0):
            return t[:, off:off + kc * n].rearrange("p (k n) -> p k n", k=kc)

        with ExitStack() as pes:
            cT_sb = k.sb("cT_sb", [128, 8, nseq], F32, pes)
            cond = k.sb("cond", [128, 8, nseq], BF16, pes)
            k.dma("sp", cT_sb[:], I["cT"])
            k.act(cond[:], cT_sb[:], AF.Silu)
            adav = I["ada_w"].rearrange("(kc p) n -> p kc n", p=128)
            mbank = M_B[0]
            first = True
            for jt in range(12):
                wt = wbuf()
                wv = wview(wt, 8, 512)
                k.dma("pool", wv, adav[:, :, jt * 512:(jt + 1) * 512])
                for jj in range(4):
                    j = jt * 4 + jj
                    for kc in range(8):
                        k.mm(mbank[:, j * nseq:(j + 1) * nseq], wv[:, kc, jj * 128:(jj + 1) * 128],
                             cond[:, kc, :], start=first, stop=False)
                        first = False
            k.tt(modFM[:], mbank[:, 0:48 * nseq].rearrange("p (j b) -> p j b", b=nseq),
                 ada_bT[:].unsqueeze(2).to_broadcast([128, 48, nseq]), ALU.add)
            k.stt(sc1[:], modFM[:, 8:16, :], 1.0, nmg[:].unsqueeze(2).to_broadcast([128, 8, nseq]),
                  ALU.add, ALU.mult)
            k.stt(sc2[:], modFM[:, 32:40, :], 1.0, nfg[:].unsqueeze(2).to_broadcast([128, 8, nseq]),
                  ALU.add, ALU.mult)
            gts = k.sb("gts", [8, 2 * nseq, 128], F32, pes)
            for b in range(nseq):
                gbank = (M_B[1], S_B[0])[b % 2]
                for gi, j0 in enumerate((16, 40)):
                    k.tr(gbank[0:8, gi * 128:(gi + 1) * 128], modFM[:, j0:j0 + 8, b], ident_f[:])
                k.cp(gts[:, 2 * b:2 * b + 2, :], gbank[0:8, 0:256].rearrange("p (a n) -> p a n", n=128))
            for b in range(nseq):
                for gi in range(2):
                    k.dma("sp", gD.ap()[b, gi, :].rearrange("(j p) -> j p", p=128), gts[:, b * 2 + gi, :])
            tab = k.sb("tab", [33, 16], F32, pes)
            t31 = k.sb("t31", [32, 16], F32, pes)
            ohe = k.sb("ohe", [33, 1152], F32, pes)
            ve_sb = k.sb("ve_sb", [16, 1152], F32, pes)
            k.dma("sp", tab[0:32, :], I["rel"])
            k.dma("sp", t31[:], I["rel"][31, :].partition_broadcast(32))
            k.dma("sp", ohe[:], I["ohe"])
            k.tt(tab[0:32, :], tab[0:32, :], t31[:], ALU.subtract)
            k.ts(tab[0:32, :], tab[0:32, :], 8.0, None, ALU.mult)
            k.memset(tab[32:33, :], NEG)
            for i in range(3):
                k.mm(mbank[0:16, 0:384], tab[:], ohe[:, i * 384:(i + 1) * 384])
                k.cp(ve_sb[:, i * 384:(i + 1) * 384], mbank[0:16, 0:384])
            k.dma("sp", veD.ap(), ve_sb[:])
            k.dma("sp", vP.ap(), ve_sb[:, 384:768].unsqueeze(1).to_broadcast([16, 128, 384]))
            w1f = k.sb("w1f", [128, 16, 256], F32, pes)
            for kv, (wn, pt) in enumerate((("w1k", posk), ("w1v", posv))):
                k.dma("sp", w1f[:], I[wn].rearrange("(c p) n -> p c n", p=128))
                for hc in range(2):
                    for lp in range(16):
                        k.mm(mbank[:, 256 + kv * 2 + hc:256 + kv * 2 + hc + 1], w1f[:, lp, hc * 128:(hc + 1) * 128],
                             pt[:, lp:lp + 1], start=(lp == 0), stop=(lp == 15))
            k.cp(posb[:], mbank[:, 256:260].rearrange("p (a b) -> p a b", b=2))
            if "modFM" in DBG:
                finals.append(k.dma("sp", DBG["modFM"], modFM[:]))
            if "ve" in DBG:
                finals.append(k.dma("sp", DBG["ve"], ve_sb[:]))
            if "posb" in DBG:
                finals.append(k.dma("sp", DBG["posb"], posb[:]))
            k.barrier()

        if stop_after == "prologue":
            k.finish(finals)
            return nc, k

        xv = I["x"]
        for b in range(nseq):
            with ExitStack() as ses, ExitStack() as ses2:
                gm_b = k.sb("gm_b", [128, D], F32, ses)
                gf_b = k.sb("gf_b", [128, D], F32, ses)
                ss4 = k.sb("ss4", [128, 8], F32, ses)
                k.dma("sp", gm_b[:], gD.ap()[b, 0, :].partition_broadcast(128))
                k.dma("sp", gf_b[:], gD.ap()[b, 1, :].partition_broadcast(128))
                hT = k.sb("hT", [128, 8, S], BF16, ses)
                ons = k.sb("ons", [128, NQT, D], BF16, ses2)

                def norm_to_T(src_tiles_fn, dstT, scv, shv, es_):
                    xts = [k.sb(f"xt{i}", [128, D], F32, es_) for i in range(8)]
                    xns_all = [k.sb(f"xn{i}", [128, D], BF16, es_) for i in range(8)]
                    junk = k.sb("junk", [128, D], BF16, es_)
                    ss_all = k.sb("ss", [128, 16], F32, es_)
                    for ch in range(NCH):
                        xns = xns_all[(ch % 2) * 4:(ch % 2) * 4 + 4]
                        ss = ss_all[:, ch * 4:ch * 4 + 4]
                        xtl = []
                        for qi in range(4):
                            qt = ch * 4 + qi
                            xt = src_tiles_fn(qt, xts[qt % 8])
                            xtl.append(xt)
                            k.act(junk[:], xt, AF.Square, accum_out=ss[:, qi:qi + 1])
                        k.act(ss, ss, AF.Sqrt, bias=1e-6, scale=1.0 / D)
                        k.recip(ss, ss)
                        for qi in range(4):
                            k.ts(xns[qi][:], xtl[qi], ss[:, qi:qi + 1], None, ALU.mult)
                        for kp in range(4):
                            bank = M_B[kp % 2]
                            bv = bf_view(bank)
                            for kk in range(2):
                                kc = kp * 2 + kk
                                for qi in range(4):
                                    k.tr(bv[:, (kk * 4 + qi) * 128:(kk * 4 + qi + 1) * 128],
                                         xns[qi][:, kc * 128:(kc + 1) * 128], ident_bf[:])
                            for kk in range(2):
                                kc = kp * 2 + kk
                                k.act(dstT[:, kc, ch * 512:(ch + 1) * 512], bv[:, kk * 512:(kk + 1) * 512],
                                      AF.Identity, bias=shv(kc), scale=scv(kc))

                with ExitStack() as e1:
                    def load_x(qt, buf):
                        k.dma("sp", buf[:], xv[b, qt * 128:(qt + 1) * 128, :])
                        return buf[:]
                    norm_to_T(load_x, hT, lambda kc: sc1[:, kc, b:b + 1], lambda kc: modFM[:, kc, b:b + 1], e1)
                    k.barrier()
                if "hT" in DBG and b == 0:
                    finals.append(k.dma("sp", DBG["hT"], hT[:]))
                if stop_after == "s1":
                    continue
                STAGES(k, nc, I, DBG, finals, b, locals())
        k.finish(finals)
    return nc, k


class AttnStream:
    def __init__(self, k, S_B, PTs, look=1):
        self.k, self.S_B, self.PTs, self.look = k, S_B, PTs, look
        self.si = 0
        self.pi = 0
        self.pend = []

    def add(self, s_fn, npart, n, scale, pv_fn, after=None):
        k = self.k
        Sb = self.S_B[self.si % len(self.S_B)]
        self.si += 1
        s_fn(Sb)
        pt = self.PTs[self.pi % len(self.PTs)]
        self.pi += 1
        k.act(pt[0:npart, 0:n], Sb[0:npart, 0:n], AF.Exp, scale=scale)
        self.pend.append((pv_fn, pt, after))
        while len(self.pend) > self.look:
            self._pop()

    def _pop(self):
        pv_fn, pt, after = self.pend.pop(0)
        pv_fn(pt)
        if after is not None:
            after()

    def flush(self):
        while self.pend:
            self._pop()


def STAGES(k, nc, I, DBG, finals, b, L):
    hT, ons = L["hT"], L["ons"]
    stop_after = L["stop_after"]
    nsa_stage(k, nc, I, DBG, finals, b, L)
    if stop_after == "nsa":
        return
    mla_stage(k, nc, I, DBG, finals, b, L)
    if stop_after == "mla":
        return
    out_stage(k, nc, I, DBG, finals, b, L)


def nsa_stage(k, nc, I, DBG, finals, b, L):
    hT, ons, wbuf, wview = L["hT"], L["ons"], L["wbuf"], L["wview"]
    S_B, O_B, M_B = L["S_B"], L["O_B"], L["M_B"]
    ident_bf, tri, anti, ach, m1, m2, vcconst = (L[n] for n in ("ident_bf", "tri", "anti", "ach", "m1", "m2", "vcconst"))
    posb, w2k, w2v = L["posb"], L["w2k"], L["w2v"]
    veD, vP = L["veD"], L["vP"]
    w_inv = I["w_in"].rearrange("(kc p) n -> p kc n", p=128)
    rr = {"s": 0, "o": 0, "m": 0, "pt": 0, "ev": 0}

    def nxt(key, lst):
        v = lst[rr[key] % len(lst)]
        rr[key] += 1
        return v

    def evac(out, in_):
        rr["ev"] += 1
        if rr["ev"] % 2:
            k.cp(out, in_, eng="act")
        else:
            k.cp(out, in_, eng="dve")

    with ExitStack() as e2:
        w1s = [k.sb(f"w1s{i}", [128, 16, 256], BF16, e2) for i in range(2)]
        k.dma("pool", w1s[0][:], I["w1k"].rearrange("(c p) n -> p c n", p=128))
        k.dma("pool", w1s[1][:], I["w1v"].rearrange("(c p) n -> p c n", p=128))
        Qaug = k.sb("Qaug", [128, 4, S], BF16, e2)
        KSaug = k.sb("KSaug", [128, S], BF16, e2)
        KWt = k.sb("KWt", [128, S], BF16, e2)
        KKs = [k.sb(f"KK{i}", [128, S], BF16, e2) for i in range(2)]
        VS = k.sb("VS", [128, NQT, 65], BF16, e2)
        VW = k.sb("VW", [128, NQT, 65], BF16, e2)
        VCaug = k.sb("VCaug", [128, 97], BF16, e2)
        sig = k.sb("sig", [128, NQT, 12], F32, e2)
        hid = [k.sb(f"hid{i}", [128, 2, 127], BF16, e2) for i in range(2)]
        kcT = k.sb("kcT", [128, 127], BF16, e2)
        Tns = [k.sb(f"Tn{i}", [128, 4, 256], BF16, e2) for i in range(2)]
        Wcs = [k.sb(f"Wc{i}", [128, 4, 512], BF16, e2) for i in range(2)]
        PTs = [k.sb(f"pt{i}", [128, 512], BF16, e2) for i in range(4)]
        accn = k.sb("accn", [128, 4, 4, 64], F32, e2)
        sc = k.sb("sc", [128, 4, 32], F32, e2)
        s2 = k.sb("s2", [128, 4, 32], F32, e2)
        cmpm = k.sb("cmpm", [128, 32, 32], BF16, e2)
        rank = k.sb("rank", [128, 4, 32], F32, e2)
        negpad = k.sb("negpad", [128, 4, 96], BF16, e2)
        rd = k.sb("rd", [128, 8, 4], F32, e2)
        ff = k.sb("ff", [128, 8, 4], F32, e2)
        tmp64 = k.sb("tmp64", [128, 4, 64], F32, e2)
        tmp32 = k.sb("tmp32", [128, 4, 32], F32, e2)

        k.memset(Qaug[64:128, :, :], 0.0, eng="pool")
        k.memset(KSaug[64:128, :], 0.0, eng="pool")
        k.memset(KWt[64:128, :], 0.0, eng="pool")
        k.memset(kcT[64:128, :], 0.0)
        for Wc_ in Wcs:
            k.memset(Wc_[:], 0.0, eng="pool")
            k.memset(Wc_[0:1, :, :], NEG)
        k.dma("sp", KSaug[64:96, :], I["blockind"])
        k.memset(VS[:, :, 64:65], 1.0)
        k.memset(VW[:, :, 64:65], 1.0)
        k.memset(VCaug[:], 0.0)
        k.cp(VCaug[:, 64:97], vcconst[:])
        k.memset(negpad[:], 0.0)

        rdi = [0]

        def norm_f(Ob, width, gate_ap):
            i = rdi[0] % 8
            rdi[0] += 1
            den = Ob[:, 0:4 * width].rearrange("p (q c) -> p q c", c=width)[:, :, 64]
            k.ts(rd[:, i, :], den, 1e-30, None, ALU.max)
            k.recip(rd[:, i, :], rd[:, i, :])
            if gate_ap is None:
                return rd[:, i, :]
            k.tt(ff[:, i, :], rd[:, i, :], gate_ap, ALU.mult)
            return ff[:, i, :]

        def load_group(g):
            wt = wbuf()
            wv = wview(wt, 8, 780)
            segs = [(0, 256, C_Q + 256 * g), (256, 64, C_KS + 64 * g), (320, 64, C_KW + 64 * g),
                    (384, 64, C_KC + 64 * g), (448, 64, C_KC + 64 * g), (512, 64, C_VC + 64 * g),
                    (576, 64, C_VC + 64 * g), (640, 64, C_VS + 64 * g), (704, 64, C_VW + 64 * g),
                    (768, 12, C_G + 12 * g)]
            for (o, n, c0) in segs:
                k.dma("pool", wv[:, :, o:o + n], w_inv[:, :, c0:c0 + n])
            for hl in range(4):
                h = 4 * g + hl
                k.dma("pool", Tns[g % 2][:, hl, :], bass.AP(vP, h * 128 * 384 + 127, [[383, 128], [1, 256]]))
                k.dma("pool", Wcs[g % 2][1:41, hl, :], bass.AP(veD, h * 1152, [[16, 40], [1, 512]]))
            return wv

        wv_next = load_group(0)
        for g in range(4):
            wv = wv_next
            Tn = Tns[g % 2]
            Wc = Wcs[g % 2]
            for ch in range(NCH):
                cs = slice(ch * 512, (ch + 1) * 512)
                for hl in range(4):
                    bank = nxt("m", M_B)
                    for kc in range(8):
                        k.mm(bank[0:64, :], wv[:, kc, hl * 64:(hl + 1) * 64], hT[:, kc, cs],
                             start=(kc == 0), stop=(kc == 7))
                    evac(Qaug[0:64, hl, cs], bank[0:64, :])
                for (o, dst) in ((256, KSaug), (320, KWt)):
                    bank = nxt("m", M_B)
                    for kc in range(8):
                        k.mm(bank[0:64, :], wv[:, kc, o:o + 64], hT[:, kc, cs], start=(kc == 0), stop=(kc == 7))
                    evac(dst[0:64, cs], bank[0:64, :])
                for (o, dst) in ((384, KKs[0]), (512, KKs[1])):
                    bank = nxt("m", M_B)
                    for kc in range(8):
                        k.mm(bank[:, :], wv[:, kc, o:o + 128], hT[:, kc, cs], start=(kc == 0), stop=(kc == 7))
                    evac(dst[0:64, cs], bank[0:64, :])
                    if ch == 0:
                        evac(dst[64:128, 0:511], bank[64:128, 1:512])
                    else:
                        evac(dst[64:128, ch * 512 - 1:(ch + 1) * 512 - 1], bank[64:128, :])
            for qt in range(NQT):
                bank = nxt("m", M_B)
                for kc in range(8):
                    k.mm(bank[:, 0:140], hT[:, kc, qt * 128:(qt + 1) * 128], wv[:, kc, 640:780],
                         start=(kc == 0), stop=(kc == 7))
                k.cp(VS[:, qt, 0:64], bank[:, 0:64], eng="dve")
                k.cp(VW[:, qt, 0:64], bank[:, 64:128], eng="dve")
                k.act(sig[:, qt, :], bank[:, 128:140], AF.Sigmoid)
            if g + 1 < 4:
                wv_next = load_group(g + 1)
            for kv in range(2):
                src = KKs[kv]
                for hc in range(2):
                    bank = nxt("m", M_B)
                    for lp in range(16):
                        k.mm(bank[:, 0:127], w1s[kv][:, lp, hc * 128:(hc + 1) * 128],
                             src[:, 2 * lp:2 * lp + 16 * 126 + 1:16], start=(lp == 0), stop=(lp == 15))
                    k.act(hid[kv][:, hc, :], bank[:, 0:127], AF.Gelu_apprx_tanh, bias=posb[:, kv, hc:hc + 1])
            bank = nxt("m", M_B)
            for hc in range(2):
                k.mm(bank[0:64, 0:127], w2k[:, hc, :], hid[0][:, hc, :], start=(hc == 0), stop=(hc == 1))
            k.cp(kcT[0:64, :], bank[0:64, 0:127], eng="dve")
            bank = nxt("m", M_B)
            for hc in range(2):
                k.mm(bank[0:127, 0:64], hid[1][:, hc, :], w2v[:, hc, :], start=(hc == 0), stop=(hc == 1))
            k.cp(VCaug[0:127, 0:64], bank[0:127, 0:64], eng="dve")
            if b == 0 and g == 0 and "kcT" in DBG:
                finals.append(k.dma("sp", DBG["kcT"], kcT[0:64, :]))
                finals.append(k.dma("sp", DBG["vc"], VCaug[:]))
            st = AttnStream(k, S_B + [M_B[1]], PTs, look=2)
            for ch in range(NCH):
                cs = slice(ch * 512, (ch + 1) * 512)
                q0 = 4 * ch

                def cmp_item(hl):
                    Ob = nxt("o", O_B)

                    def s_fn(Sb):
                        k.mm(Sb[0:127, :], kcT[:, :], Qaug[:, hl, cs], start=True, stop=False)
                        k.mm(Sb[0:127, :], ach[:, ch, :], Wc[:, hl, :], start=False, stop=True)

                    def pv_fn(pt):
                        for qi in range(4):
                            k.mm(Ob[:, qi * 97:(qi + 1) * 97], pt[0:127, qi * 128:(qi + 1) * 128], VCaug[0:127, :],
                                 start=(qi == 0), stop=False)

                    def after():
                        Ov = Ob[:, 0:388].rearrange("p (q c) -> p q c", c=97)
                        r1 = norm_f(Ob, 97, None)
                        if hl == 0:
                            k.tt(sc[:], Ov[:, :, 65:97], r1.unsqueeze(2).to_broadcast([128, 4, 32]), ALU.mult)
                        else:
                            k.tt(tmp32[:], Ov[:, :, 65:97], r1.unsqueeze(2).to_broadcast([128, 4, 32]), ALU.mult)
                            k.tt(sc[:], sc[:], tmp32[:], ALU.add)
                        i_ = rdi[0] % 8
                        rdi[0] += 1
                        k.tt(ff[:, i_, :], r1, sig[:, q0:q0 + 4, 3 * hl], ALU.mult)
                        k.tt(accn[:, :, hl, :], Ov[:, :, 0:64], ff[:, i_, :].unsqueeze(2).to_broadcast([128, 4, 64]), ALU.mult)
                    st.add(s_fn, 127, 512, 0.125, pv_fn, after)
                for hl in range(4):
                    cmp_item(hl)
                st.flush()
                k.tt(s2[:], sc[:], m1[:, q0:q0 + 4, :], ALU.mult)
                k.tt(s2[:], s2[:], m2[:, q0:q0 + 4, :], ALU.add)
                for qi in range(4):
                    k.tt(cmpm[:], s2[:, qi, :].unsqueeze(1).to_broadcast([128, 32, 32]),
                         s2[:, qi, :].unsqueeze(2).to_broadcast([128, 32, 32]), ALU.is_gt)
                    k.red(rank[:, qi, :], cmpm[:], ALU.add)
                k.ts(negpad[:, :, 64:96], rank[:], 15.5, NEG, ALU.is_gt, ALU.mult)
                if b == 0 and g == 0 and ch == 3 and "rank" in DBG:
                    finals.append(k.dma("sp", DBG["rank"], rank[:]))
                    finals.append(k.dma("sp", DBG["sc"], sc[:]))

                def br_items(hl, br):
                    Ob = nxt("o", O_B)
                    Ov = Ob[:, 0:260].rearrange("p (q c) -> p q c", c=65)
                    kts = list(range(max(0, q0 - 4), q0 + 4)) if br == 0 else list(range(0, q0 + 4))
                    state = {"first": True}
                    for kt in kts:
                        qlo = max(kt, q0)
                        qhi = min(kt + 4, q0 + 3) if br == 0 else q0 + 3
                        n = 128 * (qhi - qlo + 1)
                        ks = slice(kt * 128, (kt + 1) * 128)
                        qs = slice(qlo * 128, (qhi + 1) * 128)

                        def s_fn(Sb, kt=kt, qlo=qlo, qhi=qhi, n=n, ks=ks, qs=qs):
                            if br == 0:
                                k.mm(Sb[:, 0:n], KWt[:, ks], Qaug[:, hl, qs], start=True, stop=False)
                            else:
                                k.mm(Sb[:, 0:n], KSaug[:, ks], Qaug[:, hl, qs], start=True, stop=False)
                            nlo = max(kt, qlo)
                            nhi = min(kt + 1, qhi)
                            if nlo <= nhi:
                                o = (nlo - qlo) * 128
                                w_ = (nhi - nlo + 1) * 128
                                k.mm(Sb[:, o:o + w_], ident_bf[:], Tn[:, hl, (nlo - kt) * 128:(nhi - kt + 1) * 128],
                                     start=False, stop=False)
                            if br == 0 and qlo <= kt + 4 <= qhi:
                                o = (kt + 4 - qlo) * 128
                                k.mm(Sb[:, o:o + 128], ident_bf[:], anti[:], start=False, stop=False)

                        def pv_fn(pt, kt=kt, qlo=qlo, qhi=qhi):
                            V = VW if br == 0 else VS
                            for qt in range(qlo, qhi + 1):
                                o = (qt - qlo) * 128
                                k.mm(Ob[:, (qt - q0) * 65:(qt - q0 + 1) * 65], pt[:, o:o + 128], V[:, kt, :],
                                     start=state["first"], stop=False)
                                state["first"] = False

                        after = None
                        if kt == kts[-1]:
                            def after():
                                f = norm_f(Ob, 65, sig[:, q0:q0 + 4, 3 * hl + (2 if br == 0 else 1)])
                                k.tt(tmp64[:], Ov[:, :, 0:64], f.unsqueeze(2).to_broadcast([128, 4, 64]), ALU.mult)
                                k.tt(accn[:, :, hl, :], accn[:, :, hl, :], tmp64[:], ALU.add)
                        st.add(s_fn, 128, n, 0.125, pv_fn, after)
                for hl in range(4):
                    br_items(hl, 0)
                st.flush()
                bank = nxt("m", M_B)
                for qi in range(4):
                    k.mm(bank[0:96, qi * 128:(qi + 1) * 128], negpad[:, qi, :], ident_bf[:],
                         start=(qi == 0), stop=False)
                for hl in range(4):
                    evac(Qaug[64:96, hl, cs], bank[64:96, :])
                for hl in range(4):
                    br_items(hl, 1)
                st.flush()
                k.cp(ons[:, q0:q0 + 4, 256 * g:256 * (g + 1)], accn[:].rearrange("p q h d -> p q (h d)"), eng="dve")
        if b == 0 and "ons" in DBG:
            finals.append(k.dma("sp", DBG["ons"], ons[:]))
        k.barrier()
    print("sbuf remaining after nsa scope", nc.sbuf_bytes_remaining)


def mla_stage(k, nc, I, DBG, finals, b, L):
    hT, ons, wbuf, wview = L["hT"], L["ons"], L["wbuf"], L["wview"]
    S_B, O_B, M_B = L["S_B"], L["O_B"], L["M_B"]
    ident_bf, tri, ones_bf, invf, sgn, qg, kvg = (L[n] for n in ("ident_bf", "tri", "ones_bf", "invf", "sgn", "qg", "kvg"))
    w_inv = I["w_in"].rearrange("(kc p) n -> p kc n", p=128)
    rr = {"s": 0, "o": 0, "m": 0, "pt": 0, "ev": 0}
    PI = math.pi

    def nxt(key, lst):
        v = lst[rr[key] % len(lst)]
        rr[key] += 1
        return v

    def evac(out, in_):
        rr["ev"] += 1
        if rr["ev"] % 2:
            k.cp(out, in_, eng="act")
        else:
            k.cp(out, in_, eng="dve")

    with ExitStack() as e3:
        cqn = k.sb("cqn", [128, 2, S], BF16, e3)
        ckvn = k.sb("ckvn", [128, S], BF16, e3)
        COS2 = k.sb("COS2", [96, S], F32, e3)
        SIN2 = k.sb("SIN2", [96, S], F32, e3)
        krr = k.sb("krr", [96, S], BF16, e3)
        QmT = [k.sb(f"QmT{i}", [128, S], BF16, e3) for i in range(2)]
        KmT = [k.sb(f"KmT{i}", [128, S], BF16, e3) for i in range(2)]
        for i in range(2):
            k.memset(QmT[i][64:128, :], 0.0, eng="pool")
            k.memset(KmT[i][64:128, :], 0.0, eng="pool")
        Vh = [k.sb(f"Vh{i}", [128, NQT, 129], BF16, e3) for i in range(2)]
        sbh = [k.sb(f"sbh{i}", [128, NQT, 128], BF16, e3) for i in range(2)]
        wmb = [k.sb(f"wmb{i}", [128, 8, 128], BF16, e3) for i in range(2)]
        PTs = [k.sb(f"mpt{i}", [128, 512], BF16, e3) for i in range(3)]
        raw = [k.sb(f"raw{i}", [128, 512], F32, e3) for i in range(3)]
        sq = [k.sb(f"sq{i}", [128, 512], BF16, e3) for i in range(3)]
        rstd = k.sb("rstd", [128, 512], F32, e3)
        posi = k.sb("posi", [96, 512], I32, e3)
        ang = raw[0]
        targ = raw[1]
        t1 = k.sb("t1", [96, 512], F32, e3)
        t2 = k.sb("t2", [96, 512], F32, e3)
        satmp = sq[0]
        otmp = k.sb("otmp", [128, 2, 128], F32, e3)
        rd = k.sb("mrd", [128, 8, 2], F32, e3)
        for i in range(2):
            k.memset(Vh[i][:, :, 128:129], 1.0)

        R = slice(64, 96)
        for ch in range(NCH):
            cs = slice(ch * 512, (ch + 1) * 512)
            k.dma("sp", posi[:], I["pos"][b, cs].partition_broadcast(96))
            k.cp(ang[R, :], posi[R, :])
            k.ts(ang[R, :], ang[R, :], invf[R, 0:1], None, ALU.mult)
            k.ts(targ[R, :], ang[R, :], 1.0 / (2 * PI), None, ALU.mult)
            k.cp(posi[R, :], targ[R, :])
            k.cp(targ[R, :], posi[R, :])
            k.stt(ang[R, :], targ[R, :], -2 * PI, ang[R, :], ALU.mult, ALU.add)
            k.ts(targ[R, :], ang[R, :], PI, -2 * PI, ALU.is_gt, ALU.mult)
            k.tt(ang[R, :], ang[R, :], targ[R, :], ALU.add)
            k.act(SIN2[R, cs], ang[R, :], AF.Sin)
            k.ts(SIN2[R, cs], SIN2[R, cs], sgn[R, 0:1], None, ALU.mult)
            k.ts(ang[R, :], ang[R, :], 0.5 * PI, None, ALU.add)
            k.ts(targ[R, :], ang[R, :], PI, -2 * PI, ALU.is_gt, ALU.mult)
            k.tt(ang[R, :], ang[R, :], targ[R, :], ALU.add)
            k.act(COS2[R, cs], ang[R, :], AF.Sin)

        for j in range(2):
            wt = wbuf()
            wv = wview(wt, 8, 512)
            k.dma("pool", wv, w_inv[:, :, C_MA + 512 * j:C_MA + 512 * (j + 1)])
            for qt in range(NQT):
                bank = nxt("m", M_B)
                for kc in range(8):
                    k.mm(bank[:, :], hT[:, kc, qt * 128:(qt + 1) * 128], wv[:, kc, :], start=(kc == 0), stop=(kc == 7))
                k.act(satmp[:], bank[:, :], AF.Sigmoid)
                k.tt(ons[:, qt, 512 * j:512 * (j + 1)], ons[:, qt, 512 * j:512 * (j + 1)], satmp[:], ALU.mult)

        wt = wbuf()
        wv = wview(wt, 8, 576)
        k.memset(wv[:, :, 384:576], 0.0)
        k.dma("pool", wv[:, :, 0:256], w_inv[:, :, C_CQ:C_CQ + 256])
        k.dma("pool", wv[:, :, 256:384], w_inv[:, :, C_CKV:C_CKV + 128])
        k.dma("pool", wv[:, :, 448:480], w_inv[:, :, C_KR:C_KR + 32])
        k.dma("pool", wv[:, :, 544:560], w_inv[:, :, C_KR + 16:C_KR + 32])
        k.dma("pool", wv[:, :, 560:576], w_inv[:, :, C_KR:C_KR + 16])
        for ch in range(NCH):
            cs = slice(ch * 512, (ch + 1) * 512)
            for m in range(3):
                bank = nxt("m", M_B)
                for kc in range(8):
                    k.mm(bank[:, :], wv[:, kc, m * 128:(m + 1) * 128], hT[:, kc, cs], start=(kc == 0), stop=(kc == 7))
                k.act(sq[m][:], bank[:, :], AF.Square)
                k.cp(raw[m][:], bank[:, :], eng="dve")
            bank = nxt("m", M_B)
            k.mm(bank[:, :], ones_bf[:], sq[0][:], start=True, stop=False)
            k.mm(bank[:, :], ones_bf[:], sq[1][:], start=False, stop=True)
            k.act(rstd[:], bank[:, :], AF.Sqrt, bias=1e-6, scale=1.0 / 256)
            k.recip(rstd[:], rstd[:])
            for m in range(2):
                k.stt(cqn[:, m, cs], raw[m][:], qg[:, m:m + 1], rstd[:], ALU.mult, ALU.mult)
            bank = nxt("m", M_B)
            k.mm(bank[:, :], ones_bf[:], sq[2][:], start=True, stop=True)
            k.act(rstd[:], bank[:, :], AF.Sqrt, bias=1e-6, scale=1.0 / 128)
            k.recip(rstd[:], rstd[:])
            k.stt(ckvn[:, cs], raw[2][:], kvg[:, 0:1], rstd[:], ALU.mult, ALU.mult)
            bank = nxt("m", M_B)
            for kc in range(8):
                k.mm(bank[0:96, :], wv[:, kc, 384:480], hT[:, kc, cs], start=(kc == 0), stop=(kc == 7))
            k.tt(t1[R, :], bank[R, :], COS2[R, cs], ALU.mult)
            bank = nxt("m", M_B)
            for kc in range(8):
                k.mm(bank[0:96, :], wv[:, kc, 480:576], hT[:, kc, cs], start=(kc == 0), stop=(kc == 7))
            k.tt(t2[R, :], bank[R, :], SIN2[R, cs], ALU.mult)
            k.tt(krr[R, cs], t1[R, :], t2[R, :], ALU.add)

        wt = wbuf()
        wuq = wview(wt, 2, 768, 0)
        wus = wview(wt, 2, 768, 1536)
        wkv = wt[:, 3072:3072 + 1536]
        k.dma("pool", wuq, I["w_uq"].rearrange("(c p) n -> p c n", p=128))
        k.dma("pool", wus, I["w_uq_sw"].rearrange("(c p) n -> p c n", p=128))
        k.dma("pool", wkv, I["w_ukv"])
        scale = 96 ** -0.5
        def proj(h):
            Q, Kt, V, sb_, wm = QmT[h % 2], KmT[h % 2], Vh[h % 2], sbh[h % 2], wmb[h % 2]
            k.dma("pool", wm[:], w_inv[:, :, C_MB + 128 * h:C_MB + 128 * (h + 1)])
            k.cp(Kt[R, :], krr[R, :], eng="dve")
            for ch in range(NCH):
                cs = slice(ch * 512, (ch + 1) * 512)
                bankA = nxt("m", M_B)
                for kc in range(2):
                    k.mm(bankA[0:96, :], wuq[:, kc, h * 96:(h + 1) * 96], cqn[:, kc, cs], start=(kc == 0), stop=(kc == 1))
                k.cp(Q[0:64, cs], bankA[0:64, :], eng="dve")
                k.tt(t1[R, :], bankA[R, :], COS2[R, cs], ALU.mult)
                bankB = nxt("m", M_B)
                for kc in range(2):
                    k.mm(bankB[0:96, :], wus[:, kc, h * 96:(h + 1) * 96], cqn[:, kc, cs], start=(kc == 0), stop=(kc == 1))
                k.tt(t2[R, :], bankB[R, :], SIN2[R, cs], ALU.mult)
                k.tt(Q[R, cs], t1[R, :], t2[R, :], ALU.add)
                yield
                bank = nxt("m", M_B)
                k.mm(bank[0:64, :], wkv[:, h * 192:h * 192 + 64], ckvn[:, cs], start=True, stop=True)
                k.cp(Kt[0:64, cs], bank[0:64, :], eng="dve")
                yield
                bank = nxt("m", M_B)
                for qi in range(4):
                    qt = ch * 4 + qi
                    k.mm(bank[:, qi * 128:(qi + 1) * 128], ckvn[:, qt * 128:(qt + 1) * 128],
                         wkv[:, h * 192 + 64:h * 192 + 192], start=(qi == 0), stop=False)
                k.cp(V[:, ch * 4:ch * 4 + 4, 0:128], bank[:, :].rearrange("p (q c) -> p q c", c=128), eng="dve")
                yield
                bank = nxt("m", M_B)
                first = True
                for qi in range(4):
                    qt = ch * 4 + qi
                    for kc in range(8):
                        k.mm(bank[:, qi * 128:(qi + 1) * 128], hT[:, kc, qt * 128:(qt + 1) * 128], wm[:, kc, :],
                             start=first, stop=False)
                        first = False
                k.act(sb_[:, ch * 4:ch * 4 + 4, :], bank[:, :].rearrange("p (q c) -> p q c", c=128), AF.Sigmoid)
                yield

        st = AttnStream(k, S_B, PTs, look=1)

        def attn(h, gen):
            Q, Kt, V, sb_ = QmT[h % 2], KmT[h % 2], Vh[h % 2], sbh[h % 2]
            cnt = [0]
            for ch in range(NCH):
                q0 = 4 * ch
                OA = O_B[(rr["o"] % 2) * 2]
                OBk = O_B[(rr["o"] % 2) * 2 + 1]
                rr["o"] += 1
                firsts = {0: True, 1: True}
                for kt in range(0, q0 + 4):
                    qlo = max(kt, q0)
                    qhi = q0 + 3
                    n = 128 * (qhi - qlo + 1)

                    def s_fn(Sb, kt=kt, qlo=qlo, qhi=qhi, n=n, q0=q0):
                        k.mm(Sb[:, 0:n], Kt[:, kt * 128:(kt + 1) * 128], Q[:, qlo * 128:(qhi + 1) * 128],
                             start=True, stop=False)
                        if kt >= q0:
                            k.mm(Sb[:, 0:128], ident_bf[:], tri[:], start=False, stop=False)

                    def pv_fn(pt, kt=kt, qlo=qlo, qhi=qhi, firsts=firsts, OA=OA, OBk=OBk, q0=q0):
                        for qt in range(qlo, qhi + 1):
                            o = (qt - qlo) * 128
                            bi = (qt - q0) // 2
                            sl = (qt - q0) % 2
                            bank = OA if bi == 0 else OBk
                            k.mm(bank[:, sl * 129:(sl + 1) * 129], pt[:, o:o + 128], V[:, kt, :],
                                 start=firsts[bi], stop=False)
                            firsts[bi] = False

                    after = None
                    if kt == q0 + 3:
                        def after(OA=OA, OBk=OBk, q0=q0):
                            for bi, bank in enumerate((OA, OBk)):
                                qa = q0 + 2 * bi
                                Ov = bank[:, 0:258].rearrange("p (q c) -> p q c", c=129)
                                i_ = rr["ev"] % 8
                                rr["ev"] += 1
                                k.ts(rd[:, i_, :], Ov[:, :, 128], 1e-30, None, ALU.max)
                                k.recip(rd[:, i_, :], rd[:, i_, :])
                                k.tt(otmp[:], Ov[:, :, 0:128], rd[:, i_, :].unsqueeze(2).to_broadcast([128, 2, 128]), ALU.mult)
                                k.tt(otmp[:], otmp[:], sb_[:, qa:qa + 2, :], ALU.mult)
                                k.tt(ons[:, qa:qa + 2, 128 * h:128 * (h + 1)], ons[:, qa:qa + 2, 128 * h:128 * (h + 1)],
                                     otmp[:], ALU.add)
                    st.add(s_fn, 128, n, scale, pv_fn, after)
                    cnt[0] += 1
                    if gen is not None and cnt[0] % 2 == 0:
                        next(gen, None)

        for _ in proj(0):
            pass
        for h in range(8):
            gen = proj(h + 1) if h + 1 < 8 else None
            attn(h, gen)
            st.flush()
            if gen is not None:
                for _ in gen:
                    pass
        if b == 0 and "y" in DBG:
            finals.append(k.dma("sp", DBG["y"], ons[:]))
        k.barrier()


def sgn_negpi(L, k):
    if "negpi" not in L["cache"]:
        t = k.sb("negpi", [128, 1], F32)
        k.memset(t[:], -math.pi)
        L["cache"]["negpi"] = t
    return L["cache"]["negpi"]


def out_stage(k, nc, I, DBG, finals, b, L):
    ons, wview, WB = L["ons"], L["wview"], L["WB"]
    PS = L["PS"]
    M_B = L["M_B"]
    ident_bf, sc2, modFM, fng_b, convw, convb = (L[n] for n in ("ident_bf", "sc2", "modFM", "fng_b", "convw", "convb"))
    gD, x1D, out = L["gD"], L["x1D"], L["out"]
    xv = I["x"]
    rr = {"b": 0, "ev": 0}

    def nb():
        v = PS[rr["b"] % 8]
        rr["b"] += 1
        return v

    def bfv(bank):
        return bank[:].bitcast(BF16)

    with ExitStack() as e4:
        h2T = L["hT"]
        gm_b, gf_b, ss = L["gm_b"], L["gf_b"], L["ss4"]
        with ExitStack() as e4a:
            yT = k.sb("yT", [128, 8, 512], BF16, e4a)
            xts = [k.sb(f"x4t{i}", [128, D], F32, e4a) for i in range(2)]
            x1ts = [k.sb(f"x1t{i}", [128, D], F32, e4a) for i in range(2)]
            xns = [k.sb(f"x4n{i}", [128, D], BF16, e4a) for i in range(4)]
            junk = k.sb("junk4", [128, D], BF16, e4a)
            tmp = k.sb("tmp4", [128, 512], F32, e4a)
            wo = wview(WB[0], 8, 1024)
            k.dma("pool", wo, I["w_o"].rearrange("(kc p) n -> p kc n", p=128))
            for ch in range(NCH):
                for kp in range(4):
                    bank = nb()
                    bv = bfv(bank)
                    for kk in range(2):
                        kc = kp * 2 + kk
                        for qi in range(4):
                            k.tr(bv[:, (kk * 4 + qi) * 128:(kk * 4 + qi + 1) * 128],
                                 ons[:, ch * 4 + qi, kc * 128:(kc + 1) * 128], ident_bf[:])
                    k.cp(yT[:, kp * 2:kp * 2 + 2, :], bv[:, :].rearrange("p (a n) -> p a n", a=2),
                         eng=("act" if kp % 2 else "dve"))
                for qi in range(4):
                    qt = ch * 4 + qi
                    xt = xts[qt % 2]
                    x1t = x1ts[qt % 2]
                    k.dma("sp", xt[:], xv[b, qt * 128:(qt + 1) * 128, :])
                    for n in range(2):
                        bank = nb()
                        for kc in range(8):
                            k.mm(bank[:, :], yT[:, kc, qi * 128:(qi + 1) * 128], wo[:, kc, n * 512:(n + 1) * 512],
                                 start=(kc == 0), stop=(kc == 7))
                        k.tt(tmp[:], bank[:, :], gm_b[:, n * 512:(n + 1) * 512], ALU.mult)
                        k.tt(x1t[:, n * 512:(n + 1) * 512], tmp[:], xt[:, n * 512:(n + 1) * 512], ALU.add)
                    k.dma("sp", x1D.ap()[b, qt * 128:(qt + 1) * 128, :], x1t[:])
                    k.act(junk[:], x1t[:], AF.Square, accum_out=ss[:, qi:qi + 1])
                    k.act(ss[:, qi:qi + 1], ss[:, qi:qi + 1], AF.Sqrt, bias=1e-6, scale=1.0 / D)
                    k.recip(ss[:, qi:qi + 1], ss[:, qi:qi + 1])
                    k.ts(xns[qi][:], x1t[:], ss[:, qi:qi + 1], None, ALU.mult)
                for kp in range(4):
                    bank = nb()
                    bv = bfv(bank)
                    for kk in range(2):
                        kc = kp * 2 + kk
                        for qi in range(4):
                            k.tr(bv[:, (kk * 4 + qi) * 128:(kk * 4 + qi + 1) * 128],
                                 xns[qi][:, kc * 128:(kc + 1) * 128], ident_bf[:])
                    for kk in range(2):
                        kc = kp * 2 + kk
                        k.act(h2T[:, kc, ch * 512:(ch + 1) * 512], bv[:, kk * 512:(kk + 1) * 512],
                              AF.Identity, bias=modFM[:, 24 + kc, b:b + 1], scale=sc2[:, kc, b:b + 1])
            k.barrier()
        L["ses2"].close()
        if b == 0 and "h2T" in DBG:
            finals.append(k.dma("sp", DBG["h2T"], h2T[:]))
        with ExitStack() as e4b:
            aT = k.sb("aT", [128, NFC, S], BF16, e4b)
            gbuf = k.sb("gbuf", [128, S + 2], F32, e4b)
            cv = k.sb("cv", [128, S], F32, e4b)
            sg = k.sb("sg", [128, S], BF16, e4b)
            k.memset(gbuf[:, 0:2], 0.0)
            wgv = I["w_gate"].rearrange("(kc p) n -> p kc n", p=128)
            wuv = I["w_up"].rearrange("(kc p) n -> p kc n", p=128)
            gi = 0
            for f0 in range(0, NFC, 4):
                nf = min(4, NFC - f0)
                wt = WB[gi % 2]
                gi += 1
                wg = wview(wt, 8, 512, 0)
                wu = wview(wt, 8, 512, 4096)
                k.dma("pool", wg[:, :, 0:nf * 128], wgv[:, :, f0 * 128:(f0 + nf) * 128])
                k.dma("pool", wu[:, :, 0:nf * 128], wuv[:, :, f0 * 128:(f0 + nf) * 128])
                for fl in range(nf):
                    fc = f0 + fl
                    for tch in range(NCH):
                        cs = slice(tch * 512, (tch + 1) * 512)
                        bank = nb()
                        for kc in range(8):
                            k.mm(bank[:, :], wg[:, kc, fl * 128:(fl + 1) * 128], h2T[:, kc, cs], start=(kc == 0), stop=(kc == 7))
                        k.cp(gbuf[:, 2 + tch * 512:2 + (tch + 1) * 512], bank[:, :], eng="act")
                    k.ts(cv[:], gbuf[:, 2:S + 2], convw[:, fc, 2:3], convb[:, fc:fc + 1], ALU.mult, ALU.add)
                    k.stt(cv[:], gbuf[:, 1:S + 1], convw[:, fc, 1:2], cv[:], ALU.mult, ALU.add)
                    k.stt(cv[:], gbuf[:, 0:S], convw[:, fc, 0:1], cv[:], ALU.mult, ALU.add)
                    k.act(sg[:], cv[:], AF.Silu)
                    for tch in range(NCH):
                        cs = slice(tch * 512, (tch + 1) * 512)
                        bank = nb()
                        for kc in range(8):
                            k.mm(bank[:, :], wu[:, kc, fl * 128:(fl + 1) * 128], h2T[:, kc, cs], start=(kc == 0), stop=(kc == 7))
                        k.tt(aT[:, fc, cs], bank[:, :], sg[:, cs], ALU.mult)
            x1r = [cv[:, i * D:(i + 1) * D] for i in range(2)]
            x2 = [gbuf[:, i * D:(i + 1) * D] for i in range(2)]
            junk = sg[:, 0:D]
            tmp = k.sb("tmp5", [128, 512], F32, e4b)
            wdv = I["w_down"].rearrange("(fc p) n -> p fc n", p=128)
            for qg_ in range(4):
                banks = [[nb() for _ in range(2)] for _ in range(4)]
                for f0 in range(0, NFC, 8):
                    nf = min(8, NFC - f0)
                    wt = WB[gi % 2]
                    gi += 1
                    wd = wview(wt, 8, 1024)
                    k.dma("pool", wd[:, 0:nf, :], wdv[:, f0:f0 + nf, :])
                    for fl in range(nf):
                        fc = f0 + fl
                        for qi in range(4):
                            qt = qg_ * 4 + qi
                            for n in range(2):
                                k.mm(banks[qi][n][:, :], aT[:, fc, qt * 128:(qt + 1) * 128], wd[:, fl, n * 512:(n + 1) * 512],
                                     start=(fc == 0), stop=(fc == NFC - 1))
                for qi in range(4):
                    qt = qg_ * 4 + qi
                    xr = x1r[qt % 2]
                    xo = x2[qt % 2]
                    k.dma("sp", xr, x1D.ap()[b, qt * 128:(qt + 1) * 128, :])
                    for n in range(2):
                        k.tt(tmp[:], banks[qi][n][:, :], gf_b[:, n * 512:(n + 1) * 512], ALU.mult)
                        k.tt(xo[:, n * 512:(n + 1) * 512], tmp[:], xr[:, n * 512:(n + 1) * 512], ALU.add)
                    k.act(junk, xo, AF.Square, accum_out=ss[:, 4 + qi:5 + qi])
                    k.act(ss[:, 4 + qi:5 + qi], ss[:, 4 + qi:5 + qi], AF.Sqrt, bias=1e-6, scale=1.0 / D)
                    k.recip(ss[:, 4 + qi:5 + qi], ss[:, 4 + qi:5 + qi])
                    k.stt(xo, xo, ss[:, 4 + qi:5 + qi], fng_b[:], ALU.mult, ALU.mult)
                    finals.append(k.dma("sp", out[b, qt * 128:(qt + 1) * 128, :], xo))
            k.barrier()
        k.barrier()


def shared_inputs(inp):
    f = np.float32
    m = {}
    m["rel"] = inp["rel_bias_table"].astype(f)
    m["ada_w"] = inp["ada_w"][0]
    m["ada_bT"] = np.ascontiguousarray(inp["ada_b"][0].reshape(48, 128).T)
    m["nmg"] = np.ascontiguousarray(inp["norm_mix_g"][0].reshape(8, 128).T)
    m["w_in"] = inp["w_in"][0]
    m["posk"] = np.ascontiguousarray(inp["cmp_pos_k"][0].reshape(16, 128).T)
    m["posv"] = np.ascontiguousarray(inp["cmp_pos_v"][0].reshape(16, 128).T)
    m["w1k"] = inp["cmp_w1_k"][0]
    m["w2k"] = inp["cmp_w2_k"][0]
    m["w1v"] = inp["cmp_w1_v"][0]
    m["w2v"] = inp["cmp_w2_v"][0]
    m["qg"] = np.ascontiguousarray(inp["mla_q_norm_g"][0].reshape(2, 128).T)
    m["kvg"] = np.ascontiguousarray(inp["mla_kv_norm_g"][0].reshape(1, 128).T)
    wuq = inp["mla_w_uq"][0]
    m["w_uq"] = wuq
    sw = wuq.reshape(256, 8, 96).copy()
    sw[:, :, 64:80] = wuq.reshape(256, 8, 96)[:, :, 80:96]
    sw[:, :, 80:96] = wuq.reshape(256, 8, 96)[:, :, 64:80]
    m["w_uq_sw"] = np.ascontiguousarray(sw.reshape(256, 768))
    m["w_ukv"] = inp["mla_w_ukv"][0]
    m["w_o"] = inp["w_o"][0]
    m["nfg"] = np.ascontiguousarray(inp["norm_ffn_g"][0].reshape(8, 128).T)
    m["w_gate"] = inp["ffn_w_gate"][0]
    m["w_up"] = inp["ffn_w_up"][0]
    m["convw"] = np.ascontiguousarray(inp["ffn_conv_w"][0].reshape(3, NFC, 128).transpose(2, 1, 0))
    m["convb"] = np.ascontiguousarray(inp["ffn_conv_b"][0].reshape(NFC, 128).T)
    m["w_down"] = inp["ffn_w_down"][0]
    m["fng"] = inp["final_norm_g"]
    m.update(make_consts())
    return {k_: np.ascontiguousarray(v) for k_, v in m.items()}


_CACHE = {}


def kernel(**inputs):
    inp = {k_: np.asarray(v) for k_, v in inputs.items()}
    shared = shared_inputs(inp)
    if "nc" not in _CACHE:
        _CACHE["nc"] = build(nseq=4)[0]
    nc = _CACHE["nc"]
    in_maps = []
    for core in range(8):
        bsel = list(range(core * 4, core * 4 + 4))
        m = dict(shared)
        m["x"] = np.ascontiguousarray(inp["x"][bsel], dtype=np.float32)
        m["cT"] = np.ascontiguousarray(inp["c"][bsel].reshape(4, 8, 128).transpose(2, 1, 0), dtype=np.float32)
        m["pos"] = np.ascontiguousarray(inp["positions"][bsel], dtype=np.int32)
        in_maps.append(m)
    res = run_bass_kernel_spmd(nc, in_maps, core_ids=list(range(8)))
    out = np.concatenate([np.asarray(r["out"]) for r in res.results], axis=0)
    return out.astype(np.float32)
```

```python
from contextlib import ExitStack
import math
import numpy as np
import ml_dtypes
import concourse.bass as bass
import concourse.mybir as mybir
from concourse.bass_utils import run_bass_kernel_spmd

F32 = mybir.dt.float32
BF16 = mybir.dt.bfloat16
I32 = mybir.dt.int32
AF = mybir.ActivationFunctionType
ALU = mybir.AluOpType
AX = mybir.AxisListType

EPOCH = 30000
NDSEM = 12

D = 1024
S = 2048
NQT = 16
NCH = 4
DFF = 2816
NFC = 22
NEG = -30000.0
INCOLS = 5072
C_Q, C_KC, C_VC, C_KS, C_VS, C_KW, C_VW, C_G, C_CQ, C_CKV, C_KR, C_MA, C_MB = (
    0, 1024, 1280, 1536, 1792, 2048, 2304, 2560, 2608, 2864, 2992, 3024, 4048)


class KB:
    def __init__(self, nc):
        self.nc = nc
        self.es = ExitStack()
        self.eng = {"pe": nc.tensor, "act": nc.scalar, "dve": nc.vector,
                    "pool": nc.gpsimd, "sp": nc.sync}
        self.cnt = {e: 0 for e in ("pe", "act", "dve", "pool")}
        self.csem = {e: [] for e in self.cnt}
        self.dcnt = {}
        self.dsem = {}
        self.waited = {}
        self.recs = {}
        self.nwaits = 0
        self.ninst = 0

    def sb(self, name, shape, dt, es=None):
        self.uid = getattr(self, "uid", 0) + 1
        return (es or self.es).enter_context(self.nc.sbuf_tensor(f"s{self.uid}_{name}", list(shape), dt))

    def ps(self, name, shape, dt):
        return self.es.enter_context(self.nc.psum_tensor(name, list(shape), dt))

    def _sem(self, name):
        return self.es.enter_context(self.nc.semaphore(name))

    @staticmethod
    def box(ap):
        t = ap.tensor
        space = str(ap.space)
        if space == "DRAM":
            lo = int(ap.offset)
            hi = lo
            for st, n in ap.ap:
                if n > 1:
                    if st >= 0:
                        hi += st * (n - 1)
                    else:
                        lo += st * (n - 1)
            return (t.name, 0, 1, lo, hi + 1, False)
        if space == "PSUM":
            return (t.name, 0, 128, 0, 1 << 30, True)
        row = 1
        for s_ in list(t.shape)[1:]:
            row *= s_
        off = int(ap.offset)
        p0 = off // row
        f0 = off % row
        apl = list(ap.ap)
        lo = f0
        hi = f0
        for st, n in apl[1:]:
            if n > 1:
                if st >= 0:
                    hi += st * (n - 1)
                else:
                    lo += st * (n - 1)
        return (t.name, p0, p0 + apl[0][1], lo, hi + 1, False)

    def _wait_compute(self, waiter, e, idx):
        key = (waiter, e)
        if self.waited.get(key, -1) >= idx:
            return
        self.waited[key] = idx
        self.eng[waiter].wait_ge(self.csem[e][idx // EPOCH], (idx % EPOCH) + 1)
        self.nwaits += 1

    def _wait_dma(self, waiter, q, k):
        slot = k % NDSEM
        val = 16 * (k // NDSEM + 1)
        key = (waiter, "dma", q, slot)
        if self.waited.get(key, 0) >= val:
            return
        self.waited[key] = val
        self.eng[waiter].wait_ge(self.dsem[q][slot], val)
        self.nwaits += 1

    def _wait(self, waiter, who):
        if who[0] == "dma":
            self._wait_dma(waiter, who[1], who[2])
        else:
            self._wait_compute(waiter, who[0], who[1])

    @staticmethod
    def _ovl(a, b):
        return a[1] < b[2] and b[1] < a[2] and a[3] < b[4] and b[3] < a[4]

    @staticmethod
    def _covers(a, b):
        return a[1] <= b[1] and a[2] >= b[2] and a[3] <= b[3] and a[4] >= b[4]

    def _deps(self, eng, reads, writes, is_dma):
        deps = []
        for ap in reads:
            bx = self.box(ap)
            for (rb, kind, who) in self.recs.get(bx[0], ()):
                if not self._ovl(bx, rb):
                    continue
                same = (not is_dma) and who[0] == eng
                if kind == "w":
                    if same and eng == "pe":
                        continue
                    deps.append(who)
                elif bx[5] and not same:
                    deps.append(who)
        for ap in writes:
            bx = self.box(ap)
            for (rb, kind, who) in self.recs.get(bx[0], ()):
                if not self._ovl(bx, rb):
                    continue
                same = (not is_dma) and who[0] == eng
                if same and eng == "pe":
                    continue
                deps.append(who)
        return deps

    def _record(self, who, reads, writes):
        for ap in writes:
            bx = self.box(ap)
            lst = self.recs.setdefault(bx[0], [])
            if bx[5]:
                lst[:] = []
            else:
                lst[:] = [r for r in lst if not self._covers(bx, r[0])]
            lst.append((bx, "w", who))
        for ap in reads:
            bx = self.box(ap)
            lst = self.recs.setdefault(bx[0], [])
            if who[0] != "dma":
                lst[:] = [r for r in lst
                          if not (r[1] == "r" and r[2][0] == who[0] and r[0] == bx)]
            lst.append((bx, "r", who))

    def op(self, eng, fn, reads, writes):
        for who in self._deps(eng, reads, writes, False):
            self._wait(eng, who)
        idx = self.cnt[eng]
        ep = idx // EPOCH
        while len(self.csem[eng]) <= ep:
            self.csem[eng].append(self._sem(f"c_{eng}_{len(self.csem[eng])}"))
        ins = fn()
        ins.then_inc(self.csem[eng][ep], 1)
        self.cnt[eng] = idx + 1
        self.ninst += 1
        self._record((eng, idx), reads, writes)
        return (eng, idx)

    def dma(self, q, out, in_, **kw):
        if q not in self.dsem:
            self.dsem[q] = [self._sem(f"d_{q}_{i}") for i in range(NDSEM)]
            self.dcnt[q] = 0
        k = self.dcnt[q]
        if k >= NDSEM:
            self._wait_dma(q, q, k - NDSEM)
        for who in self._deps(q, [in_], [out], True):
            self._wait(q, who)
        ins = self.eng[q].dma_start(out=out, in_=in_, **kw)
        ins.then_inc(self.dsem[q][k % NDSEM], 16)
        self.dcnt[q] = k + 1
        self.ninst += 1
        who = ("dma", q, k)
        self._record(who, [in_], [out])
        return who

    def finish(self, whos, eng="sp"):
        for who in whos:
            self._wait(eng, who)

    def barrier(self):
        for w in ("pe", "act", "dve", "pool", "sp"):
            for e in ("pe", "act", "dve", "pool"):
                if e != w and self.cnt[e] > 0:
                    self._wait_compute(w, e, self.cnt[e] - 1)
            for q in self.dsem:
                n = self.dcnt[q]
                for k in range(max(0, n - NDSEM), n):
                    self._wait_dma(w, q, k)
        self.recs = {}

    def mm(self, out, lhsT, rhs, start=True, stop=True):
        return self.op("pe", lambda: self.nc.tensor.matmul(
            out, lhsT, rhs, start=start, stop=stop, skip_group_check=True), [lhsT, rhs], [out])

    def tr(self, out, in_, ident):
        return self.op("pe", lambda: self.nc.tensor.transpose(out, in_, ident), [in_, ident], [out])

    def act(self, out, in_, func, bias=None, scale=None, accum_out=None):
        kw = {}
        rd = [in_]
        wr = [out]
        if bias is not None:
            kw["bias"] = bias
            if not isinstance(bias, (int, float)):
                rd.append(bias)
        if scale is not None:
            kw["scale"] = scale
            if not isinstance(scale, (int, float)):
                rd.append(scale)
        if accum_out is not None:
            kw["accum_out"] = accum_out
            wr.append(accum_out)
        return self.op("act", lambda: self.nc.scalar.activation(out, in_, func, **kw), rd, wr)

    def _ve(self, eng):
        return self.nc.vector if eng == "dve" else self.nc.gpsimd

    def tt(self, out, in0, in1, op, eng="dve"):
        return self.op(eng, lambda: self._ve(eng).tensor_tensor(out, in0, in1, op), [in0, in1], [out])

    def ts(self, out, in0, s1, s2, op0, op1=None, eng="dve"):
        rd = [in0]
        for s_ in (s1, s2):
            if s_ is not None and not isinstance(s_, (int, float)):
                rd.append(s_)
        kw = {}
        if op1 is not None:
            kw["op1"] = op1
        return self.op(eng, lambda: self._ve(eng).tensor_scalar(out, in0, s1, s2, op0, **kw), rd, [out])

    def stt(self, out, in0, scalar, in1, op0, op1, eng="dve"):
        rd = [in0, in1]
        if not isinstance(scalar, (int, float)):
            rd.append(scalar)
        return self.op(eng, lambda: self._ve(eng).scalar_tensor_tensor(
            out, in0, scalar, in1, op0, op1), rd, [out])

    def cp(self, out, in_, eng="dve"):
        if eng == "act":
            return self.op("act", lambda: self.nc.scalar.copy(out, in_), [in_], [out])
        return self.op(eng, lambda: self._ve(eng).tensor_copy(out, in_), [in_], [out])

    def red(self, out, in_, op, axis=AX.X, eng="dve"):
        return self.op(eng, lambda: self._ve(eng).tensor_reduce(out, in_, axis, op), [in_], [out])

    def memset(self, ap, v, eng="dve"):
        return self.op(eng, lambda: self._ve(eng).memset(ap, v), [], [ap])

    def recip(self, out, in_):
        return self.op("dve", lambda: self.nc.vector.reciprocal(out, in_), [in_], [out])


def _t5_bucket(n):
    n = np.maximum(n, 0)
    exact = 16
    lr = np.log(np.maximum(n, exact).astype(np.float32) / np.float32(exact)) / np.float32(math.log(128 / exact))
    large = np.minimum(exact + (lr.astype(np.float32) * np.float32(16)).astype(np.int32), 31)
    return np.where(n < exact, n, large)


def make_consts():
    bf = ml_dtypes.bfloat16
    c = {}
    c["ident_bf"] = np.eye(128, dtype=np.float32).astype(bf)
    c["ident_f"] = np.eye(128, dtype=np.float32)
    k = np.arange(128)[:, None]
    q = np.arange(128)[None, :]
    c["tri"] = np.where(q >= k, 0.0, NEG).astype(bf)
    c["anti"] = np.where(q < k, 0.0, NEG).astype(bf)
    t = np.arange(S)[None, :]
    c["blockind"] = (t // 64 == np.arange(32)[:, None]).astype(np.float32).astype(bf)
    n = np.arange(1152)
    dist = n - 511
    oh = np.zeros((33, 1152), np.float32)
    bk = _t5_bucket(dist)
    for i in range(1152):
        if dist[i] >= 0:
            oh[bk[i], i] = 1.0
        else:
            oh[32, i] = 1.0
    c["ohe"] = oh
    a = np.zeros((4, 41, 127), np.float32)
    for ch in range(4):
        for cc in range(127):
            r = cc - 32 * ch
            if r >= 31:
                a[ch, 0, cc] = 1.0
            elif r >= -9:
                a[ch, 31 - r, cc] = 1.0
    ap_ = np.zeros((128, 4, 127), np.float32)
    ap_[:41] = a.transpose(1, 0, 2)
    c["ach"] = ap_.astype(bf)
    m1 = np.ones((128, 16, 32), np.float32)
    m2 = np.zeros((128, 16, 32), np.float32)
    for qt in range(16):
        for p in range(128):
            cur = (qt * 128 + p) // 64
            for j in range(32):
                if j == 0 or j == cur or j == cur - 1:
                    m1[p, qt, j] = 0.0
                    m2[p, qt, j] = 1e30
                elif j > cur:
                    m1[p, qt, j] = 0.0
                    m2[p, qt, j] = -1e30
    c["m1"] = m1
    c["m2"] = m2
    cs = np.arange(127) * 16
    bs = np.arange(32) * 64
    ov = np.clip(np.minimum(cs[:, None] + 32, bs[None, :] + 64) - np.maximum(cs[:, None], bs[None, :]), 0, None) / 32.0
    vcc = np.zeros((128, 33), np.float32)
    vcc[:127, 0] = 1.0
    vcc[:127, 1:] = ov
    c["vcconst"] = vcc.astype(bf)
    invf = np.zeros((128, 1), np.float32)
    sgn = np.ones((128, 1), np.float32)
    fr = (np.float32(10000.0) ** (-np.arange(0, 32, 2, dtype=np.float32) / np.float32(32))).astype(np.float32)
    for i in range(16):
        invf[64 + i, 0] = fr[i]
        invf[80 + i, 0] = fr[i]
        sgn[64 + i, 0] = -1.0
    c["invf"] = invf
    c["sgn"] = sgn
    return c


CONST_DT = {"ident_bf": BF16, "ident_f": F32, "tri": BF16, "anti": BF16, "blockind": BF16, "ohe": F32,
            "ach": BF16, "m1": F32, "m2": F32, "vcconst": BF16, "invf": F32, "sgn": F32}

IN_SPECS = [
    ("x", None, F32), ("cT", None, F32), ("pos", None, I32), ("rel", [32, 16], F32),
    ("ada_w", [D, 6 * D], F32), ("ada_bT", [128, 48], F32), ("nmg", [128, 8], F32),
    ("w_in", [D, INCOLS], F32), ("posk", [128, 16], F32), ("posv", [128, 16], F32),
    ("w1k", [2048, 256], F32), ("w2k", [256, 64], F32), ("w1v", [2048, 256], F32), ("w2v", [256, 64], F32),
    ("qg", [128, 2], F32), ("kvg", [128, 1], F32), ("w_uq", [256, 768], F32), ("w_uq_sw", [256, 768], F32),
    ("w_ukv", [128, 1536], F32), ("w_o", [D, D], F32), ("nfg", [128, 8], F32),
    ("w_gate", [D, DFF], F32), ("w_up", [D, DFF], F32), ("convw", [128, NFC, 3], F32), ("convb", [128, NFC], F32),
    ("w_down", [DFF, D], F32), ("fng", [D], F32),
]


def build(nseq=4, stop_after=None, dbg=()):
    nc = bass.Bass("TRN2", target_bir_lowering=False)
    I = {}
    for name, shape, dt in IN_SPECS:
        if name == "x":
            shape = [nseq, S, D]
        elif name == "cT":
            shape = [128, 8, nseq]
        elif name == "pos":
            shape = [nseq, S]
        I[name] = nc.dram_tensor(name, list(shape), dt, kind="ExternalInput").ap()
    consts = make_consts()
    for name, arr in consts.items():
        I[name] = nc.dram_tensor(name, list(arr.shape), CONST_DT[name], kind="ExternalInput").ap()
    out = nc.dram_tensor("out", [nseq, S, D], F32, kind="ExternalOutput").ap()
    DBG = {}
    for name, shape, dt in dbg:
        DBG[name] = nc.dram_tensor("dbg_" + name, list(shape), dt, kind="ExternalOutput").ap()
    veD = nc.dram_tensor("veD", [16, 1152], F32, kind="Internal")
    vP = nc.dram_tensor("vP", [16, 128, 384], F32, kind="Internal")
    gD = nc.dram_tensor("gD", [nseq, 2, D], F32, kind="Internal")
    x1D = nc.dram_tensor("x1D", [nseq, S, D], F32, kind="Internal")

    k = KB(nc)
    finals = []
    with k.es:
        PS = [k.ps(f"ps{i}", [128, 512], F32) for i in range(8)]
        S_B = PS[0:2]
        O_B = PS[2:6]
        M_B = PS[6:8]

        def bf_view(bank):
            return bank[:].bitcast(BF16)

        def gconst(name, shape, dt, q="sp"):
            t = k.sb("c_" + name, shape, dt)
            k.dma(q, t[:], I[name])
            return t
        ident_bf = gconst("ident_bf", [128, 128], BF16)
        ident_f = gconst("ident_f", [128, 128], F32)
        tri = gconst("tri", [128, 128], BF16)
        anti = gconst("anti", [128, 128], BF16)
        ach = gconst("ach", [128, 4, 127], BF16)
        m1 = gconst("m1", [128, 16, 32], F32)
        m2 = gconst("m2", [128, 16, 32], F32)
        vcconst = gconst("vcconst", [128, 33], BF16)
        invf = gconst("invf", [128, 1], F32)
        sgn = gconst("sgn", [128, 1], F32)
        ada_bT = gconst("ada_bT", [128, 48], F32)
        nmg = gconst("nmg", [128, 8], F32)
        nfg = gconst("nfg", [128, 8], F32)
        qg = gconst("qg", [128, 2], F32)
        kvg = gconst("kvg", [128, 1], F32)
        convw = gconst("convw", [128, NFC, 3], F32)
        convb = gconst("convb", [128, NFC], F32)
        posk = gconst("posk", [128, 16], F32)
        posv = gconst("posv", [128, 16], F32)
        fng_b = k.sb("fng_b", [128, D], F32)
        k.dma("sp", fng_b[:], I["fng"].partition_broadcast(128))
        ones_bf = k.sb("ones_bf", [128, 128], BF16)
        k.memset(ones_bf[:], 1.0)
        w2k = k.sb("w2k", [128, 2, 64], BF16)
        w2v = k.sb("w2v", [128, 2, 64], BF16)
        k.dma("pool", w2k[:], I["w2k"].rearrange("(c p) n -> p c n", p=128))
        k.dma("pool", w2v[:], I["w2v"].rearrange("(c p) n -> p c n", p=128))
        modFM = k.sb("modFM", [128, 48, nseq], F32)
        sc1 = k.sb("sc1", [128, 8, nseq], F32)
        sc2 = k.sb("sc2", [128, 8, nseq], F32)
        posb = k.sb("posb", [128, 2, 2], F32)
        WB = [k.sb(f"wb{i}", [128, 8192], BF16) for i in range(2)]
        wb_i = [0]
        negpi = k.sb("negpi", [128, 1], F32)
        k.memset(negpi[:], -math.pi)

        def wbuf():
            t = WB[wb_i[0] % 2]
            wb_i[0] += 1
            return t

        def wview(t, kc, n, off=0):
            return t[:, off:off + kc * n].rearrange("p (k n) -> p k n", k=kc)

        with ExitStack() as pes:
            cT_sb = k.sb("cT_sb", [128, 8, nseq], F32, pes)
            cond = k.sb("cond", [128, 8, nseq], BF16, pes)
            k.dma("sp", cT_sb[:], I["cT"])
            k.act(cond[:], cT_sb[:], AF.Silu)
            adav = I["ada_w"].rearrange("(kc p) n -> p kc n", p=128)
            mbank = M_B[0]
            first = True
            for jt in range(12):
                wt = wbuf()
                wv = wview(wt, 8, 512)
                k.dma("pool", wv, adav[:, :, jt * 512:(jt + 1) * 512])
                for jj in range(4):
                    j = jt * 4 + jj
                    for kc in range(8):
                        k.mm(mbank[:, j * nseq:(j + 1) * nseq], wv[:, kc, jj * 128:(jj + 1) * 128],
                             cond[:, kc, :], start=first, stop=False)
                        first = False
            k.tt(modFM[:], mbank[:, 0:48 * nseq].rearrange("p (j b) -> p j b", b=nseq),
                 ada_bT[:].unsqueeze(2).to_broadcast([128, 48, nseq]), ALU.add)
            k.stt(sc1[:], modFM[:, 8:16, :], 1.0, nmg[:].unsqueeze(2).to_broadcast([128, 8, nseq]),
                  ALU.add, ALU.mult)
            k.stt(sc2[:], modFM[:, 32:40, :], 1.0, nfg[:].unsqueeze(2).to_broadcast([128, 8, nseq]),
                  ALU.add, ALU.mult)
            gts = k.sb("gts", [8, 2 * nseq, 128], F32, pes)
            for b in range(nseq):
                gbank = (M_B[1], S_B[0])[b % 2]
                for gi, j0 in enumerate((16, 40)):
                    k.tr(gbank[0:8, gi * 128:(gi + 1) * 128], modFM[:, j0:j0 + 8, b], ident_f[:])
                k.cp(gts[:, 2 * b:2 * b + 2, :], gbank[0:8, 0:256].rearrange("p (a n) -> p a n", n=128))
            for b in range(nseq):
                for gi in range(2):
                    k.dma("sp", gD.ap()[b, gi, :].rearrange("(j p) -> j p", p=128), gts[:, b * 2 + gi, :])
            tab = k.sb("tab", [33, 16], F32, pes)
            t31 = k.sb("t31", [32, 16], F32, pes)
            ohe = k.sb("ohe", [33, 1152], F32, pes)
            ve_sb = k.sb("ve_sb", [16, 1152], F32, pes)
            k.dma("sp", tab[0:32, :], I["rel"])
            k.dma("sp", t31[:], I["rel"][31, :].partition_broadcast(32))
            k.dma("sp", ohe[:], I["ohe"])
            k.tt(tab[0:32, :], tab[0:32, :], t31[:], ALU.subtract)
            k.ts(tab[0:32, :], tab[0:32, :], 8.0, None, ALU.mult)
            k.memset(tab[32:33, :], NEG)
            for i in range(3):
                k.mm(mbank[0:16, 0:384], tab[:], ohe[:, i * 384:(i + 1) * 384])
                k.cp(ve_sb[:, i * 384:(i + 1) * 384], mbank[0:16, 0:384])
            k.dma("sp", veD.ap(), ve_sb[:])
            k.dma("sp", vP.ap(), ve_sb[:, 384:768].unsqueeze(1).to_broadcast([16, 128, 384]))
            w1f = k.sb("w1f", [128, 16, 256], F32, pes)
            for kv, (wn, pt) in enumerate((("w1k", posk), ("w1v", posv))):
                k.dma("sp", w1f[:], I[wn].rearrange("(c p) n -> p c n", p=128))
                for hc in range(2):
                    for lp in range(16):
                        k.mm(mbank[:, 256 + kv * 2 + hc:256 + kv * 2 + hc + 1], w1f[:, lp, hc * 128:(hc + 1) * 128],
                             pt[:, lp:lp + 1], start=(lp == 0), stop=(lp == 15))
            k.cp(posb[:], mbank[:, 256:260].rearrange("p (a b) -> p a b", b=2))
            if "modFM" in DBG:
                finals.append(k.dma("sp", DBG["modFM"], modFM[:]))
            if "ve" in DBG:
                finals.append(k.dma("sp", DBG["ve"], ve_sb[:]))
            if "posb" in DBG:
                finals.append(k.dma("sp", DBG["posb"], posb[:]))
            k.barrier()

        if stop_after == "prologue":
            k.finish(finals)
            return nc, k

        xv = I["x"]
        for b in range(nseq):
            with ExitStack() as ses, ExitStack() as ses2:
                gm_b = k.sb("gm_b", [128, D], F32, ses)
                gf_b = k.sb("gf_b", [128, D], F32, ses)
                ss4 = k.sb("ss4", [128, 8], F32, ses)
                k.dma("sp", gm_b[:], gD.ap()[b, 0, :].partition_broadcast(128))
                k.dma("sp", gf_b[:], gD.ap()[b, 1, :].partition_broadcast(128))
                hT = k.sb("hT", [128, 8, S], BF16, ses)
                ons = k.sb("ons", [128, NQT, D], BF16, ses2)

                def norm_to_T(src_tiles_fn, dstT, scv, shv, es_):
                    xts = [k.sb(f"xt{i}", [128, D], F32, es_) for i in range(8)]
                    xns_all = [k.sb(f"xn{i}", [128, D], BF16, es_) for i in range(8)]
                    junk = k.sb("junk", [128, D], BF16, es_)
                    ss_all = k.sb("ss", [128, 16], F32, es_)
                    for ch in range(NCH):
                        xns = xns_all[(ch % 2) * 4:(ch % 2) * 4 + 4]
                        ss = ss_all[:, ch * 4:ch * 4 + 4]
                        xtl = []
                        for qi in range(4):
                            qt = ch * 4 + qi
                            xt = src_tiles_fn(qt, xts[qt % 8])
                            xtl.append(xt)
                            k.act(junk[:], xt, AF.Square, accum_out=ss[:, qi:qi + 1])
                        k.act(ss, ss, AF.Sqrt, bias=1e-6, scale=1.0 / D)
                        k.recip(ss, ss)
                        for qi in range(4):
                            k.ts(xns[qi][:], xtl[qi], ss[:, qi:qi + 1], None, ALU.mult)
                        for kp in range(4):
                            bank = M_B[kp % 2]
                            bv = bf_view(bank)
                            for kk in range(2):
                                kc = kp * 2 + kk
                                for qi in range(4):
                                    k.tr(bv[:, (kk * 4 + qi) * 128:(kk * 4 + qi + 1) * 128],
                                         xns[qi][:, kc * 128:(kc + 1) * 128], ident_bf[:])
                            for kk in range(2):
                                kc = kp * 2 + kk
                                k.act(dstT[:, kc, ch * 512:(ch + 1) * 512], bv[:, kk * 512:(kk + 1) * 512],
                                      AF.Identity, bias=shv(kc), scale=scv(kc))

                with ExitStack() as e1:
                    def load_x(qt, buf):
                        k.dma("sp", buf[:], xv[b, qt * 128:(qt + 1) * 128, :])
                        return buf[:]
                    norm_to_T(load_x, hT, lambda kc: sc1[:, kc, b:b + 1], lambda kc: modFM[:, kc, b:b + 1], e1)
                    k.barrier()
                if "hT" in DBG and b == 0:
                    finals.append(k.dma("sp", DBG["hT"], hT[:]))
                if stop_after == "s1":
                    continue
                STAGES(k, nc, I, DBG, finals, b, locals())
        k.finish(finals)
    return nc, k


class AttnStream:
    def __init__(self, k, S_B, PTs, look=1):
        self.k, self.S_B, self.PTs, self.look = k, S_B, PTs, look
        self.si = 0
        self.pi = 0
        self.pend = []

    def add(self, s_fn, npart, n, scale, pv_fn, after=None):
        k = self.k
        Sb = self.S_B[self.si % len(self.S_B)]
        self.si += 1
        s_fn(Sb)
        pt = self.PTs[self.pi % len(self.PTs)]
        self.pi += 1
        k.act(pt[0:npart, 0:n], Sb[0:npart, 0:n], AF.Exp, scale=scale)
        self.pend.append([pv_fn, pt, after, []])
        while len(self.pend) > self.look:
            self._pop()

    def defer(self, fn):
        if self.pend:
            self.pend[-1][3].append(fn)
        else:
            fn()

    def _pop(self):
        pv_fn, pt, after, deferred = self.pend.pop(0)
        pv_fn(pt)
        if after is not None:
            after()
        for fn in deferred:
            fn()

    def flush(self):
        while self.pend:
            self._pop()


def STAGES(k, nc, I, DBG, finals, b, L):
    hT, ons = L["hT"], L["ons"]
    stop_after = L["stop_after"]
    nsa_stage(k, nc, I, DBG, finals, b, L)
    if stop_after == "nsa":
        return
    mla_stage(k, nc, I, DBG, finals, b, L)
    if stop_after == "mla":
        return
    out_stage(k, nc, I, DBG, finals, b, L)


def nsa_stage(k, nc, I, DBG, finals, b, L):
    hT, ons, wbuf, wview = L["hT"], L["ons"], L["wbuf"], L["wview"]
    S_B, O_B, M_B = L["S_B"], L["O_B"], L["M_B"]
    ident_bf, tri, anti, ach, m1, m2, vcconst = (L[n] for n in ("ident_bf", "tri", "anti", "ach", "m1", "m2", "vcconst"))
    posb, w2k, w2v = L["posb"], L["w2k"], L["w2v"]
    veD, vP = L["veD"], L["vP"]
    w_inv = I["w_in"].rearrange("(kc p) n -> p kc n", p=128)
    rr = {"s": 0, "o": 0, "m": 0, "pt": 0, "ev": 0}

    def nxt(key, lst):
        v = lst[rr[key] % len(lst)]
        rr[key] += 1
        return v

    def evac(out, in_):
        rr["ev"] += 1
        if rr["ev"] % 2:
            k.cp(out, in_, eng="act")
        else:
            k.cp(out, in_, eng="dve")

    with ExitStack() as e2:
        w1s = [k.sb(f"w1s{i}", [128, 16, 256], BF16, e2) for i in range(2)]
        k.dma("pool", w1s[0][:], I["w1k"].rearrange("(c p) n -> p c n", p=128))
        k.dma("pool", w1s[1][:], I["w1v"].rearrange("(c p) n -> p c n", p=128))
        Qaug = k.sb("Qaug", [128, 4, S], BF16, e2)
        KSaug = k.sb("KSaug", [128, S], BF16, e2)
        KWt = k.sb("KWt", [128, S], BF16, e2)
        KKs = [k.sb(f"KK{i}", [128, S], BF16, e2) for i in range(2)]
        VS = k.sb("VS", [128, NQT, 65], BF16, e2)
        VW = k.sb("VW", [128, NQT, 65], BF16, e2)
        VCaug = k.sb("VCaug", [128, 97], BF16, e2)
        sig = k.sb("sig", [128, NQT, 12], F32, e2)
        hid = [k.sb(f"hid{i}", [128, 2, 127], BF16, e2) for i in range(2)]
        kcT = k.sb("kcT", [128, 127], BF16, e2)
        Tns = [k.sb(f"Tn{i}", [128, 4, 256], BF16, e2) for i in range(2)]
        Wcs = [k.sb(f"Wc{i}", [128, 4, 512], BF16, e2) for i in range(2)]
        PTs = [k.sb(f"pt{i}", [128, 512], BF16, e2) for i in range(4)]
        accns = [k.sb(f"accn{i}", [128, 4, 4, 64], F32, e2) for i in range(2)]
        sc = k.sb("sc", [128, 4, 32], F32, e2)
        s2 = k.sb("s2", [128, 4, 32], F32, e2)
        cmpm = k.sb("cmpm", [128, 32, 32], BF16, e2)
        cmpm2 = k.sb("cmpm2", [128, 32, 32], BF16, e2)
        rank = k.sb("rank", [128, 4, 32], F32, e2)
        negpad = k.sb("negpad", [128, 4, 96], BF16, e2)
        rd = k.sb("rd", [128, 8, 4], F32, e2)
        ff = k.sb("ff", [128, 8, 4], F32, e2)
        tmp64 = k.sb("tmp64", [128, 4, 64], F32, e2)
        tmp32 = k.sb("tmp32", [128, 4, 32], F32, e2)

        k.memset(Qaug[64:128, :, :], 0.0, eng="pool")
        k.memset(KSaug[64:128, :], 0.0, eng="pool")
        k.memset(KWt[64:128, :], 0.0, eng="pool")
        k.memset(kcT[64:128, :], 0.0)
        for Wc_ in Wcs:
            k.memset(Wc_[:], 0.0, eng="pool")
            k.memset(Wc_[0:1, :, :], NEG)
        k.dma("sp", KSaug[64:96, :], I["blockind"])
        k.memset(VS[:, :, 64:65], 1.0)
        k.memset(VW[:, :, 64:65], 1.0)
        k.memset(VCaug[:], 0.0)
        k.cp(VCaug[:, 64:97], vcconst[:])
        k.memset(negpad[:], 0.0)

        rdi = [0]

        def norm_f(Ob, width, gate_ap):
            i = rdi[0] % 8
            rdi[0] += 1
            den = Ob[:, 0:4 * width].rearrange("p (q c) -> p q c", c=width)[:, :, 64]
            k.ts(rd[:, i, :], den, 1e-30, None, ALU.max)
            k.recip(rd[:, i, :], rd[:, i, :])
            if gate_ap is None:
                return rd[:, i, :]
            k.tt(ff[:, i, :], rd[:, i, :], gate_ap, ALU.mult)
            return ff[:, i, :]

        def load_group(g):
            wt = wbuf()
            wv = wview(wt, 8, 780)
            segs = [(0, 256, C_Q + 256 * g), (256, 64, C_KS + 64 * g), (320, 64, C_KW + 64 * g),
                    (384, 64, C_KC + 64 * g), (448, 64, C_KC + 64 * g), (512, 64, C_VC + 64 * g),
                    (576, 64, C_VC + 64 * g), (640, 64, C_VS + 64 * g), (704, 64, C_VW + 64 * g),
                    (768, 12, C_G + 12 * g)]
            for (o, n, c0) in segs:
                k.dma("pool", wv[:, :, o:o + n], w_inv[:, :, c0:c0 + n])
            for hl in range(4):
                h = 4 * g + hl
                k.dma("pool", Tns[g % 2][:, hl, :], bass.AP(vP, h * 128 * 384 + 127, [[383, 128], [1, 256]]))
                k.dma("pool", Wcs[g % 2][1:41, hl, :], bass.AP(veD, h * 1152, [[16, 40], [1, 512]]))
            return wv

        wv_next = load_group(0)
        for g in range(4):
            wv = wv_next
            Tn = Tns[g % 2]
            Wc = Wcs[g % 2]
            for ch in range(NCH):
                cs = slice(ch * 512, (ch + 1) * 512)
                for hl in range(4):
                    bank = nxt("m", M_B)
                    for kc in range(8):
                        k.mm(bank[0:64, :], wv[:, kc, hl * 64:(hl + 1) * 64], hT[:, kc, cs],
                             start=(kc == 0), stop=(kc == 7))
                    evac(Qaug[0:64, hl, cs], bank[0:64, :])
                for (o, dst) in ((256, KSaug), (320, KWt)):
                    bank = nxt("m", M_B)
                    for kc in range(8):
                        k.mm(bank[0:64, :], wv[:, kc, o:o + 64], hT[:, kc, cs], start=(kc == 0), stop=(kc == 7))
                    evac(dst[0:64, cs], bank[0:64, :])
                for (o, dst) in ((384, KKs[0]), (512, KKs[1])):
                    bank = nxt("m", M_B)
                    for kc in range(8):
                        k.mm(bank[:, :], wv[:, kc, o:o + 128], hT[:, kc, cs], start=(kc == 0), stop=(kc == 7))
                    evac(dst[0:64, cs], bank[0:64, :])
                    if ch == 0:
                        evac(dst[64:128, 0:511], bank[64:128, 1:512])
                    else:
                        evac(dst[64:128, ch * 512 - 1:(ch + 1) * 512 - 1], bank[64:128, :])
            for qt in range(NQT):
                bank = nxt("m", M_B)
                for kc in range(8):
                    k.mm(bank[:, 0:140], hT[:, kc, qt * 128:(qt + 1) * 128], wv[:, kc, 640:780],
                         start=(kc == 0), stop=(kc == 7))
                k.cp(VS[:, qt, 0:64], bank[:, 0:64], eng="dve")
                k.cp(VW[:, qt, 0:64], bank[:, 64:128], eng="dve")
                k.act(sig[:, qt, :], bank[:, 128:140], AF.Tanh, scale=0.5)
            k.ts(sig[:], sig[:], 0.5, 0.5, ALU.mult, ALU.add)
            if g + 1 < 4:
                wv_next = load_group(g + 1)
            for kv in range(2):
                src = KKs[kv]
                for hc in range(2):
                    bank = nxt("m", M_B)
                    for lp in range(16):
                        k.mm(bank[:, 0:127], w1s[kv][:, lp, hc * 128:(hc + 1) * 128],
                             src[:, 2 * lp:2 * lp + 16 * 126 + 1:16], start=(lp == 0), stop=(lp == 15))
                    k.act(hid[kv][:, hc, :], bank[:, 0:127], AF.Gelu_apprx_tanh, bias=posb[:, kv, hc:hc + 1])
            bank = nxt("m", M_B)
            for hc in range(2):
                k.mm(bank[0:64, 0:127], w2k[:, hc, :], hid[0][:, hc, :], start=(hc == 0), stop=(hc == 1))
            k.cp(kcT[0:64, :], bank[0:64, 0:127], eng="dve")
            bank = nxt("m", M_B)
            for hc in range(2):
                k.mm(bank[0:127, 0:64], hid[1][:, hc, :], w2v[:, hc, :], start=(hc == 0), stop=(hc == 1))
            k.cp(VCaug[0:127, 0:64], bank[0:127, 0:64], eng="dve")
            if b == 0 and g == 0 and "kcT" in DBG:
                finals.append(k.dma("sp", DBG["kcT"], kcT[0:64, :]))
                finals.append(k.dma("sp", DBG["vc"], VCaug[:]))
            st = AttnStream(k, S_B + [M_B[1]], PTs, look=2)

            def do_chunk(ch, part, g=g, Tn=Tn, Wc=Wc):
                cs = slice(ch * 512, (ch + 1) * 512)
                q0 = 4 * ch
                accn = accns[ch % 2]

                def cmp_item(hl):
                    Ob = nxt("o", O_B)

                    def s_fn(Sb):
                        k.mm(Sb[0:127, :], kcT[:, :], Qaug[:, hl, cs], start=True, stop=False)
                        k.mm(Sb[0:127, :], ach[:, ch, :], Wc[:, hl, :], start=False, stop=True)

                    def pv_fn(pt):
                        for qi in range(4):
                            k.mm(Ob[:, qi * 97:(qi + 1) * 97], pt[0:127, qi * 128:(qi + 1) * 128], VCaug[0:127, :],
                                 start=(qi == 0), stop=False)

                    def after():
                        Ov = Ob[:, 0:388].rearrange("p (q c) -> p q c", c=97)
                        r1 = norm_f(Ob, 97, None)
                        if hl == 0:
                            k.tt(sc[:], Ov[:, :, 65:97], r1.unsqueeze(2).to_broadcast([128, 4, 32]), ALU.mult)
                        else:
                            k.tt(tmp32[:], Ov[:, :, 65:97], r1.unsqueeze(2).to_broadcast([128, 4, 32]), ALU.mult)
                            k.tt(sc[:], sc[:], tmp32[:], ALU.add)
                        i_ = rdi[0] % 8
                        rdi[0] += 1
                        k.tt(ff[:, i_, :], r1, sig[:, q0:q0 + 4, 3 * hl], ALU.mult)
                        k.tt(accn[:, :, hl, :], Ov[:, :, 0:64], ff[:, i_, :].unsqueeze(2).to_broadcast([128, 4, 64]), ALU.mult)
                    st.add(s_fn, 127, 512, 0.125, pv_fn, after)
                if part == "cmp":
                    for hl in range(4):
                        cmp_item(hl)

                def topk(q0=q0, ch=ch):
                    k.tt(s2[:], sc[:], m1[:, q0:q0 + 4, :], ALU.mult)
                    k.tt(s2[:], s2[:], m2[:, q0:q0 + 4, :], ALU.add)
                    for qi in range(4):
                        en = "dve"
                        cm = cmpm2 if qi % 2 else cmpm
                        k.tt(cm[:], s2[:, qi, :].unsqueeze(1).to_broadcast([128, 32, 32]),
                             s2[:, qi, :].unsqueeze(2).to_broadcast([128, 32, 32]), ALU.is_gt, eng=en)
                        k.red(rank[:, qi, :], cm[:], ALU.add)
                    k.ts(negpad[:, :, 64:96], rank[:], 15.5, NEG, ALU.is_gt, ALU.mult)
                    if b == 0 and g == 0 and ch == 3 and "rank" in DBG:
                        finals.append(k.dma("sp", DBG["rank"], rank[:]))
                        finals.append(k.dma("sp", DBG["sc"], sc[:]))
                if part == "cmp":
                    st.defer(topk)
                    return

                def br_items(hl, br):
                    Ob = nxt("o", O_B)
                    Ov = Ob[:, 0:260].rearrange("p (q c) -> p q c", c=65)
                    kts = list(range(max(0, q0 - 4), q0 + 4)) if br == 0 else list(range(0, q0 + 4))
                    state = {"first": True}
                    for kt in kts:
                        qlo = max(kt, q0)
                        qhi = min(kt + 4, q0 + 3) if br == 0 else q0 + 3
                        n = 128 * (qhi - qlo + 1)
                        ks = slice(kt * 128, (kt + 1) * 128)
                        qs = slice(qlo * 128, (qhi + 1) * 128)

                        def s_fn(Sb, kt=kt, qlo=qlo, qhi=qhi, n=n, ks=ks, qs=qs):
                            if br == 0:
                                k.mm(Sb[:, 0:n], KWt[:, ks], Qaug[:, hl, qs], start=True, stop=False)
                            else:
                                k.mm(Sb[:, 0:n], KSaug[:, ks], Qaug[:, hl, qs], start=True, stop=False)
                            nlo = max(kt, qlo)
                            nhi = min(kt + 1, qhi)
                            if nlo <= nhi:
                                o = (nlo - qlo) * 128
                                w_ = (nhi - nlo + 1) * 128
                                k.mm(Sb[:, o:o + w_], ident_bf[:], Tn[:, hl, (nlo - kt) * 128:(nhi - kt + 1) * 128],
                                     start=False, stop=False)
                            if br == 0 and qlo <= kt + 4 <= qhi:
                                o = (kt + 4 - qlo) * 128
                                k.mm(Sb[:, o:o + 128], ident_bf[:], anti[:], start=False, stop=False)

                        def pv_fn(pt, kt=kt, qlo=qlo, qhi=qhi):
                            V = VW if br == 0 else VS
                            for qt in range(qlo, qhi + 1):
                                o = (qt - qlo) * 128
                                k.mm(Ob[:, (qt - q0) * 65:(qt - q0 + 1) * 65], pt[:, o:o + 128], V[:, kt, :],
                                     start=state["first"], stop=False)
                                state["first"] = False

                        after = None
                        if kt == kts[-1]:
                            def after():
                                f = norm_f(Ob, 65, sig[:, q0:q0 + 4, 3 * hl + (2 if br == 0 else 1)])
                                k.tt(tmp64[:], Ov[:, :, 0:64], f.unsqueeze(2).to_broadcast([128, 4, 64]), ALU.mult)
                                k.tt(accn[:, :, hl, :], accn[:, :, hl, :], tmp64[:], ALU.add)
                        st.add(s_fn, 128, n, 0.125, pv_fn, after)
                def selmask(cs=cs):
                    bank = M_B[0]
                    for qi in range(4):
                        k.mm(bank[0:96, qi * 128:(qi + 1) * 128], negpad[:, qi, :], ident_bf[:],
                             start=(qi == 0), stop=False)
                    for hl in range(4):
                        evac(Qaug[64:96, hl, cs], bank[64:96, :])
                if part == "win":
                    for hl in range(4):
                        br_items(hl, 0)
                    st.defer(selmask)
                    return
                for hl in range(4):
                    br_items(hl, 1)

                def write_ons(q0=q0):
                    k.cp(ons[:, q0:q0 + 4, 256 * g:256 * (g + 1)], accn[:].rearrange("p q h d -> p q (h d)"), eng="dve")
                st.defer(write_ons)
            do_chunk(0, "cmp")
            for ch in range(NCH):
                do_chunk(ch, "win")
                if ch + 1 < NCH:
                    do_chunk(ch + 1, "cmp")
                else:
                    st.flush()
                do_chunk(ch, "sel")
            st.flush()
        if b == 0 and "ons" in DBG:
            finals.append(k.dma("sp", DBG["ons"], ons[:]))
        k.barrier()
    print("sbuf remaining after nsa scope", nc.sbuf_bytes_remaining)


def mla_stage(k, nc, I, DBG, finals, b, L):
    hT, ons, wbuf, wview = L["hT"], L["ons"], L["wbuf"], L["wview"]
    S_B, O_B, M_B = L["S_B"], L["O_B"], L["M_B"]
    ident_bf, tri, ones_bf, invf, sgn, qg, kvg = (L[n] for n in ("ident_bf", "tri", "ones_bf", "invf", "sgn", "qg", "kvg"))
    w_inv = I["w_in"].rearrange("(kc p) n -> p kc n", p=128)
    rr = {"s": 0, "o": 0, "m": 0, "pt": 0, "ev": 0}
    PI = math.pi

    def nxt(key, lst):
        v = lst[rr[key] % len(lst)]
        rr[key] += 1
        return v

    def evac(out, in_):
        rr["ev"] += 1
        if rr["ev"] % 2:
            k.cp(out, in_, eng="act")
        else:
            k.cp(out, in_, eng="dve")

    with ExitStack() as e3:
        cqn = k.sb("cqn", [128, 2, S], BF16, e3)
        ckvn = k.sb("ckvn", [128, S], BF16, e3)
        COS2 = k.sb("COS2", [96, S], F32, e3)
        SIN2 = k.sb("SIN2", [96, S], F32, e3)
        krr = k.sb("krr", [96, S], BF16, e3)
        QmT = [k.sb(f"QmT{i}", [128, S], BF16, e3) for i in range(2)]
        KmT = [k.sb(f"KmT{i}", [128, S], BF16, e3) for i in range(2)]
        for i in range(2):
            k.memset(QmT[i][64:128, :], 0.0, eng="pool")
            k.memset(KmT[i][64:128, :], 0.0, eng="pool")
        Vh = [k.sb(f"Vh{i}", [128, NQT, 129], BF16, e3) for i in range(2)]
        sbh = [k.sb(f"sbh{i}", [128, NQT, 128], BF16, e3) for i in range(2)]
        wmb = [k.sb(f"wmb{i}", [128, 8, 128], BF16, e3) for i in range(2)]
        PTs = [k.sb(f"mpt{i}", [128, 512], BF16, e3) for i in range(3)]
        raw = [k.sb(f"raw{i}", [128, 512], F32, e3) for i in range(3)]
        sq = [k.sb(f"sq{i}", [128, 512], BF16, e3) for i in range(3)]
        rstd = k.sb("rstd", [128, 512], F32, e3)
        posi = k.sb("posi", [96, 512], I32, e3)
        ang = raw[0]
        targ = raw[1]
        t1 = k.sb("t1", [96, 512], F32, e3)
        t2 = k.sb("t2", [96, 512], F32, e3)
        satmp = sq[0]
        otmp = k.sb("otmp", [128, 2, 128], F32, e3)
        rd = k.sb("mrd", [128, 8, 2], F32, e3)
        for i in range(2):
            k.memset(Vh[i][:, :, 128:129], 1.0)

        R = slice(64, 96)
        for ch in range(NCH):
            cs = slice(ch * 512, (ch + 1) * 512)
            k.dma("sp", posi[:], I["pos"][b, cs].partition_broadcast(96))
            k.cp(ang[R, :], posi[R, :])
            k.ts(ang[R, :], ang[R, :], invf[R, 0:1], None, ALU.mult)
            k.ts(targ[R, :], ang[R, :], 1.0 / (2 * PI), None, ALU.mult)
            k.cp(posi[R, :], targ[R, :])
            k.cp(targ[R, :], posi[R, :])
            k.stt(ang[R, :], targ[R, :], -2 * PI, ang[R, :], ALU.mult, ALU.add)
            k.ts(targ[R, :], ang[R, :], PI, -2 * PI, ALU.is_gt, ALU.mult)
            k.tt(ang[R, :], ang[R, :], targ[R, :], ALU.add)
            k.act(SIN2[R, cs], ang[R, :], AF.Sin)
            k.ts(SIN2[R, cs], SIN2[R, cs], sgn[R, 0:1], None, ALU.mult)
            k.ts(ang[R, :], ang[R, :], 0.5 * PI, None, ALU.add)
            k.ts(targ[R, :], ang[R, :], PI, -2 * PI, ALU.is_gt, ALU.mult)
            k.tt(ang[R, :], ang[R, :], targ[R, :], ALU.add)
            k.act(COS2[R, cs], ang[R, :], AF.Sin)

        for j in range(2):
            wt = wbuf()
            wv = wview(wt, 8, 512)
            k.dma("pool", wv, w_inv[:, :, C_MA + 512 * j:C_MA + 512 * (j + 1)])
            for qt in range(NQT):
                bank = nxt("m", M_B)
                for kc in range(8):
                    k.mm(bank[:, :], hT[:, kc, qt * 128:(qt + 1) * 128], wv[:, kc, :], start=(kc == 0), stop=(kc == 7))
                k.act(satmp[:], bank[:, :], AF.Tanh, scale=0.5)
                k.ts(satmp[:], satmp[:], 0.5, 0.5, ALU.mult, ALU.add)
                k.tt(ons[:, qt, 512 * j:512 * (j + 1)], ons[:, qt, 512 * j:512 * (j + 1)], satmp[:], ALU.mult)

        wt = wbuf()
        wv = wview(wt, 8, 576)
        k.memset(wv[:, :, 384:576], 0.0)
        k.dma("pool", wv[:, :, 0:256], w_inv[:, :, C_CQ:C_CQ + 256])
        k.dma("pool", wv[:, :, 256:384], w_inv[:, :, C_CKV:C_CKV + 128])
        k.dma("pool", wv[:, :, 448:480], w_inv[:, :, C_KR:C_KR + 32])
        k.dma("pool", wv[:, :, 544:560], w_inv[:, :, C_KR + 16:C_KR + 32])
        k.dma("pool", wv[:, :, 560:576], w_inv[:, :, C_KR:C_KR + 16])
        for ch in range(NCH):
            cs = slice(ch * 512, (ch + 1) * 512)
            for m in range(3):
                bank = nxt("m", M_B)
                for kc in range(8):
                    k.mm(bank[:, :], wv[:, kc, m * 128:(m + 1) * 128], hT[:, kc, cs], start=(kc == 0), stop=(kc == 7))
                k.act(sq[m][:], bank[:, :], AF.Square)
                k.cp(raw[m][:], bank[:, :], eng="dve")
            bank = nxt("m", M_B)
            k.mm(bank[:, :], ones_bf[:], sq[0][:], start=True, stop=False)
            k.mm(bank[:, :], ones_bf[:], sq[1][:], start=False, stop=True)
            k.act(rstd[:], bank[:, :], AF.Sqrt, bias=1e-6, scale=1.0 / 256)
            k.recip(rstd[:], rstd[:])
            for m in range(2):
                k.stt(cqn[:, m, cs], raw[m][:], qg[:, m:m + 1], rstd[:], ALU.mult, ALU.mult)
            bank = nxt("m", M_B)
            k.mm(bank[:, :], ones_bf[:], sq[2][:], start=True, stop=True)
            k.act(rstd[:], bank[:, :], AF.Sqrt, bias=1e-6, scale=1.0 / 128)
            k.recip(rstd[:], rstd[:])
            k.stt(ckvn[:, cs], raw[2][:], kvg[:, 0:1], rstd[:], ALU.mult, ALU.mult)
            bank = nxt("m", M_B)
            for kc in range(8):
                k.mm(bank[0:96, :], wv[:, kc, 384:480], hT[:, kc, cs], start=(kc == 0), stop=(kc == 7))
            k.tt(t1[R, :], bank[R, :], COS2[R, cs], ALU.mult)
            bank = nxt("m", M_B)
            for kc in range(8):
                k.mm(bank[0:96, :], wv[:, kc, 480:576], hT[:, kc, cs], start=(kc == 0), stop=(kc == 7))
            k.tt(t2[R, :], bank[R, :], SIN2[R, cs], ALU.mult)
            k.tt(krr[R, cs], t1[R, :], t2[R, :], ALU.add)

        wt = wbuf()
        wuq = wview(wt, 2, 768, 0)
        wus = wview(wt, 2, 768, 1536)
        wkv = wt[:, 3072:3072 + 1536]
        k.dma("pool", wuq, I["w_uq"].rearrange("(c p) n -> p c n", p=128))
        k.dma("pool", wus, I["w_uq_sw"].rearrange("(c p) n -> p c n", p=128))
        k.dma("pool", wkv, I["w_ukv"])
        scale = 96 ** -0.5
        def proj(h):
            Q, Kt, V, sb_, wm = QmT[h % 2], KmT[h % 2], Vh[h % 2], sbh[h % 2], wmb[h % 2]
            k.dma("pool", wm[:], w_inv[:, :, C_MB + 128 * h:C_MB + 128 * (h + 1)])
            k.cp(Kt[R, :], krr[R, :], eng="dve")
            for ch in range(NCH):
                cs = slice(ch * 512, (ch + 1) * 512)
                bankA = nxt("m", M_B)
                for kc in range(2):
                    k.mm(bankA[0:96, :], wuq[:, kc, h * 96:(h + 1) * 96], cqn[:, kc, cs], start=(kc == 0), stop=(kc == 1))
                k.cp(Q[0:64, cs], bankA[0:64, :], eng="dve")
                k.tt(t1[R, :], bankA[R, :], COS2[R, cs], ALU.mult)
                bankB = nxt("m", M_B)
                for kc in range(2):
                    k.mm(bankB[0:96, :], wus[:, kc, h * 96:(h + 1) * 96], cqn[:, kc, cs], start=(kc == 0), stop=(kc == 1))
                k.tt(t2[R, :], bankB[R, :], SIN2[R, cs], ALU.mult)
                k.tt(Q[R, cs], t1[R, :], t2[R, :], ALU.add)
                yield
                bank = nxt("m", M_B)
                k.mm(bank[0:64, :], wkv[:, h * 192:h * 192 + 64], ckvn[:, cs], start=True, stop=True)
                k.cp(Kt[0:64, cs], bank[0:64, :], eng="dve")
                yield
                bank = nxt("m", M_B)
                for qi in range(4):
                    qt = ch * 4 + qi
                    k.mm(bank[:, qi * 128:(qi + 1) * 128], ckvn[:, qt * 128:(qt + 1) * 128],
                         wkv[:, h * 192 + 64:h * 192 + 192], start=(qi == 0), stop=False)
                k.cp(V[:, ch * 4:ch * 4 + 4, 0:128], bank[:, :].rearrange("p (q c) -> p q c", c=128), eng="dve")
                yield
                bank = nxt("m", M_B)
                first = True
                for qi in range(4):
                    qt = ch * 4 + qi
                    for kc in range(8):
                        k.mm(bank[:, qi * 128:(qi + 1) * 128], hT[:, kc, qt * 128:(qt + 1) * 128], wm[:, kc, :],
                             start=first, stop=False)
                        first = False
                k.act(sb_[:, ch * 4:ch * 4 + 4, :], bank[:, :].rearrange("p (q c) -> p q c", c=128), AF.Tanh, scale=0.5)
                k.ts(sb_[:, ch * 4:ch * 4 + 4, :], sb_[:, ch * 4:ch * 4 + 4, :], 0.5, 0.5, ALU.mult, ALU.add)
                yield

        st = AttnStream(k, S_B, PTs, look=1)

        def attn(h, gen):
            Q, Kt, V, sb_ = QmT[h % 2], KmT[h % 2], Vh[h % 2], sbh[h % 2]
            cnt = [0]
            for ch in range(NCH):
                q0 = 4 * ch
                OA = O_B[(rr["o"] % 2) * 2]
                OBk = O_B[(rr["o"] % 2) * 2 + 1]
                rr["o"] += 1
                firsts = {0: True, 1: True}
                for kt in range(0, q0 + 4):
                    qlo = max(kt, q0)
                    qhi = q0 + 3
                    n = 128 * (qhi - qlo + 1)

                    def s_fn(Sb, kt=kt, qlo=qlo, qhi=qhi, n=n, q0=q0):
                        k.mm(Sb[:, 0:n], Kt[:, kt * 128:(kt + 1) * 128], Q[:, qlo * 128:(qhi + 1) * 128],
                             start=True, stop=False)
                        if kt >= q0:
                            k.mm(Sb[:, 0:128], ident_bf[:], tri[:], start=False, stop=False)

                    def pv_fn(pt, kt=kt, qlo=qlo, qhi=qhi, firsts=firsts, OA=OA, OBk=OBk, q0=q0):
                        for qt in range(qlo, qhi + 1):
                            o = (qt - qlo) * 128
                            bi = (qt - q0) // 2
                            sl = (qt - q0) % 2
                            bank = OA if bi == 0 else OBk
                            k.mm(bank[:, sl * 129:(sl + 1) * 129], pt[:, o:o + 128], V[:, kt, :],
                                 start=firsts[bi], stop=False)
                            firsts[bi] = False

                    after = None
                    if kt == q0 + 3:
                        def after(OA=OA, OBk=OBk, q0=q0):
                            for bi, bank in enumerate((OA, OBk)):
                                qa = q0 + 2 * bi
                                Ov = bank[:, 0:258].rearrange("p (q c) -> p q c", c=129)
                                i_ = rr["ev"] % 8
                                rr["ev"] += 1
                                k.ts(rd[:, i_, :], Ov[:, :, 128], 1e-30, None, ALU.max)
                                k.recip(rd[:, i_, :], rd[:, i_, :])
                                k.tt(otmp[:], Ov[:, :, 0:128], rd[:, i_, :].unsqueeze(2).to_broadcast([128, 2, 128]), ALU.mult)
                                k.tt(otmp[:], otmp[:], sb_[:, qa:qa + 2, :], ALU.mult)
                                k.tt(ons[:, qa:qa + 2, 128 * h:128 * (h + 1)], ons[:, qa:qa + 2, 128 * h:128 * (h + 1)],
                                     otmp[:], ALU.add)
                    st.add(s_fn, 128, n, scale, pv_fn, after)
                    cnt[0] += 1
                    if gen is not None and cnt[0] % 2 == 0:
                        next(gen, None)

        for _ in proj(0):
            pass
        for h in range(8):
            gen = proj(h + 1) if h + 1 < 8 else None
            attn(h, gen)
            st.flush()
            if gen is not None:
                for _ in gen:
                    pass
        if b == 0 and "y" in DBG:
            finals.append(k.dma("sp", DBG["y"], ons[:]))
        k.barrier()


def sgn_negpi(L, k):
    if "negpi" not in L["cache"]:
        t = k.sb("negpi", [128, 1], F32)
        k.memset(t[:], -math.pi)
        L["cache"]["negpi"] = t
    return L["cache"]["negpi"]


def out_stage(k, nc, I, DBG, finals, b, L):
    ons, wview, WB = L["ons"], L["wview"], L["WB"]
    PS = L["PS"]
    M_B = L["M_B"]
    ident_bf, sc2, modFM, fng_b, convw, convb = (L[n] for n in ("ident_bf", "sc2", "modFM", "fng_b", "convw", "convb"))
    gD, x1D, out = L["gD"], L["x1D"], L["out"]
    xv = I["x"]
    rr = {"b": 0, "ev": 0}

    def nb():
        v = PS[rr["b"] % 8]
        rr["b"] += 1
        return v

    def bfv(bank):
        return bank[:].bitcast(BF16)

    with ExitStack() as e4:
        h2T = L["hT"]
        gm_b, gf_b, ss = L["gm_b"], L["gf_b"], L["ss4"]
        with ExitStack() as e4a:
            yT = k.sb("yT", [128, 8, 512], BF16, e4a)
            xts = [k.sb(f"x4t{i}", [128, D], F32, e4a) for i in range(2)]
            x1ts = [k.sb(f"x1t{i}", [128, D], F32, e4a) for i in range(2)]
            xns = [k.sb(f"x4n{i}", [128, D], BF16, e4a) for i in range(4)]
            junk = k.sb("junk4", [128, D], BF16, e4a)
            tmp = k.sb("tmp4", [128, 512], F32, e4a)
            wo = wview(WB[0], 8, 1024)
            k.dma("pool", wo, I["w_o"].rearrange("(kc p) n -> p kc n", p=128))
            for ch in range(NCH):
                for kp in range(4):
                    bank = nb()
                    bv = bfv(bank)
                    for kk in range(2):
                        kc = kp * 2 + kk
                        for qi in range(4):
                            k.tr(bv[:, (kk * 4 + qi) * 128:(kk * 4 + qi + 1) * 128],
                                 ons[:, ch * 4 + qi, kc * 128:(kc + 1) * 128], ident_bf[:])
                    k.cp(yT[:, kp * 2:kp * 2 + 2, :], bv[:, :].rearrange("p (a n) -> p a n", a=2),
                         eng=("act" if kp % 2 else "dve"))
                for qi in range(4):
                    qt = ch * 4 + qi
                    xt = xts[qt % 2]
                    x1t = x1ts[qt % 2]
                    k.dma("sp", xt[:], xv[b, qt * 128:(qt + 1) * 128, :])
                    for n in range(2):
                        bank = nb()
                        for kc in range(8):
                            k.mm(bank[:, :], yT[:, kc, qi * 128:(qi + 1) * 128], wo[:, kc, n * 512:(n + 1) * 512],
                                 start=(kc == 0), stop=(kc == 7))
                        k.tt(tmp[:], bank[:, :], gm_b[:, n * 512:(n + 1) * 512], ALU.mult)
                        k.tt(x1t[:, n * 512:(n + 1) * 512], tmp[:], xt[:, n * 512:(n + 1) * 512], ALU.add)
                    k.dma("sp", x1D.ap()[b, qt * 128:(qt + 1) * 128, :], x1t[:])
                    k.act(junk[:], x1t[:], AF.Square, accum_out=ss[:, qi:qi + 1])
                    k.act(ss[:, qi:qi + 1], ss[:, qi:qi + 1], AF.Sqrt, bias=1e-6, scale=1.0 / D)
                    k.recip(ss[:, qi:qi + 1], ss[:, qi:qi + 1])
                    k.ts(xns[qi][:], x1t[:], ss[:, qi:qi + 1], None, ALU.mult)
                for kp in range(4):
                    bank = nb()
                    bv = bfv(bank)
                    for kk in range(2):
                        kc = kp * 2 + kk
                        for qi in range(4):
                            k.tr(bv[:, (kk * 4 + qi) * 128:(kk * 4 + qi + 1) * 128],
                                 xns[qi][:, kc * 128:(kc + 1) * 128], ident_bf[:])
                    for kk in range(2):
                        kc = kp * 2 + kk
                        k.act(h2T[:, kc, ch * 512:(ch + 1) * 512], bv[:, kk * 512:(kk + 1) * 512],
                              AF.Identity, bias=modFM[:, 24 + kc, b:b + 1], scale=sc2[:, kc, b:b + 1])
            k.barrier()
        L["ses2"].close()
        if b == 0 and "h2T" in DBG:
            finals.append(k.dma("sp", DBG["h2T"], h2T[:]))
        with ExitStack() as e4b:
            aT = k.sb("aT", [128, NFC, S], BF16, e4b)
            gbuf = k.sb("gbuf", [128, S + 2], F32, e4b)
            cv = k.sb("cv", [128, S], F32, e4b)
            sg = k.sb("sg", [128, S], BF16, e4b)
            k.memset(gbuf[:, 0:2], 0.0)
            wgv = I["w_gate"].rearrange("(kc p) n -> p kc n", p=128)
            wuv = I["w_up"].rearrange("(kc p) n -> p kc n", p=128)
            gi = 0
            for f0 in range(0, NFC, 4):
                nf = min(4, NFC - f0)
                wt = WB[gi % 2]
                gi += 1
                wg = wview(wt, 8, 512, 0)
                wu = wview(wt, 8, 512, 4096)
                k.dma("pool", wg[:, :, 0:nf * 128], wgv[:, :, f0 * 128:(f0 + nf) * 128])
                k.dma("pool", wu[:, :, 0:nf * 128], wuv[:, :, f0 * 128:(f0 + nf) * 128])
                for fl in range(nf):
                    fc = f0 + fl
                    for tch in range(NCH):
                        cs = slice(tch * 512, (tch + 1) * 512)
                        bank = nb()
                        for kc in range(8):
                            k.mm(bank[:, :], wg[:, kc, fl * 128:(fl + 1) * 128], h2T[:, kc, cs], start=(kc == 0), stop=(kc == 7))
                        k.cp(gbuf[:, 2 + tch * 512:2 + (tch + 1) * 512], bank[:, :], eng="act")
                    k.ts(cv[:], gbuf[:, 2:S + 2], convw[:, fc, 2:3], convb[:, fc:fc + 1], ALU.mult, ALU.add)
                    k.stt(cv[:], gbuf[:, 1:S + 1], convw[:, fc, 1:2], cv[:], ALU.mult, ALU.add)
                    k.stt(cv[:], gbuf[:, 0:S], convw[:, fc, 0:1], cv[:], ALU.mult, ALU.add)
                    k.act(sg[:], cv[:], AF.Silu)
                    for tch in range(NCH):
                        cs = slice(tch * 512, (tch + 1) * 512)
                        bank = nb()
                        for kc in range(8):
                            k.mm(bank[:, :], wu[:, kc, fl * 128:(fl + 1) * 128], h2T[:, kc, cs], start=(kc == 0), stop=(kc == 7))
                        k.tt(aT[:, fc, cs], bank[:, :], sg[:, cs], ALU.mult)
            x1r = [cv[:, i * D:(i + 1) * D] for i in range(2)]
            x2 = [gbuf[:, i * D:(i + 1) * D] for i in range(2)]
            junk = sg[:, 0:D]
            tmp = k.sb("tmp5", [128, 512], F32, e4b)
            wdv = I["w_down"].rearrange("(fc p) n -> p fc n", p=128)
            for qg_ in range(4):
                banks = [[nb() for _ in range(2)] for _ in range(4)]
                for f0 in range(0, NFC, 8):
                    nf = min(8, NFC - f0)
                    wt = WB[gi % 2]
                    gi += 1
                    wd = wview(wt, 8, 1024)
                    k.dma("pool", wd[:, 0:nf, :], wdv[:, f0:f0 + nf, :])
                    for fl in range(nf):
                        fc = f0 + fl
                        for qi in range(4):
                            qt = qg_ * 4 + qi
                            for n in range(2):
                                k.mm(banks[qi][n][:, :], aT[:, fc, qt * 128:(qt + 1) * 128], wd[:, fl, n * 512:(n + 1) * 512],
                                     start=(fc == 0), stop=(fc == NFC - 1))
                for qi in range(4):
                    qt = qg_ * 4 + qi
                    xr = x1r[qt % 2]
                    xo = x2[qt % 2]
                    k.dma("sp", xr, x1D.ap()[b, qt * 128:(qt + 1) * 128, :])
                    for n in range(2):
                        k.tt(tmp[:], banks[qi][n][:, :], gf_b[:, n * 512:(n + 1) * 512], ALU.mult)
                        k.tt(xo[:, n * 512:(n + 1) * 512], tmp[:], xr[:, n * 512:(n + 1) * 512], ALU.add)
                    k.act(junk, xo, AF.Square, accum_out=ss[:, 4 + qi:5 + qi])
                    k.act(ss[:, 4 + qi:5 + qi], ss[:, 4 + qi:5 + qi], AF.Sqrt, bias=1e-6, scale=1.0 / D)
                    k.recip(ss[:, 4 + qi:5 + qi], ss[:, 4 + qi:5 + qi])
                    k.stt(xo, xo, ss[:, 4 + qi:5 + qi], fng_b[:], ALU.mult, ALU.mult)
                    finals.append(k.dma("sp", out[b, qt * 128:(qt + 1) * 128, :], xo))
            k.barrier()
        k.barrier()


def shared_inputs(inp):
    f = np.float32
    m = {}
    m["rel"] = inp["rel_bias_table"].astype(f)
    m["ada_w"] = inp["ada_w"][0]
    m["ada_bT"] = np.ascontiguousarray(inp["ada_b"][0].reshape(48, 128).T)
    m["nmg"] = np.ascontiguousarray(inp["norm_mix_g"][0].reshape(8, 128).T)
    m["w_in"] = inp["w_in"][0]
    m["posk"] = np.ascontiguousarray(inp["cmp_pos_k"][0].reshape(16, 128).T)
    m["posv"] = np.ascontiguousarray(inp["cmp_pos_v"][0].reshape(16, 128).T)
    m["w1k"] = inp["cmp_w1_k"][0]
    m["w2k"] = inp["cmp_w2_k"][0]
    m["w1v"] = inp["cmp_w1_v"][0]
    m["w2v"] = inp["cmp_w2_v"][0]
    m["qg"] = np.ascontiguousarray(inp["mla_q_norm_g"][0].reshape(2, 128).T)
    m["kvg"] = np.ascontiguousarray(inp["mla_kv_norm_g"][0].reshape(1, 128).T)
    wuq = inp["mla_w_uq"][0]
    m["w_uq"] = wuq
    sw = wuq.reshape(256, 8, 96).copy()
    sw[:, :, 64:80] = wuq.reshape(256, 8, 96)[:, :, 80:96]
    sw[:, :, 80:96] = wuq.reshape(256, 8, 96)[:, :, 64:80]
    m["w_uq_sw"] = np.ascontiguousarray(sw.reshape(256, 768))
    m["w_ukv"] = inp["mla_w_ukv"][0]
    m["w_o"] = inp["w_o"][0]
    m["nfg"] = np.ascontiguousarray(inp["norm_ffn_g"][0].reshape(8, 128).T)
    m["w_gate"] = inp["ffn_w_gate"][0]
    m["w_up"] = inp["ffn_w_up"][0]
    m["convw"] = np.ascontiguousarray(inp["ffn_conv_w"][0].reshape(3, NFC, 128).transpose(2, 1, 0))
    m["convb"] = np.ascontiguousarray(inp["ffn_conv_b"][0].reshape(NFC, 128).T)
    m["w_down"] = inp["ffn_w_down"][0]
    m["fng"] = inp["final_norm_g"]
    m.update(make_consts())
    return {k_: np.ascontiguousarray(v) for k_, v in m.items()}


_CACHE = {}


def kernel(**inputs):
    inp = {k_: np.asarray(v) for k_, v in inputs.items()}
    shared = shared_inputs(inp)
    if "nc" not in _CACHE:
        _CACHE["nc"] = build(nseq=4)[0]
    nc = _CACHE["nc"]
    in_maps = []
    for core in range(8):
        bsel = list(range(core * 4, core * 4 + 4))
        m = dict(shared)
        m["x"] = np.ascontiguousarray(inp["x"][bsel], dtype=np.float32)
        m["cT"] = np.ascontiguousarray(inp["c"][bsel].reshape(4, 8, 128).transpose(2, 1, 0), dtype=np.float32)
        m["pos"] = np.ascontiguousarray(inp["positions"][bsel], dtype=np.int32)
        in_maps.append(m)
    res = run_bass_kernel_spmd(nc, in_maps, core_ids=list(range(8)))
    out = np.concatenate([np.asarray(r["out"]) for r in res.results], axis=0)
    return out.astype(np.float32)
```

```python
from contextlib import ExitStack
import math
import numpy as np
import ml_dtypes
import concourse.bass as bass
import concourse.mybir as mybir
from concourse.bass_utils import run_bass_kernel_spmd

F32 = mybir.dt.float32
BF16 = mybir.dt.bfloat16
I32 = mybir.dt.int32
AF = mybir.ActivationFunctionType
ALU = mybir.AluOpType
AX = mybir.AxisListType

EPOCH = 30000
NDSEM = 12

D = 1024
S = 2048
NQT = 16
NCH = 4
DFF = 2816
NFC = 22
NEG = -30000.0
INCOLS = 5072
C_Q, C_KC, C_VC, C_KS, C_VS, C_KW, C_VW, C_G, C_CQ, C_CKV, C_KR, C_MA, C_MB = (
    0, 1024, 1280, 1536, 1792, 2048, 2304, 2560, 2608, 2864, 2992, 3024, 4048)


class KB:
    def __init__(self, nc):
        self.nc = nc
        self.es = ExitStack()
        self.eng = {"pe": nc.tensor, "act": nc.scalar, "dve": nc.vector,
                    "pool": nc.gpsimd, "sp": nc.sync}
        self.cnt = {e: 0 for e in ("pe", "act", "dve", "pool")}
        self.csem = {e: [] for e in self.cnt}
        self.dcnt = {}
        self.dsem = {}
        self.waited = {}
        self.recs = {}
        self.nwaits = 0
        self.ninst = 0

    def sb(self, name, shape, dt, es=None):
        self.uid = getattr(self, "uid", 0) + 1
        return (es or self.es).enter_context(self.nc.sbuf_tensor(f"s{self.uid}_{name}", list(shape), dt))

    def ps(self, name, shape, dt):
        return self.es.enter_context(self.nc.psum_tensor(name, list(shape), dt))

    def _sem(self, name):
        return self.es.enter_context(self.nc.semaphore(name))

    @staticmethod
    def box(ap):
        t = ap.tensor
        space = str(ap.space)
        if space == "DRAM":
            lo = int(ap.offset)
            hi = lo
            for st, n in ap.ap:
                if n > 1:
                    if st >= 0:
                        hi += st * (n - 1)
                    else:
                        lo += st * (n - 1)
            return (t.name, 0, 1, lo, hi + 1, False)
        if space == "PSUM":
            return (t.name, 0, 128, 0, 1 << 30, True)
        row = 1
        for s_ in list(t.shape)[1:]:
            row *= s_
        off = int(ap.offset)
        p0 = off // row
        f0 = off % row
        apl = list(ap.ap)
        lo = f0
        hi = f0
        for st, n in apl[1:]:
            if n > 1:
                if st >= 0:
                    hi += st * (n - 1)
                else:
                    lo += st * (n - 1)
        return (t.name, p0, p0 + apl[0][1], lo, hi + 1, False)

    def _wait_compute(self, waiter, e, idx):
        key = (waiter, e)
        if self.waited.get(key, -1) >= idx:
            return
        self.waited[key] = idx
        self.eng[waiter].wait_ge(self.csem[e][idx // EPOCH], (idx % EPOCH) + 1)
        self.nwaits += 1

    def _wait_dma(self, waiter, q, k):
        slot = k % NDSEM
        val = 16 * (k // NDSEM + 1)
        key = (waiter, "dma", q, slot)
        if self.waited.get(key, 0) >= val:
            return
        self.waited[key] = val
        self.eng[waiter].wait_ge(self.dsem[q][slot], val)
        self.nwaits += 1

    def _wait(self, waiter, who):
        if who[0] == "dma":
            self._wait_dma(waiter, who[1], who[2])
        else:
            self._wait_compute(waiter, who[0], who[1])

    @staticmethod
    def _ovl(a, b):
        return a[1] < b[2] and b[1] < a[2] and a[3] < b[4] and b[3] < a[4]

    @staticmethod
    def _covers(a, b):
        return a[1] <= b[1] and a[2] >= b[2] and a[3] <= b[3] and a[4] >= b[4]

    def _deps(self, eng, reads, writes, is_dma):
        deps = []
        for ap in reads:
            bx = self.box(ap)
            for (rb, kind, who) in self.recs.get(bx[0], ()):
                if not self._ovl(bx, rb):
                    continue
                same = (not is_dma) and who[0] == eng
                if kind == "w":
                    if same and eng == "pe":
                        continue
                    deps.append(who)
                elif bx[5] and not same:
                    deps.append(who)
        for ap in writes:
            bx = self.box(ap)
            for (rb, kind, who) in self.recs.get(bx[0], ()):
                if not self._ovl(bx, rb):
                    continue
                same = (not is_dma) and who[0] == eng
                if same and eng == "pe":
                    continue
                deps.append(who)
        return deps

    def _record(self, who, reads, writes):
        for ap in writes:
            bx = self.box(ap)
            lst = self.recs.setdefault(bx[0], [])
            if bx[5]:
                lst[:] = []
            else:
                lst[:] = [r for r in lst if not self._covers(bx, r[0])]
            lst.append((bx, "w", who))
        for ap in reads:
            bx = self.box(ap)
            lst = self.recs.setdefault(bx[0], [])
            if who[0] != "dma":
                lst[:] = [r for r in lst
                          if not (r[1] == "r" and r[2][0] == who[0] and r[0] == bx)]
            lst.append((bx, "r", who))

    def op(self, eng, fn, reads, writes):
        for who in self._deps(eng, reads, writes, False):
            self._wait(eng, who)
        idx = self.cnt[eng]
        ep = idx // EPOCH
        while len(self.csem[eng]) <= ep:
            self.csem[eng].append(self._sem(f"c_{eng}_{len(self.csem[eng])}"))
        ins = fn()
        ins.then_inc(self.csem[eng][ep], 1)
        self.cnt[eng] = idx + 1
        self.ninst += 1
        self._record((eng, idx), reads, writes)
        return (eng, idx)

    def dma(self, q, out, in_, **kw):
        if q not in self.dsem:
            self.dsem[q] = [self._sem(f"d_{q}_{i}") for i in range(NDSEM)]
            self.dcnt[q] = 0
        k = self.dcnt[q]
        if k >= NDSEM:
            self._wait_dma(q, q, k - NDSEM)
        for who in self._deps(q, [in_], [out], True):
            self._wait(q, who)
        ins = self.eng[q].dma_start(out=out, in_=in_, **kw)
        ins.then_inc(self.dsem[q][k % NDSEM], 16)
        self.dcnt[q] = k + 1
        self.ninst += 1
        who = ("dma", q, k)
        self._record(who, [in_], [out])
        return who

    def finish(self, whos, eng="sp"):
        for who in whos:
            self._wait(eng, who)

    def barrier(self):
        for w in ("pe", "act", "dve", "pool", "sp"):
            for e in ("pe", "act", "dve", "pool"):
                if e != w and self.cnt[e] > 0:
                    self._wait_compute(w, e, self.cnt[e] - 1)
            for q in self.dsem:
                n = self.dcnt[q]
                for k in range(max(0, n - NDSEM), n):
                    self._wait_dma(w, q, k)
        self.recs = {}

    def mm(self, out, lhsT, rhs, start=True, stop=True):
        return self.op("pe", lambda: self.nc.tensor.matmul(
            out, lhsT, rhs, start=start, stop=stop, skip_group_check=True), [lhsT, rhs], [out])

    def tr(self, out, in_, ident):
        return self.op("pe", lambda: self.nc.tensor.transpose(out, in_, ident), [in_, ident], [out])

    def act(self, out, in_, func, bias=None, scale=None, accum_out=None):
        kw = {}
        rd = [in_]
        wr = [out]
        if bias is not None:
            kw["bias"] = bias
            if not isinstance(bias, (int, float)):
                rd.append(bias)
        if scale is not None:
            kw["scale"] = scale
            if not isinstance(scale, (int, float)):
                rd.append(scale)
        if accum_out is not None:
            kw["accum_out"] = accum_out
            wr.append(accum_out)
        return self.op("act", lambda: self.nc.scalar.activation(out, in_, func, **kw), rd, wr)

    def _ve(self, eng):
        return self.nc.vector if eng == "dve" else self.nc.gpsimd

    def tt(self, out, in0, in1, op, eng="dve"):
        return self.op(eng, lambda: self._ve(eng).tensor_tensor(out, in0, in1, op), [in0, in1], [out])

    def ts(self, out, in0, s1, s2, op0, op1=None, eng="dve"):
        rd = [in0]
        for s_ in (s1, s2):
            if s_ is not None and not isinstance(s_, (int, float)):
                rd.append(s_)
        kw = {}
        if op1 is not None:
            kw["op1"] = op1
        return self.op(eng, lambda: self._ve(eng).tensor_scalar(out, in0, s1, s2, op0, **kw), rd, [out])

    def stt(self, out, in0, scalar, in1, op0, op1, eng="dve"):
        rd = [in0, in1]
        if not isinstance(scalar, (int, float)):
            rd.append(scalar)
        return self.op(eng, lambda: self._ve(eng).scalar_tensor_tensor(
            out, in0, scalar, in1, op0, op1), rd, [out])

    def cp(self, out, in_, eng="dve"):
        if eng == "act":
            return self.op("act", lambda: self.nc.scalar.copy(out, in_), [in_], [out])
        return self.op(eng, lambda: self._ve(eng).tensor_copy(out, in_), [in_], [out])

    def red(self, out, in_, op, axis=AX.X, eng="dve"):
        return self.op(eng, lambda: self._ve(eng).tensor_reduce(out, in_, axis, op), [in_], [out])

    def memset(self, ap, v, eng="dve"):
        return self.op(eng, lambda: self._ve(eng).memset(ap, v), [], [ap])

    def recip(self, out, in_):
        return self.op("dve", lambda: self.nc.vector.reciprocal(out, in_), [in_], [out])


def _t5_bucket(n):
    n = np.maximum(n, 0)
    exact = 16
    lr = np.log(np.maximum(n, exact).astype(np.float32) / np.float32(exact)) / np.float32(math.log(128 / exact))
    large = np.minimum(exact + (lr.astype(np.float32) * np.float32(16)).astype(np.int32), 31)
    return np.where(n < exact, n, large)


def make_consts():
    bf = ml_dtypes.bfloat16
    c = {}
    c["ident_bf"] = np.eye(128, dtype=np.float32).astype(bf)
    c["ident_f"] = np.eye(128, dtype=np.float32)
    k = np.arange(128)[:, None]
    q = np.arange(128)[None, :]
    c["tri"] = np.where(q >= k, 0.0, NEG).astype(bf)
    c["anti"] = np.where(q < k, 0.0, NEG).astype(bf)
    t = np.arange(S)[None, :]
    c["blockind"] = (t // 64 == np.arange(32)[:, None]).astype(np.float32).astype(bf)
    n = np.arange(1152)
    dist = n - 511
    oh = np.zeros((33, 1152), np.float32)
    bk = _t5_bucket(dist)
    for i in range(1152):
        if dist[i] >= 0:
            oh[bk[i], i] = 1.0
        else:
            oh[32, i] = 1.0
    c["ohe"] = oh
    a = np.zeros((4, 41, 127), np.float32)
    for ch in range(4):
        for cc in range(127):
            r = cc - 32 * ch
            if r >= 31:
                a[ch, 0, cc] = 1.0
            elif r >= -9:
                a[ch, 31 - r, cc] = 1.0
    ap_ = np.zeros((128, 4, 127), np.float32)
    ap_[:41] = a.transpose(1, 0, 2)
    c["ach"] = ap_.astype(bf)
    m1 = np.ones((128, 16, 32), np.float32)
    m2 = np.zeros((128, 16, 32), np.float32)
    for qt in range(16):
        for p in range(128):
            cur = (qt * 128 + p) // 64
            for j in range(32):
                if j == 0 or j == cur or j == cur - 1:
                    m1[p, qt, j] = 0.0
                    m2[p, qt, j] = 1e30
                elif j > cur:
                    m1[p, qt, j] = 0.0
                    m2[p, qt, j] = -1e30
    c["m1"] = m1
    c["m2"] = m2
    cs = np.arange(127) * 16
    bs = np.arange(32) * 64
    ov = np.clip(np.minimum(cs[:, None] + 32, bs[None, :] + 64) - np.maximum(cs[:, None], bs[None, :]), 0, None) / 32.0
    vcc = np.zeros((128, 33), np.float32)
    vcc[:127, 0] = 1.0
    vcc[:127, 1:] = ov
    c["vcconst"] = vcc.astype(bf)
    invf = np.zeros((128, 1), np.float32)
    sgn = np.ones((128, 1), np.float32)
    fr = (np.float32(10000.0) ** (-np.arange(0, 32, 2, dtype=np.float32) / np.float32(32))).astype(np.float32)
    for i in range(16):
        invf[64 + i, 0] = fr[i]
        invf[80 + i, 0] = fr[i]
        sgn[64 + i, 0] = -1.0
    c["invf"] = invf
    c["sgn"] = sgn
    return c


CONST_DT = {"ident_bf": BF16, "ident_f": F32, "tri": BF16, "anti": BF16, "blockind": BF16, "ohe": F32,
            "ach": BF16, "m1": F32, "m2": F32, "vcconst": BF16, "invf": F32, "sgn": F32}

IN_SPECS = [
    ("x", None, F32), ("cT", None, F32), ("pos", None, I32), ("rel", [32, 16], F32),
    ("ada_w", [D, 6 * D], F32), ("ada_bT", [128, 48], F32), ("nmg", [128, 8], F32),
    ("w_in", [D, INCOLS], F32), ("posk", [128, 16], F32), ("posv", [128, 16], F32),
    ("w1k", [2048, 256], F32), ("w2k", [256, 64], F32), ("w1v", [2048, 256], F32), ("w2v", [256, 64], F32),
    ("qg", [128, 2], F32), ("kvg", [128, 1], F32), ("w_uq", [256, 768], F32), ("w_uq_sw", [256, 768], F32),
    ("w_ukv", [128, 1536], F32), ("w_o", [D, D], F32), ("nfg", [128, 8], F32),
    ("w_gate", [D, DFF], F32), ("w_up", [D, DFF], F32), ("convw", [128, NFC, 3], F32), ("convb", [128, NFC], F32),
    ("w_down", [DFF, D], F32), ("fng", [D], F32),
]


def build(nseq=4, stop_after=None, dbg=()):
    nc = bass.Bass("TRN2", target_bir_lowering=False)
    I = {}
    for name, shape, dt in IN_SPECS:
        if name == "x":
            shape = [nseq, S, D]
        elif name == "cT":
            shape = [128, 8, nseq]
        elif name == "pos":
            shape = [nseq, S]
        I[name] = nc.dram_tensor(name, list(shape), dt, kind="ExternalInput").ap()
    consts = make_consts()
    for name, arr in consts.items():
        I[name] = nc.dram_tensor(name, list(arr.shape), CONST_DT[name], kind="ExternalInput").ap()
    out = nc.dram_tensor("out", [nseq, S, D], F32, kind="ExternalOutput").ap()
    DBG = {}
    for name, shape, dt in dbg:
        DBG[name] = nc.dram_tensor("dbg_" + name, list(shape), dt, kind="ExternalOutput").ap()
    veD = nc.dram_tensor("veD", [16, 1152], F32, kind="Internal")
    vP = nc.dram_tensor("vP", [16, 128, 384], F32, kind="Internal")
    gD = nc.dram_tensor("gD", [nseq, 2, D], F32, kind="Internal")
    x1D = nc.dram_tensor("x1D", [nseq, S, D], F32, kind="Internal")

    k = KB(nc)
    finals = []
    with k.es:
        PS = [k.ps(f"ps{i}", [128, 512], F32) for i in range(8)]
        S_B = PS[0:2]
        O_B = PS[2:6]
        M_B = PS[6:8]

        def bf_view(bank):
            return bank[:].bitcast(BF16)

        def gconst(name, shape, dt, q="sp"):
            t = k.sb("c_" + name, shape, dt)
            k.dma(q, t[:], I[name])
            return t
        ident_bf = gconst("ident_bf", [128, 128], BF16)
        ident_f = gconst("ident_f", [128, 128], F32)
        tri = gconst("tri", [128, 128], BF16)
        anti = gconst("anti", [128, 128], BF16)
        ach = gconst("ach", [128, 4, 127], BF16)
        m1 = gconst("m1", [128, 16, 32], F32)
        m2 = gconst("m2", [128, 16, 32], F32)
        vcconst = gconst("vcconst", [128, 33], BF16)
        invf = gconst("invf", [128, 1], F32)
        sgn = gconst("sgn", [128, 1], F32)
        ada_bT = gconst("ada_bT", [128, 48], F32)
        nmg = gconst("nmg", [128, 8], F32)
        nfg = gconst("nfg", [128, 8], F32)
        qg = gconst("qg", [128, 2], F32)
        kvg = gconst("kvg", [128, 1], F32)
        convw = gconst("convw", [128, NFC, 3], F32)
        convb = gconst("convb", [128, NFC], F32)
        posk = gconst("posk", [128, 16], F32)
        posv = gconst("posv", [128, 16], F32)
        fng_b = k.sb("fng_b", [128, D], F32)
        k.dma("sp", fng_b[:], I["fng"].partition_broadcast(128))
        ones_bf = k.sb("ones_bf", [128, 128], BF16)
        k.memset(ones_bf[:], 1.0)
        w2k = k.sb("w2k", [128, 2, 64], BF16)
        w2v = k.sb("w2v", [128, 2, 64], BF16)
        k.dma("pool", w2k[:], I["w2k"].rearrange("(c p) n -> p c n", p=128))
        k.dma("pool", w2v[:], I["w2v"].rearrange("(c p) n -> p c n", p=128))
        modFM = k.sb("modFM", [128, 48, nseq], F32)
        sc1 = k.sb("sc1", [128, 8, nseq], F32)
        sc2 = k.sb("sc2", [128, 8, nseq], F32)
        posb = k.sb("posb", [128, 2, 2], F32)
        WB = [k.sb(f"wb{i}", [128, 8192], BF16) for i in range(2)]
        wb_i = [0]
        negpi = k.sb("negpi", [128, 1], F32)
        k.memset(negpi[:], -math.pi)

        def wbuf():
            t = WB[wb_i[0] % 2]
            wb_i[0] += 1
            return t

        def wview(t, kc, n, off=0):
            return t[:, off:off + kc * n].rearrange("p (k n) -> p k n", k=kc)

        with ExitStack() as pes:
            cT_sb = k.sb("cT_sb", [128, 8, nseq], F32, pes)
            cond = k.sb("cond", [128, 8, nseq], BF16, pes)
            k.dma("sp", cT_sb[:], I["cT"])
            k.act(cond[:], cT_sb[:], AF.Silu)
            adav = I["ada_w"].rearrange("(kc p) n -> p kc n", p=128)
            mbank = M_B[0]
            first = True
            for jt in range(12):
                wt = wbuf()
                wv = wview(wt, 8, 512)
                k.dma("pool", wv, adav[:, :, jt * 512:(jt + 1) * 512])
                for jj in range(4):
                    j = jt * 4 + jj
                    for kc in range(8):
                        k.mm(mbank[:, j * nseq:(j + 1) * nseq], wv[:, kc, jj * 128:(jj + 1) * 128],
                             cond[:, kc, :], start=first, stop=False)
                        first = False
            k.tt(modFM[:], mbank[:, 0:48 * nseq].rearrange("p (j b) -> p j b", b=nseq),
                 ada_bT[:].unsqueeze(2).to_broadcast([128, 48, nseq]), ALU.add)
            k.stt(sc1[:], modFM[:, 8:16, :], 1.0, nmg[:].unsqueeze(2).to_broadcast([128, 8, nseq]),
                  ALU.add, ALU.mult)
            k.stt(sc2[:], modFM[:, 32:40, :], 1.0, nfg[:].unsqueeze(2).to_broadcast([128, 8, nseq]),
                  ALU.add, ALU.mult)
            gts = k.sb("gts", [8, 2 * nseq, 128], F32, pes)
            for b in range(nseq):
                gbank = (M_B[1], S_B[0])[b % 2]
                for gi, j0 in enumerate((16, 40)):
                    k.tr(gbank[0:8, gi * 128:(gi + 1) * 128], modFM[:, j0:j0 + 8, b], ident_f[:])
                k.cp(gts[:, 2 * b:2 * b + 2, :], gbank[0:8, 0:256].rearrange("p (a n) -> p a n", n=128))
            for b in range(nseq):
                for gi in range(2):
                    k.dma("sp", gD.ap()[b, gi, :].rearrange("(j p) -> j p", p=128), gts[:, b * 2 + gi, :])
            tab = k.sb("tab", [33, 16], F32, pes)
            t31 = k.sb("t31", [32, 16], F32, pes)
            ohe = k.sb("ohe", [33, 1152], F32, pes)
            ve_sb = k.sb("ve_sb", [16, 1152], F32, pes)
            k.dma("sp", tab[0:32, :], I["rel"])
            k.dma("sp", t31[:], I["rel"][31, :].partition_broadcast(32))
            k.dma("sp", ohe[:], I["ohe"])
            k.tt(tab[0:32, :], tab[0:32, :], t31[:], ALU.subtract)
            k.ts(tab[0:32, :], tab[0:32, :], 8.0, None, ALU.mult)
            k.memset(tab[32:33, :], NEG)
            for i in range(3):
                k.mm(mbank[0:16, 0:384], tab[:], ohe[:, i * 384:(i + 1) * 384])
                k.cp(ve_sb[:, i * 384:(i + 1) * 384], mbank[0:16, 0:384])
            k.dma("sp", veD.ap(), ve_sb[:])
            k.dma("sp", vP.ap(), ve_sb[:, 384:768].unsqueeze(1).to_broadcast([16, 128, 384]))
            w1f = k.sb("w1f", [128, 16, 256], F32, pes)
            for kv, (wn, pt) in enumerate((("w1k", posk), ("w1v", posv))):
                k.dma("sp", w1f[:], I[wn].rearrange("(c p) n -> p c n", p=128))
                for hc in range(2):
                    for lp in range(16):
                        k.mm(mbank[:, 256 + kv * 2 + hc:256 + kv * 2 + hc + 1], w1f[:, lp, hc * 128:(hc + 1) * 128],
                             pt[:, lp:lp + 1], start=(lp == 0), stop=(lp == 15))
            k.cp(posb[:], mbank[:, 256:260].rearrange("p (a b) -> p a b", b=2))
            if "modFM" in DBG:
                finals.append(k.dma("sp", DBG["modFM"], modFM[:]))
            if "ve" in DBG:
                finals.append(k.dma("sp", DBG["ve"], ve_sb[:]))
            if "posb" in DBG:
                finals.append(k.dma("sp", DBG["posb"], posb[:]))
            k.barrier()

        if stop_after == "prologue":
            k.finish(finals)
            return nc, k

        xv = I["x"]
        for b in range(nseq):
            with ExitStack() as ses, ExitStack() as ses2:
                gm_b = k.sb("gm_b", [128, D], F32, ses)
                gf_b = k.sb("gf_b", [128, D], F32, ses)
                ss4 = k.sb("ss4", [128, 8], F32, ses)
                k.dma("sp", gm_b[:], gD.ap()[b, 0, :].partition_broadcast(128))
                k.dma("sp", gf_b[:], gD.ap()[b, 1, :].partition_broadcast(128))
                hT = k.sb("hT", [128, 8, S], BF16, ses)
                ons = k.sb("ons", [128, NQT, D], BF16, ses2)

                def norm_to_T(src_tiles_fn, dstT, scv, shv, es_):
                    xts = [k.sb(f"xt{i}", [128, D], F32, es_) for i in range(8)]
                    xns_all = [k.sb(f"xn{i}", [128, D], BF16, es_) for i in range(8)]
                    junk = k.sb("junk", [128, D], BF16, es_)
                    ss_all = k.sb("ss", [128, 16], F32, es_)
                    for ch in range(NCH):
                        xns = xns_all[(ch % 2) * 4:(ch % 2) * 4 + 4]
                        ss = ss_all[:, ch * 4:ch * 4 + 4]
                        xtl = []
                        for qi in range(4):
                            qt = ch * 4 + qi
                            xt = src_tiles_fn(qt, xts[qt % 8])
                            xtl.append(xt)
                            k.act(junk[:], xt, AF.Square, accum_out=ss[:, qi:qi + 1])
                        k.act(ss, ss, AF.Sqrt, bias=1e-6, scale=1.0 / D)
                        k.recip(ss, ss)
                        for qi in range(4):
                            k.ts(xns[qi][:], xtl[qi], ss[:, qi:qi + 1], None, ALU.mult)
                        for kp in range(4):
                            bank = M_B[kp % 2]
                            bv = bf_view(bank)
                            for kk in range(2):
                                kc = kp * 2 + kk
                                for qi in range(4):
                                    k.tr(bv[:, (kk * 4 + qi) * 128:(kk * 4 + qi + 1) * 128],
                                         xns[qi][:, kc * 128:(kc + 1) * 128], ident_bf[:])
                            for kk in range(2):
                                kc = kp * 2 + kk
                                k.act(dstT[:, kc, ch * 512:(ch + 1) * 512], bv[:, kk * 512:(kk + 1) * 512],
                                      AF.Identity, bias=shv(kc), scale=scv(kc))

                with ExitStack() as e1:
                    def load_x(qt, buf):
                        k.dma("sp", buf[:], xv[b, qt * 128:(qt + 1) * 128, :])
                        return buf[:]
                    norm_to_T(load_x, hT, lambda kc: sc1[:, kc, b:b + 1], lambda kc: modFM[:, kc, b:b + 1], e1)
                    k.barrier()
                if "hT" in DBG and b == 0:
                    finals.append(k.dma("sp", DBG["hT"], hT[:]))
                if stop_after == "s1":
                    continue
                STAGES(k, nc, I, DBG, finals, b, locals())
        k.finish(finals)
    return nc, k


class AttnStream:
    def __init__(self, k, S_B, PTs, look=1):
        self.k, self.S_B, self.PTs, self.look = k, S_B, PTs, look
        self.si = 0
        self.pi = 0
        self.pend = []

    def add(self, s_fn, npart, n, scale, pv_fn, after=None):
        k = self.k
        Sb = self.S_B[self.si % len(self.S_B)]
        self.si += 1
        s_fn(Sb)
        pt = self.PTs[self.pi % len(self.PTs)]
        self.pi += 1
        k.act(pt[0:npart, 0:n], Sb[0:npart, 0:n], AF.Exp, scale=scale)
        self.pend.append([pv_fn, pt, after, []])
        while len(self.pend) > self.look:
            self._pop()

    def defer(self, fn):
        if self.pend:
            self.pend[-1][3].append(fn)
        else:
            fn()

    def _pop(self):
        pv_fn, pt, after, deferred = self.pend.pop(0)
        pv_fn(pt)
        if after is not None:
            after()
        for fn in deferred:
            fn()

    def flush(self):
        while self.pend:
            self._pop()


def STAGES(k, nc, I, DBG, finals, b, L):
    hT, ons = L["hT"], L["ons"]
    stop_after = L["stop_after"]
    nsa_stage(k, nc, I, DBG, finals, b, L)
    if stop_after == "nsa":
        return
    mla_stage(k, nc, I, DBG, finals, b, L)
    if stop_after == "mla":
        return
    out_stage(k, nc, I, DBG, finals, b, L)


def nsa_stage(k, nc, I, DBG, finals, b, L):
    hT, ons, wbuf, wview = L["hT"], L["ons"], L["wbuf"], L["wview"]
    S_B, O_B, M_B = L["S_B"], L["O_B"], L["M_B"]
    ident_bf, tri, anti, ach, m1, m2, vcconst = (L[n] for n in ("ident_bf", "tri", "anti", "ach", "m1", "m2", "vcconst"))
    posb, w2k, w2v = L["posb"], L["w2k"], L["w2v"]
    veD, vP = L["veD"], L["vP"]
    w_inv = I["w_in"].rearrange("(kc p) n -> p kc n", p=128)
    rr = {"s": 0, "o": 0, "m": 0, "pt": 0, "ev": 0}

    def nxt(key, lst):
        v = lst[rr[key] % len(lst)]
        rr[key] += 1
        return v

    def evac(out, in_):
        rr["ev"] += 1
        if rr["ev"] % 2:
            k.cp(out, in_, eng="act")
        else:
            k.cp(out, in_, eng="dve")

    with ExitStack() as e2:
        w1s = [k.sb(f"w1s{i}", [128, 16, 256], BF16, e2) for i in range(2)]
        k.dma("pool", w1s[0][:], I["w1k"].rearrange("(c p) n -> p c n", p=128))
        k.dma("pool", w1s[1][:], I["w1v"].rearrange("(c p) n -> p c n", p=128))
        Qaug = k.sb("Qaug", [128, 4, S], BF16, e2)
        KSaug = k.sb("KSaug", [128, S], BF16, e2)
        KWt = k.sb("KWt", [128, S], BF16, e2)
        KKs = [k.sb(f"KK{i}", [128, S], BF16, e2) for i in range(2)]
        VS = k.sb("VS", [128, NQT, 65], BF16, e2)
        VW = k.sb("VW", [128, NQT, 65], BF16, e2)
        VCaug = k.sb("VCaug", [128, 97], BF16, e2)
        sig = k.sb("sig", [128, NQT, 12], F32, e2)
        hid = [k.sb(f"hid{i}", [128, 2, 127], BF16, e2) for i in range(2)]
        kcT = k.sb("kcT", [128, 127], BF16, e2)
        Tns = [k.sb(f"Tn{i}", [128, 4, 256], BF16, e2) for i in range(2)]
        Wcs = [k.sb(f"Wc{i}", [128, 4, 512], BF16, e2) for i in range(2)]
        PTs = [k.sb(f"pt{i}", [128, 512], BF16, e2) for i in range(4)]
        accns = [k.sb(f"accn{i}", [128, 4, 4, 64], F32, e2) for i in range(2)]
        sc = k.sb("sc", [128, 4, 32], F32, e2)
        s2 = k.sb("s2", [128, 4, 32], F32, e2)
        cmpm = k.sb("cmpm", [128, 32, 32], BF16, e2)
        cmpm2 = k.sb("cmpm2", [128, 32, 32], BF16, e2)
        rank = k.sb("rank", [128, 4, 32], F32, e2)
        negpad = k.sb("negpad", [128, 4, 96], BF16, e2)
        rd = k.sb("rd", [128, 8, 4], F32, e2)
        ff = k.sb("ff", [128, 8, 4], F32, e2)
        tmp64 = k.sb("tmp64", [128, 4, 64], F32, e2)
        tmp32 = k.sb("tmp32", [128, 4, 32], F32, e2)

        k.memset(Qaug[64:128, :, :], 0.0, eng="pool")
        k.memset(KSaug[64:128, :], 0.0, eng="pool")
        k.memset(KWt[64:128, :], 0.0, eng="pool")
        k.memset(kcT[64:128, :], 0.0)
        for Wc_ in Wcs:
            k.memset(Wc_[:], 0.0, eng="pool")
            k.memset(Wc_[0:1, :, :], NEG)
        k.dma("sp", KSaug[64:96, :], I["blockind"])
        k.memset(VS[:, :, 64:65], 1.0)
        k.memset(VW[:, :, 64:65], 1.0)
        k.memset(VCaug[:], 0.0)
        k.cp(VCaug[:, 64:97], vcconst[:])
        k.memset(negpad[:], 0.0)

        rdi = [0]

        def norm_f(Ob, width, gate_ap):
            i = rdi[0] % 8
            rdi[0] += 1
            den = Ob[:, 0:4 * width].rearrange("p (q c) -> p q c", c=width)[:, :, 64]
            k.ts(rd[:, i, :], den, 1e-30, None, ALU.max)
            k.recip(rd[:, i, :], rd[:, i, :])
            if gate_ap is None:
                return rd[:, i, :]
            k.tt(ff[:, i, :], rd[:, i, :], gate_ap, ALU.mult)
            return ff[:, i, :]

        def load_group(g):
            wt = wbuf()
            wv = wview(wt, 8, 780)
            segs = [(0, 256, C_Q + 256 * g), (256, 64, C_KS + 64 * g), (320, 64, C_KW + 64 * g),
                    (384, 64, C_KC + 64 * g), (448, 64, C_KC + 64 * g), (512, 64, C_VC + 64 * g),
                    (576, 64, C_VC + 64 * g), (640, 64, C_VS + 64 * g), (704, 64, C_VW + 64 * g),
                    (768, 12, C_G + 12 * g)]
            for (o, n, c0) in segs:
                k.dma("pool", wv[:, :, o:o + n], w_inv[:, :, c0:c0 + n])
            for hl in range(4):
                h = 4 * g + hl
                k.dma("pool", Tns[g % 2][:, hl, :], bass.AP(vP, h * 128 * 384 + 127, [[383, 128], [1, 256]]))
                k.dma("pool", Wcs[g % 2][1:41, hl, :], bass.AP(veD, h * 1152, [[16, 40], [1, 512]]))
            return wv

        wv_next = load_group(0)
        for g in range(4):
            wv = wv_next
            Tn = Tns[g % 2]
            Wc = Wcs[g % 2]
            for ch in range(NCH):
                cs = slice(ch * 512, (ch + 1) * 512)
                for hl in range(4):
                    bank = nxt("m", M_B)
                    for kc in range(8):
                        k.mm(bank[0:64, :], wv[:, kc, hl * 64:(hl + 1) * 64], hT[:, kc, cs],
                             start=(kc == 0), stop=(kc == 7))
                    evac(Qaug[0:64, hl, cs], bank[0:64, :])
                for (o, dst) in ((256, KSaug), (320, KWt)):
                    bank = nxt("m", M_B)
                    for kc in range(8):
                        k.mm(bank[0:64, :], wv[:, kc, o:o + 64], hT[:, kc, cs], start=(kc == 0), stop=(kc == 7))
                    evac(dst[0:64, cs], bank[0:64, :])
                for (o, dst) in ((384, KKs[0]), (512, KKs[1])):
                    bank = nxt("m", M_B)
                    for kc in range(8):
                        k.mm(bank[:, :], wv[:, kc, o:o + 128], hT[:, kc, cs], start=(kc == 0), stop=(kc == 7))
                    evac(dst[0:64, cs], bank[0:64, :])
                    if ch == 0:
                        evac(dst[64:128, 0:511], bank[64:128, 1:512])
                    else:
                        evac(dst[64:128, ch * 512 - 1:(ch + 1) * 512 - 1], bank[64:128, :])
            for qt in range(NQT):
                bank = nxt("m", M_B)
                for kc in range(8):
                    k.mm(bank[:, 0:140], hT[:, kc, qt * 128:(qt + 1) * 128], wv[:, kc, 640:780],
                         start=(kc == 0), stop=(kc == 7))
                k.cp(VS[:, qt, 0:64], bank[:, 0:64], eng="dve")
                k.cp(VW[:, qt, 0:64], bank[:, 64:128], eng="dve")
                k.act(sig[:, qt, :], bank[:, 128:140], AF.Tanh, scale=0.5)
            k.ts(sig[:], sig[:], 0.5, 0.5, ALU.mult, ALU.add)
            if g + 1 < 4:
                wv_next = load_group(g + 1)
            for kv in range(2):
                src = KKs[kv]
                for hc in range(2):
                    bank = nxt("m", M_B)
                    for lp in range(16):
                        k.mm(bank[:, 0:127], w1s[kv][:, lp, hc * 128:(hc + 1) * 128],
                             src[:, 2 * lp:2 * lp + 16 * 126 + 1:16], start=(lp == 0), stop=(lp == 15))
                    k.act(hid[kv][:, hc, :], bank[:, 0:127], AF.Gelu_apprx_tanh, bias=posb[:, kv, hc:hc + 1])
            bank = nxt("m", M_B)
            for hc in range(2):
                k.mm(bank[0:64, 0:127], w2k[:, hc, :], hid[0][:, hc, :], start=(hc == 0), stop=(hc == 1))
            k.cp(kcT[0:64, :], bank[0:64, 0:127], eng="dve")
            bank = nxt("m", M_B)
            for hc in range(2):
                k.mm(bank[0:127, 0:64], hid[1][:, hc, :], w2v[:, hc, :], start=(hc == 0), stop=(hc == 1))
            k.cp(VCaug[0:127, 0:64], bank[0:127, 0:64], eng="dve")
            if b == 0 and g == 0 and "kcT" in DBG:
                finals.append(k.dma("sp", DBG["kcT"], kcT[0:64, :]))
                finals.append(k.dma("sp", DBG["vc"], VCaug[:]))
            st = AttnStream(k, S_B + [M_B[1]], PTs, look=2)

            def do_chunk(ch, part, g=g, Tn=Tn, Wc=Wc):
                cs = slice(ch * 512, (ch + 1) * 512)
                q0 = 4 * ch
                accn = accns[ch % 2]

                def cmp_item(hl):
                    Ob = nxt("o", O_B)

                    def s_fn(Sb):
                        k.mm(Sb[0:127, :], kcT[:, :], Qaug[:, hl, cs], start=True, stop=False)
                        k.mm(Sb[0:127, :], ach[:, ch, :], Wc[:, hl, :], start=False, stop=True)

                    def pv_fn(pt):
                        for qi in range(4):
                            k.mm(Ob[:, qi * 97:(qi + 1) * 97], pt[0:127, qi * 128:(qi + 1) * 128], VCaug[0:127, :],
                                 start=(qi == 0), stop=False)

                    def after():
                        Ov = Ob[:, 0:388].rearrange("p (q c) -> p q c", c=97)
                        r1 = norm_f(Ob, 97, None)
                        if hl == 0:
                            k.tt(sc[:], Ov[:, :, 65:97], r1.unsqueeze(2).to_broadcast([128, 4, 32]), ALU.mult)
                        else:
                            k.tt(tmp32[:], Ov[:, :, 65:97], r1.unsqueeze(2).to_broadcast([128, 4, 32]), ALU.mult)
                            k.tt(sc[:], sc[:], tmp32[:], ALU.add)
                        i_ = rdi[0] % 8
                        rdi[0] += 1
                        k.tt(ff[:, i_, :], r1, sig[:, q0:q0 + 4, 3 * hl], ALU.mult)
                        k.tt(accn[:, :, hl, :], Ov[:, :, 0:64], ff[:, i_, :].unsqueeze(2).to_broadcast([128, 4, 64]), ALU.mult)
                    st.add(s_fn, 127, 512, 0.125, pv_fn, after)
                if part == "cmp":
                    for hl in range(4):
                        cmp_item(hl)

                def topk(q0=q0, ch=ch):
                    k.tt(s2[:], sc[:], m1[:, q0:q0 + 4, :], ALU.mult)
                    k.tt(s2[:], s2[:], m2[:, q0:q0 + 4, :], ALU.add)
                    for qi in range(4):
                        en = "dve"
                        cm = cmpm2 if qi % 2 else cmpm
                        k.tt(cm[:], s2[:, qi, :].unsqueeze(1).to_broadcast([128, 32, 32]),
                             s2[:, qi, :].unsqueeze(2).to_broadcast([128, 32, 32]), ALU.is_gt, eng=en)
                        k.red(rank[:, qi, :], cm[:], ALU.add)
                    k.ts(negpad[:, :, 64:96], rank[:], 15.5, NEG, ALU.is_gt, ALU.mult)
                    if b == 0 and g == 0 and ch == 3 and "rank" in DBG:
                        finals.append(k.dma("sp", DBG["rank"], rank[:]))
                        finals.append(k.dma("sp", DBG["sc"], sc[:]))
                if part == "cmp":
                    st.defer(topk)
                    return

                def br_items(hl, br):
                    Ob = nxt("o", O_B)
                    Ov = Ob[:, 0:260].rearrange("p (q c) -> p q c", c=65)
                    kts = list(range(max(0, q0 - 4), q0 + 4)) if br == 0 else list(range(0, q0 + 4))
                    state = {"first": True}
                    for kt in kts:
                        qlo = max(kt, q0)
                        qhi = min(kt + 4, q0 + 3) if br == 0 else q0 + 3
                        n = 128 * (qhi - qlo + 1)
                        ks = slice(kt * 128, (kt + 1) * 128)
                        qs = slice(qlo * 128, (qhi + 1) * 128)

                        def s_fn(Sb, kt=kt, qlo=qlo, qhi=qhi, n=n, ks=ks, qs=qs):
                            if br == 0:
                                k.mm(Sb[:, 0:n], KWt[:, ks], Qaug[:, hl, qs], start=True, stop=False)
                            else:
                                k.mm(Sb[:, 0:n], KSaug[:, ks], Qaug[:, hl, qs], start=True, stop=False)
                            nlo = max(kt, qlo)
                            nhi = min(kt + 1, qhi)
                            if nlo <= nhi:
                                o = (nlo - qlo) * 128
                                w_ = (nhi - nlo + 1) * 128
                                k.mm(Sb[:, o:o + w_], ident_bf[:], Tn[:, hl, (nlo - kt) * 128:(nhi - kt + 1) * 128],
                                     start=False, stop=False)
                            if br == 0 and qlo <= kt + 4 <= qhi:
                                o = (kt + 4 - qlo) * 128
                                k.mm(Sb[:, o:o + 128], ident_bf[:], anti[:], start=False, stop=False)

                        def pv_fn(pt, kt=kt, qlo=qlo, qhi=qhi):
                            V = VW if br == 0 else VS
                            for qt in range(qlo, qhi + 1):
                                o = (qt - qlo) * 128
                                k.mm(Ob[:, (qt - q0) * 65:(qt - q0 + 1) * 65], pt[:, o:o + 128], V[:, kt, :],
                                     start=state["first"], stop=False)
                                state["first"] = False

                        after = None
                        if kt == kts[-1]:
                            def after():
                                f = norm_f(Ob, 65, sig[:, q0:q0 + 4, 3 * hl + (2 if br == 0 else 1)])
                                k.tt(tmp64[:], Ov[:, :, 0:64], f.unsqueeze(2).to_broadcast([128, 4, 64]), ALU.mult)
                                k.tt(accn[:, :, hl, :], accn[:, :, hl, :], tmp64[:], ALU.add)
                        st.add(s_fn, 128, n, 0.125, pv_fn, after)
                def selmask(cs=cs):
                    bank = M_B[0]
                    for qi in range(4):
                        k.mm(bank[0:96, qi * 128:(qi + 1) * 128], negpad[:, qi, :], ident_bf[:],
                             start=(qi == 0), stop=False)
                    for hl in range(4):
                        evac(Qaug[64:96, hl, cs], bank[64:96, :])
                if part == "win":
                    for hl in range(4):
                        br_items(hl, 0)
                    st.defer(selmask)
                    return
                for hl in range(4):
                    br_items(hl, 1)

                def write_ons(q0=q0):
                    k.cp(ons[:, q0:q0 + 4, 256 * g:256 * (g + 1)], accn[:].rearrange("p q h d -> p q (h d)"), eng="dve")
                st.defer(write_ons)
            do_chunk(0, "cmp")
            for ch in range(NCH):
                do_chunk(ch, "win")
                if ch + 1 < NCH:
                    do_chunk(ch + 1, "cmp")
                else:
                    st.flush()
                do_chunk(ch, "sel")
            st.flush()
        if b == 0 and "ons" in DBG:
            finals.append(k.dma("sp", DBG["ons"], ons[:]))
        k.barrier()
    print("sbuf remaining after nsa scope", nc.sbuf_bytes_remaining)


def mla_stage(k, nc, I, DBG, finals, b, L):
    hT, ons, wbuf, wview = L["hT"], L["ons"], L["wbuf"], L["wview"]
    S_B, O_B, M_B = L["S_B"], L["O_B"], L["M_B"]
    ident_bf, tri, ones_bf, invf, sgn, qg, kvg = (L[n] for n in ("ident_bf", "tri", "ones_bf", "invf", "sgn", "qg", "kvg"))
    w_inv = I["w_in"].rearrange("(kc p) n -> p kc n", p=128)
    rr = {"s": 0, "o": 0, "m": 0, "pt": 0, "ev": 0}
    PI = math.pi

    def nxt(key, lst):
        v = lst[rr[key] % len(lst)]
        rr[key] += 1
        return v

    def evac(out, in_):
        rr["ev"] += 1
        if rr["ev"] % 2:
            k.cp(out, in_, eng="act")
        else:
            k.cp(out, in_, eng="dve")

    with ExitStack() as e3:
        cqn = k.sb("cqn", [128, 2, S], BF16, e3)
        ckvn = k.sb("ckvn", [128, S], BF16, e3)
        COS2 = k.sb("COS2", [96, S], F32, e3)
        SIN2 = k.sb("SIN2", [96, S], F32, e3)
        krr = k.sb("krr", [96, S], BF16, e3)
        QmT = [k.sb(f"QmT{i}", [128, S], BF16, e3) for i in range(2)]
        KmT = [k.sb(f"KmT{i}", [128, S], BF16, e3) for i in range(2)]
        for i in range(2):
            k.memset(QmT[i][64:128, :], 0.0, eng="pool")
            k.memset(KmT[i][64:128, :], 0.0, eng="pool")
        Vh = [k.sb(f"Vh{i}", [128, NQT, 129], BF16, e3) for i in range(2)]
        sbh = [k.sb(f"sbh{i}", [128, NQT, 128], BF16, e3) for i in range(2)]
        wmb = [k.sb(f"wmb{i}", [128, 8, 128], BF16, e3) for i in range(2)]
        PTs = [k.sb(f"mpt{i}", [128, 512], BF16, e3) for i in range(3)]
        raw = [k.sb(f"raw{i}", [128, 512], F32, e3) for i in range(3)]
        sq = [k.sb(f"sq{i}", [128, 512], BF16, e3) for i in range(3)]
        rstd = k.sb("rstd", [128, 512], F32, e3)
        posi = k.sb("posi", [96, 512], I32, e3)
        ang = raw[0]
        targ = raw[1]
        t1 = k.sb("t1", [96, 512], F32, e3)
        t2 = k.sb("t2", [96, 512], F32, e3)
        satmps = [sq[0], sq[1]]
        otmp = k.sb("otmp", [128, 2, 128], F32, e3)
        rd = k.sb("mrd", [128, 8, 2], F32, e3)
        for i in range(2):
            k.memset(Vh[i][:, :, 128:129], 1.0)

        R = slice(64, 96)
        for ch in range(NCH):
            cs = slice(ch * 512, (ch + 1) * 512)
            k.dma("sp", posi[:], I["pos"][b, cs].partition_broadcast(96))
            k.cp(ang[R, :], posi[R, :])
            k.ts(ang[R, :], ang[R, :], invf[R, 0:1], None, ALU.mult)
            k.ts(targ[R, :], ang[R, :], 1.0 / (2 * PI), None, ALU.mult)
            k.cp(posi[R, :], targ[R, :])
            k.cp(targ[R, :], posi[R, :])
            k.stt(ang[R, :], targ[R, :], -2 * PI, ang[R, :], ALU.mult, ALU.add)
            k.ts(targ[R, :], ang[R, :], PI, -2 * PI, ALU.is_gt, ALU.mult)
            k.tt(ang[R, :], ang[R, :], targ[R, :], ALU.add)
            k.act(SIN2[R, cs], ang[R, :], AF.Sin)
            k.ts(SIN2[R, cs], SIN2[R, cs], sgn[R, 0:1], None, ALU.mult)
            k.ts(ang[R, :], ang[R, :], 0.5 * PI, None, ALU.add)
            k.ts(targ[R, :], ang[R, :], PI, -2 * PI, ALU.is_gt, ALU.mult)
            k.tt(ang[R, :], ang[R, :], targ[R, :], ALU.add)
            k.act(COS2[R, cs], ang[R, :], AF.Sin)

        for j in range(2):
            wt = wbuf()
            wv = wview(wt, 8, 512)
            k.dma("pool", wv, w_inv[:, :, C_MA + 512 * j:C_MA + 512 * (j + 1)])
            for qt in range(NQT):
                bank = nxt("m", M_B)
                for kc in range(8):
                    k.mm(bank[:, :], hT[:, kc, qt * 128:(qt + 1) * 128], wv[:, kc, :], start=(kc == 0), stop=(kc == 7))
                satmp = satmps[qt % 2]
                k.act(satmp[:], bank[:, :], AF.Tanh, scale=0.5)
                k.ts(satmp[:], satmp[:], 0.5, 0.5, ALU.mult, ALU.add)
                k.tt(ons[:, qt, 512 * j:512 * (j + 1)], ons[:, qt, 512 * j:512 * (j + 1)], satmp[:], ALU.mult)

        wt = wbuf()
        wv = wview(wt, 8, 576)
        k.memset(wv[:, :, 384:576], 0.0)
        k.dma("pool", wv[:, :, 0:256], w_inv[:, :, C_CQ:C_CQ + 256])
        k.dma("pool", wv[:, :, 256:384], w_inv[:, :, C_CKV:C_CKV + 128])
        k.dma("pool", wv[:, :, 448:480], w_inv[:, :, C_KR:C_KR + 32])
        k.dma("pool", wv[:, :, 544:560], w_inv[:, :, C_KR + 16:C_KR + 32])
        k.dma("pool", wv[:, :, 560:576], w_inv[:, :, C_KR:C_KR + 16])
        for ch in range(NCH):
            cs = slice(ch * 512, (ch + 1) * 512)
            for m in range(3):
                bank = nxt("m", M_B)
                for kc in range(8):
                    k.mm(bank[:, :], wv[:, kc, m * 128:(m + 1) * 128], hT[:, kc, cs], start=(kc == 0), stop=(kc == 7))
                k.act(sq[m][:], bank[:, :], AF.Square)
                k.cp(raw[m][:], bank[:, :], eng="dve")
            bank = nxt("m", M_B)
            k.mm(bank[:, :], ones_bf[:], sq[0][:], start=True, stop=False)
            k.mm(bank[:, :], ones_bf[:], sq[1][:], start=False, stop=True)
            k.act(rstd[:], bank[:, :], AF.Sqrt, bias=1e-6, scale=1.0 / 256)
            k.recip(rstd[:], rstd[:])
            for m in range(2):
                k.stt(cqn[:, m, cs], raw[m][:], qg[:, m:m + 1], rstd[:], ALU.mult, ALU.mult)
            bank = nxt("m", M_B)
            k.mm(bank[:, :], ones_bf[:], sq[2][:], start=True, stop=True)
            k.act(rstd[:], bank[:, :], AF.Sqrt, bias=1e-6, scale=1.0 / 128)
            k.recip(rstd[:], rstd[:])
            k.stt(ckvn[:, cs], raw[2][:], kvg[:, 0:1], rstd[:], ALU.mult, ALU.mult)
            bank = nxt("m", M_B)
            for kc in range(8):
                k.mm(bank[0:96, :], wv[:, kc, 384:480], hT[:, kc, cs], start=(kc == 0), stop=(kc == 7))
            k.tt(t1[R, :], bank[R, :], COS2[R, cs], ALU.mult)
            bank = nxt("m", M_B)
            for kc in range(8):
                k.mm(bank[0:96, :], wv[:, kc, 480:576], hT[:, kc, cs], start=(kc == 0), stop=(kc == 7))
            k.tt(t2[R, :], bank[R, :], SIN2[R, cs], ALU.mult)
            k.tt(krr[R, cs], t1[R, :], t2[R, :], ALU.add)

        wt = wbuf()
        wuq = wview(wt, 2, 768, 0)
        wus = wview(wt, 2, 768, 1536)
        wkv = wt[:, 3072:3072 + 1536]
        k.dma("pool", wuq, I["w_uq"].rearrange("(c p) n -> p c n", p=128))
        k.dma("pool", wus, I["w_uq_sw"].rearrange("(c p) n -> p c n", p=128))
        k.dma("pool", wkv, I["w_ukv"])
        scale = 96 ** -0.5
        def proj(h):
            Q, Kt, V, sb_, wm = QmT[h % 2], KmT[h % 2], Vh[h % 2], sbh[h % 2], wmb[h % 2]
            k.dma("pool", wm[:], w_inv[:, :, C_MB + 128 * h:C_MB + 128 * (h + 1)])
            k.cp(Kt[R, :], krr[R, :], eng="dve")
            for ch in range(NCH):
                cs = slice(ch * 512, (ch + 1) * 512)
                bankA = nxt("m", M_B)
                for kc in range(2):
                    k.mm(bankA[0:96, :], wuq[:, kc, h * 96:(h + 1) * 96], cqn[:, kc, cs], start=(kc == 0), stop=(kc == 1))
                k.cp(Q[0:64, cs], bankA[0:64, :], eng="dve")
                k.tt(t1[R, :], bankA[R, :], COS2[R, cs], ALU.mult)
                bankB = nxt("m", M_B)
                for kc in range(2):
                    k.mm(bankB[0:96, :], wus[:, kc, h * 96:(h + 1) * 96], cqn[:, kc, cs], start=(kc == 0), stop=(kc == 1))
                k.tt(t2[R, :], bankB[R, :], SIN2[R, cs], ALU.mult)
                k.tt(Q[R, cs], t1[R, :], t2[R, :], ALU.add)
                yield
                bank = nxt("m", M_B)
                k.mm(bank[0:64, :], wkv[:, h * 192:h * 192 + 64], ckvn[:, cs], start=True, stop=True)
                k.cp(Kt[0:64, cs], bank[0:64, :], eng="dve")
                yield
                bank = nxt("m", M_B)
                for qi in range(4):
                    qt = ch * 4 + qi
                    k.mm(bank[:, qi * 128:(qi + 1) * 128], ckvn[:, qt * 128:(qt + 1) * 128],
                         wkv[:, h * 192 + 64:h * 192 + 192], start=(qi == 0), stop=False)
                k.cp(V[:, ch * 4:ch * 4 + 4, 0:128], bank[:, :].rearrange("p (q c) -> p q c", c=128), eng="dve")
                yield
                bank = nxt("m", M_B)
                first = True
                for qi in range(4):
                    qt = ch * 4 + qi
                    for kc in range(8):
                        k.mm(bank[:, qi * 128:(qi + 1) * 128], hT[:, kc, qt * 128:(qt + 1) * 128], wm[:, kc, :],
                             start=first, stop=False)
                        first = False
                k.act(sb_[:, ch * 4:ch * 4 + 4, :], bank[:, :].rearrange("p (q c) -> p q c", c=128), AF.Tanh, scale=0.5)
                k.ts(sb_[:, ch * 4:ch * 4 + 4, :], sb_[:, ch * 4:ch * 4 + 4, :], 0.5, 0.5, ALU.mult, ALU.add)
                yield

        st = AttnStream(k, S_B, PTs, look=1)

        def attn(h, gen):
            Q, Kt, V, sb_ = QmT[h % 2], KmT[h % 2], Vh[h % 2], sbh[h % 2]
            cnt = [0]
            for ch in range(NCH):
                q0 = 4 * ch
                OA = O_B[(rr["o"] % 2) * 2]
                OBk = O_B[(rr["o"] % 2) * 2 + 1]
                rr["o"] += 1
                firsts = {0: True, 1: True}
                for kt in range(0, q0 + 4):
                    qlo = max(kt, q0)
                    qhi = q0 + 3
                    n = 128 * (qhi - qlo + 1)

                    def s_fn(Sb, kt=kt, qlo=qlo, qhi=qhi, n=n, q0=q0):
                        k.mm(Sb[:, 0:n], Kt[:, kt * 128:(kt + 1) * 128], Q[:, qlo * 128:(qhi + 1) * 128],
                             start=True, stop=False)
                        if kt >= q0:
                            k.mm(Sb[:, 0:128], ident_bf[:], tri[:], start=False, stop=False)

                    def pv_fn(pt, kt=kt, qlo=qlo, qhi=qhi, firsts=firsts, OA=OA, OBk=OBk, q0=q0):
                        for qt in range(qlo, qhi + 1):
                            o = (qt - qlo) * 128
                            bi = (qt - q0) // 2
                            sl = (qt - q0) % 2
                            bank = OA if bi == 0 else OBk
                            k.mm(bank[:, sl * 129:(sl + 1) * 129], pt[:, o:o + 128], V[:, kt, :],
                                 start=firsts[bi], stop=False)
                            firsts[bi] = False

                    after = None
                    if kt == q0 + 3:
                        def after(OA=OA, OBk=OBk, q0=q0):
                            for bi, bank in enumerate((OA, OBk)):
                                qa = q0 + 2 * bi
                                Ov = bank[:, 0:258].rearrange("p (q c) -> p q c", c=129)
                                i_ = rr["ev"] % 8
                                rr["ev"] += 1
                                k.ts(rd[:, i_, :], Ov[:, :, 128], 1e-30, None, ALU.max)
                                k.recip(rd[:, i_, :], rd[:, i_, :])
                                k.tt(otmp[:], Ov[:, :, 0:128], rd[:, i_, :].unsqueeze(2).to_broadcast([128, 2, 128]), ALU.mult)
                                k.tt(otmp[:], otmp[:], sb_[:, qa:qa + 2, :], ALU.mult)
                                k.tt(ons[:, qa:qa + 2, 128 * h:128 * (h + 1)], ons[:, qa:qa + 2, 128 * h:128 * (h + 1)],
                                     otmp[:], ALU.add)
                    st.add(s_fn, 128, n, scale, pv_fn, after)
                    cnt[0] += 1
                    if gen is not None and cnt[0] % 2 == 0:
                        next(gen, None)

        for _ in proj(0):
            pass
        for h in range(8):
            gen = proj(h + 1) if h + 1 < 8 else None
            attn(h, gen)
            st.flush()
            if gen is not None:
                for _ in gen:
                    pass
        if b == 0 and "y" in DBG:
            finals.append(k.dma("sp", DBG["y"], ons[:]))
        k.barrier()


def sgn_negpi(L, k):
    if "negpi" not in L["cache"]:
        t = k.sb("negpi", [128, 1], F32)
        k.memset(t[:], -math.pi)
        L["cache"]["negpi"] = t
    return L["cache"]["negpi"]


def out_stage(k, nc, I, DBG, finals, b, L):
    ons, wview, WB = L["ons"], L["wview"], L["WB"]
    PS = L["PS"]
    M_B = L["M_B"]
    ident_bf, sc2, modFM, fng_b, convw, convb = (L[n] for n in ("ident_bf", "sc2", "modFM", "fng_b", "convw", "convb"))
    gD, x1D, out = L["gD"], L["x1D"], L["out"]
    xv = I["x"]
    rr = {"b": 0, "ev": 0}

    def nb():
        v = PS[rr["b"] % 8]
        rr["b"] += 1
        return v

    def bfv(bank):
        return bank[:].bitcast(BF16)

    with ExitStack() as e4:
        h2T = L["hT"]
        gm_b, gf_b, ss = L["gm_b"], L["gf_b"], L["ss4"]
        with ExitStack() as e4a:
            yTs = [k.sb(f"yT{i}", [128, 8, 512], BF16, e4a) for i in range(2)]
            xts = [k.sb(f"x4t{i}", [128, D], F32, e4a) for i in range(4)]
            x1ts = [k.sb(f"x1t{i}", [128, D], F32, e4a) for i in range(4)]
            xns_all = [k.sb(f"x4n{i}", [128, D], BF16, e4a) for i in range(8)]
            junk = k.sb("junk4", [128, D], BF16, e4a)
            tmp = k.sb("tmp4", [128, 512], F32, e4a)
            wo = wview(WB[0], 8, 1024)
            k.dma("pool", wo, I["w_o"].rearrange("(kc p) n -> p kc n", p=128))
            for ch in range(NCH):
                yT = yTs[ch % 2]
                xns = xns_all[(ch % 2) * 4:(ch % 2) * 4 + 4]
                for kp in range(4):
                    bank = nb()
                    bv = bfv(bank)
                    for kk in range(2):
                        kc = kp * 2 + kk
                        for qi in range(4):
                            k.tr(bv[:, (kk * 4 + qi) * 128:(kk * 4 + qi + 1) * 128],
                                 ons[:, ch * 4 + qi, kc * 128:(kc + 1) * 128], ident_bf[:])
                    k.cp(yT[:, kp * 2:kp * 2 + 2, :], bv[:, :].rearrange("p (a n) -> p a n", a=2),
                         eng=("act" if kp % 2 else "dve"))
                for qi in range(4):
                    qt = ch * 4 + qi
                    xt = xts[qt % 4]
                    x1t = x1ts[qt % 4]
                    k.dma("sp", xt[:], xv[b, qt * 128:(qt + 1) * 128, :])
                    for n in range(2):
                        bank = nb()
                        for kc in range(8):
                            k.mm(bank[:, :], yT[:, kc, qi * 128:(qi + 1) * 128], wo[:, kc, n * 512:(n + 1) * 512],
                                 start=(kc == 0), stop=(kc == 7))
                        k.tt(tmp[:], bank[:, :], gm_b[:, n * 512:(n + 1) * 512], ALU.mult)
                        k.tt(x1t[:, n * 512:(n + 1) * 512], tmp[:], xt[:, n * 512:(n + 1) * 512], ALU.add)
                    k.dma("act", x1D.ap()[b, qt * 128:(qt + 1) * 128, :], x1t[:])
                    k.act(junk[:], x1t[:], AF.Square, accum_out=ss[:, qi:qi + 1])
                    k.act(ss[:, qi:qi + 1], ss[:, qi:qi + 1], AF.Sqrt, bias=1e-6, scale=1.0 / D)
                    k.recip(ss[:, qi:qi + 1], ss[:, qi:qi + 1])
                    k.ts(xns[qi][:], x1t[:], ss[:, qi:qi + 1], None, ALU.mult)
                for kp in range(4):
                    bank = nb()
                    bv = bfv(bank)
                    for kk in range(2):
                        kc = kp * 2 + kk
                        for qi in range(4):
                            k.tr(bv[:, (kk * 4 + qi) * 128:(kk * 4 + qi + 1) * 128],
                                 xns[qi][:, kc * 128:(kc + 1) * 128], ident_bf[:])
                    for kk in range(2):
                        kc = kp * 2 + kk
                        k.act(h2T[:, kc, ch * 512:(ch + 1) * 512], bv[:, kk * 512:(kk + 1) * 512],
                              AF.Identity, bias=modFM[:, 24 + kc, b:b + 1], scale=sc2[:, kc, b:b + 1])
            k.barrier()
        L["ses2"].close()
        if b == 0 and "h2T" in DBG:
            finals.append(k.dma("sp", DBG["h2T"], h2T[:]))
        with ExitStack() as e4b:
            aT = k.sb("aT", [128, NFC, S], BF16, e4b)
            gbuf = k.sb("gbuf", [128, S + 2], F32, e4b)
            cv = k.sb("cv", [128, S], F32, e4b)
            sg = k.sb("sg", [128, S], BF16, e4b)
            k.memset(gbuf[:, 0:2], 0.0)
            wgv = I["w_gate"].rearrange("(kc p) n -> p kc n", p=128)
            wuv = I["w_up"].rearrange("(kc p) n -> p kc n", p=128)
            gi = 0
            for f0 in range(0, NFC, 4):
                nf = min(4, NFC - f0)
                wt = WB[gi % 2]
                gi += 1
                wg = wview(wt, 8, 512, 0)
                wu = wview(wt, 8, 512, 4096)
                k.dma("pool", wg[:, :, 0:nf * 128], wgv[:, :, f0 * 128:(f0 + nf) * 128])
                k.dma("pool", wu[:, :, 0:nf * 128], wuv[:, :, f0 * 128:(f0 + nf) * 128])
                for fl in range(nf):
                    fc = f0 + fl
                    for tch in range(NCH):
                        cs = slice(tch * 512, (tch + 1) * 512)
                        bank = nb()
                        for kc in range(8):
                            k.mm(bank[:, :], wg[:, kc, fl * 128:(fl + 1) * 128], h2T[:, kc, cs], start=(kc == 0), stop=(kc == 7))
                        k.cp(gbuf[:, 2 + tch * 512:2 + (tch + 1) * 512], bank[:, :], eng="act")
                    k.ts(cv[:], gbuf[:, 2:S + 2], convw[:, fc, 2:3], convb[:, fc:fc + 1], ALU.mult, ALU.add)
                    k.stt(cv[:], gbuf[:, 1:S + 1], convw[:, fc, 1:2], cv[:], ALU.mult, ALU.add)
                    k.stt(cv[:], gbuf[:, 0:S], convw[:, fc, 0:1], cv[:], ALU.mult, ALU.add)
                    k.act(sg[:], cv[:], AF.Silu)
                    for tch in range(NCH):
                        cs = slice(tch * 512, (tch + 1) * 512)
                        bank = nb()
                        for kc in range(8):
                            k.mm(bank[:, :], wu[:, kc, fl * 128:(fl + 1) * 128], h2T[:, kc, cs], start=(kc == 0), stop=(kc == 7))
                        k.tt(aT[:, fc, cs], bank[:, :], sg[:, cs], ALU.mult)
            x1r = [cv[:, i * D:(i + 1) * D] for i in range(2)]
            x2 = [gbuf[:, i * D:(i + 1) * D] for i in range(2)]
            junk = sg[:, 0:D]
            tmp = k.sb("tmp5", [128, 512], F32, e4b)
            wdv = I["w_down"].rearrange("(fc p) n -> p fc n", p=128)
            for qg_ in range(4):
                banks = [[nb() for _ in range(2)] for _ in range(4)]
                for f0 in range(0, NFC, 8):
                    nf = min(8, NFC - f0)
                    wt = WB[gi % 2]
                    gi += 1
                    wd = wview(wt, 8, 1024)
                    k.dma("pool", wd[:, 0:nf, :], wdv[:, f0:f0 + nf, :])
                    for fl in range(nf):
                        fc = f0 + fl
                        for qi in range(4):
                            qt = qg_ * 4 + qi
                            for n in range(2):
                                k.mm(banks[qi][n][:, :], aT[:, fc, qt * 128:(qt + 1) * 128], wd[:, fl, n * 512:(n + 1) * 512],
                                     start=(fc == 0), stop=(fc == NFC - 1))
                for qi in range(4):
                    qt = qg_ * 4 + qi
                    xr = x1r[qt % 2]
                    xo = x2[qt % 2]
                    k.dma("sp", xr, x1D.ap()[b, qt * 128:(qt + 1) * 128, :])
                    for n in range(2):
                        k.tt(tmp[:], banks[qi][n][:, :], gf_b[:, n * 512:(n + 1) * 512], ALU.mult)
                        k.tt(xo[:, n * 512:(n + 1) * 512], tmp[:], xr[:, n * 512:(n + 1) * 512], ALU.add)
                    k.act(junk, xo, AF.Square, accum_out=ss[:, 4 + qi:5 + qi])
                    k.act(ss[:, 4 + qi:5 + qi], ss[:, 4 + qi:5 + qi], AF.Sqrt, bias=1e-6, scale=1.0 / D)
                    k.recip(ss[:, 4 + qi:5 + qi], ss[:, 4 + qi:5 + qi])
                    k.stt(xo, xo, ss[:, 4 + qi:5 + qi], fng_b[:], ALU.mult, ALU.mult)
                    finals.append(k.dma("act", out[b, qt * 128:(qt + 1) * 128, :], xo))
            k.barrier()
        k.barrier()


def shared_inputs(inp):
    f = np.float32
    m = {}
    m["rel"] = inp["rel_bias_table"].astype(f)
    m["ada_w"] = inp["ada_w"][0]
    m["ada_bT"] = np.ascontiguousarray(inp["ada_b"][0].reshape(48, 128).T)
    m["nmg"] = np.ascontiguousarray(inp["norm_mix_g"][0].reshape(8, 128).T)
    m["w_in"] = inp["w_in"][0]
    m["posk"] = np.ascontiguousarray(inp["cmp_pos_k"][0].reshape(16, 128).T)
    m["posv"] = np.ascontiguousarray(inp["cmp_pos_v"][0].reshape(16, 128).T)
    m["w1k"] = inp["cmp_w1_k"][0]
    m["w2k"] = inp["cmp_w2_k"][0]
    m["w1v"] = inp["cmp_w1_v"][0]
    m["w2v"] = inp["cmp_w2_v"][0]
    m["qg"] = np.ascontiguousarray(inp["mla_q_norm_g"][0].reshape(2, 128).T)
    m["kvg"] = np.ascontiguousarray(inp["mla_kv_norm_g"][0].reshape(1, 128).T)
    wuq = inp["mla_w_uq"][0]
    m["w_uq"] = wuq
    sw = wuq.reshape(256, 8, 96).copy()
    sw[:, :, 64:80] = wuq.reshape(256, 8, 96)[:, :, 80:96]
    sw[:, :, 80:96] = wuq.reshape(256, 8, 96)[:, :, 64:80]
    m["w_uq_sw"] = np.ascontiguousarray(sw.reshape(256, 768))
    m["w_ukv"] = inp["mla_w_ukv"][0]
    m["w_o"] = inp["w_o"][0]
    m["nfg"] = np.ascontiguousarray(inp["norm_ffn_g"][0].reshape(8, 128).T)
    m["w_gate"] = inp["ffn_w_gate"][0]
    m["w_up"] = inp["ffn_w_up"][0]
    m["convw"] = np.ascontiguousarray(inp["ffn_conv_w"][0].reshape(3, NFC, 128).transpose(2, 1, 0))
    m["convb"] = np.ascontiguousarray(inp["ffn_conv_b"][0].reshape(NFC, 128).T)
    m["w_down"] = inp["ffn_w_down"][0]
    m["fng"] = inp["final_norm_g"]
    m.update(make_consts())
    return {k_: np.ascontiguousarray(v) for k_, v in m.items()}


_CACHE = {}


def kernel(**inputs):
    inp = {k_: np.asarray(v) for k_, v in inputs.items()}
    shared = shared_inputs(inp)
    if "nc" not in _CACHE:
        _CACHE["nc"] = build(nseq=4)[0]
    nc = _CACHE["nc"]
    in_maps = []
    for core in range(8):
        bsel = list(range(core * 4, core * 4 + 4))
        m = dict(shared)
        m["x"] = np.ascontiguousarray(inp["x"][bsel], dtype=np.float32)
        m["cT"] = np.ascontiguousarray(inp["c"][bsel].reshape(4, 8, 128).transpose(2, 1, 0), dtype=np.float32)
        m["pos"] = np.ascontiguousarray(inp["positions"][bsel], dtype=np.int32)
        in_maps.append(m)
    res = run_bass_kernel_spmd(nc, in_maps, core_ids=list(range(8)))
    out = np.concatenate([np.asarray(r["out"]) for r in res.results], axis=0)
    return out.astype(np.float32)
```

```python
from contextlib import ExitStack
import math
import numpy as np
import ml_dtypes
import concourse.bass as bass
import concourse.mybir as mybir
from concourse.bass_utils import run_bass_kernel_spmd

F32 = mybir.dt.float32
BF16 = mybir.dt.bfloat16
I32 = mybir.dt.int32
AF = mybir.ActivationFunctionType
ALU = mybir.AluOpType
AX = mybir.AxisListType

EPOCH = 30000
NDSEM = 12

D = 1024
S = 2048
NQT = 16
NCH = 4
DFF = 2816
NFC = 22
NEG = -30000.0
INCOLS = 5072
C_Q, C_KC, C_VC, C_KS, C_VS, C_KW, C_VW, C_G, C_CQ, C_CKV, C_KR, C_MA, C_MB = (
    0, 1024, 1280, 1536, 1792, 2048, 2304, 2560, 2608, 2864, 2992, 3024, 4048)


class KB:
    def __init__(self, nc):
        self.nc = nc
        self.es = ExitStack()
        self.eng = {"pe": nc.tensor, "act": nc.scalar, "dve": nc.vector,
                    "pool": nc.gpsimd, "sp": nc.sync}
        self.cnt = {e: 0 for e in ("pe", "act", "dve", "pool")}
        self.csem = {e: [] for e in self.cnt}
        self.dcnt = {}
        self.dsem = {}
        self.waited = {}
        self.recs = {}
        self.nwaits = 0
        self.ninst = 0

    def sb(self, name, shape, dt, es=None):
        self.uid = getattr(self, "uid", 0) + 1
        return (es or self.es).enter_context(self.nc.sbuf_tensor(f"s{self.uid}_{name}", list(shape), dt))

    def ps(self, name, shape, dt):
        return self.es.enter_context(self.nc.psum_tensor(name, list(shape), dt))

    def _sem(self, name):
        return self.es.enter_context(self.nc.semaphore(name))

    @staticmethod
    def box(ap):
        t = ap.tensor
        space = str(ap.space)
        if space == "DRAM":
            lo = int(ap.offset)
            hi = lo
            for st, n in ap.ap:
                if n > 1:
                    if st >= 0:
                        hi += st * (n - 1)
                    else:
                        lo += st * (n - 1)
            return (t.name, 0, 1, lo, hi + 1, False)
        if space == "PSUM":
            return (t.name, 0, 128, 0, 1 << 30, True)
        row = 1
        for s_ in list(t.shape)[1:]:
            row *= s_
        off = int(ap.offset)
        p0 = off // row
        f0 = off % row
        apl = list(ap.ap)
        lo = f0
        hi = f0
        for st, n in apl[1:]:
            if n > 1:
                if st >= 0:
                    hi += st * (n - 1)
                else:
                    lo += st * (n - 1)
        return (t.name, p0, p0 + apl[0][1], lo, hi + 1, False)

    def _wait_compute(self, waiter, e, idx):
        key = (waiter, e)
        if self.waited.get(key, -1) >= idx:
            return
        self.waited[key] = idx
        self.eng[waiter].wait_ge(self.csem[e][idx // EPOCH], (idx % EPOCH) + 1)
        self.nwaits += 1

    def _wait_dma(self, waiter, q, k):
        slot = k % NDSEM
        val = 16 * (k // NDSEM + 1)
        key = (waiter, "dma", q, slot)
        if self.waited.get(key, 0) >= val:
            return
        self.waited[key] = val
        self.eng[waiter].wait_ge(self.dsem[q][slot], val)
        self.nwaits += 1

    def _wait(self, waiter, who):
        if who[0] == "dma":
            self._wait_dma(waiter, who[1], who[2])
        else:
            self._wait_compute(waiter, who[0], who[1])

    @staticmethod
    def _ovl(a, b):
        return a[1] < b[2] and b[1] < a[2] and a[3] < b[4] and b[3] < a[4]

    @staticmethod
    def _covers(a, b):
        return a[1] <= b[1] and a[2] >= b[2] and a[3] <= b[3] and a[4] >= b[4]

    def _deps(self, eng, reads, writes, is_dma):
        deps = []
        for ap in reads:
            bx = self.box(ap)
            for (rb, kind, who) in self.recs.get(bx[0], ()):
                if not self._ovl(bx, rb):
                    continue
                same = (not is_dma) and who[0] == eng
                if kind == "w":
                    if same and eng == "pe":
                        continue
                    deps.append(who)
                elif bx[5] and not same:
                    deps.append(who)
        for ap in writes:
            bx = self.box(ap)
            for (rb, kind, who) in self.recs.get(bx[0], ()):
                if not self._ovl(bx, rb):
                    continue
                same = (not is_dma) and who[0] == eng
                if same and eng == "pe":
                    continue
                deps.append(who)
        return deps

    def _record(self, who, reads, writes):
        for ap in writes:
            bx = self.box(ap)
            lst = self.recs.setdefault(bx[0], [])
            if bx[5]:
                lst[:] = []
            else:
                lst[:] = [r for r in lst if not self._covers(bx, r[0])]
            lst.append((bx, "w", who))
        for ap in reads:
            bx = self.box(ap)
            lst = self.recs.setdefault(bx[0], [])
            if who[0] != "dma":
                lst[:] = [r for r in lst
                          if not (r[1] == "r" and r[2][0] == who[0] and r[0] == bx)]
            lst.append((bx, "r", who))

    def op(self, eng, fn, reads, writes):
        for who in self._deps(eng, reads, writes, False):
            self._wait(eng, who)
        idx = self.cnt[eng]
        ep = idx // EPOCH
        while len(self.csem[eng]) <= ep:
            self.csem[eng].append(self._sem(f"c_{eng}_{len(self.csem[eng])}"))
        ins = fn()
        ins.then_inc(self.csem[eng][ep], 1)
        self.cnt[eng] = idx + 1
        self.ninst += 1
        self._record((eng, idx), reads, writes)
        return (eng, idx)

    def dma(self, q, out, in_, **kw):
        if q not in self.dsem:
            self.dsem[q] = [self._sem(f"d_{q}_{i}") for i in range(NDSEM)]
            self.dcnt[q] = 0
        k = self.dcnt[q]
        if k >= NDSEM:
            self._wait_dma(q, q, k - NDSEM)
        for who in self._deps(q, [in_], [out], True):
            self._wait(q, who)
        ins = self.eng[q].dma_start(out=out, in_=in_, **kw)
        ins.then_inc(self.dsem[q][k % NDSEM], 16)
        self.dcnt[q] = k + 1
        self.ninst += 1
        who = ("dma", q, k)
        self._record(who, [in_], [out])
        return who

    def finish(self, whos, eng="sp"):
        for who in whos:
            self._wait(eng, who)

    def barrier(self):
        for w in ("pe", "act", "dve", "pool", "sp"):
            for e in ("pe", "act", "dve", "pool"):
                if e != w and self.cnt[e] > 0:
                    self._wait_compute(w, e, self.cnt[e] - 1)
            for q in self.dsem:
                n = self.dcnt[q]
                for k in range(max(0, n - NDSEM), n):
                    self._wait_dma(w, q, k)
        self.recs = {}

    def mm(self, out, lhsT, rhs, start=True, stop=True):
        return self.op("pe", lambda: self.nc.tensor.matmul(
            out, lhsT, rhs, start=start, stop=stop, skip_group_check=True), [lhsT, rhs], [out])

    def tr(self, out, in_, ident):
        return self.op("pe", lambda: self.nc.tensor.transpose(out, in_, ident), [in_, ident], [out])

    def act(self, out, in_, func, bias=None, scale=None, accum_out=None):
        kw = {}
        rd = [in_]
        wr = [out]
        if bias is not None:
            kw["bias"] = bias
            if not isinstance(bias, (int, float)):
                rd.append(bias)
        if scale is not None:
            kw["scale"] = scale
            if not isinstance(scale, (int, float)):
                rd.append(scale)
        if accum_out is not None:
            kw["accum_out"] = accum_out
            wr.append(accum_out)
        return self.op("act", lambda: self.nc.scalar.activation(out, in_, func, **kw), rd, wr)

    def _ve(self, eng):
        return self.nc.vector if eng == "dve" else self.nc.gpsimd

    def tt(self, out, in0, in1, op, eng="dve"):
        return self.op(eng, lambda: self._ve(eng).tensor_tensor(out, in0, in1, op), [in0, in1], [out])

    def ts(self, out, in0, s1, s2, op0, op1=None, eng="dve"):
        rd = [in0]
        for s_ in (s1, s2):
            if s_ is not None and not isinstance(s_, (int, float)):
                rd.append(s_)
        kw = {}
        if op1 is not None:
            kw["op1"] = op1
        return self.op(eng, lambda: self._ve(eng).tensor_scalar(out, in0, s1, s2, op0, **kw), rd, [out])

    def stt(self, out, in0, scalar, in1, op0, op1, eng="dve"):
        rd = [in0, in1]
        if not isinstance(scalar, (int, float)):
            rd.append(scalar)
        return self.op(eng, lambda: self._ve(eng).scalar_tensor_tensor(
            out, in0, scalar, in1, op0, op1), rd, [out])

    def cp(self, out, in_, eng="dve"):
        if eng == "act":
            return self.op("act", lambda: self.nc.scalar.copy(out, in_), [in_], [out])
        return self.op(eng, lambda: self._ve(eng).tensor_copy(out, in_), [in_], [out])

    def red(self, out, in_, op, axis=AX.X, eng="dve"):
        return self.op(eng, lambda: self._ve(eng).tensor_reduce(out, in_, axis, op), [in_], [out])

    def memset(self, ap, v, eng="dve"):
        return self.op(eng, lambda: self._ve(eng).memset(ap, v), [], [ap])

    def recip(self, out, in_):
        return self.op("dve", lambda: self.nc.vector.reciprocal(out, in_), [in_], [out])


def _t5_bucket(n):
    n = np.maximum(n, 0)
    exact = 16
    lr = np.log(np.maximum(n, exact).astype(np.float32) / np.float32(exact)) / np.float32(math.log(128 / exact))
    large = np.minimum(exact + (lr.astype(np.float32) * np.float32(16)).astype(np.int32), 31)
    return np.where(n < exact, n, large)


def make_consts():
    bf = ml_dtypes.bfloat16
    c = {}
    c["ident_bf"] = np.eye(128, dtype=np.float32).astype(bf)
    c["ident_f"] = np.eye(128, dtype=np.float32)
    k = np.arange(128)[:, None]
    q = np.arange(128)[None, :]
    c["tri"] = np.where(q >= k, 0.0, NEG).astype(bf)
    c["anti"] = np.where(q < k, 0.0, NEG).astype(bf)
    t = np.arange(S)[None, :]
    c["blockind"] = (t // 64 == np.arange(32)[:, None]).astype(np.float32).astype(bf)
    n = np.arange(1152)
    dist = n - 511
    oh = np.zeros((33, 1152), np.float32)
    bk = _t5_bucket(dist)
    for i in range(1152):
        if dist[i] >= 0:
            oh[bk[i], i] = 1.0
        else:
            oh[32, i] = 1.0
    c["ohe"] = oh
    a = np.zeros((4, 41, 127), np.float32)
    for ch in range(4):
        for cc in range(127):
            r = cc - 32 * ch
            if r >= 31:
                a[ch, 0, cc] = 1.0
            elif r >= -9:
                a[ch, 31 - r, cc] = 1.0
    ap_ = np.zeros((128, 4, 127), np.float32)
    ap_[:41] = a.transpose(1, 0, 2)
    c["ach"] = ap_.astype(bf)
    m1 = np.ones((128, 16, 32), np.float32)
    m2 = np.zeros((128, 16, 32), np.float32)
    for qt in range(16):
        for p in range(128):
            cur = (qt * 128 + p) // 64
            for j in range(32):
                if j == 0 or j == cur or j == cur - 1:
                    m1[p, qt, j] = 0.0
                    m2[p, qt, j] = 1e30
                elif j > cur:
                    m1[p, qt, j] = 0.0
                    m2[p, qt, j] = -1e30
    c["m1"] = m1
    c["m2"] = m2
    cs = np.arange(127) * 16
    bs = np.arange(32) * 64
    ov = np.clip(np.minimum(cs[:, None] + 32, bs[None, :] + 64) - np.maximum(cs[:, None], bs[None, :]), 0, None) / 32.0
    vcc = np.zeros((128, 33), np.float32)
    vcc[:127, 0] = 1.0
    vcc[:127, 1:] = ov
    c["vcconst"] = vcc.astype(bf)
    invf = np.zeros((128, 1), np.float32)
    sgn = np.ones((128, 1), np.float32)
    fr = (np.float32(10000.0) ** (-np.arange(0, 32, 2, dtype=np.float32) / np.float32(32))).astype(np.float32)
    for i in range(16):
        invf[64 + i, 0] = fr[i]
        invf[80 + i, 0] = fr[i]
        sgn[64 + i, 0] = -1.0
    c["invf"] = invf
    c["sgn"] = sgn
    return c


CONST_DT = {"ident_bf": BF16, "ident_f": F32, "tri": BF16, "anti": BF16, "blockind": BF16, "ohe": F32,
            "ach": BF16, "m1": F32, "m2": F32, "vcconst": BF16, "invf": F32, "sgn": F32}

IN_SPECS = [
    ("x", None, F32), ("cT", None, F32), ("pos", None, I32), ("rel", [32, 16], F32),
    ("ada_w", [D, 6 * D], F32), ("ada_bT", [128, 48], F32), ("nmg", [128, 8], F32),
    ("w_in", [D, INCOLS], F32), ("posk", [128, 16], F32), ("posv", [128, 16], F32),
    ("w1k", [2048, 256], F32), ("w2k", [256, 64], F32), ("w1v", [2048, 256], F32), ("w2v", [256, 64], F32),
    ("qg", [128, 2], F32), ("kvg", [128, 1], F32), ("w_uq", [256, 768], F32), ("w_uq_sw", [256, 768], F32),
    ("w_ukv", [128, 1536], F32), ("w_o", [D, D], F32), ("nfg", [128, 8], F32),
    ("w_gate", [D, DFF], F32), ("w_up", [D, DFF], F32), ("convw", [128, NFC, 3], F32), ("convb", [128, NFC], F32),
    ("w_down", [DFF, D], F32), ("fng", [D], F32),
]


def build(nseq=4, stop_after=None, dbg=()):
    nc = bass.Bass("TRN2", target_bir_lowering=False)
    I = {}
    for name, shape, dt in IN_SPECS:
        if name == "x":
            shape = [nseq, S, D]
        elif name == "cT":
            shape = [128, 8, nseq]
        elif name == "pos":
            shape = [nseq, S]
        I[name] = nc.dram_tensor(name, list(shape), dt, kind="ExternalInput").ap()
    consts = make_consts()
    for name, arr in consts.items():
        I[name] = nc.dram_tensor(name, list(arr.shape), CONST_DT[name], kind="ExternalInput").ap()
    out = nc.dram_tensor("out", [nseq, S, D], F32, kind="ExternalOutput").ap()
    DBG = {}
    for name, shape, dt in dbg:
        DBG[name] = nc.dram_tensor("dbg_" + name, list(shape), dt, kind="ExternalOutput").ap()
    veD = nc.dram_tensor("veD", [16, 1152], F32, kind="Internal")
    vP = nc.dram_tensor("vP", [16, 128, 384], F32, kind="Internal")
    gD = nc.dram_tensor("gD", [nseq, 2, D], F32, kind="Internal")
    x1D = nc.dram_tensor("x1D", [nseq, S, D], F32, kind="Internal")

    k = KB(nc)
    finals = []
    with k.es:
        PS = [k.ps(f"ps{i}", [128, 512], F32) for i in range(8)]
        S_B = PS[0:2]
        O_B = PS[2:6]
        M_B = PS[6:8]

        def bf_view(bank):
            return bank[:].bitcast(BF16)

        def gconst(name, shape, dt, q="sp"):
            t = k.sb("c_" + name, shape, dt)
            k.dma(q, t[:], I[name])
            return t
        ident_bf = gconst("ident_bf", [128, 128], BF16)
        ident_f = gconst("ident_f", [128, 128], F32)
        tri = gconst("tri", [128, 128], BF16)
        anti = gconst("anti", [128, 128], BF16)
        ach = gconst("ach", [128, 4, 127], BF16)
        m1 = gconst("m1", [128, 16, 32], F32)
        m2 = gconst("m2", [128, 16, 32], F32)
        vcconst = gconst("vcconst", [128, 33], BF16)
        invf = gconst("invf", [128, 1], F32)
        sgn = gconst("sgn", [128, 1], F32)
        ada_bT = gconst("ada_bT", [128, 48], F32)
        nmg = gconst("nmg", [128, 8], F32)
        nfg = gconst("nfg", [128, 8], F32)
        qg = gconst("qg", [128, 2], F32)
        kvg = gconst("kvg", [128, 1], F32)
        convw = gconst("convw", [128, NFC, 3], F32)
        convb = gconst("convb", [128, NFC], F32)
        posk = gconst("posk", [128, 16], F32)
        posv = gconst("posv", [128, 16], F32)
        fng_b = k.sb("fng_b", [128, D], F32)
        k.dma("sp", fng_b[:], I["fng"].partition_broadcast(128))
        ones_bf = k.sb("ones_bf", [128, 128], BF16)
        k.memset(ones_bf[:], 1.0)
        w2k = k.sb("w2k", [128, 2, 64], BF16)
        w2v = k.sb("w2v", [128, 2, 64], BF16)
        k.dma("pool", w2k[:], I["w2k"].rearrange("(c p) n -> p c n", p=128))
        k.dma("pool", w2v[:], I["w2v"].rearrange("(c p) n -> p c n", p=128))
        modFM = k.sb("modFM", [128, 48, nseq], F32)
        sc1 = k.sb("sc1", [128, 8, nseq], F32)
        sc2 = k.sb("sc2", [128, 8, nseq], F32)
        posb = k.sb("posb", [128, 2, 2], F32)
        WB = [k.sb(f"wb{i}", [128, 8192], BF16) for i in range(2)]
        wb_i = [0]
        negpi = k.sb("negpi", [128, 1], F32)
        k.memset(negpi[:], -math.pi)

        def wbuf():
            t = WB[wb_i[0] % 2]
            wb_i[0] += 1
            return t

        def wview(t, kc, n, off=0):
            return t[:, off:off + kc * n].rearrange("p (k n) -> p k n", k=kc)

        with ExitStack() as pes:
            cT_sb = k.sb("cT_sb", [128, 8, nseq], F32, pes)
            cond = k.sb("cond", [128, 8, nseq], BF16, pes)
            k.dma("sp", cT_sb[:], I["cT"])
            k.act(cond[:], cT_sb[:], AF.Silu)
            adav = I["ada_w"].rearrange("(kc p) n -> p kc n", p=128)
            mbank = M_B[0]
            first = True
            for jt in range(12):
                wt = wbuf()
                wv = wview(wt, 8, 512)
                k.dma("pool", wv, adav[:, :, jt * 512:(jt + 1) * 512])
                for jj in range(4):
                    j = jt * 4 + jj
                    for kc in range(8):
                        k.mm(mbank[:, j * nseq:(j + 1) * nseq], wv[:, kc, jj * 128:(jj + 1) * 128],
                             cond[:, kc, :], start=first, stop=False)
                        first = False
            k.tt(modFM[:], mbank[:, 0:48 * nseq].rearrange("p (j b) -> p j b", b=nseq),
                 ada_bT[:].unsqueeze(2).to_broadcast([128, 48, nseq]), ALU.add)
            k.stt(sc1[:], modFM[:, 8:16, :], 1.0, nmg[:].unsqueeze(2).to_broadcast([128, 8, nseq]),
                  ALU.add, ALU.mult)
            k.stt(sc2[:], modFM[:, 32:40, :], 1.0, nfg[:].unsqueeze(2).to_broadcast([128, 8, nseq]),
                  ALU.add, ALU.mult)
            gts = k.sb("gts", [8, 2 * nseq, 128], F32, pes)
            for b in range(nseq):
                gbank = (M_B[1], S_B[0])[b % 2]
                for gi, j0 in enumerate((16, 40)):
                    k.tr(gbank[0:8, gi * 128:(gi + 1) * 128], modFM[:, j0:j0 + 8, b], ident_f[:])
                k.cp(gts[:, 2 * b:2 * b + 2, :], gbank[0:8, 0:256].rearrange("p (a n) -> p a n", n=128))
            for b in range(nseq):
                for gi in range(2):
                    k.dma("sp", gD.ap()[b, gi, :].rearrange("(j p) -> j p", p=128), gts[:, b * 2 + gi, :])
            tab = k.sb("tab", [33, 16], F32, pes)
            t31 = k.sb("t31", [32, 16], F32, pes)
            ohe = k.sb("ohe", [33, 1152], F32, pes)
            ve_sb = k.sb("ve_sb", [16, 1152], F32, pes)
            k.dma("sp", tab[0:32, :], I["rel"])
            k.dma("sp", t31[:], I["rel"][31, :].partition_broadcast(32))
            k.dma("sp", ohe[:], I["ohe"])
            k.tt(tab[0:32, :], tab[0:32, :], t31[:], ALU.subtract)
            k.ts(tab[0:32, :], tab[0:32, :], 8.0, None, ALU.mult)
            k.memset(tab[32:33, :], NEG)
            for i in range(3):
                k.mm(mbank[0:16, 0:384], tab[:], ohe[:, i * 384:(i + 1) * 384])
                k.cp(ve_sb[:, i * 384:(i + 1) * 384], mbank[0:16, 0:384])
            k.dma("sp", veD.ap(), ve_sb[:])
            k.dma("sp", vP.ap(), ve_sb[:, 384:768].unsqueeze(1).to_broadcast([16, 128, 384]))
            w1f = k.sb("w1f", [128, 16, 256], F32, pes)
            for kv, (wn, pt) in enumerate((("w1k", posk), ("w1v", posv))):
                k.dma("sp", w1f[:], I[wn].rearrange("(c p) n -> p c n", p=128))
                for hc in range(2):
                    for lp in range(16):
                        k.mm(mbank[:, 256 + kv * 2 + hc:256 + kv * 2 + hc + 1], w1f[:, lp, hc * 128:(hc + 1) * 128],
                             pt[:, lp:lp + 1], start=(lp == 0), stop=(lp == 15))
            k.cp(posb[:], mbank[:, 256:260].rearrange("p (a b) -> p a b", b=2))
            if "modFM" in DBG:
                finals.append(k.dma("sp", DBG["modFM"], modFM[:]))
            if "ve" in DBG:
                finals.append(k.dma("sp", DBG["ve"], ve_sb[:]))
            if "posb" in DBG:
                finals.append(k.dma("sp", DBG["posb"], posb[:]))
            k.barrier()

        if stop_after == "prologue":
            k.finish(finals)
            return nc, k

        xv = I["x"]
        for b in range(nseq):
            with ExitStack() as ses, ExitStack() as ses2:
                gm_b = k.sb("gm_b", [128, D], F32, ses)
                gf_b = k.sb("gf_b", [128, D], F32, ses)
                ss4 = k.sb("ss4", [128, 8], F32, ses)
                k.dma("sp", gm_b[:], gD.ap()[b, 0, :].partition_broadcast(128))
                k.dma("sp", gf_b[:], gD.ap()[b, 1, :].partition_broadcast(128))
                hT = k.sb("hT", [128, 8, S], BF16, ses)
                ons = k.sb("ons", [128, NQT, D], BF16, ses2)

                def norm_to_T(src_tiles_fn, dstT, scv, shv, es_):
                    xts = [k.sb(f"xt{i}", [128, D], F32, es_) for i in range(8)]
                    xns_all = [k.sb(f"xn{i}", [128, D], BF16, es_) for i in range(8)]
                    junk = k.sb("junk", [128, D], BF16, es_)
                    ss_all = k.sb("ss", [128, 16], F32, es_)
                    for ch in range(NCH):
                        xns = xns_all[(ch % 2) * 4:(ch % 2) * 4 + 4]
                        ss = ss_all[:, ch * 4:ch * 4 + 4]
                        xtl = []
                        for qi in range(4):
                            qt = ch * 4 + qi
                            xt = src_tiles_fn(qt, xts[qt % 8])
                            xtl.append(xt)
                            k.act(junk[:], xt, AF.Square, accum_out=ss[:, qi:qi + 1])
                        k.act(ss, ss, AF.Sqrt, bias=1e-6, scale=1.0 / D)
                        k.recip(ss, ss)
                        for qi in range(4):
                            k.ts(xns[qi][:], xtl[qi], ss[:, qi:qi + 1], None, ALU.mult)
                        for kp in range(4):
                            bank = M_B[kp % 2]
                            bv = bf_view(bank)
                            for kk in range(2):
                                kc = kp * 2 + kk
                                for qi in range(4):
                                    k.tr(bv[:, (kk * 4 + qi) * 128:(kk * 4 + qi + 1) * 128],
                                         xns[qi][:, kc * 128:(kc + 1) * 128], ident_bf[:])
                            for kk in range(2):
                                kc = kp * 2 + kk
                                k.act(dstT[:, kc, ch * 512:(ch + 1) * 512], bv[:, kk * 512:(kk + 1) * 512],
                                      AF.Identity, bias=shv(kc), scale=scv(kc))

                with ExitStack() as e1:
                    def load_x(qt, buf):
                        k.dma("sp", buf[:], xv[b, qt * 128:(qt + 1) * 128, :])
                        return buf[:]
                    norm_to_T(load_x, hT, lambda kc: sc1[:, kc, b:b + 1], lambda kc: modFM[:, kc, b:b + 1], e1)
                    k.barrier()
                if "hT" in DBG and b == 0:
                    finals.append(k.dma("sp", DBG["hT"], hT[:]))
                if stop_after == "s1":
                    continue
                STAGES(k, nc, I, DBG, finals, b, locals())
        k.finish(finals)
    return nc, k


class AttnStream:
    def __init__(self, k, S_B, PTs, look=1):
        self.k, self.S_B, self.PTs, self.look = k, S_B, PTs, look
        self.si = 0
        self.pi = 0
        self.pend = []

    def add(self, s_fn, npart, n, scale, pv_fn, after=None):
        k = self.k
        Sb = self.S_B[self.si % len(self.S_B)]
        self.si += 1
        s_fn(Sb)
        pt = self.PTs[self.pi % len(self.PTs)]
        self.pi += 1
        k.act(pt[0:npart, 0:n], Sb[0:npart, 0:n], AF.Exp, scale=scale)
        self.pend.append([pv_fn, pt, after, []])
        while len(self.pend) > self.look:
            self._pop()

    def defer(self, fn):
        if self.pend:
            self.pend[-1][3].append(fn)
        else:
            fn()

    def _pop(self):
        pv_fn, pt, after, deferred = self.pend.pop(0)
        pv_fn(pt)
        if after is not None:
            after()
        for fn in deferred:
            fn()

    def flush(self):
        while self.pend:
            self._pop()


def STAGES(k, nc, I, DBG, finals, b, L):
    hT, ons = L["hT"], L["ons"]
    stop_after = L["stop_after"]
    nsa_stage(k, nc, I, DBG, finals, b, L)
    if stop_after == "nsa":
        return
    mla_stage(k, nc, I, DBG, finals, b, L)
    if stop_after == "mla":
        return
    out_stage(k, nc, I, DBG, finals, b, L)


def nsa_stage(k, nc, I, DBG, finals, b, L):
    hT, ons, wbuf, wview = L["hT"], L["ons"], L["wbuf"], L["wview"]
    S_B, O_B, M_B = L["S_B"], L["O_B"], L["M_B"]
    ident_bf, tri, anti, ach, m1, m2, vcconst = (L[n] for n in ("ident_bf", "tri", "anti", "ach", "m1", "m2", "vcconst"))
    posb, w2k, w2v = L["posb"], L["w2k"], L["w2v"]
    veD, vP = L["veD"], L["vP"]
    w_inv = I["w_in"].rearrange("(kc p) n -> p kc n", p=128)
    rr = {"s": 0, "o": 0, "m": 0, "pt": 0, "ev": 0}

    def nxt(key, lst):
        v = lst[rr[key] % len(lst)]
        rr[key] += 1
        return v

    def evac(out, in_):
        rr["ev"] += 1
        if rr["ev"] % 2:
            k.cp(out, in_, eng="act")
        else:
            k.cp(out, in_, eng="dve")

    with ExitStack() as e2:
        w1s = [k.sb(f"w1s{i}", [128, 16, 256], BF16, e2) for i in range(2)]
        k.dma("pool", w1s[0][:], I["w1k"].rearrange("(c p) n -> p c n", p=128))
        k.dma("pool", w1s[1][:], I["w1v"].rearrange("(c p) n -> p c n", p=128))
        Qaug = k.sb("Qaug", [128, 4, S], BF16, e2)
        KSaug = k.sb("KSaug", [128, S], BF16, e2)
        KWt = k.sb("KWt", [128, S], BF16, e2)
        KKs = [k.sb(f"KK{i}", [128, S], BF16, e2) for i in range(2)]
        VS = k.sb("VS", [128, NQT, 65], BF16, e2)
        VW = k.sb("VW", [128, NQT, 65], BF16, e2)
        VCaug = k.sb("VCaug", [128, 97], BF16, e2)
        sig = k.sb("sig", [128, NQT, 12], F32, e2)
        hid = [k.sb(f"hid{i}", [128, 2, 127], BF16, e2) for i in range(2)]
        kcT = k.sb("kcT", [128, 127], BF16, e2)
        Tns = [k.sb(f"Tn{i}", [128, 4, 256], BF16, e2) for i in range(2)]
        Wcs = [k.sb(f"Wc{i}", [128, 4, 512], BF16, e2) for i in range(2)]
        PTs = [k.sb(f"pt{i}", [128, 512], BF16, e2) for i in range(4)]
        accns = [k.sb(f"accn{i}", [128, 4, 4, 64], F32, e2) for i in range(2)]
        sc = k.sb("sc", [128, 4, 32], F32, e2)
        s2 = k.sb("s2", [128, 4, 32], F32, e2)
        cmpm = k.sb("cmpm", [128, 32, 32], BF16, e2)
        cmpm2 = k.sb("cmpm2", [128, 32, 32], BF16, e2)
        rank = k.sb("rank", [128, 4, 32], F32, e2)
        negpad = k.sb("negpad", [128, 4, 96], BF16, e2)
        rd = k.sb("rd", [128, 8, 4], F32, e2)
        ff = k.sb("ff", [128, 8, 4], F32, e2)
        tmp64 = k.sb("tmp64", [128, 4, 64], F32, e2)
        tmp32 = k.sb("tmp32", [128, 4, 32], F32, e2)

        k.memset(Qaug[64:128, :, :], 0.0, eng="pool")
        k.memset(KSaug[64:128, :], 0.0, eng="pool")
        k.memset(KWt[64:128, :], 0.0, eng="pool")
        k.memset(kcT[64:128, :], 0.0)
        for Wc_ in Wcs:
            k.memset(Wc_[:], 0.0, eng="pool")
            k.memset(Wc_[0:1, :, :], NEG)
        k.dma("sp", KSaug[64:96, :], I["blockind"])
        k.memset(VS[:, :, 64:65], 1.0)
        k.memset(VW[:, :, 64:65], 1.0)
        k.memset(VCaug[:], 0.0)
        k.cp(VCaug[:, 64:97], vcconst[:])
        k.memset(negpad[:], 0.0)

        rdi = [0]

        def norm_f(Ob, width, gate_ap):
            i = rdi[0] % 8
            rdi[0] += 1
            den = Ob[:, 0:4 * width].rearrange("p (q c) -> p q c", c=width)[:, :, 64]
            k.ts(rd[:, i, :], den, 1e-30, None, ALU.max)
            k.recip(rd[:, i, :], rd[:, i, :])
            if gate_ap is None:
                return rd[:, i, :]
            k.tt(ff[:, i, :], rd[:, i, :], gate_ap, ALU.mult)
            return ff[:, i, :]

        def load_group(g):
            wt = wbuf()
            wv = wview(wt, 8, 780)
            segs = [(0, 256, C_Q + 256 * g), (256, 64, C_KS + 64 * g), (320, 64, C_KW + 64 * g),
                    (384, 64, C_KC + 64 * g), (448, 64, C_KC + 64 * g), (512, 64, C_VC + 64 * g),
                    (576, 64, C_VC + 64 * g), (640, 64, C_VS + 64 * g), (704, 64, C_VW + 64 * g),
                    (768, 12, C_G + 12 * g)]
            for (o, n, c0) in segs:
                k.dma("pool", wv[:, :, o:o + n], w_inv[:, :, c0:c0 + n])
            for hl in range(4):
                h = 4 * g + hl
                k.dma("pool", Tns[g % 2][:, hl, :], bass.AP(vP, h * 128 * 384 + 127, [[383, 128], [1, 256]]))
                k.dma("pool", Wcs[g % 2][1:41, hl, :], bass.AP(veD, h * 1152, [[16, 40], [1, 512]]))
            return wv

        wv_next = load_group(0)
        for g in range(4):
            wv = wv_next
            Tn = Tns[g % 2]
            Wc = Wcs[g % 2]
            for ch in range(NCH):
                cs = slice(ch * 512, (ch + 1) * 512)
                for hl in range(4):
                    bank = nxt("m", M_B)
                    for kc in range(8):
                        k.mm(bank[0:64, :], wv[:, kc, hl * 64:(hl + 1) * 64], hT[:, kc, cs],
                             start=(kc == 0), stop=(kc == 7))
                    evac(Qaug[0:64, hl, cs], bank[0:64, :])
                for (o, dst) in ((256, KSaug), (320, KWt)):
                    bank = nxt("m", M_B)
                    for kc in range(8):
                        k.mm(bank[0:64, :], wv[:, kc, o:o + 64], hT[:, kc, cs], start=(kc == 0), stop=(kc == 7))
                    evac(dst[0:64, cs], bank[0:64, :])
                for (o, dst) in ((384, KKs[0]), (512, KKs[1])):
                    bank = nxt("m", M_B)
                    for kc in range(8):
                        k.mm(bank[:, :], wv[:, kc, o:o + 128], hT[:, kc, cs], start=(kc == 0), stop=(kc == 7))
                    evac(dst[0:64, cs], bank[0:64, :])
                    if ch == 0:
                        evac(dst[64:128, 0:511], bank[64:128, 1:512])
                    else:
                        evac(dst[64:128, ch * 512 - 1:(ch + 1) * 512 - 1], bank[64:128, :])
            for qt in range(NQT):
                bank = nxt("m", M_B)
                for kc in range(8):
                    k.mm(bank[:, 0:140], hT[:, kc, qt * 128:(qt + 1) * 128], wv[:, kc, 640:780],
                         start=(kc == 0), stop=(kc == 7))
                k.cp(VS[:, qt, 0:64], bank[:, 0:64], eng="dve")
                k.cp(VW[:, qt, 0:64], bank[:, 64:128], eng="dve")
                k.act(sig[:, qt, :], bank[:, 128:140], AF.Tanh, scale=0.5)
            k.ts(sig[:], sig[:], 0.5, 0.5, ALU.mult, ALU.add)
            if g + 1 < 4:
                wv_next = load_group(g + 1)
            for kv in range(2):
                src = KKs[kv]
                for hc in range(2):
                    bank = nxt("m", M_B)
                    for lp in range(16):
                        k.mm(bank[:, 0:127], w1s[kv][:, lp, hc * 128:(hc + 1) * 128],
                             src[:, 2 * lp:2 * lp + 16 * 126 + 1:16], start=(lp == 0), stop=(lp == 15))
                    k.act(hid[kv][:, hc, :], bank[:, 0:127], AF.Gelu_apprx_tanh, bias=posb[:, kv, hc:hc + 1])
            bank = nxt("m", M_B)
            for hc in range(2):
                k.mm(bank[0:64, 0:127], w2k[:, hc, :], hid[0][:, hc, :], start=(hc == 0), stop=(hc == 1))
            k.cp(kcT[0:64, :], bank[0:64, 0:127], eng="dve")
            bank = nxt("m", M_B)
            for hc in range(2):
                k.mm(bank[0:127, 0:64], hid[1][:, hc, :], w2v[:, hc, :], start=(hc == 0), stop=(hc == 1))
            k.cp(VCaug[0:127, 0:64], bank[0:127, 0:64], eng="dve")
            if b == 0 and g == 0 and "kcT" in DBG:
                finals.append(k.dma("sp", DBG["kcT"], kcT[0:64, :]))
                finals.append(k.dma("sp", DBG["vc"], VCaug[:]))
            st = AttnStream(k, S_B + [M_B[1]], PTs, look=2)

            def do_chunk(ch, part, g=g, Tn=Tn, Wc=Wc):
                cs = slice(ch * 512, (ch + 1) * 512)
                q0 = 4 * ch
                accn = accns[ch % 2]

                def cmp_item(hl):
                    Ob = nxt("o", O_B)

                    def s_fn(Sb):
                        k.mm(Sb[0:127, :], kcT[:, :], Qaug[:, hl, cs], start=True, stop=False)
                        k.mm(Sb[0:127, :], ach[:, ch, :], Wc[:, hl, :], start=False, stop=True)

                    def pv_fn(pt):
                        for qi in range(4):
                            k.mm(Ob[:, qi * 97:(qi + 1) * 97], pt[0:127, qi * 128:(qi + 1) * 128], VCaug[0:127, :],
                                 start=(qi == 0), stop=False)

                    def after():
                        Ov = Ob[:, 0:388].rearrange("p (q c) -> p q c", c=97)
                        r1 = norm_f(Ob, 97, None)
                        if hl == 0:
                            k.tt(sc[:], Ov[:, :, 65:97], r1.unsqueeze(2).to_broadcast([128, 4, 32]), ALU.mult)
                        else:
                            k.tt(tmp32[:], Ov[:, :, 65:97], r1.unsqueeze(2).to_broadcast([128, 4, 32]), ALU.mult)
                            k.tt(sc[:], sc[:], tmp32[:], ALU.add)
                        i_ = rdi[0] % 8
                        rdi[0] += 1
                        k.tt(ff[:, i_, :], r1, sig[:, q0:q0 + 4, 3 * hl], ALU.mult)
                        k.tt(accn[:, :, hl, :], Ov[:, :, 0:64], ff[:, i_, :].unsqueeze(2).to_broadcast([128, 4, 64]), ALU.mult)
                    st.add(s_fn, 127, 512, 0.125, pv_fn, after)
                if part == "cmp":
                    for hl in range(4):
                        cmp_item(hl)

                def topk(q0=q0, ch=ch):
                    k.tt(s2[:], sc[:], m1[:, q0:q0 + 4, :], ALU.mult)
                    k.tt(s2[:], s2[:], m2[:, q0:q0 + 4, :], ALU.add)
                    for qi in range(4):
                        en = "dve"
                        cm = cmpm2 if qi % 2 else cmpm
                        k.tt(cm[:], s2[:, qi, :].unsqueeze(1).to_broadcast([128, 32, 32]),
                             s2[:, qi, :].unsqueeze(2).to_broadcast([128, 32, 32]), ALU.is_gt, eng=en)
                        k.red(rank[:, qi, :], cm[:], ALU.add)
                    k.ts(negpad[:, :, 64:96], rank[:], 15.5, NEG, ALU.is_gt, ALU.mult)
                    if b == 0 and g == 0 and ch == 3 and "rank" in DBG:
                        finals.append(k.dma("sp", DBG["rank"], rank[:]))
                        finals.append(k.dma("sp", DBG["sc"], sc[:]))
                if part == "cmp":
                    st.defer(topk)
                    return

                def br_items(hl, br):
                    Ob = nxt("o", O_B)
                    Ov = Ob[:, 0:260].rearrange("p (q c) -> p q c", c=65)
                    kts = list(range(max(0, q0 - 4), q0 + 4)) if br == 0 else list(range(0, q0 + 4))
                    state = {"first": True}
                    for kt in kts:
                        qlo = max(kt, q0)
                        qhi = min(kt + 4, q0 + 3) if br == 0 else q0 + 3
                        n = 128 * (qhi - qlo + 1)
                        ks = slice(kt * 128, (kt + 1) * 128)
                        qs = slice(qlo * 128, (qhi + 1) * 128)

                        def s_fn(Sb, kt=kt, qlo=qlo, qhi=qhi, n=n, ks=ks, qs=qs):
                            if br == 0:
                                k.mm(Sb[:, 0:n], KWt[:, ks], Qaug[:, hl, qs], start=True, stop=False)
                            else:
                                k.mm(Sb[:, 0:n], KSaug[:, ks], Qaug[:, hl, qs], start=True, stop=False)
                            nlo = max(kt, qlo)
                            nhi = min(kt + 1, qhi)
                            if nlo <= nhi:
                                o = (nlo - qlo) * 128
                                w_ = (nhi - nlo + 1) * 128
                                k.mm(Sb[:, o:o + w_], ident_bf[:], Tn[:, hl, (nlo - kt) * 128:(nhi - kt + 1) * 128],
                                     start=False, stop=False)
                            if br == 0 and qlo <= kt + 4 <= qhi:
                                o = (kt + 4 - qlo) * 128
                                k.mm(Sb[:, o:o + 128], ident_bf[:], anti[:], start=False, stop=False)

                        def pv_fn(pt, kt=kt, qlo=qlo, qhi=qhi):
                            V = VW if br == 0 else VS
                            for qt in range(qlo, qhi + 1):
                                o = (qt - qlo) * 128
                                k.mm(Ob[:, (qt - q0) * 65:(qt - q0 + 1) * 65], pt[:, o:o + 128], V[:, kt, :],
                                     start=state["first"], stop=False)
                                state["first"] = False

                        after = None
                        if kt == kts[-1]:
                            def after():
                                f = norm_f(Ob, 65, sig[:, q0:q0 + 4, 3 * hl + (2 if br == 0 else 1)])
                                k.tt(tmp64[:], Ov[:, :, 0:64], f.unsqueeze(2).to_broadcast([128, 4, 64]), ALU.mult)
                                k.tt(accn[:, :, hl, :], accn[:, :, hl, :], tmp64[:], ALU.add)
                        st.add(s_fn, 128, n, 0.125, pv_fn, after)
                def selmask(cs=cs):
                    bank = M_B[0]
                    for qi in range(4):
                        k.mm(bank[0:96, qi * 128:(qi + 1) * 128], negpad[:, qi, :], ident_bf[:],
                             start=(qi == 0), stop=False)
                    for hl in range(4):
                        evac(Qaug[64:96, hl, cs], bank[64:96, :])
                if part == "win":
                    for hl in range(4):
                        br_items(hl, 0)
                    st.defer(selmask)
                    return
                for hl in range(4):
                    br_items(hl, 1)

                def write_ons(q0=q0):
                    k.cp(ons[:, q0:q0 + 4, 256 * g:256 * (g + 1)], accn[:].rearrange("p q h d -> p q (h d)"), eng="dve")
                st.defer(write_ons)
            do_chunk(0, "cmp")
            for ch in range(NCH):
                do_chunk(ch, "win")
                if ch + 1 < NCH:
                    do_chunk(ch + 1, "cmp")
                else:
                    st.flush()
                do_chunk(ch, "sel")
            st.flush()
        if b == 0 and "ons" in DBG:
            finals.append(k.dma("sp", DBG["ons"], ons[:]))
        k.barrier()
    print("sbuf remaining after nsa scope", nc.sbuf_bytes_remaining)


def mla_stage(k, nc, I, DBG, finals, b, L):
    hT, ons, wbuf, wview = L["hT"], L["ons"], L["wbuf"], L["wview"]
    S_B, O_B, M_B = L["S_B"], L["O_B"], L["M_B"]
    ident_bf, tri, ones_bf, invf, sgn, qg, kvg = (L[n] for n in ("ident_bf", "tri", "ones_bf", "invf", "sgn", "qg", "kvg"))
    w_inv = I["w_in"].rearrange("(kc p) n -> p kc n", p=128)
    rr = {"s": 0, "o": 0, "m": 0, "pt": 0, "ev": 0}
    PI = math.pi

    def nxt(key, lst):
        v = lst[rr[key] % len(lst)]
        rr[key] += 1
        return v

    def evac(out, in_):
        rr["ev"] += 1
        if rr["ev"] % 2:
            k.cp(out, in_, eng="act")
        else:
            k.cp(out, in_, eng="dve")

    with ExitStack() as e3:
        cqn = k.sb("cqn", [128, 2, S], BF16, e3)
        ckvn = k.sb("ckvn", [128, S], BF16, e3)
        COS2 = k.sb("COS2", [96, S], F32, e3)
        SIN2 = k.sb("SIN2", [96, S], F32, e3)
        krr = k.sb("krr", [96, S], BF16, e3)
        QmT = [k.sb(f"QmT{i}", [128, S], BF16, e3) for i in range(2)]
        KmT = [k.sb(f"KmT{i}", [128, S], BF16, e3) for i in range(2)]
        for i in range(2):
            k.memset(QmT[i][64:128, :], 0.0, eng="pool")
            k.memset(KmT[i][64:128, :], 0.0, eng="pool")
        Vh = [k.sb(f"Vh{i}", [128, NQT, 129], BF16, e3) for i in range(2)]
        sbh = [k.sb(f"sbh{i}", [128, NQT, 128], BF16, e3) for i in range(2)]
        wmb = [k.sb(f"wmb{i}", [128, 8, 128], BF16, e3) for i in range(2)]
        PTs = [k.sb(f"mpt{i}", [128, 512], BF16, e3) for i in range(3)]
        raw = [k.sb(f"raw{i}", [128, 512], F32, e3) for i in range(3)]
        sq = [k.sb(f"sq{i}", [128, 512], BF16, e3) for i in range(3)]
        rstd = k.sb("rstd", [128, 512], F32, e3)
        posi = k.sb("posi", [96, 512], I32, e3)
        ang = raw[0]
        targ = raw[1]
        t1 = k.sb("t1", [96, 512], F32, e3)
        t2 = k.sb("t2", [96, 512], F32, e3)
        satmps = [sq[0], sq[1]]
        otmp = k.sb("otmp", [128, 2, 128], F32, e3)
        rd = k.sb("mrd", [128, 8, 2], F32, e3)
        for i in range(2):
            k.memset(Vh[i][:, :, 128:129], 1.0)

        R = slice(64, 96)
        for ch in range(NCH):
            cs = slice(ch * 512, (ch + 1) * 512)
            k.dma("sp", posi[:], I["pos"][b, cs].partition_broadcast(96))
            k.cp(ang[R, :], posi[R, :])
            k.ts(ang[R, :], ang[R, :], invf[R, 0:1], None, ALU.mult)
            k.ts(targ[R, :], ang[R, :], 1.0 / (2 * PI), None, ALU.mult)
            k.cp(posi[R, :], targ[R, :])
            k.cp(targ[R, :], posi[R, :])
            k.stt(ang[R, :], targ[R, :], -2 * PI, ang[R, :], ALU.mult, ALU.add)
            k.ts(targ[R, :], ang[R, :], PI, -2 * PI, ALU.is_gt, ALU.mult)
            k.tt(ang[R, :], ang[R, :], targ[R, :], ALU.add)
            k.act(SIN2[R, cs], ang[R, :], AF.Sin)
            k.ts(SIN2[R, cs], SIN2[R, cs], sgn[R, 0:1], None, ALU.mult)
            k.ts(ang[R, :], ang[R, :], 0.5 * PI, None, ALU.add)
            k.ts(targ[R, :], ang[R, :], PI, -2 * PI, ALU.is_gt, ALU.mult)
            k.tt(ang[R, :], ang[R, :], targ[R, :], ALU.add)
            k.act(COS2[R, cs], ang[R, :], AF.Sin)

        for j in range(2):
            wt = wbuf()
            wv = wview(wt, 8, 512)
            k.dma("pool", wv, w_inv[:, :, C_MA + 512 * j:C_MA + 512 * (j + 1)])
            for qt in range(NQT):
                bank = nxt("m", M_B)
                for kc in range(8):
                    k.mm(bank[:, :], hT[:, kc, qt * 128:(qt + 1) * 128], wv[:, kc, :], start=(kc == 0), stop=(kc == 7))
                satmp = satmps[qt % 2]
                k.act(satmp[:], bank[:, :], AF.Tanh, scale=0.5)
                k.ts(satmp[:], satmp[:], 0.5, 0.5, ALU.mult, ALU.add)
                k.tt(ons[:, qt, 512 * j:512 * (j + 1)], ons[:, qt, 512 * j:512 * (j + 1)], satmp[:], ALU.mult)

        wt = wbuf()
        wv = wview(wt, 8, 576)
        k.memset(wv[:, :, 384:576], 0.0)
        k.dma("pool", wv[:, :, 0:256], w_inv[:, :, C_CQ:C_CQ + 256])
        k.dma("pool", wv[:, :, 256:384], w_inv[:, :, C_CKV:C_CKV + 128])
        k.dma("pool", wv[:, :, 448:480], w_inv[:, :, C_KR:C_KR + 32])
        k.dma("pool", wv[:, :, 544:560], w_inv[:, :, C_KR + 16:C_KR + 32])
        k.dma("pool", wv[:, :, 560:576], w_inv[:, :, C_KR:C_KR + 16])
        for ch in range(NCH):
            cs = slice(ch * 512, (ch + 1) * 512)
            for m in range(3):
                bank = nxt("m", M_B)
                for kc in range(8):
                    k.mm(bank[:, :], wv[:, kc, m * 128:(m + 1) * 128], hT[:, kc, cs], start=(kc == 0), stop=(kc == 7))
                k.act(sq[m][:], bank[:, :], AF.Square)
                k.cp(raw[m][:], bank[:, :], eng="dve")
            bank = nxt("m", M_B)
            k.mm(bank[:, :], ones_bf[:], sq[0][:], start=True, stop=False)
            k.mm(bank[:, :], ones_bf[:], sq[1][:], start=False, stop=True)
            k.act(rstd[:], bank[:, :], AF.Sqrt, bias=1e-6, scale=1.0 / 256)
            k.recip(rstd[:], rstd[:])
            for m in range(2):
                k.stt(cqn[:, m, cs], raw[m][:], qg[:, m:m + 1], rstd[:], ALU.mult, ALU.mult)
            bank = nxt("m", M_B)
            k.mm(bank[:, :], ones_bf[:], sq[2][:], start=True, stop=True)
            k.act(rstd[:], bank[:, :], AF.Sqrt, bias=1e-6, scale=1.0 / 128)
            k.recip(rstd[:], rstd[:])
            k.stt(ckvn[:, cs], raw[2][:], kvg[:, 0:1], rstd[:], ALU.mult, ALU.mult)
            bank = nxt("m", M_B)
            for kc in range(8):
                k.mm(bank[0:96, :], wv[:, kc, 384:480], hT[:, kc, cs], start=(kc == 0), stop=(kc == 7))
            k.tt(t1[R, :], bank[R, :], COS2[R, cs], ALU.mult)
            bank = nxt("m", M_B)
            for kc in range(8):
                k.mm(bank[0:96, :], wv[:, kc, 480:576], hT[:, kc, cs], start=(kc == 0), stop=(kc == 7))
            k.tt(t2[R, :], bank[R, :], SIN2[R, cs], ALU.mult)
            k.tt(krr[R, cs], t1[R, :], t2[R, :], ALU.add)

        wt = wbuf()
        wuq = wview(wt, 2, 768, 0)
        wus = wview(wt, 2, 768, 1536)
        wkv = wt[:, 3072:3072 + 1536]
        k.dma("pool", wuq, I["w_uq"].rearrange("(c p) n -> p c n", p=128))
        k.dma("pool", wus, I["w_uq_sw"].rearrange("(c p) n -> p c n", p=128))
        k.dma("pool", wkv, I["w_ukv"])
        scale = 96 ** -0.5
        def proj(h):
            Q, Kt, V, sb_, wm = QmT[h % 2], KmT[h % 2], Vh[h % 2], sbh[h % 2], wmb[h % 2]
            k.dma("pool", wm[:], w_inv[:, :, C_MB + 128 * h:C_MB + 128 * (h + 1)])
            k.cp(Kt[R, :], krr[R, :], eng="dve")
            for ch in range(NCH):
                cs = slice(ch * 512, (ch + 1) * 512)
                bankA = nxt("m", M_B)
                for kc in range(2):
                    k.mm(bankA[0:96, :], wuq[:, kc, h * 96:(h + 1) * 96], cqn[:, kc, cs], start=(kc == 0), stop=(kc == 1))
                k.cp(Q[0:64, cs], bankA[0:64, :], eng="dve")
                k.tt(t1[R, :], bankA[R, :], COS2[R, cs], ALU.mult)
                bankB = nxt("m", M_B)
                for kc in range(2):
                    k.mm(bankB[0:96, :], wus[:, kc, h * 96:(h + 1) * 96], cqn[:, kc, cs], start=(kc == 0), stop=(kc == 1))
                k.tt(t2[R, :], bankB[R, :], SIN2[R, cs], ALU.mult)
                k.tt(Q[R, cs], t1[R, :], t2[R, :], ALU.add)
                yield
                bank = nxt("m", M_B)
                k.mm(bank[0:64, :], wkv[:, h * 192:h * 192 + 64], ckvn[:, cs], start=True, stop=True)
                k.cp(Kt[0:64, cs], bank[0:64, :], eng="dve")
                yield
                bank = nxt("m", M_B)
                for qi in range(4):
                    qt = ch * 4 + qi
                    k.mm(bank[:, qi * 128:(qi + 1) * 128], ckvn[:, qt * 128:(qt + 1) * 128],
                         wkv[:, h * 192 + 64:h * 192 + 192], start=(qi == 0), stop=False)
                k.cp(V[:, ch * 4:ch * 4 + 4, 0:128], bank[:, :].rearrange("p (q c) -> p q c", c=128), eng="dve")
                yield
                bank = nxt("m", M_B)
                first = True
                for qi in range(4):
                    qt = ch * 4 + qi
                    for kc in range(8):
                        k.mm(bank[:, qi * 128:(qi + 1) * 128], hT[:, kc, qt * 128:(qt + 1) * 128], wm[:, kc, :],
                             start=first, stop=False)
                        first = False
                k.act(sb_[:, ch * 4:ch * 4 + 4, :], bank[:, :].rearrange("p (q c) -> p q c", c=128), AF.Tanh, scale=0.5)
                k.ts(sb_[:, ch * 4:ch * 4 + 4, :], sb_[:, ch * 4:ch * 4 + 4, :], 0.5, 0.5, ALU.mult, ALU.add)
                yield

        st = AttnStream(k, S_B, PTs, look=1)

        def attn(h, gen):
            Q, Kt, V, sb_ = QmT[h % 2], KmT[h % 2], Vh[h % 2], sbh[h % 2]
            cnt = [0]
            for ch in range(NCH):
                q0 = 4 * ch
                OA = O_B[(rr["o"] % 2) * 2]
                OBk = O_B[(rr["o"] % 2) * 2 + 1]
                rr["o"] += 1
                firsts = {0: True, 1: True}
                for kt in range(0, q0 + 4):
                    qlo = max(kt, q0)
                    qhi = q0 + 3
                    n = 128 * (qhi - qlo + 1)

                    def s_fn(Sb, kt=kt, qlo=qlo, qhi=qhi, n=n, q0=q0):
                        k.mm(Sb[:, 0:n], Kt[:, kt * 128:(kt + 1) * 128], Q[:, qlo * 128:(qhi + 1) * 128],
                             start=True, stop=False)
                        if kt >= q0:
                            k.mm(Sb[:, 0:128], ident_bf[:], tri[:], start=False, stop=False)

                    def pv_fn(pt, kt=kt, qlo=qlo, qhi=qhi, firsts=firsts, OA=OA, OBk=OBk, q0=q0):
                        for qt in range(qlo, qhi + 1):
                            o = (qt - qlo) * 128
                            bi = (qt - q0) // 2
                            sl = (qt - q0) % 2
                            bank = OA if bi == 0 else OBk
                            k.mm(bank[:, sl * 129:(sl + 1) * 129], pt[:, o:o + 128], V[:, kt, :],
                                 start=firsts[bi], stop=False)
                            firsts[bi] = False

                    after = None
                    if kt == q0 + 3:
                        def after(OA=OA, OBk=OBk, q0=q0):
                            for bi, bank in enumerate((OA, OBk)):
                                qa = q0 + 2 * bi
                                Ov = bank[:, 0:258].rearrange("p (q c) -> p q c", c=129)
                                i_ = rr["ev"] % 8
                                rr["ev"] += 1
                                k.ts(rd[:, i_, :], Ov[:, :, 128], 1e-30, None, ALU.max)
                                k.recip(rd[:, i_, :], rd[:, i_, :])
                                k.tt(otmp[:], Ov[:, :, 0:128], rd[:, i_, :].unsqueeze(2).to_broadcast([128, 2, 128]), ALU.mult)
                                k.tt(otmp[:], otmp[:], sb_[:, qa:qa + 2, :], ALU.mult)
                                k.tt(ons[:, qa:qa + 2, 128 * h:128 * (h + 1)], ons[:, qa:qa + 2, 128 * h:128 * (h + 1)],
                                     otmp[:], ALU.add)
                    st.add(s_fn, 128, n, scale, pv_fn, after)
                    cnt[0] += 1
                    if gen is not None and cnt[0] % 2 == 0:
                        next(gen, None)

        for _ in proj(0):
            pass
        for h in range(8):
            gen = proj(h + 1) if h + 1 < 8 else None
            attn(h, gen)
            st.flush()
            if gen is not None:
                for _ in gen:
                    pass
        if b == 0 and "y" in DBG:
            finals.append(k.dma("sp", DBG["y"], ons[:]))
        k.barrier()


def sgn_negpi(L, k):
    if "negpi" not in L["cache"]:
        t = k.sb("negpi", [128, 1], F32)
        k.memset(t[:], -math.pi)
        L["cache"]["negpi"] = t
    return L["cache"]["negpi"]


def out_stage(k, nc, I, DBG, finals, b, L):
    ons, wview, WB = L["ons"], L["wview"], L["WB"]
    PS = L["PS"]
    M_B = L["M_B"]
    ident_bf, sc2, modFM, fng_b, convw, convb = (L[n] for n in ("ident_bf", "sc2", "modFM", "fng_b", "convw", "convb"))
    gD, x1D, out = L["gD"], L["x1D"], L["out"]
    xv = I["x"]
    rr = {"b": 0, "ev": 0}

    def nb():
        v = PS[rr["b"] % 8]
        rr["b"] += 1
        return v

    def bfv(bank):
        return bank[:].bitcast(BF16)

    with ExitStack() as e4:
        h2T = L["hT"]
        gm_b, gf_b, ss = L["gm_b"], L["gf_b"], L["ss4"]
        with ExitStack() as e4a:
            yTs = [k.sb(f"yT{i}", [128, 8, 512], BF16, e4a) for i in range(2)]
            xts = [k.sb(f"x4t{i}", [128, D], F32, e4a) for i in range(4)]
            x1ts = [k.sb(f"x1t{i}", [128, D], F32, e4a) for i in range(4)]
            xns_all = [k.sb(f"x4n{i}", [128, D], BF16, e4a) for i in range(8)]
            junk = k.sb("junk4", [128, D], BF16, e4a)
            tmp = k.sb("tmp4", [128, 512], F32, e4a)
            wo = wview(WB[0], 8, 1024)
            k.dma("pool", wo, I["w_o"].rearrange("(kc p) n -> p kc n", p=128))
            ss8 = k.sb("ss8", [128, 8], F32, e4a)

            def part_a(ch):
                yT = yTs[ch % 2]
                xns = xns_all[(ch % 2) * 4:(ch % 2) * 4 + 4]
                ssv = ss8[:, (ch % 2) * 4:(ch % 2) * 4 + 4]
                for kp in range(4):
                    bank = nb()
                    bv = bfv(bank)
                    for kk in range(2):
                        kc = kp * 2 + kk
                        for qi in range(4):
                            k.tr(bv[:, (kk * 4 + qi) * 128:(kk * 4 + qi + 1) * 128],
                                 ons[:, ch * 4 + qi, kc * 128:(kc + 1) * 128], ident_bf[:])
                    k.cp(yT[:, kp * 2:kp * 2 + 2, :], bv[:, :].rearrange("p (a n) -> p a n", a=2),
                         eng=("act" if kp % 2 else "dve"))
                for qi in range(4):
                    qt = ch * 4 + qi
                    xt = xts[qt % 4]
                    x1t = x1ts[qt % 4]
                    k.dma("sp", xt[:], xv[b, qt * 128:(qt + 1) * 128, :])
                    for n in range(2):
                        bank = nb()
                        for kc in range(8):
                            k.mm(bank[:, :], yT[:, kc, qi * 128:(qi + 1) * 128], wo[:, kc, n * 512:(n + 1) * 512],
                                 start=(kc == 0), stop=(kc == 7))
                        k.tt(tmp[:], bank[:, :], gm_b[:, n * 512:(n + 1) * 512], ALU.mult)
                        k.tt(x1t[:, n * 512:(n + 1) * 512], tmp[:], xt[:, n * 512:(n + 1) * 512], ALU.add)
                    k.dma("act", x1D.ap()[b, qt * 128:(qt + 1) * 128, :], x1t[:])
                    k.act(junk[:], x1t[:], AF.Square, accum_out=ssv[:, qi:qi + 1])
                k.act(ssv, ssv, AF.Sqrt, bias=1e-6, scale=1.0 / D)
                k.recip(ssv, ssv)
                for qi in range(4):
                    k.ts(xns[qi][:], x1ts[(ch * 4 + qi) % 4][:], ssv[:, qi:qi + 1], None, ALU.mult)

            def part_b(ch):
                xns = xns_all[(ch % 2) * 4:(ch % 2) * 4 + 4]
                for kp in range(4):
                    bank = nb()
                    bv = bfv(bank)
                    for kk in range(2):
                        kc = kp * 2 + kk
                        for qi in range(4):
                            k.tr(bv[:, (kk * 4 + qi) * 128:(kk * 4 + qi + 1) * 128],
                                 xns[qi][:, kc * 128:(kc + 1) * 128], ident_bf[:])
                    for kk in range(2):
                        kc = kp * 2 + kk
                        k.act(h2T[:, kc, ch * 512:(ch + 1) * 512], bv[:, kk * 512:(kk + 1) * 512],
                              AF.Identity, bias=modFM[:, 24 + kc, b:b + 1], scale=sc2[:, kc, b:b + 1])
            part_a(0)
            for ch in range(NCH):
                if ch + 1 < NCH:
                    part_a(ch + 1)
                part_b(ch)
            k.barrier()
        L["ses2"].close()
        if b == 0 and "h2T" in DBG:
            finals.append(k.dma("sp", DBG["h2T"], h2T[:]))
        with ExitStack() as e4b:
            aT = k.sb("aT", [128, NFC, S], BF16, e4b)
            gbuf = k.sb("gbuf", [128, S + 2], F32, e4b)
            cv = k.sb("cv", [128, S], F32, e4b)
            sg = k.sb("sg", [128, S], BF16, e4b)
            k.memset(gbuf[:, 0:2], 0.0)
            wgv = I["w_gate"].rearrange("(kc p) n -> p kc n", p=128)
            wuv = I["w_up"].rearrange("(kc p) n -> p kc n", p=128)
            gi = 0
            for f0 in range(0, NFC, 4):
                nf = min(4, NFC - f0)
                wt = WB[gi % 2]
                gi += 1
                wg = wview(wt, 8, 512, 0)
                wu = wview(wt, 8, 512, 4096)
                k.dma("pool", wg[:, :, 0:nf * 128], wgv[:, :, f0 * 128:(f0 + nf) * 128])
                k.dma("pool", wu[:, :, 0:nf * 128], wuv[:, :, f0 * 128:(f0 + nf) * 128])
                for fl in range(nf):
                    fc = f0 + fl
                    for tch in range(NCH):
                        cs = slice(tch * 512, (tch + 1) * 512)
                        bank = nb()
                        for kc in range(8):
                            k.mm(bank[:, :], wg[:, kc, fl * 128:(fl + 1) * 128], h2T[:, kc, cs], start=(kc == 0), stop=(kc == 7))
                        k.cp(gbuf[:, 2 + tch * 512:2 + (tch + 1) * 512], bank[:, :], eng="act")
                    k.ts(cv[:], gbuf[:, 2:S + 2], convw[:, fc, 2:3], convb[:, fc:fc + 1], ALU.mult, ALU.add)
                    k.stt(cv[:], gbuf[:, 1:S + 1], convw[:, fc, 1:2], cv[:], ALU.mult, ALU.add)
                    k.stt(cv[:], gbuf[:, 0:S], convw[:, fc, 0:1], cv[:], ALU.mult, ALU.add)
                    k.act(sg[:], cv[:], AF.Silu)
                    for tch in range(NCH):
                        cs = slice(tch * 512, (tch + 1) * 512)
                        bank = nb()
                        for kc in range(8):
                            k.mm(bank[:, :], wu[:, kc, fl * 128:(fl + 1) * 128], h2T[:, kc, cs], start=(kc == 0), stop=(kc == 7))
                        k.tt(aT[:, fc, cs], bank[:, :], sg[:, cs], ALU.mult)
            x1r = [cv[:, i * D:(i + 1) * D] for i in range(2)]
            x2 = [gbuf[:, i * D:(i + 1) * D] for i in range(2)]
            junk = sg[:, 0:D]
            tmp = k.sb("tmp5", [128, 512], F32, e4b)
            wdv = I["w_down"].rearrange("(fc p) n -> p fc n", p=128)
            for qg_ in range(4):
                banks = [[nb() for _ in range(2)] for _ in range(4)]
                for f0 in range(0, NFC, 8):
                    nf = min(8, NFC - f0)
                    wt = WB[gi % 2]
                    gi += 1
                    wd = wview(wt, 8, 1024)
                    k.dma("pool", wd[:, 0:nf, :], wdv[:, f0:f0 + nf, :])
                    for fl in range(nf):
                        fc = f0 + fl
                        for qi in range(4):
                            qt = qg_ * 4 + qi
                            for n in range(2):
                                k.mm(banks[qi][n][:, :], aT[:, fc, qt * 128:(qt + 1) * 128], wd[:, fl, n * 512:(n + 1) * 512],
                                     start=(fc == 0), stop=(fc == NFC - 1))
                for qi in range(4):
                    qt = qg_ * 4 + qi
                    xr = x1r[qt % 2]
                    xo = x2[qt % 2]
                    k.dma("sp", xr, x1D.ap()[b, qt * 128:(qt + 1) * 128, :])
                    for n in range(2):
                        k.tt(tmp[:], banks[qi][n][:, :], gf_b[:, n * 512:(n + 1) * 512], ALU.mult)
                        k.tt(xo[:, n * 512:(n + 1) * 512], tmp[:], xr[:, n * 512:(n + 1) * 512], ALU.add)
                    k.act(junk, xo, AF.Square, accum_out=ss[:, 4 + qi:5 + qi])
                    k.act(ss[:, 4 + qi:5 + qi], ss[:, 4 + qi:5 + qi], AF.Sqrt, bias=1e-6, scale=1.0 / D)
                    k.recip(ss[:, 4 + qi:5 + qi], ss[:, 4 + qi:5 + qi])
                    k.stt(xo, xo, ss[:, 4 + qi:5 + qi], fng_b[:], ALU.mult, ALU.mult)
                    finals.append(k.dma("act", out[b, qt * 128:(qt + 1) * 128, :], xo))
            k.barrier()
        k.barrier()


def shared_inputs(inp):
    f = np.float32
    m = {}
    m["rel"] = inp["rel_bias_table"].astype(f)
    m["ada_w"] = inp["ada_w"][0]
    m["ada_bT"] = np.ascontiguousarray(inp["ada_b"][0].reshape(48, 128).T)
    m["nmg"] = np.ascontiguousarray(inp["norm_mix_g"][0].reshape(8, 128).T)
    m["w_in"] = inp["w_in"][0]
    m["posk"] = np.ascontiguousarray(inp["cmp_pos_k"][0].reshape(16, 128).T)
    m["posv"] = np.ascontiguousarray(inp["cmp_pos_v"][0].reshape(16, 128).T)
    m["w1k"] = inp["cmp_w1_k"][0]
    m["w2k"] = inp["cmp_w2_k"][0]
    m["w1v"] = inp["cmp_w1_v"][0]
    m["w2v"] = inp["cmp_w2_v"][0]
    m["qg"] = np.ascontiguousarray(inp["mla_q_norm_g"][0].reshape(2, 128).T)
    m["kvg"] = np.ascontiguousarray(inp["mla_kv_norm_g"][0].reshape(1, 128).T)
    wuq = inp["mla_w_uq"][0]
    m["w_uq"] = wuq
    sw = wuq.reshape(256, 8, 96).copy()
    sw[:, :, 64:80] = wuq.reshape(256, 8, 96)[:, :, 80:96]
    sw[:, :, 80:96] = wuq.reshape(256, 8, 96)[:, :, 64:80]
    m["w_uq_sw"] = np.ascontiguousarray(sw.reshape(256, 768))
    m["w_ukv"] = inp["mla_w_ukv"][0]
    m["w_o"] = inp["w_o"][0]
    m["nfg"] = np.ascontiguousarray(inp["norm_ffn_g"][0].reshape(8, 128).T)
    m["w_gate"] = inp["ffn_w_gate"][0]
    m["w_up"] = inp["ffn_w_up"][0]
    m["convw"] = np.ascontiguousarray(inp["ffn_conv_w"][0].reshape(3, NFC, 128).transpose(2, 1, 0))
    m["convb"] = np.ascontiguousarray(inp["ffn_conv_b"][0].reshape(NFC, 128).T)
    m["w_down"] = inp["ffn_w_down"][0]
    m["fng"] = inp["final_norm_g"]
    m.update(make_consts())
    return {k_: np.ascontiguousarray(v) for k_, v in m.items()}


_CACHE = {}


def kernel(**inputs):
    inp = {k_: np.asarray(v) for k_, v in inputs.items()}
    shared = shared_inputs(inp)
    if "nc" not in _CACHE:
        _CACHE["nc"] = build(nseq=4)[0]
    nc = _CACHE["nc"]
    in_maps = []
    for core in range(8):
        bsel = list(range(core * 4, core * 4 + 4))
        m = dict(shared)
        m["x"] = np.ascontiguousarray(inp["x"][bsel], dtype=np.float32)
        m["cT"] = np.ascontiguousarray(inp["c"][bsel].reshape(4, 8, 128).transpose(2, 1, 0), dtype=np.float32)
        m["pos"] = np.ascontiguousarray(inp["positions"][bsel], dtype=np.int32)
        in_maps.append(m)
    res = run_bass_kernel_spmd(nc, in_maps, core_ids=list(range(8)))
    out = np.concatenate([np.asarray(r["out"]) for r in res.results], axis=0)
    return out.astype(np.float32)
```

```python
from contextlib import ExitStack
import math
import numpy as np
import ml_dtypes
import concourse.bass as bass
import concourse.mybir as mybir
from concourse.bass_utils import run_bass_kernel_spmd

F32 = mybir.dt.float32
BF16 = mybir.dt.bfloat16
I32 = mybir.dt.int32
AF = mybir.ActivationFunctionType
ALU = mybir.AluOpType
AX = mybir.AxisListType

EPOCH = 30000
NDSEM = 12

D = 1024
S = 2048
NQT = 16
NCH = 4
DFF = 2816
NFC = 22
NEG = -30000.0
INCOLS = 5072
C_Q, C_KC, C_VC, C_KS, C_VS, C_KW, C_VW, C_G, C_CQ, C_CKV, C_KR, C_MA, C_MB = (
    0, 1024, 1280, 1536, 1792, 2048, 2304, 2560, 2608, 2864, 2992, 3024, 4048)


class KB:
    def __init__(self, nc):
        self.nc = nc
        self.es = ExitStack()
        self.eng = {"pe": nc.tensor, "act": nc.scalar, "dve": nc.vector,
                    "pool": nc.gpsimd, "sp": nc.sync}
        self.cnt = {e: 0 for e in ("pe", "act", "dve", "pool")}
        self.csem = {e: [] for e in self.cnt}
        self.dcnt = {}
        self.dsem = {}
        self.waited = {}
        self.recs = {}
        self.nwaits = 0
        self.ninst = 0

    def sb(self, name, shape, dt, es=None):
        self.uid = getattr(self, "uid", 0) + 1
        return (es or self.es).enter_context(self.nc.sbuf_tensor(f"s{self.uid}_{name}", list(shape), dt))

    def ps(self, name, shape, dt):
        return self.es.enter_context(self.nc.psum_tensor(name, list(shape), dt))

    def _sem(self, name):
        return self.es.enter_context(self.nc.semaphore(name))

    @staticmethod
    def box(ap):
        t = ap.tensor
        space = str(ap.space)
        if space == "DRAM":
            lo = int(ap.offset)
            hi = lo
            for st, n in ap.ap:
                if n > 1:
                    if st >= 0:
                        hi += st * (n - 1)
                    else:
                        lo += st * (n - 1)
            return (t.name, 0, 1, lo, hi + 1, False)
        if space == "PSUM":
            return (t.name, 0, 128, 0, 1 << 30, True)
        row = 1
        for s_ in list(t.shape)[1:]:
            row *= s_
        off = int(ap.offset)
        p0 = off // row
        f0 = off % row
        apl = list(ap.ap)
        lo = f0
        hi = f0
        for st, n in apl[1:]:
            if n > 1:
                if st >= 0:
                    hi += st * (n - 1)
                else:
                    lo += st * (n - 1)
        return (t.name, p0, p0 + apl[0][1], lo, hi + 1, False)

    def _wait_compute(self, waiter, e, idx):
        key = (waiter, e)
        if self.waited.get(key, -1) >= idx:
            return
        self.waited[key] = idx
        self.eng[waiter].wait_ge(self.csem[e][idx // EPOCH], (idx % EPOCH) + 1)
        self.nwaits += 1

    def _wait_dma(self, waiter, q, k):
        slot = k % NDSEM
        val = 16 * (k // NDSEM + 1)
        key = (waiter, "dma", q, slot)
        if self.waited.get(key, 0) >= val:
            return
        self.waited[key] = val
        self.eng[waiter].wait_ge(self.dsem[q][slot], val)
        self.nwaits += 1

    def _wait(self, waiter, who):
        if who[0] == "dma":
            self._wait_dma(waiter, who[1], who[2])
        else:
            self._wait_compute(waiter, who[0], who[1])

    @staticmethod
    def _ovl(a, b):
        return a[1] < b[2] and b[1] < a[2] and a[3] < b[4] and b[3] < a[4]

    @staticmethod
    def _covers(a, b):
        return a[1] <= b[1] and a[2] >= b[2] and a[3] <= b[3] and a[4] >= b[4]

    def _deps(self, eng, reads, writes, is_dma):
        deps = []
        for ap in reads:
            bx = self.box(ap)
            for (rb, kind, who) in self.recs.get(bx[0], ()):
                if not self._ovl(bx, rb):
                    continue
                same = (not is_dma) and who[0] == eng
                if kind == "w":
                    if same and eng == "pe":
                        continue
                    deps.append(who)
                elif bx[5] and not same:
                    deps.append(who)
        for ap in writes:
            bx = self.box(ap)
            for (rb, kind, who) in self.recs.get(bx[0], ()):
                if not self._ovl(bx, rb):
                    continue
                same = (not is_dma) and who[0] == eng
                if same and eng == "pe":
                    continue
                deps.append(who)
        return deps

    def _record(self, who, reads, writes):
        for ap in writes:
            bx = self.box(ap)
            lst = self.recs.setdefault(bx[0], [])
            if bx[5]:
                lst[:] = []
            else:
                lst[:] = [r for r in lst if not self._covers(bx, r[0])]
            lst.append((bx, "w", who))
        for ap in reads:
            bx = self.box(ap)
            lst = self.recs.setdefault(bx[0], [])
            if who[0] != "dma":
                lst[:] = [r for r in lst
                          if not (r[1] == "r" and r[2][0] == who[0] and r[0] == bx)]
            lst.append((bx, "r", who))

    def op(self, eng, fn, reads, writes):
        for who in self._deps(eng, reads, writes, False):
            self._wait(eng, who)
        idx = self.cnt[eng]
        ep = idx // EPOCH
        while len(self.csem[eng]) <= ep:
            self.csem[eng].append(self._sem(f"c_{eng}_{len(self.csem[eng])}"))
        ins = fn()
        ins.then_inc(self.csem[eng][ep], 1)
        self.cnt[eng] = idx + 1
        self.ninst += 1
        self._record((eng, idx), reads, writes)
        return (eng, idx)

    def dma(self, q, out, in_, **kw):
        if q not in self.dsem:
            self.dsem[q] = [self._sem(f"d_{q}_{i}") for i in range(NDSEM)]
            self.dcnt[q] = 0
        k = self.dcnt[q]
        if k >= NDSEM:
            self._wait_dma(q, q, k - NDSEM)
        for who in self._deps(q, [in_], [out], True):
            self._wait(q, who)
        ins = self.eng[q].dma_start(out=out, in_=in_, **kw)
        ins.then_inc(self.dsem[q][k % NDSEM], 16)
        self.dcnt[q] = k + 1
        self.ninst += 1
        who = ("dma", q, k)
        self._record(who, [in_], [out])
        return who

    def finish(self, whos, eng="sp"):
        for who in whos:
            self._wait(eng, who)

    def barrier(self):
        for w in ("pe", "act", "dve", "pool", "sp"):
            for e in ("pe", "act", "dve", "pool"):
                if e != w and self.cnt[e] > 0:
                    self._wait_compute(w, e, self.cnt[e] - 1)
            for q in self.dsem:
                n = self.dcnt[q]
                for k in range(max(0, n - NDSEM), n):
                    self._wait_dma(w, q, k)
        self.recs = {}

    def mm(self, out, lhsT, rhs, start=True, stop=True):
        return self.op("pe", lambda: self.nc.tensor.matmul(
            out, lhsT, rhs, start=start, stop=stop, skip_group_check=True), [lhsT, rhs], [out])

    def tr(self, out, in_, ident):
        return self.op("pe", lambda: self.nc.tensor.transpose(out, in_, ident), [in_, ident], [out])

    def act(self, out, in_, func, bias=None, scale=None, accum_out=None):
        kw = {}
        rd = [in_]
        wr = [out]
        if bias is not None:
            kw["bias"] = bias
            if not isinstance(bias, (int, float)):
                rd.append(bias)
        if scale is not None:
            kw["scale"] = scale
            if not isinstance(scale, (int, float)):
                rd.append(scale)
        if accum_out is not None:
            kw["accum_out"] = accum_out
            wr.append(accum_out)
        return self.op("act", lambda: self.nc.scalar.activation(out, in_, func, **kw), rd, wr)

    def _ve(self, eng):
        return self.nc.vector if eng == "dve" else self.nc.gpsimd

    def tt(self, out, in0, in1, op, eng="dve"):
        return self.op(eng, lambda: self._ve(eng).tensor_tensor(out, in0, in1, op), [in0, in1], [out])

    def ts(self, out, in0, s1, s2, op0, op1=None, eng="dve"):
        rd = [in0]
        for s_ in (s1, s2):
            if s_ is not None and not isinstance(s_, (int, float)):
                rd.append(s_)
        kw = {}
        if op1 is not None:
            kw["op1"] = op1
        return self.op(eng, lambda: self._ve(eng).tensor_scalar(out, in0, s1, s2, op0, **kw), rd, [out])

    def stt(self, out, in0, scalar, in1, op0, op1, eng="dve"):
        rd = [in0, in1]
        if not isinstance(scalar, (int, float)):
            rd.append(scalar)
        return self.op(eng, lambda: self._ve(eng).scalar_tensor_tensor(
            out, in0, scalar, in1, op0, op1), rd, [out])

    def cp(self, out, in_, eng="dve"):
        if eng == "act":
            return self.op("act", lambda: self.nc.scalar.copy(out, in_), [in_], [out])
        return self.op(eng, lambda: self._ve(eng).tensor_copy(out, in_), [in_], [out])

    def red(self, out, in_, op, axis=AX.X, eng="dve"):
        return self.op(eng, lambda: self._ve(eng).tensor_reduce(out, in_, axis, op), [in_], [out])

    def memset(self, ap, v, eng="dve"):
        return self.op(eng, lambda: self._ve(eng).memset(ap, v), [], [ap])

    def recip(self, out, in_):
        return self.op("dve", lambda: self.nc.vector.reciprocal(out, in_), [in_], [out])


def _t5_bucket(n):
    n = np.maximum(n, 0)
    exact = 16
    lr = np.log(np.maximum(n, exact).astype(np.float32) / np.float32(exact)) / np.float32(math.log(128 / exact))
    large = np.minimum(exact + (lr.astype(np.float32) * np.float32(16)).astype(np.int32), 31)
    return np.where(n < exact, n, large)


def make_consts():
    bf = ml_dtypes.bfloat16
    c = {}
    c["ident_bf"] = np.eye(128, dtype=np.float32).astype(bf)
    c["ident_f"] = np.eye(128, dtype=np.float32)
    k = np.arange(128)[:, None]
    q = np.arange(128)[None, :]
    c["tri"] = np.where(q >= k, 0.0, NEG).astype(bf)
    c["anti"] = np.where(q < k, 0.0, NEG).astype(bf)
    t = np.arange(S)[None, :]
    c["blockind"] = (t // 64 == np.arange(32)[:, None]).astype(np.float32).astype(bf)
    n = np.arange(1152)
    dist = n - 511
    oh = np.zeros((33, 1152), np.float32)
    bk = _t5_bucket(dist)
    for i in range(1152):
        if dist[i] >= 0:
            oh[bk[i], i] = 1.0
        else:
            oh[32, i] = 1.0
    c["ohe"] = oh
    a = np.zeros((4, 41, 127), np.float32)
    for ch in range(4):
        for cc in range(127):
            r = cc - 32 * ch
            if r >= 31:
                a[ch, 0, cc] = 1.0
            elif r >= -9:
                a[ch, 31 - r, cc] = 1.0
    ap_ = np.zeros((128, 4, 127), np.float32)
    ap_[:41] = a.transpose(1, 0, 2)
    c["ach"] = ap_.astype(bf)
    m1 = np.ones((128, 16, 32), np.float32)
    m2 = np.zeros((128, 16, 32), np.float32)
    for qt in range(16):
        for p in range(128):
            cur = (qt * 128 + p) // 64
            for j in range(32):
                if j == 0 or j == cur or j == cur - 1:
                    m1[p, qt, j] = 0.0
                    m2[p, qt, j] = 1e30
                elif j > cur:
                    m1[p, qt, j] = 0.0
                    m2[p, qt, j] = -1e30
    c["m1"] = m1
    c["m2"] = m2
    cs = np.arange(127) * 16
    bs = np.arange(32) * 64
    ov = np.clip(np.minimum(cs[:, None] + 32, bs[None, :] + 64) - np.maximum(cs[:, None], bs[None, :]), 0, None) / 32.0
    vcc = np.zeros((128, 33), np.float32)
    vcc[:127, 0] = 1.0
    vcc[:127, 1:] = ov
    c["vcconst"] = vcc.astype(bf)
    invf = np.zeros((128, 1), np.float32)
    sgn = np.ones((128, 1), np.float32)
    fr = (np.float32(10000.0) ** (-np.arange(0, 32, 2, dtype=np.float32) / np.float32(32))).astype(np.float32)
    for i in range(16):
        invf[64 + i, 0] = fr[i]
        invf[80 + i, 0] = fr[i]
        sgn[64 + i, 0] = -1.0
    c["invf"] = invf
    c["sgn"] = sgn
    return c


CONST_DT = {"ident_bf": BF16, "ident_f": F32, "tri": BF16, "anti": BF16, "blockind": BF16, "ohe": F32,
            "ach": BF16, "m1": F32, "m2": F32, "vcconst": BF16, "invf": F32, "sgn": F32}

IN_SPECS = [
    ("x", None, F32), ("cT", None, F32), ("pos", None, I32), ("rel", [32, 16], F32),
    ("ada_w", [D, 6 * D], F32), ("ada_bT", [128, 48], F32), ("nmg", [128, 8], F32),
    ("w_in", [D, INCOLS], F32), ("posk", [128, 16], F32), ("posv", [128, 16], F32),
    ("w1k", [2048, 256], F32), ("w2k", [256, 64], F32), ("w1v", [2048, 256], F32), ("w2v", [256, 64], F32),
    ("qg", [128, 2], F32), ("kvg", [128, 1], F32), ("w_uq", [256, 768], F32), ("w_uq_sw", [256, 768], F32),
    ("w_ukv", [128, 1536], F32), ("w_o", [D, D], F32), ("nfg", [128, 8], F32),
    ("w_gate", [D, DFF], F32), ("w_up", [D, DFF], F32), ("convw", [128, NFC, 3], F32), ("convb", [128, NFC], F32),
    ("w_down", [DFF, D], F32), ("fng", [D], F32),
]


def build(nseq=4, stop_after=None, dbg=()):
    nc = bass.Bass("TRN2", target_bir_lowering=False)
    I = {}
    for name, shape, dt in IN_SPECS:
        if name == "x":
            shape = [nseq, S, D]
        elif name == "cT":
            shape = [128, 8, nseq]
        elif name == "pos":
            shape = [nseq, S]
        I[name] = nc.dram_tensor(name, list(shape), dt, kind="ExternalInput").ap()
    consts = make_consts()
    for name, arr in consts.items():
        I[name] = nc.dram_tensor(name, list(arr.shape), CONST_DT[name], kind="ExternalInput").ap()
    out = nc.dram_tensor("out", [nseq, S, D], F32, kind="ExternalOutput").ap()
    DBG = {}
    for name, shape, dt in dbg:
        DBG[name] = nc.dram_tensor("dbg_" + name, list(shape), dt, kind="ExternalOutput").ap()
    veD = nc.dram_tensor("veD", [16, 1152], F32, kind="Internal")
    vP = nc.dram_tensor("vP", [16, 128, 384], F32, kind="Internal")
    gD = nc.dram_tensor("gD", [nseq, 2, D], F32, kind="Internal")
    x1D = nc.dram_tensor("x1D", [nseq, S, D], F32, kind="Internal")

    k = KB(nc)
    finals = []
    with k.es:
        PS = [k.ps(f"ps{i}", [128, 512], F32) for i in range(8)]
        S_B = PS[0:2]
        O_B = PS[2:6]
        M_B = PS[6:8]

        def bf_view(bank):
            return bank[:].bitcast(BF16)

        def gconst(name, shape, dt, q="sp"):
            t = k.sb("c_" + name, shape, dt)
            k.dma(q, t[:], I[name])
            return t
        ident_bf = gconst("ident_bf", [128, 128], BF16)
        ident_f = gconst("ident_f", [128, 128], F32)
        tri = gconst("tri", [128, 128], BF16)
        anti = gconst("anti", [128, 128], BF16)
        ach = gconst("ach", [128, 4, 127], BF16)
        m1 = gconst("m1", [128, 16, 32], F32)
        m2 = gconst("m2", [128, 16, 32], F32)
        vcconst = gconst("vcconst", [128, 33], BF16)
        invf = gconst("invf", [128, 1], F32)
        sgn = gconst("sgn", [128, 1], F32)
        ada_bT = gconst("ada_bT", [128, 48], F32)
        nmg = gconst("nmg", [128, 8], F32)
        nfg = gconst("nfg", [128, 8], F32)
        qg = gconst("qg", [128, 2], F32)
        kvg = gconst("kvg", [128, 1], F32)
        convw = gconst("convw", [128, NFC, 3], F32)
        convb = gconst("convb", [128, NFC], F32)
        posk = gconst("posk", [128, 16], F32)
        posv = gconst("posv", [128, 16], F32)
        fng_b = k.sb("fng_b", [128, D], F32)
        k.dma("sp", fng_b[:], I["fng"].partition_broadcast(128))
        ones_bf = k.sb("ones_bf", [128, 128], BF16)
        k.memset(ones_bf[:], 1.0)
        w2k = k.sb("w2k", [128, 2, 64], BF16)
        w2v = k.sb("w2v", [128, 2, 64], BF16)
        k.dma("pool", w2k[:], I["w2k"].rearrange("(c p) n -> p c n", p=128))
        k.dma("pool", w2v[:], I["w2v"].rearrange("(c p) n -> p c n", p=128))
        modFM = k.sb("modFM", [128, 48, nseq], F32)
        sc1 = k.sb("sc1", [128, 8, nseq], F32)
        sc2 = k.sb("sc2", [128, 8, nseq], F32)
        posb = k.sb("posb", [128, 2, 2], F32)
        WB = [k.sb(f"wb{i}", [128, 8192], BF16) for i in range(2)]
        wb_i = [0]
        negpi = k.sb("negpi", [128, 1], F32)
        k.memset(negpi[:], -math.pi)

        def wbuf():
            t = WB[wb_i[0] % 2]
            wb_i[0] += 1
            return t

        def wview(t, kc, n, off=0):
            return t[:, off:off + kc * n].rearrange("p (k n) -> p k n", k=kc)

        with ExitStack() as pes:
            cT_sb = k.sb("cT_sb", [128, 8, nseq], F32, pes)
            cond = k.sb("cond", [128, 8, nseq], BF16, pes)
            k.dma("sp", cT_sb[:], I["cT"])
            k.act(cond[:], cT_sb[:], AF.Silu)
            adav = I["ada_w"].rearrange("(kc p) n -> p kc n", p=128)
            mbank = M_B[0]
            first = True
            for jt in range(12):
                wt = wbuf()
                wv = wview(wt, 8, 512)
                k.dma("pool", wv, adav[:, :, jt * 512:(jt + 1) * 512])
                for jj in range(4):
                    j = jt * 4 + jj
                    for kc in range(8):
                        k.mm(mbank[:, j * nseq:(j + 1) * nseq], wv[:, kc, jj * 128:(jj + 1) * 128],
                             cond[:, kc, :], start=first, stop=False)
                        first = False
            k.tt(modFM[:], mbank[:, 0:48 * nseq].rearrange("p (j b) -> p j b", b=nseq),
                 ada_bT[:].unsqueeze(2).to_broadcast([128, 48, nseq]), ALU.add)
            k.stt(sc1[:], modFM[:, 8:16, :], 1.0, nmg[:].unsqueeze(2).to_broadcast([128, 8, nseq]),
                  ALU.add, ALU.mult)
            k.stt(sc2[:], modFM[:, 32:40, :], 1.0, nfg[:].unsqueeze(2).to_broadcast([128, 8, nseq]),
                  ALU.add, ALU.mult)
            gts = k.sb("gts", [8, 2 * nseq, 128], F32, pes)
            for b in range(nseq):
                gbank = (M_B[1], S_B[0])[b % 2]
                for gi, j0 in enumerate((16, 40)):
                    k.tr(gbank[0:8, gi * 128:(gi + 1) * 128], modFM[:, j0:j0 + 8, b], ident_f[:])
                k.cp(gts[:, 2 * b:2 * b + 2, :], gbank[0:8, 0:256].rearrange("p (a n) -> p a n", n=128))
            for b in range(nseq):
                for gi in range(2):
                    k.dma("sp", gD.ap()[b, gi, :].rearrange("(j p) -> j p", p=128), gts[:, b * 2 + gi, :])
            tab = k.sb("tab", [33, 16], F32, pes)
            t31 = k.sb("t31", [32, 16], F32, pes)
            ohe = k.sb("ohe", [33, 1152], F32, pes)
            ve_sb = k.sb("ve_sb", [16, 1152], F32, pes)
            k.dma("sp", tab[0:32, :], I["rel"])
            k.dma("sp", t31[:], I["rel"][31, :].partition_broadcast(32))
            k.dma("sp", ohe[:], I["ohe"])
            k.tt(tab[0:32, :], tab[0:32, :], t31[:], ALU.subtract)
            k.ts(tab[0:32, :], tab[0:32, :], 8.0, None, ALU.mult)
            k.memset(tab[32:33, :], NEG)
            for i in range(3):
                k.mm(mbank[0:16, 0:384], tab[:], ohe[:, i * 384:(i + 1) * 384])
                k.cp(ve_sb[:, i * 384:(i + 1) * 384], mbank[0:16, 0:384])
            k.dma("sp", veD.ap(), ve_sb[:])
            k.dma("sp", vP.ap(), ve_sb[:, 384:768].unsqueeze(1).to_broadcast([16, 128, 384]))
            w1f = k.sb("w1f", [128, 16, 256], F32, pes)
            for kv, (wn, pt) in enumerate((("w1k", posk), ("w1v", posv))):
                k.dma("sp", w1f[:], I[wn].rearrange("(c p) n -> p c n", p=128))
                for hc in range(2):
                    for lp in range(16):
                        k.mm(mbank[:, 256 + kv * 2 + hc:256 + kv * 2 + hc + 1], w1f[:, lp, hc * 128:(hc + 1) * 128],
                             pt[:, lp:lp + 1], start=(lp == 0), stop=(lp == 15))
            k.cp(posb[:], mbank[:, 256:260].rearrange("p (a b) -> p a b", b=2))
            if "modFM" in DBG:
                finals.append(k.dma("sp", DBG["modFM"], modFM[:]))
            if "ve" in DBG:
                finals.append(k.dma("sp", DBG["ve"], ve_sb[:]))
            if "posb" in DBG:
                finals.append(k.dma("sp", DBG["posb"], posb[:]))
            k.barrier()

        if stop_after == "prologue":
            k.finish(finals)
            return nc, k

        xv = I["x"]
        for b in range(nseq):
            with ExitStack() as ses, ExitStack() as ses2:
                gm_b = k.sb("gm_b", [128, D], F32, ses)
                gf_b = k.sb("gf_b", [128, D], F32, ses)
                ss4 = k.sb("ss4", [128, 8], F32, ses)
                k.dma("sp", gm_b[:], gD.ap()[b, 0, :].partition_broadcast(128))
                k.dma("sp", gf_b[:], gD.ap()[b, 1, :].partition_broadcast(128))
                hT = k.sb("hT", [128, 8, S], BF16, ses)
                ons = k.sb("ons", [128, NQT, D], BF16, ses2)

                def norm_to_T(src_tiles_fn, dstT, scv, shv, es_):
                    xts = [k.sb(f"xt{i}", [128, D], F32, es_) for i in range(8)]
                    xns_all = [k.sb(f"xn{i}", [128, D], BF16, es_) for i in range(8)]
                    junk = k.sb("junk", [128, D], BF16, es_)
                    ss_all = k.sb("ss", [128, 16], F32, es_)
                    for ch in range(NCH):
                        xns = xns_all[(ch % 2) * 4:(ch % 2) * 4 + 4]
                        ss = ss_all[:, ch * 4:ch * 4 + 4]
                        xtl = []
                        for qi in range(4):
                            qt = ch * 4 + qi
                            xt = src_tiles_fn(qt, xts[qt % 8])
                            xtl.append(xt)
                            k.act(junk[:], xt, AF.Square, accum_out=ss[:, qi:qi + 1])
                        k.act(ss, ss, AF.Sqrt, bias=1e-6, scale=1.0 / D)
                        k.recip(ss, ss)
                        for qi in range(4):
                            k.ts(xns[qi][:], xtl[qi], ss[:, qi:qi + 1], None, ALU.mult)
                        for kp in range(4):
                            bank = M_B[kp % 2]
                            bv = bf_view(bank)
                            for kk in range(2):
                                kc = kp * 2 + kk
                                for qi in range(4):
                                    k.tr(bv[:, (kk * 4 + qi) * 128:(kk * 4 + qi + 1) * 128],
                                         xns[qi][:, kc * 128:(kc + 1) * 128], ident_bf[:])
                            for kk in range(2):
                                kc = kp * 2 + kk
                                k.act(dstT[:, kc, ch * 512:(ch + 1) * 512], bv[:, kk * 512:(kk + 1) * 512],
                                      AF.Identity, bias=shv(kc), scale=scv(kc))

                with ExitStack() as e1:
                    def load_x(qt, buf):
                        k.dma("sp", buf[:], xv[b, qt * 128:(qt + 1) * 128, :])
                        return buf[:]
                    norm_to_T(load_x, hT, lambda kc: sc1[:, kc, b:b + 1], lambda kc: modFM[:, kc, b:b + 1], e1)
                    k.barrier()
                if "hT" in DBG and b == 0:
                    finals.append(k.dma("sp", DBG["hT"], hT[:]))
                if stop_after == "s1":
                    continue
                STAGES(k, nc, I, DBG, finals, b, locals())
        k.finish(finals)
    return nc, k


class AttnStream:
    def __init__(self, k, S_B, PTs, look=1):
        self.k, self.S_B, self.PTs, self.look = k, S_B, PTs, look
        self.si = 0
        self.pi = 0
        self.pend = []

    def add(self, s_fn, npart, n, scale, pv_fn, after=None):
        k = self.k
        Sb = self.S_B[self.si % len(self.S_B)]
        self.si += 1
        s_fn(Sb)
        pt = self.PTs[self.pi % len(self.PTs)]
        self.pi += 1
        k.act(pt[0:npart, 0:n], Sb[0:npart, 0:n], AF.Exp, scale=scale)
        self.pend.append([pv_fn, pt, after, []])
        while len(self.pend) > self.look:
            self._pop()

    def defer(self, fn):
        if self.pend:
            self.pend[-1][3].append(fn)
        else:
            fn()

    def _pop(self):
        pv_fn, pt, after, deferred = self.pend.pop(0)
        pv_fn(pt)
        if after is not None:
            after()
        for fn in deferred:
            fn()

    def flush(self):
        while self.pend:
            self._pop()


def STAGES(k, nc, I, DBG, finals, b, L):
    hT, ons = L["hT"], L["ons"]
    stop_after = L["stop_after"]
    nsa_stage(k, nc, I, DBG, finals, b, L)
    if stop_after == "nsa":
        return
    mla_stage(k, nc, I, DBG, finals, b, L)
    if stop_after == "mla":
        return
    out_stage(k, nc, I, DBG, finals, b, L)


def nsa_stage(k, nc, I, DBG, finals, b, L):
    hT, ons, wbuf, wview = L["hT"], L["ons"], L["wbuf"], L["wview"]
    S_B, O_B, M_B = L["S_B"], L["O_B"], L["M_B"]
    ident_bf, tri, anti, ach, m1, m2, vcconst = (L[n] for n in ("ident_bf", "tri", "anti", "ach", "m1", "m2", "vcconst"))
    posb, w2k, w2v = L["posb"], L["w2k"], L["w2v"]
    veD, vP = L["veD"], L["vP"]
    w_inv = I["w_in"].rearrange("(kc p) n -> p kc n", p=128)
    rr = {"s": 0, "o": 0, "m": 0, "pt": 0, "ev": 0}

    def nxt(key, lst):
        v = lst[rr[key] % len(lst)]
        rr[key] += 1
        return v

    def evac(out, in_):
        rr["ev"] += 1
        if rr["ev"] % 2:
            k.cp(out, in_, eng="act")
        else:
            k.cp(out, in_, eng="dve")

    with ExitStack() as e2:
        w1s = [k.sb(f"w1s{i}", [128, 16, 256], BF16, e2) for i in range(2)]
        k.dma("pool", w1s[0][:], I["w1k"].rearrange("(c p) n -> p c n", p=128))
        k.dma("pool", w1s[1][:], I["w1v"].rearrange("(c p) n -> p c n", p=128))
        Qaug = k.sb("Qaug", [128, 4, S], BF16, e2)
        KSaug = k.sb("KSaug", [128, S], BF16, e2)
        KWt = k.sb("KWt", [128, S], BF16, e2)
        KKs = [k.sb(f"KK{i}", [128, S], BF16, e2) for i in range(2)]
        VS = k.sb("VS", [128, NQT, 65], BF16, e2)
        VW = k.sb("VW", [128, NQT, 65], BF16, e2)
        VCaug = k.sb("VCaug", [128, 97], BF16, e2)
        sig = k.sb("sig", [128, NQT, 12], F32, e2)
        hid = [k.sb(f"hid{i}", [128, 2, 127], BF16, e2) for i in range(2)]
        kcT = k.sb("kcT", [128, 127], BF16, e2)
        Tns = [k.sb(f"Tn{i}", [128, 4, 256], BF16, e2) for i in range(2)]
        Wcs = [k.sb(f"Wc{i}", [128, 4, 512], BF16, e2) for i in range(2)]
        PTs = [k.sb(f"pt{i}", [128, 512], BF16, e2) for i in range(4)]
        accns = [k.sb(f"accn{i}", [128, 4, 4, 64], F32, e2) for i in range(2)]
        sc = k.sb("sc", [128, 4, 32], F32, e2)
        s2 = k.sb("s2", [128, 4, 32], F32, e2)
        cmpm = k.sb("cmpm", [128, 32, 32], BF16, e2)
        cmpm2 = k.sb("cmpm2", [128, 32, 32], BF16, e2)
        rank = k.sb("rank", [128, 4, 32], F32, e2)
        negpad = k.sb("negpad", [128, 4, 96], BF16, e2)
        rd = k.sb("rd", [128, 8, 4], F32, e2)
        ff = k.sb("ff", [128, 8, 4], F32, e2)
        tmp64 = k.sb("tmp64", [128, 4, 64], F32, e2)
        tmp32 = k.sb("tmp32", [128, 4, 32], F32, e2)

        k.memset(Qaug[64:128, :, :], 0.0, eng="pool")
        k.memset(KSaug[64:128, :], 0.0, eng="pool")
        k.memset(KWt[64:128, :], 0.0, eng="pool")
        k.memset(kcT[64:128, :], 0.0)
        for Wc_ in Wcs:
            k.memset(Wc_[:], 0.0, eng="pool")
            k.memset(Wc_[0:1, :, :], NEG)
        k.dma("sp", KSaug[64:96, :], I["blockind"])
        k.memset(VS[:, :, 64:65], 1.0)
        k.memset(VW[:, :, 64:65], 1.0)
        k.memset(VCaug[:], 0.0)
        k.cp(VCaug[:, 64:97], vcconst[:])
        k.memset(negpad[:], 0.0)

        rdi = [0]

        def norm_f(Ob, width, gate_ap):
            i = rdi[0] % 8
            rdi[0] += 1
            den = Ob[:, 0:4 * width].rearrange("p (q c) -> p q c", c=width)[:, :, 64]
            k.ts(rd[:, i, :], den, 1e-30, None, ALU.max)
            k.recip(rd[:, i, :], rd[:, i, :])
            if gate_ap is None:
                return rd[:, i, :]
            k.tt(ff[:, i, :], rd[:, i, :], gate_ap, ALU.mult)
            return ff[:, i, :]

        def load_group(g):
            wt = wbuf()
            wv = wview(wt, 8, 780)
            segs = [(0, 256, C_Q + 256 * g), (256, 64, C_KS + 64 * g), (320, 64, C_KW + 64 * g),
                    (384, 64, C_KC + 64 * g), (448, 64, C_KC + 64 * g), (512, 64, C_VC + 64 * g),
                    (576, 64, C_VC + 64 * g), (640, 64, C_VS + 64 * g), (704, 64, C_VW + 64 * g),
                    (768, 12, C_G + 12 * g)]
            for (o, n, c0) in segs:
                k.dma("pool", wv[:, :, o:o + n], w_inv[:, :, c0:c0 + n])
            for hl in range(4):
                h = 4 * g + hl
                k.dma("pool", Tns[g % 2][:, hl, :], bass.AP(vP, h * 128 * 384 + 127, [[383, 128], [1, 256]]))
                k.dma("pool", Wcs[g % 2][1:41, hl, :], bass.AP(veD, h * 1152, [[16, 40], [1, 512]]))
            return wv

        wv_next = load_group(0)
        for g in range(4):
            wv = wv_next
            Tn = Tns[g % 2]
            Wc = Wcs[g % 2]
            for ch in range(NCH):
                cs = slice(ch * 512, (ch + 1) * 512)
                for hl in range(4):
                    bank = nxt("m", M_B)
                    for kc in range(8):
                        k.mm(bank[0:64, :], wv[:, kc, hl * 64:(hl + 1) * 64], hT[:, kc, cs],
                             start=(kc == 0), stop=(kc == 7))
                    evac(Qaug[0:64, hl, cs], bank[0:64, :])
                for (o, dst) in ((256, KSaug), (320, KWt)):
                    bank = nxt("m", M_B)
                    for kc in range(8):
                        k.mm(bank[0:64, :], wv[:, kc, o:o + 64], hT[:, kc, cs], start=(kc == 0), stop=(kc == 7))
                    evac(dst[0:64, cs], bank[0:64, :])
                for (o, dst) in ((384, KKs[0]), (512, KKs[1])):
                    bank = nxt("m", M_B)
                    for kc in range(8):
                        k.mm(bank[:, :], wv[:, kc, o:o + 128], hT[:, kc, cs], start=(kc == 0), stop=(kc == 7))
                    evac(dst[0:64, cs], bank[0:64, :])
                    if ch == 0:
                        evac(dst[64:128, 0:511], bank[64:128, 1:512])
                    else:
                        evac(dst[64:128, ch * 512 - 1:(ch + 1) * 512 - 1], bank[64:128, :])
            for qt in range(NQT):
                bank = nxt("m", M_B)
                for kc in range(8):
                    k.mm(bank[:, 0:140], hT[:, kc, qt * 128:(qt + 1) * 128], wv[:, kc, 640:780],
                         start=(kc == 0), stop=(kc == 7))
                k.cp(VS[:, qt, 0:64], bank[:, 0:64], eng="dve")
                k.cp(VW[:, qt, 0:64], bank[:, 64:128], eng="dve")
                k.act(sig[:, qt, :], bank[:, 128:140], AF.Tanh, scale=0.5)
            k.ts(sig[:], sig[:], 0.5, 0.5, ALU.mult, ALU.add)
            if g + 1 < 4:
                wv_next = load_group(g + 1)
            for kv in range(2):
                src = KKs[kv]
                for hc in range(2):
                    bank = nxt("m", M_B)
                    for lp in range(16):
                        k.mm(bank[:, 0:127], w1s[kv][:, lp, hc * 128:(hc + 1) * 128],
                             src[:, 2 * lp:2 * lp + 16 * 126 + 1:16], start=(lp == 0), stop=(lp == 15))
                    k.act(hid[kv][:, hc, :], bank[:, 0:127], AF.Gelu_apprx_tanh, bias=posb[:, kv, hc:hc + 1])
            bank = nxt("m", M_B)
            for hc in range(2):
                k.mm(bank[0:64, 0:127], w2k[:, hc, :], hid[0][:, hc, :], start=(hc == 0), stop=(hc == 1))
            k.cp(kcT[0:64, :], bank[0:64, 0:127], eng="dve")
            bank = nxt("m", M_B)
            for hc in range(2):
                k.mm(bank[0:127, 0:64], hid[1][:, hc, :], w2v[:, hc, :], start=(hc == 0), stop=(hc == 1))
            k.cp(VCaug[0:127, 0:64], bank[0:127, 0:64], eng="dve")
            if b == 0 and g == 0 and "kcT" in DBG:
                finals.append(k.dma("sp", DBG["kcT"], kcT[0:64, :]))
                finals.append(k.dma("sp", DBG["vc"], VCaug[:]))
            st = AttnStream(k, S_B + [M_B[1]], PTs, look=2)

            def do_chunk(ch, part, g=g, Tn=Tn, Wc=Wc):
                cs = slice(ch * 512, (ch + 1) * 512)
                q0 = 4 * ch
                accn = accns[ch % 2]

                def cmp_item(hl):
                    Ob = nxt("o", O_B)

                    def s_fn(Sb):
                        k.mm(Sb[0:127, :], kcT[:, :], Qaug[:, hl, cs], start=True, stop=False)
                        k.mm(Sb[0:127, :], ach[:, ch, :], Wc[:, hl, :], start=False, stop=True)

                    def pv_fn(pt):
                        for qi in range(4):
                            k.mm(Ob[:, qi * 97:(qi + 1) * 97], pt[0:127, qi * 128:(qi + 1) * 128], VCaug[0:127, :],
                                 start=(qi == 0), stop=False)

                    def after():
                        Ov = Ob[:, 0:388].rearrange("p (q c) -> p q c", c=97)
                        r1 = norm_f(Ob, 97, None)
                        if ch >= 2 and hl == 0:
                            k.tt(sc[:], Ov[:, :, 65:97], r1.unsqueeze(2).to_broadcast([128, 4, 32]), ALU.mult)
                        elif ch >= 2:
                            k.tt(tmp32[:], Ov[:, :, 65:97], r1.unsqueeze(2).to_broadcast([128, 4, 32]), ALU.mult)
                            k.tt(sc[:], sc[:], tmp32[:], ALU.add)
                        i_ = rdi[0] % 8
                        rdi[0] += 1
                        k.tt(ff[:, i_, :], r1, sig[:, q0:q0 + 4, 3 * hl], ALU.mult)
                        k.tt(accn[:, :, hl, :], Ov[:, :, 0:64], ff[:, i_, :].unsqueeze(2).to_broadcast([128, 4, 64]), ALU.mult)
                    st.add(s_fn, 127, 512, 0.125, pv_fn, after)
                if part == "cmp":
                    for hl in range(4):
                        cmp_item(hl)

                def topk(q0=q0, ch=ch):
                    k.tt(s2[:], sc[:], m1[:, q0:q0 + 4, :], ALU.mult)
                    k.tt(s2[:], s2[:], m2[:, q0:q0 + 4, :], ALU.add)
                    for qi in range(4):
                        en = "dve"
                        cm = cmpm2 if qi % 2 else cmpm
                        k.tt(cm[:], s2[:, qi, :].unsqueeze(1).to_broadcast([128, 32, 32]),
                             s2[:, qi, :].unsqueeze(2).to_broadcast([128, 32, 32]), ALU.is_gt, eng=en)
                        k.red(rank[:, qi, :], cm[:], ALU.add)
                    k.ts(negpad[:, :, 64:96], rank[:], 15.5, NEG, ALU.is_gt, ALU.mult)
                    if b == 0 and g == 0 and ch == 3 and "rank" in DBG:
                        finals.append(k.dma("sp", DBG["rank"], rank[:]))
                        finals.append(k.dma("sp", DBG["sc"], sc[:]))
                if part == "cmp":
                    if ch >= 2:
                        st.defer(topk)
                    return

                def br_items(hl, br):
                    Ob = nxt("o", O_B)
                    Ov = Ob[:, 0:260].rearrange("p (q c) -> p q c", c=65)
                    kts = list(range(max(0, q0 - 4), q0 + 4)) if br == 0 else list(range(0, q0 + 4))
                    state = {"first": True}
                    for kt in kts:
                        qlo = max(kt, q0)
                        qhi = min(kt + 4, q0 + 3) if br == 0 else q0 + 3
                        n = 128 * (qhi - qlo + 1)
                        ks = slice(kt * 128, (kt + 1) * 128)
                        qs = slice(qlo * 128, (qhi + 1) * 128)

                        def s_fn(Sb, kt=kt, qlo=qlo, qhi=qhi, n=n, ks=ks, qs=qs):
                            if br == 0:
                                k.mm(Sb[:, 0:n], KWt[:, ks], Qaug[:, hl, qs], start=True, stop=False)
                            else:
                                k.mm(Sb[:, 0:n], KSaug[:, ks], Qaug[:, hl, qs], start=True, stop=False)
                            nlo = max(kt, qlo)
                            nhi = min(kt + 1, qhi)
                            if nlo <= nhi:
                                o = (nlo - qlo) * 128
                                w_ = (nhi - nlo + 1) * 128
                                k.mm(Sb[:, o:o + w_], ident_bf[:], Tn[:, hl, (nlo - kt) * 128:(nhi - kt + 1) * 128],
                                     start=False, stop=False)
                            if br == 0 and qlo <= kt + 4 <= qhi:
                                o = (kt + 4 - qlo) * 128
                                k.mm(Sb[:, o:o + 128], ident_bf[:], anti[:], start=False, stop=False)

                        def pv_fn(pt, kt=kt, qlo=qlo, qhi=qhi):
                            V = VW if br == 0 else VS
                            for qt in range(qlo, qhi + 1):
                                o = (qt - qlo) * 128
                                k.mm(Ob[:, (qt - q0) * 65:(qt - q0 + 1) * 65], pt[:, o:o + 128], V[:, kt, :],
                                     start=state["first"], stop=False)
                                state["first"] = False

                        after = None
                        if kt == kts[-1]:
                            def after():
                                f = norm_f(Ob, 65, sig[:, q0:q0 + 4, 3 * hl + (2 if br == 0 else 1)])
                                k.tt(tmp64[:], Ov[:, :, 0:64], f.unsqueeze(2).to_broadcast([128, 4, 64]), ALU.mult)
                                k.tt(accn[:, :, hl, :], accn[:, :, hl, :], tmp64[:], ALU.add)
                        st.add(s_fn, 128, n, 0.125, pv_fn, after)
                def selmask(cs=cs):
                    bank = M_B[0]
                    for qi in range(4):
                        k.mm(bank[0:96, qi * 128:(qi + 1) * 128], negpad[:, qi, :], ident_bf[:],
                             start=(qi == 0), stop=False)
                    for hl in range(4):
                        evac(Qaug[64:96, hl, cs], bank[64:96, :])
                if part == "win":
                    for hl in range(4):
                        br_items(hl, 0)
                    if ch >= 2:
                        st.defer(selmask)
                    return
                for hl in range(4):
                    br_items(hl, 1)

                def write_ons(q0=q0):
                    k.cp(ons[:, q0:q0 + 4, 256 * g:256 * (g + 1)], accn[:].rearrange("p q h d -> p q (h d)"), eng="dve")
                st.defer(write_ons)
            do_chunk(0, "cmp")
            for ch in range(NCH):
                do_chunk(ch, "win")
                if ch + 1 < NCH:
                    do_chunk(ch + 1, "cmp")
                else:
                    st.flush()
                do_chunk(ch, "sel")
            st.flush()
        if b == 0 and "ons" in DBG:
            finals.append(k.dma("sp", DBG["ons"], ons[:]))
        k.barrier()
    print("sbuf remaining after nsa scope", nc.sbuf_bytes_remaining)


def mla_stage(k, nc, I, DBG, finals, b, L):
    hT, ons, wbuf, wview = L["hT"], L["ons"], L["wbuf"], L["wview"]
    S_B, O_B, M_B = L["S_B"], L["O_B"], L["M_B"]
    ident_bf, tri, ones_bf, invf, sgn, qg, kvg = (L[n] for n in ("ident_bf", "tri", "ones_bf", "invf", "sgn", "qg", "kvg"))
    w_inv = I["w_in"].rearrange("(kc p) n -> p kc n", p=128)
    rr = {"s": 0, "o": 0, "m": 0, "pt": 0, "ev": 0}
    PI = math.pi

    def nxt(key, lst):
        v = lst[rr[key] % len(lst)]
        rr[key] += 1
        return v

    def evac(out, in_):
        rr["ev"] += 1
        if rr["ev"] % 2:
            k.cp(out, in_, eng="act")
        else:
            k.cp(out, in_, eng="dve")

    with ExitStack() as e3:
        cqn = k.sb("cqn", [128, 2, S], BF16, e3)
        ckvn = k.sb("ckvn", [128, S], BF16, e3)
        COS2 = k.sb("COS2", [96, S], F32, e3)
        SIN2 = k.sb("SIN2", [96, S], F32, e3)
        krr = k.sb("krr", [96, S], BF16, e3)
        QmT = [k.sb(f"QmT{i}", [128, S], BF16, e3) for i in range(2)]
        KmT = [k.sb(f"KmT{i}", [128, S], BF16, e3) for i in range(2)]
        for i in range(2):
            k.memset(QmT[i][64:128, :], 0.0, eng="pool")
            k.memset(KmT[i][64:128, :], 0.0, eng="pool")
        Vh = [k.sb(f"Vh{i}", [128, NQT, 129], BF16, e3) for i in range(2)]
        sbh = [k.sb(f"sbh{i}", [128, NQT, 128], BF16, e3) for i in range(2)]
        wmb = [k.sb(f"wmb{i}", [128, 8, 128], BF16, e3) for i in range(2)]
        PTs = [k.sb(f"mpt{i}", [128, 512], BF16, e3) for i in range(3)]
        raw = [k.sb(f"raw{i}", [128, 512], F32, e3) for i in range(3)]
        sq = [k.sb(f"sq{i}", [128, 512], BF16, e3) for i in range(3)]
        rstd = k.sb("rstd", [128, 512], F32, e3)
        posi = k.sb("posi", [96, 512], I32, e3)
        ang = raw[0]
        targ = raw[1]
        t1 = k.sb("t1", [96, 512], F32, e3)
        t2 = k.sb("t2", [96, 512], F32, e3)
        satmps = [sq[0], sq[1]]
        otmp = k.sb("otmp", [128, 2, 128], F32, e3)
        rd = k.sb("mrd", [128, 8, 2], F32, e3)
        for i in range(2):
            k.memset(Vh[i][:, :, 128:129], 1.0)

        R = slice(64, 96)
        for ch in range(NCH):
            cs = slice(ch * 512, (ch + 1) * 512)
            k.dma("sp", posi[:], I["pos"][b, cs].partition_broadcast(96))
            k.cp(ang[R, :], posi[R, :])
            k.ts(ang[R, :], ang[R, :], invf[R, 0:1], None, ALU.mult)
            k.ts(targ[R, :], ang[R, :], 1.0 / (2 * PI), None, ALU.mult)
            k.cp(posi[R, :], targ[R, :])
            k.cp(targ[R, :], posi[R, :])
            k.stt(ang[R, :], targ[R, :], -2 * PI, ang[R, :], ALU.mult, ALU.add)
            k.ts(targ[R, :], ang[R, :], PI, -2 * PI, ALU.is_gt, ALU.mult)
            k.tt(ang[R, :], ang[R, :], targ[R, :], ALU.add)
            k.act(SIN2[R, cs], ang[R, :], AF.Sin)
            k.ts(SIN2[R, cs], SIN2[R, cs], sgn[R, 0:1], None, ALU.mult)
            k.ts(ang[R, :], ang[R, :], 0.5 * PI, None, ALU.add)
            k.ts(targ[R, :], ang[R, :], PI, -2 * PI, ALU.is_gt, ALU.mult)
            k.tt(ang[R, :], ang[R, :], targ[R, :], ALU.add)
            k.act(COS2[R, cs], ang[R, :], AF.Sin)

        for j in range(2):
            wt = wbuf()
            wv = wview(wt, 8, 512)
            k.dma("pool", wv, w_inv[:, :, C_MA + 512 * j:C_MA + 512 * (j + 1)])
            for qt in range(NQT):
                bank = nxt("m", M_B)
                for kc in range(8):
                    k.mm(bank[:, :], hT[:, kc, qt * 128:(qt + 1) * 128], wv[:, kc, :], start=(kc == 0), stop=(kc == 7))
                satmp = satmps[qt % 2]
                k.act(satmp[:], bank[:, :], AF.Tanh, scale=0.5)
                k.ts(satmp[:], satmp[:], 0.5, 0.5, ALU.mult, ALU.add)
                k.tt(ons[:, qt, 512 * j:512 * (j + 1)], ons[:, qt, 512 * j:512 * (j + 1)], satmp[:], ALU.mult)

        wt = wbuf()
        wv = wview(wt, 8, 576)
        k.memset(wv[:, :, 384:576], 0.0)
        k.dma("pool", wv[:, :, 0:256], w_inv[:, :, C_CQ:C_CQ + 256])
        k.dma("pool", wv[:, :, 256:384], w_inv[:, :, C_CKV:C_CKV + 128])
        k.dma("pool", wv[:, :, 448:480], w_inv[:, :, C_KR:C_KR + 32])
        k.dma("pool", wv[:, :, 544:560], w_inv[:, :, C_KR + 16:C_KR + 32])
        k.dma("pool", wv[:, :, 560:576], w_inv[:, :, C_KR:C_KR + 16])
        for ch in range(NCH):
            cs = slice(ch * 512, (ch + 1) * 512)
            for m in range(3):
                bank = nxt("m", M_B)
                for kc in range(8):
                    k.mm(bank[:, :], wv[:, kc, m * 128:(m + 1) * 128], hT[:, kc, cs], start=(kc == 0), stop=(kc == 7))
                k.act(sq[m][:], bank[:, :], AF.Square)
                k.cp(raw[m][:], bank[:, :], eng="dve")
            bank = nxt("m", M_B)
            k.mm(bank[:, :], ones_bf[:], sq[0][:], start=True, stop=False)
            k.mm(bank[:, :], ones_bf[:], sq[1][:], start=False, stop=True)
            k.act(rstd[:], bank[:, :], AF.Sqrt, bias=1e-6, scale=1.0 / 256)
            k.recip(rstd[:], rstd[:])
            for m in range(2):
                k.stt(cqn[:, m, cs], raw[m][:], qg[:, m:m + 1], rstd[:], ALU.mult, ALU.mult)
            bank = nxt("m", M_B)
            k.mm(bank[:, :], ones_bf[:], sq[2][:], start=True, stop=True)
            k.act(rstd[:], bank[:, :], AF.Sqrt, bias=1e-6, scale=1.0 / 128)
            k.recip(rstd[:], rstd[:])
            k.stt(ckvn[:, cs], raw[2][:], kvg[:, 0:1], rstd[:], ALU.mult, ALU.mult)
            bank = nxt("m", M_B)
            for kc in range(8):
                k.mm(bank[0:96, :], wv[:, kc, 384:480], hT[:, kc, cs], start=(kc == 0), stop=(kc == 7))
            k.tt(t1[R, :], bank[R, :], COS2[R, cs], ALU.mult)
            bank = nxt("m", M_B)
            for kc in range(8):
                k.mm(bank[0:96, :], wv[:, kc, 480:576], hT[:, kc, cs], start=(kc == 0), stop=(kc == 7))
            k.tt(t2[R, :], bank[R, :], SIN2[R, cs], ALU.mult)
            k.tt(krr[R, cs], t1[R, :], t2[R, :], ALU.add)

        wt = wbuf()
        wuq = wview(wt, 2, 768, 0)
        wus = wview(wt, 2, 768, 1536)
        wkv = wt[:, 3072:3072 + 1536]
        k.dma("pool", wuq, I["w_uq"].rearrange("(c p) n -> p c n", p=128))
        k.dma("pool", wus, I["w_uq_sw"].rearrange("(c p) n -> p c n", p=128))
        k.dma("pool", wkv, I["w_ukv"])
        scale = 96 ** -0.5
        def proj(h):
            Q, Kt, V, sb_, wm = QmT[h % 2], KmT[h % 2], Vh[h % 2], sbh[h % 2], wmb[h % 2]
            k.dma("pool", wm[:], w_inv[:, :, C_MB + 128 * h:C_MB + 128 * (h + 1)])
            k.cp(Kt[R, :], krr[R, :], eng="dve")
            for ch in range(NCH):
                cs = slice(ch * 512, (ch + 1) * 512)
                bankA = nxt("m", M_B)
                for kc in range(2):
                    k.mm(bankA[0:96, :], wuq[:, kc, h * 96:(h + 1) * 96], cqn[:, kc, cs], start=(kc == 0), stop=(kc == 1))
                k.cp(Q[0:64, cs], bankA[0:64, :], eng="dve")
                k.tt(t1[R, :], bankA[R, :], COS2[R, cs], ALU.mult)
                bankB = nxt("m", M_B)
                for kc in range(2):
                    k.mm(bankB[0:96, :], wus[:, kc, h * 96:(h + 1) * 96], cqn[:, kc, cs], start=(kc == 0), stop=(kc == 1))
                k.tt(t2[R, :], bankB[R, :], SIN2[R, cs], ALU.mult)
                k.tt(Q[R, cs], t1[R, :], t2[R, :], ALU.add)
                yield
                bank = nxt("m", M_B)
                k.mm(bank[0:64, :], wkv[:, h * 192:h * 192 + 64], ckvn[:, cs], start=True, stop=True)
                k.cp(Kt[0:64, cs], bank[0:64, :], eng="dve")
                yield
                bank = nxt("m", M_B)
                for qi in range(4):
                    qt = ch * 4 + qi
                    k.mm(bank[:, qi * 128:(qi + 1) * 128], ckvn[:, qt * 128:(qt + 1) * 128],
                         wkv[:, h * 192 + 64:h * 192 + 192], start=(qi == 0), stop=False)
                k.cp(V[:, ch * 4:ch * 4 + 4, 0:128], bank[:, :].rearrange("p (q c) -> p q c", c=128), eng="dve")
                yield
                bank = nxt("m", M_B)
                first = True
                for qi in range(4):
                    qt = ch * 4 + qi
                    for kc in range(8):
                        k.mm(bank[:, qi * 128:(qi + 1) * 128], hT[:, kc, qt * 128:(qt + 1) * 128], wm[:, kc, :],
                             start=first, stop=False)
                        first = False
                k.act(sb_[:, ch * 4:ch * 4 + 4, :], bank[:, :].rearrange("p (q c) -> p q c", c=128), AF.Tanh, scale=0.5)
                k.ts(sb_[:, ch * 4:ch * 4 + 4, :], sb_[:, ch * 4:ch * 4 + 4, :], 0.5, 0.5, ALU.mult, ALU.add)
                yield

        st = AttnStream(k, S_B, PTs, look=1)

        def attn(h, gen):
            Q, Kt, V, sb_ = QmT[h % 2], KmT[h % 2], Vh[h % 2], sbh[h % 2]
            cnt = [0]
            for ch in range(NCH):
                q0 = 4 * ch
                OA = O_B[(rr["o"] % 2) * 2]
                OBk = O_B[(rr["o"] % 2) * 2 + 1]
                rr["o"] += 1
                firsts = {0: True, 1: True}
                for kt in range(0, q0 + 4):
                    qlo = max(kt, q0)
                    qhi = q0 + 3
                    n = 128 * (qhi - qlo + 1)

                    def s_fn(Sb, kt=kt, qlo=qlo, qhi=qhi, n=n, q0=q0):
                        k.mm(Sb[:, 0:n], Kt[:, kt * 128:(kt + 1) * 128], Q[:, qlo * 128:(qhi + 1) * 128],
                             start=True, stop=False)
                        if kt >= q0:
                            k.mm(Sb[:, 0:128], ident_bf[:], tri[:], start=False, stop=False)

                    def pv_fn(pt, kt=kt, qlo=qlo, qhi=qhi, firsts=firsts, OA=OA, OBk=OBk, q0=q0):
                        for qt in range(qlo, qhi + 1):
                            o = (qt - qlo) * 128
                            bi = (qt - q0) // 2
                            sl = (qt - q0) % 2
                            bank = OA if bi == 0 else OBk
                            k.mm(bank[:, sl * 129:(sl + 1) * 129], pt[:, o:o + 128], V[:, kt, :],
                                 start=firsts[bi], stop=False)
                            firsts[bi] = False

                    after = None
                    if kt == q0 + 3:
                        def after(OA=OA, OBk=OBk, q0=q0):
                            for bi, bank in enumerate((OA, OBk)):
                                qa = q0 + 2 * bi
                                Ov = bank[:, 0:258].rearrange("p (q c) -> p q c", c=129)
                                i_ = rr["ev"] % 8
                                rr["ev"] += 1
                                k.ts(rd[:, i_, :], Ov[:, :, 128], 1e-30, None, ALU.max)
                                k.recip(rd[:, i_, :], rd[:, i_, :])
                                k.tt(otmp[:], Ov[:, :, 0:128], rd[:, i_, :].unsqueeze(2).to_broadcast([128, 2, 128]), ALU.mult)
                                k.tt(otmp[:], otmp[:], sb_[:, qa:qa + 2, :], ALU.mult)
                                k.tt(ons[:, qa:qa + 2, 128 * h:128 * (h + 1)], ons[:, qa:qa + 2, 128 * h:128 * (h + 1)],
                                     otmp[:], ALU.add)
                    st.add(s_fn, 128, n, scale, pv_fn, after)
                    cnt[0] += 1
                    if gen is not None and cnt[0] % 2 == 0:
                        next(gen, None)

        for _ in proj(0):
            pass
        for h in range(8):
            gen = proj(h + 1) if h + 1 < 8 else None
            attn(h, gen)
            st.flush()
            if gen is not None:
                for _ in gen:
                    pass
        if b == 0 and "y" in DBG:
            finals.append(k.dma("sp", DBG["y"], ons[:]))
        k.barrier()


def sgn_negpi(L, k):
    if "negpi" not in L["cache"]:
        t = k.sb("negpi", [128, 1], F32)
        k.memset(t[:], -math.pi)
        L["cache"]["negpi"] = t
    return L["cache"]["negpi"]


def out_stage(k, nc, I, DBG, finals, b, L):
    ons, wview, WB = L["ons"], L["wview"], L["WB"]
    PS = L["PS"]
    M_B = L["M_B"]
    ident_bf, sc2, modFM, fng_b, convw, convb = (L[n] for n in ("ident_bf", "sc2", "modFM", "fng_b", "convw", "convb"))
    gD, x1D, out = L["gD"], L["x1D"], L["out"]
    xv = I["x"]
    rr = {"b": 0, "ev": 0}

    def nb():
        v = PS[rr["b"] % 8]
        rr["b"] += 1
        return v

    def bfv(bank):
        return bank[:].bitcast(BF16)

    with ExitStack() as e4:
        h2T = L["hT"]
        gm_b, gf_b, ss = L["gm_b"], L["gf_b"], L["ss4"]
        with ExitStack() as e4a:
            yTs = [k.sb(f"yT{i}", [128, 8, 512], BF16, e4a) for i in range(2)]
            xts = [k.sb(f"x4t{i}", [128, D], F32, e4a) for i in range(4)]
            x1ts = [k.sb(f"x1t{i}", [128, D], F32, e4a) for i in range(4)]
            xns_all = [k.sb(f"x4n{i}", [128, D], BF16, e4a) for i in range(8)]
            junk = k.sb("junk4", [128, D], BF16, e4a)
            tmp = k.sb("tmp4", [128, 512], F32, e4a)
            wo = wview(WB[0], 8, 1024)
            k.dma("pool", wo, I["w_o"].rearrange("(kc p) n -> p kc n", p=128))
            ss8 = k.sb("ss8", [128, 8], F32, e4a)

            def part_a(ch):
                yT = yTs[ch % 2]
                xns = xns_all[(ch % 2) * 4:(ch % 2) * 4 + 4]
                ssv = ss8[:, (ch % 2) * 4:(ch % 2) * 4 + 4]
                for kp in range(4):
                    bank = nb()
                    bv = bfv(bank)
                    for kk in range(2):
                        kc = kp * 2 + kk
                        for qi in range(4):
                            k.tr(bv[:, (kk * 4 + qi) * 128:(kk * 4 + qi + 1) * 128],
                                 ons[:, ch * 4 + qi, kc * 128:(kc + 1) * 128], ident_bf[:])
                    k.cp(yT[:, kp * 2:kp * 2 + 2, :], bv[:, :].rearrange("p (a n) -> p a n", a=2),
                         eng=("act" if kp % 2 else "dve"))
                for qi in range(4):
                    qt = ch * 4 + qi
                    xt = xts[qt % 4]
                    x1t = x1ts[qt % 4]
                    k.dma("sp", xt[:], xv[b, qt * 128:(qt + 1) * 128, :])
                    for n in range(2):
                        bank = nb()
                        for kc in range(8):
                            k.mm(bank[:, :], yT[:, kc, qi * 128:(qi + 1) * 128], wo[:, kc, n * 512:(n + 1) * 512],
                                 start=(kc == 0), stop=(kc == 7))
                        k.tt(tmp[:], bank[:, :], gm_b[:, n * 512:(n + 1) * 512], ALU.mult)
                        k.tt(x1t[:, n * 512:(n + 1) * 512], tmp[:], xt[:, n * 512:(n + 1) * 512], ALU.add)
                    k.dma("act", x1D.ap()[b, qt * 128:(qt + 1) * 128, :], x1t[:])
                    k.act(junk[:], x1t[:], AF.Square, accum_out=ssv[:, qi:qi + 1])
                k.act(ssv, ssv, AF.Sqrt, bias=1e-6, scale=1.0 / D)
                k.recip(ssv, ssv)
                for qi in range(4):
                    k.ts(xns[qi][:], x1ts[(ch * 4 + qi) % 4][:], ssv[:, qi:qi + 1], None, ALU.mult)

            def part_b(ch):
                xns = xns_all[(ch % 2) * 4:(ch % 2) * 4 + 4]
                for kp in range(4):
                    bank = nb()
                    bv = bfv(bank)
                    for kk in range(2):
                        kc = kp * 2 + kk
                        for qi in range(4):
                            k.tr(bv[:, (kk * 4 + qi) * 128:(kk * 4 + qi + 1) * 128],
                                 xns[qi][:, kc * 128:(kc + 1) * 128], ident_bf[:])
                    for kk in range(2):
                        kc = kp * 2 + kk
                        k.act(h2T[:, kc, ch * 512:(ch + 1) * 512], bv[:, kk * 512:(kk + 1) * 512],
                              AF.Identity, bias=modFM[:, 24 + kc, b:b + 1], scale=sc2[:, kc, b:b + 1])
            part_a(0)
            for ch in range(NCH):
                if ch + 1 < NCH:
                    part_a(ch + 1)
                part_b(ch)
            k.barrier()
        L["ses2"].close()
        if b == 0 and "h2T" in DBG:
            finals.append(k.dma("sp", DBG["h2T"], h2T[:]))
        with ExitStack() as e4b:
            aT = k.sb("aT", [128, NFC, S], BF16, e4b)
            gbuf = k.sb("gbuf", [128, S + 2], F32, e4b)
            cv = k.sb("cv", [128, S], F32, e4b)
            sg = k.sb("sg", [128, S], BF16, e4b)
            k.memset(gbuf[:, 0:2], 0.0)
            wgv = I["w_gate"].rearrange("(kc p) n -> p kc n", p=128)
            wuv = I["w_up"].rearrange("(kc p) n -> p kc n", p=128)
            gi = 0
            for f0 in range(0, NFC, 4):
                nf = min(4, NFC - f0)
                wt = WB[gi % 2]
                gi += 1
                wg = wview(wt, 8, 512, 0)
                wu = wview(wt, 8, 512, 4096)
                k.dma("pool", wg[:, :, 0:nf * 128], wgv[:, :, f0 * 128:(f0 + nf) * 128])
                k.dma("pool", wu[:, :, 0:nf * 128], wuv[:, :, f0 * 128:(f0 + nf) * 128])
                for fl in range(nf):
                    fc = f0 + fl
                    for tch in range(NCH):
                        cs = slice(tch * 512, (tch + 1) * 512)
                        bank = nb()
                        for kc in range(8):
                            k.mm(bank[:, :], wg[:, kc, fl * 128:(fl + 1) * 128], h2T[:, kc, cs], start=(kc == 0), stop=(kc == 7))
                        k.cp(gbuf[:, 2 + tch * 512:2 + (tch + 1) * 512], bank[:, :], eng="act")
                    k.ts(cv[:], gbuf[:, 2:S + 2], convw[:, fc, 2:3], convb[:, fc:fc + 1], ALU.mult, ALU.add)
                    k.stt(cv[:], gbuf[:, 1:S + 1], convw[:, fc, 1:2], cv[:], ALU.mult, ALU.add)
                    k.stt(cv[:], gbuf[:, 0:S], convw[:, fc, 0:1], cv[:], ALU.mult, ALU.add)
                    k.act(sg[:], cv[:], AF.Silu)
                    for tch in range(NCH):
                        cs = slice(tch * 512, (tch + 1) * 512)
                        bank = nb()
                        for kc in range(8):
                            k.mm(bank[:, :], wu[:, kc, fl * 128:(fl + 1) * 128], h2T[:, kc, cs], start=(kc == 0), stop=(kc == 7))
                        k.tt(aT[:, fc, cs], bank[:, :], sg[:, cs], ALU.mult)
            x1r = [cv[:, i * D:(i + 1) * D] for i in range(2)]
            x2 = [gbuf[:, i * D:(i + 1) * D] for i in range(2)]
            junk = sg[:, 0:D]
            tmp = k.sb("tmp5", [128, 512], F32, e4b)
            wdv = I["w_down"].rearrange("(fc p) n -> p fc n", p=128)
            for qg_ in range(4):
                banks = [[nb() for _ in range(2)] for _ in range(4)]
                for f0 in range(0, NFC, 8):
                    nf = min(8, NFC - f0)
                    wt = WB[gi % 2]
                    gi += 1
                    wd = wview(wt, 8, 1024)
                    k.dma("pool", wd[:, 0:nf, :], wdv[:, f0:f0 + nf, :])
                    for fl in range(nf):
                        fc = f0 + fl
                        for qi in range(4):
                            qt = qg_ * 4 + qi
                            for n in range(2):
                                k.mm(banks[qi][n][:, :], aT[:, fc, qt * 128:(qt + 1) * 128], wd[:, fl, n * 512:(n + 1) * 512],
                                     start=(fc == 0), stop=(fc == NFC - 1))
                for qi in range(4):
                    qt = qg_ * 4 + qi
                    xr = x1r[qt % 2]
                    xo = x2[qt % 2]
                    k.dma("sp", xr, x1D.ap()[b, qt * 128:(qt + 1) * 128, :])
                    for n in range(2):
                        k.tt(tmp[:], banks[qi][n][:, :], gf_b[:, n * 512:(n + 1) * 512], ALU.mult)
                        k.tt(xo[:, n * 512:(n + 1) * 512], tmp[:], xr[:, n * 512:(n + 1) * 512], ALU.add)
                    k.act(junk, xo, AF.Square, accum_out=ss[:, 4 + qi:5 + qi])
                    k.act(ss[:, 4 + qi:5 + qi], ss[:, 4 + qi:5 + qi], AF.Sqrt, bias=1e-6, scale=1.0 / D)
                    k.recip(ss[:, 4 + qi:5 + qi], ss[:, 4 + qi:5 + qi])
                    k.stt(xo, xo, ss[:, 4 + qi:5 + qi], fng_b[:], ALU.mult, ALU.mult)
                    finals.append(k.dma("act", out[b, qt * 128:(qt + 1) * 128, :], xo))
            k.barrier()
        k.barrier()


def shared_inputs(inp):
    f = np.float32
    m = {}
    m["rel"] = inp["rel_bias_table"].astype(f)
    m["ada_w"] = inp["ada_w"][0]
    m["ada_bT"] = np.ascontiguousarray(inp["ada_b"][0].reshape(48, 128).T)
    m["nmg"] = np.ascontiguousarray(inp["norm_mix_g"][0].reshape(8, 128).T)
    m["w_in"] = inp["w_in"][0]
    m["posk"] = np.ascontiguousarray(inp["cmp_pos_k"][0].reshape(16, 128).T)
    m["posv"] = np.ascontiguousarray(inp["cmp_pos_v"][0].reshape(16, 128).T)
    m["w1k"] = inp["cmp_w1_k"][0]
    m["w2k"] = inp["cmp_w2_k"][0]
    m["w1v"] = inp["cmp_w1_v"][0]
    m["w2v"] = inp["cmp_w2_v"][0]
    m["qg"] = np.ascontiguousarray(inp["mla_q_norm_g"][0].reshape(2, 128).T)
    m["kvg"] = np.ascontiguousarray(inp["mla_kv_norm_g"][0].reshape(1, 128).T)
    wuq = inp["mla_w_uq"][0]
    m["w_uq"] = wuq
    sw = wuq.reshape(256, 8, 96).copy()
    sw[:, :, 64:80] = wuq.reshape(256, 8, 96)[:, :, 80:96]
    sw[:, :, 80:96] = wuq.reshape(256, 8, 96)[:, :, 64:80]
    m["w_uq_sw"] = np.ascontiguousarray(sw.reshape(256, 768))
    m["w_ukv"] = inp["mla_w_ukv"][0]
    m["w_o"] = inp["w_o"][0]
    m["nfg"] = np.ascontiguousarray(inp["norm_ffn_g"][0].reshape(8, 128).T)
    m["w_gate"] = inp["ffn_w_gate"][0]
    m["w_up"] = inp["ffn_w_up"][0]
    m["convw"] = np.ascontiguousarray(inp["ffn_conv_w"][0].reshape(3, NFC, 128).transpose(2, 1, 0))
    m["convb"] = np.ascontiguousarray(inp["ffn_conv_b"][0].reshape(NFC, 128).T)
    m["w_down"] = inp["ffn_w_down"][0]
    m["fng"] = inp["final_norm_g"]
    m.update(make_consts())
    return {k_: np.ascontiguousarray(v) for k_, v in m.items()}


_CACHE = {}


def kernel(**inputs):
    inp = {k_: np.asarray(v) for k_, v in inputs.items()}
    shared = shared_inputs(inp)
    if "nc" not in _CACHE:
        _CACHE["nc"] = build(nseq=4)[0]
    nc = _CACHE["nc"]
    in_maps = []
    for core in range(8):
        bsel = list(range(core * 4, core * 4 + 4))
        m = dict(shared)
        m["x"] = np.ascontiguousarray(inp["x"][bsel], dtype=np.float32)
        m["cT"] = np.ascontiguousarray(inp["c"][bsel].reshape(4, 8, 128).transpose(2, 1, 0), dtype=np.float32)
        m["pos"] = np.ascontiguousarray(inp["positions"][bsel], dtype=np.int32)
        in_maps.append(m)
    res = run_bass_kernel_spmd(nc, in_maps, core_ids=list(range(8)))
    out = np.concatenate([np.asarray(r["out"]) for r in res.results], axis=0)
    return out.astype(np.float32)
```
